# Optimizing a Trainium2 kernel written in Bass

```python
import math
import jax
import jax.numpy as jnp
from jax import lax
import numpy as np

D_MODEL = 1024
BATCH = 8
SEQ = 2048
DEPTH = 2

CTX_LEN = 256
GRID_W = 64
EPS = 1e-6
ROPE_BASE = 10000.0

GDN_HEADS = 4
GDN_DK = 128
GDN_DV = 128
GDN_CHUNK = 64
SHORT_CONV = 3
MLA_HEADS = 4
MLA_Q_RANK = 384
MLA_KV_RANK = 256
MLA_NOPE = 128
MLA_ROPE = 64
MLA_DV = 128
MLA_SCALE = (MLA_NOPE + MLA_ROPE) ** -0.5
Q_BLOCK = 128
RET_HEADS = 4
RET_DK = 128
RET_DV = 128
RET_CHUNK = 64
RET_DECAY_BASE = 5.0
RET_DIR_OFFSET = 0.5
D_FF = 2816
FFN_CONV = 3
N_BRANCH = 3

GDN_QK = GDN_HEADS * GDN_DK
GDN_V = GDN_HEADS * GDN_DV
MLA_OUT = MLA_HEADS * MLA_DV
RET_QK = RET_HEADS * RET_DK
RET_V = RET_HEADS * RET_DV
IN_SPLITS = (2 * GDN_QK + GDN_V, GDN_V, 4 * GDN_HEADS, MLA_Q_RANK, MLA_KV_RANK, MLA_ROPE,
             2 * RET_QK + 2 * RET_V, N_BRANCH * D_MODEL)
IN_COLS = sum(IN_SPLITS)

kernel_name = 'hybrid_gdn_mla_retention_flow_block'


def rms_norm(x, w):
    xf = x.astype(jnp.float32)
    y = xf * lax.rsqrt(jnp.mean(xf * xf, axis=-1, keepdims=True) + EPS)
    return (y * w.astype(jnp.float32)).astype(x.dtype)


def modulate(h, shift, scale):
    return h * (1 + scale) + shift


def l2norm(t):
    tf = t.astype(jnp.float32)
    return tf * lax.rsqrt(jnp.sum(tf * tf, axis=-1, keepdims=True) + EPS)


def flip(t):
    return jnp.flip(t, axis=1)


def dwconv(x, w):
    K, C = w.shape
    return lax.conv_general_dilated(x, w[:, None, :].astype(x.dtype), window_strides=(1,),
                                    padding=[(K // 2, K // 2)],
                                    dimension_numbers=('NWC', 'WIO', 'NWC'),
                                    feature_group_count=C)


def axial_rope(n, d):
    rows = n // GRID_W
    r = jnp.repeat(jnp.arange(rows, dtype=jnp.float32), GRID_W)
    col = jnp.tile(jnp.arange(GRID_W, dtype=jnp.float32), rows)
    quarter = d // 4
    inv = ROPE_BASE ** (-jnp.arange(quarter, dtype=jnp.float32) / quarter)
    ang = jnp.concatenate([r[:, None] * inv, col[:, None] * inv], axis=-1)
    return jnp.cos(ang), jnp.sin(ang)


def apply_rope(t, cos, sin):
    half = t.shape[-1] // 2
    t1, t2 = t[..., :half], t[..., half:]
    cos = cos[None, :, None, :].astype(t.dtype)
    sin = sin[None, :, None, :].astype(t.dtype)
    return jnp.concatenate([t1 * cos - t2 * sin, t1 * sin + t2 * cos], axis=-1)


def to_chunks(t, C):
    B, N, H, d = t.shape
    return t.reshape(B, N // C, C, H, d).transpose(1, 0, 3, 2, 4)


def from_chunks(t):
    nc, B, H, C, d = t.shape
    return t.transpose(1, 0, 3, 2, 4).reshape(B, nc * C, H, d)


def gated_head_norm(o, z, w):
    B, N, H, d = o.shape
    y = rms_norm(o, w).reshape(B, N, H * d)
    return (y * jax.nn.silu(z.astype(jnp.float32))).astype(z.dtype)


def gdn_chunked(q, k, v, g, beta, s0):
    f32 = jnp.float32
    dv = v.shape[-1]
    C = GDN_CHUNK
    qc = to_chunks(q.astype(f32), C)
    kc = to_chunks(k.astype(f32), C)
    vc = to_chunks(v.astype(f32), C)
    gcum = jnp.cumsum(to_chunks(g.astype(f32)[..., None], C)[..., 0], axis=-1)
    bc = to_chunks(beta.astype(f32)[..., None], C)
    lower = jnp.tril(jnp.ones((C, C), dtype=bool))
    strict = jnp.tril(jnp.ones((C, C), dtype=bool), -1)
    decay = jnp.exp(jnp.where(lower, gcum[..., :, None] - gcum[..., None, :], -jnp.inf))
    kb = kc * bc
    a_mat = jnp.eye(C, dtype=f32) + jnp.where(strict, jnp.einsum('nbhid,nbhjd->nbhij', kb, kc) * decay, 0.0)
    rhs = jnp.concatenate([vc * bc, kb * jnp.exp(gcum)[..., None]], axis=-1)
    sol = lax.linalg.triangular_solve(a_mat, rhs, left_side=True, lower=True, unit_diagonal=True)
    u, w = sol[..., :dv], sol[..., dv:]
    attn = jnp.einsum('nbhid,nbhjd->nbhij', qc, kc) * decay
    q_dec = qc * jnp.exp(gcum)[..., None]
    k_dec = kc * jnp.exp(gcum[..., -1:] - gcum)[..., None]
    c_dec = jnp.exp(gcum[..., -1])[..., None, None]

    def step(S, xs):
        u_i, w_i, a_i, qd_i, kd_i, cd_i = xs
        v_new = u_i - w_i @ S
        o_i = qd_i @ S + a_i @ v_new
        S = cd_i * S + jnp.swapaxes(kd_i, -1, -2) @ v_new
        return S, o_i

    s_final, o = lax.scan(step, s0, (u, w, attn, q_dec, k_dec, c_dec))
    return from_chunks(o), s_final


def retention_chunked(q, k, v, log_gamma, s0):
    f32 = jnp.float32
    C = RET_CHUNK
    qc = to_chunks(q.astype(f32), C)
    kc = to_chunks(k.astype(f32), C)
    vc = to_chunks(v.astype(f32), C)
    lg = log_gamma.astype(f32)
    pos = jnp.arange(C, dtype=f32)
    rel = pos[:, None] - pos[None, :]
    decay = jnp.where(rel >= 0, jnp.exp(lg[:, None, None] * jnp.maximum(rel, 0.0)), 0.0)
    o_intra = jnp.einsum('nbhij,nbhje->nbhie', jnp.einsum('nbhid,nbhjd->nbhij', qc, kc) * decay, vc)
    k_w = jnp.exp(lg[:, None] * (C - 1 - pos))[:, :, None]
    q_w = jnp.exp(lg[:, None] * (pos + 1))[:, :, None]
    c_dec = jnp.exp(lg * C)[:, None, None]
    kv = jnp.einsum('nbhjd,nbhje->nbhde', kc * k_w, vc)

    def step(S, kv_i):
        return c_dec * S + kv_i, S

    s_final, s_prev = lax.scan(step, s0, kv)
    o_inter = jnp.einsum('nbhid,nbhde->nbhie', qc * q_w, s_prev)
    return from_chunks(o_intra + o_inter), s_final


def retention_log_gamma():
    h = jnp.arange(RET_HEADS, dtype=jnp.float32)
    d = jnp.arange(2, dtype=jnp.float32)[:, None]
    return jnp.log1p(-(2.0 ** (-(RET_DECAY_BASE + h + RET_DIR_OFFSET * d))))


def softmax_attention(q, k, v):
    s = jnp.einsum('bqhd,bkhd->bhqk', q, k).astype(jnp.float32) * MLA_SCALE
    p = jax.nn.softmax(s, axis=-1).astype(v.dtype)
    return jnp.einsum('bhqk,bkhd->bqhd', p, v)


def mla_attention_latent(q, k_all, v_all):
    B, N, H, dq = q.shape
    nb = N // Q_BLOCK
    qb = q.reshape(B, nb, Q_BLOCK, H, dq).swapaxes(0, 1)
    ob = lax.map(lambda qi: softmax_attention(qi, k_all, v_all), qb)
    return ob.swapaxes(0, 1).reshape(B, N, H * v_all.shape[-1])


def gdn_mixer(qkv, z, ab, qkv_c, z_c, ab_c, conv_w, A_log, dt_bias, norm_w, ctx_out):
    f32 = jnp.float32

    def prep(qkv, ab):
        B, N = qkv.shape[:2]
        qkv = jax.nn.silu(dwconv(qkv, conv_w))
        q = l2norm(qkv[..., :GDN_QK].reshape(B, N, GDN_HEADS, GDN_DK)) * GDN_DK ** -0.5
        k = l2norm(qkv[..., GDN_QK:2 * GDN_QK].reshape(B, N, GDN_HEADS, GDN_DK))
        v = qkv[..., 2 * GDN_QK:].reshape(B, N, GDN_HEADS, GDN_DV)
        ab = ab.astype(f32).reshape(B, N, 2, 2, GDN_HEADS)
        g = -jnp.exp(A_log.astype(f32)) * jax.nn.softplus(ab[:, :, 0] + dt_bias.astype(f32))
        beta = jax.nn.sigmoid(ab[:, :, 1])
        return q, k, v, g, beta

    def pick(seqs, d, rev):
        q, k, v, g, beta = seqs
        out = (q, k, v, g[:, :, d], beta[:, :, d])
        return tuple(flip(t) for t in out) if rev else out

    lat = prep(qkv, ab)
    cx = prep(qkv_c, ab_c)
    s0 = jnp.zeros((qkv.shape[0], GDN_HEADS, GDN_DK, GDN_DV), f32)
    oc_f, sc_f = gdn_chunked(*pick(cx, 0, False), s0)
    ol_f, _ = gdn_chunked(*pick(lat, 0, False), sc_f)
    oc_b, sc_b = gdn_chunked(*pick(cx, 1, True), s0)
    ol_b, _ = gdn_chunked(*pick(lat, 1, True), sc_b)
    out = gated_head_norm(ol_f + flip(ol_b), z, norm_w)
    out_c = gated_head_norm(oc_f + flip(oc_b), z_c, norm_w) if ctx_out else None
    return out, out_c


def mla_mixer(cq, ckv, kr, cq_c, ckv_c, kr_c, q_norm, w_uq, kv_norm, w_ukv, rope, ctx_out):
    def queries(cq, rope):
        B, N = cq.shape[:2]
        q = (rms_norm(cq, q_norm) @ w_uq).reshape(B, N, MLA_HEADS, MLA_NOPE + MLA_ROPE)
        q_nope, q_rope = q[..., :MLA_NOPE], q[..., MLA_NOPE:]
        if rope is not None:
            q_rope = apply_rope(q_rope, *rope)
        return jnp.concatenate([q_nope, q_rope], axis=-1)

    def keys_values(ckv, kr, rope):
        B, N = ckv.shape[:2]
        kv = (rms_norm(ckv, kv_norm) @ w_ukv).reshape(B, N, MLA_HEADS, MLA_NOPE + MLA_DV)
        k_rope = kr[:, :, None, :]
        if rope is not None:
            k_rope = apply_rope(k_rope, *rope)
        k = jnp.concatenate([kv[..., :MLA_NOPE], jnp.broadcast_to(k_rope, (B, N, MLA_HEADS, MLA_ROPE))], axis=-1)
        return k, kv[..., MLA_NOPE:]

    k_l, v_l = keys_values(ckv, kr, rope)
    k_c, v_c = keys_values(ckv_c, kr_c, None)
    out = mla_attention_latent(queries(cq, rope), jnp.concatenate([k_l, k_c], axis=1),
                               jnp.concatenate([v_l, v_c], axis=1))
    out_c = None
    if ctx_out:
        B, M = cq_c.shape[:2]
        out_c = softmax_attention(queries(cq_c, None), k_c, v_c).reshape(B, M, MLA_OUT)
    return out, out_c


def ret_mixer(qkvg, qkvg_c, norm_w, rope, ctx_out):
    def prep(t, rope):
        B, N = t.shape[:2]
        q = t[..., :RET_QK].reshape(B, N, RET_HEADS, RET_DK)
        k = t[..., RET_QK:2 * RET_QK].reshape(B, N, RET_HEADS, RET_DK)
        v = t[..., 2 * RET_QK:2 * RET_QK + RET_V].reshape(B, N, RET_HEADS, RET_DV)
        g = t[..., 2 * RET_QK + RET_V:]
        if rope is not None:
            q = apply_rope(q, *rope)
            k = apply_rope(k, *rope)
        return q * RET_DK ** -0.5, k, v, g

    q_l, k_l, v_l, g_l = prep(qkvg, rope)
    q_c, k_c, v_c, g_c = prep(qkvg_c, None)
    lg = retention_log_gamma()
    s0 = jnp.zeros((qkvg.shape[0], RET_HEADS, RET_DK, RET_DV), jnp.float32)
    oc_f, sc_f = retention_chunked(q_c, k_c, v_c, lg[0], s0)
    ol_f, _ = retention_chunked(q_l, k_l, v_l, lg[0], sc_f)
    oc_b, sc_b = retention_chunked(flip(q_c), flip(k_c), flip(v_c), lg[1], s0)
    ol_b, _ = retention_chunked(flip(q_l), flip(k_l), flip(v_l), lg[1], sc_b)
    w = norm_w.reshape(RET_HEADS, RET_DV)
    out = gated_head_norm(ol_f + flip(ol_b), g_l, w)
    out_c = gated_head_norm(oc_f + flip(oc_b), g_c, w) if ctx_out else None
    return out, out_c


def token_mixer(h, hc, w_in, gdn_conv_w, gdn_A_log, gdn_dt_bias, gdn_norm_w, mla_q_norm, mla_w_uq,
                mla_kv_norm, mla_w_ukv, ret_norm_w, w_br_gdn, w_br_mla, w_br_ret, w_out,
                rope_mla, rope_ret, ctx_out):
    cuts = np.cumsum(IN_SPLITS)[:-1].tolist()
    p = jnp.split(h @ w_in, cuts, axis=-1)
    pc = jnp.split(hc @ w_in, cuts, axis=-1)
    o_a, oc_a = gdn_mixer(p[0], p[1], p[2], pc[0], pc[1], pc[2], gdn_conv_w, gdn_A_log, gdn_dt_bias,
                          gdn_norm_w, ctx_out)
    o_b, oc_b = mla_mixer(p[3], p[4], p[5], pc[3], pc[4], pc[5], mla_q_norm, mla_w_uq, mla_kv_norm,
                          mla_w_ukv, rope_mla, ctx_out)
    o_c, oc_c = ret_mixer(p[6], pc[6], ret_norm_w, rope_ret, ctx_out)

    def merge(oa, ob, oc, gate_logits):
        gate = jax.nn.sigmoid(gate_logits.astype(jnp.float32)).astype(oa.dtype)
        ga, gb, gc = jnp.split(gate, N_BRANCH, axis=-1)
        return (ga * (oa @ w_br_gdn) + gb * (ob @ w_br_mla) + gc * (oc @ w_br_ret)) @ w_out

    y = merge(o_a, o_b, o_c, p[7])
    yc = merge(oc_a, oc_b, oc_c, pc[7]) if ctx_out else None
    return y, yc


def conv_ffn(h, w_up, conv_w, conv_b, w_down):
    u = dwconv(h @ w_up, conv_w) + conv_b
    gate, val = jnp.split(u, 2, axis=-1)
    return (jax.nn.silu(gate) * val) @ w_down


def setup_inputs(seed: int = 0) -> dict:
    key = jax.random.key(seed)
    ks = jax.random.split(key, 27)
    it = iter(range(27))
    f32 = jnp.float32
    L = DEPTH

    def nrm(shape, std):
        return std * jax.random.normal(ks[next(it)], shape, f32)

    def gain(shape):
        return 1.0 + 0.02 * jax.random.normal(ks[next(it)], shape, f32)

    x = nrm((BATCH, SEQ, D_MODEL), 1.0)
    c = nrm((BATCH, D_MODEL), 1.0)
    ctx = nrm((BATCH, CTX_LEN, D_MODEL), 1.0)
    c_ctx = nrm((D_MODEL,), 1.0)
    ada_w = nrm((L, D_MODEL, 6 * D_MODEL), 0.5 * D_MODEL ** -0.5)
    ada_b = nrm((L, 6 * D_MODEL), 0.02)
    norm1_w = gain((L, D_MODEL))
    w_in = nrm((L, D_MODEL, IN_COLS), D_MODEL ** -0.5)
    gdn_conv_w = nrm((L, SHORT_CONV, 2 * GDN_QK + GDN_V), SHORT_CONV ** -0.5)
    gdn_A_log = jnp.log(jax.random.uniform(ks[next(it)], (L, 2, GDN_HEADS), f32, 1.0, 16.0))
    dt = jnp.exp(jax.random.uniform(ks[next(it)], (L, 2, GDN_HEADS), f32, math.log(1e-3), math.log(1e-1)))
    gdn_dt_bias = dt + jnp.log(-jnp.expm1(-dt))
    gdn_norm_w = gain((L, GDN_DV))
    mla_q_norm = gain((L, MLA_Q_RANK))
    mla_w_uq = nrm((L, MLA_Q_RANK, MLA_HEADS * (MLA_NOPE + MLA_ROPE)), MLA_Q_RANK ** -0.5)
    mla_kv_norm = gain((L, MLA_KV_RANK))
    mla_w_ukv = nrm((L, MLA_KV_RANK, MLA_HEADS * (MLA_NOPE + MLA_DV)), MLA_KV_RANK ** -0.5)
    ret_norm_w = gain((L, RET_V))
    w_br_gdn = nrm((L, GDN_V, D_MODEL), GDN_V ** -0.5)
    w_br_mla = nrm((L, MLA_OUT, D_MODEL), MLA_OUT ** -0.5)
    w_br_ret = nrm((L, RET_V, D_MODEL), RET_V ** -0.5)
    w_out = nrm((L, D_MODEL, D_MODEL), D_MODEL ** -0.5)
    norm2_w = gain((L, D_MODEL))
    ffn_w_up = nrm((L, D_MODEL, 2 * D_FF), D_MODEL ** -0.5)
    ffn_conv_w = nrm((L, FFN_CONV, 2 * D_FF), FFN_CONV ** -0.5)
    ffn_conv_b = nrm((L, 2 * D_FF), 0.02)
    ffn_w_down = nrm((L, D_FF, D_MODEL), D_FF ** -0.5)
    final_norm_w = gain((D_MODEL,))
    return {'x': x, 'c': c, 'ctx': ctx, 'c_ctx': c_ctx, 'ada_w': ada_w, 'ada_b': ada_b,
            'norm1_w': norm1_w, 'w_in': w_in, 'gdn_conv_w': gdn_conv_w, 'gdn_A_log': gdn_A_log,
            'gdn_dt_bias': gdn_dt_bias, 'gdn_norm_w': gdn_norm_w, 'mla_q_norm': mla_q_norm,
            'mla_w_uq': mla_w_uq, 'mla_kv_norm': mla_kv_norm, 'mla_w_ukv': mla_w_ukv,
            'ret_norm_w': ret_norm_w, 'w_br_gdn': w_br_gdn, 'w_br_mla': w_br_mla, 'w_br_ret': w_br_ret,
            'w_out': w_out, 'norm2_w': norm2_w, 'ffn_w_up': ffn_w_up, 'ffn_conv_w': ffn_conv_w,
            'ffn_conv_b': ffn_conv_b, 'ffn_w_down': ffn_w_down, 'final_norm_w': final_norm_w}


def reference(x, c, ctx, c_ctx, ada_w, ada_b, norm1_w, w_in, gdn_conv_w, gdn_A_log, gdn_dt_bias,
              gdn_norm_w, mla_q_norm, mla_w_uq, mla_kv_norm, mla_w_ukv, ret_norm_w, w_br_gdn,
              w_br_mla, w_br_ret, w_out, norm2_w, ffn_w_up, ffn_conv_w, ffn_conv_b, ffn_w_down,
              final_norm_w):
    n = x.shape[1]
    rope_mla = axial_rope(n, MLA_ROPE)
    rope_ret = axial_rope(n, RET_DK)
    c_act = jax.nn.silu(c)
    cc_act = jax.nn.silu(c_ctx)
    xc = ctx
    for l in range(DEPTH):
        ctx_out = l < DEPTH - 1
        mod = (c_act @ ada_w[l] + ada_b[l])[:, None, :]
        modc = cc_act @ ada_w[l] + ada_b[l]
        sh1, sc1, g1, sh2, sc2, g2 = jnp.split(mod, 6, axis=-1)
        sh1c, sc1c, g1c, sh2c, sc2c, g2c = jnp.split(modc, 6, axis=-1)
        h = modulate(rms_norm(x, norm1_w[l]), sh1, sc1)
        hc = modulate(rms_norm(xc, norm1_w[l]), sh1c, sc1c)
        y, yc = token_mixer(h, hc, w_in[l], gdn_conv_w[l], gdn_A_log[l], gdn_dt_bias[l], gdn_norm_w[l],
                            mla_q_norm[l], mla_w_uq[l], mla_kv_norm[l], mla_w_ukv[l], ret_norm_w[l],
                            w_br_gdn[l], w_br_mla[l], w_br_ret[l], w_out[l], rope_mla, rope_ret, ctx_out)
        x = x + g1 * y
        x = x + g2 * conv_ffn(modulate(rms_norm(x, norm2_w[l]), sh2, sc2), ffn_w_up[l], ffn_conv_w[l],
                              ffn_conv_b[l], ffn_w_down[l])
        if ctx_out:
            xc = xc + g1c * yc
            xc = xc + g2c * conv_ffn(modulate(rms_norm(xc, norm2_w[l]), sh2c, sc2c), ffn_w_up[l],
                                     ffn_conv_w[l], ffn_conv_b[l], ffn_w_down[l])
    return rms_norm(x, final_norm_w)
```

```python
import contextlib
import math
import os
import numpy as np
import concourse.bass as bass
import concourse.mybir as mybir
from concourse.bass_utils import run_bass_kernel_spmd

F32 = mybir.dt.float32
F32R = mybir.dt.float32r
BF16 = mybir.dt.bfloat16
AF = mybir.ActivationFunctionType
ALU = mybir.AluOpType

NCORES = 8
T = 2304
NT = 18
D = 1024
KD = 8
DFF = 2816
NJ = 22
EPS = 1e-6
GQ, GK, GV, GZ, GAB, CQ, CKV, KR, RQ, RK, RV, RG, GATE = 0, 512, 1024, 1536, 2048, 2064, 2448, 2704, 2768, 3280, 3792, 4304, 4816
MLA_SCALE = 192 ** -0.5
NEGBIG = -30000.0
GDN_WARM = 1


class Prog:
    def __init__(self, nc, n_dma_sems=8):
        self.nc = nc
        self.ops = []
        self.last_w = {}
        self.readers = {}
        self.n_dma_sems = n_dma_sems
        self.warm = 0
        self.dummy = None

    def op(self, eng, fn, reads=(), writes=(), dma=False, barrier=False):
        idx = len(self.ops)
        deps = {}
        reads = list(reads)
        writes = list(writes)
        if barrier:
            writes.append("__phase")
        else:
            reads.append("__phase")
        for k in reads:
            w = self.last_w.get(k)
            if w is not None:
                deps[w] = "raw"
        for k in writes:
            w = self.last_w.get(k)
            if w is not None and w not in deps:
                deps[w] = "waw"
            for r in self.readers.get(k, ()):
                if r not in deps:
                    deps[r] = "war"
        for k in reads:
            self.readers.setdefault(k, []).append(idx)
        for k in writes:
            self.last_w[k] = idx
            self.readers[k] = []
        self.ops.append(dict(eng=eng, fn=fn, deps=deps, dma=dma, barrier=barrier, warm=(self.warm if eng == "pe" else 0)))
        return idx

    def barrier(self):
        self.op("dve", lambda e: e.nop(), barrier=True)

    def dma(self, q, out, in_, reads=(), writes=()):
        return self.op(q, lambda e: e.dma_start(out=out, in_=in_), reads, writes, dma=True)

    def emit(self, final_reads=()):
        nc = self.nc
        ops = self.ops
        self.op("sp", lambda e: e.nop(), reads=final_reads)
        n = len(ops)
        pos = [0] * n
        cnt = {}
        for i, o in enumerate(ops):
            c = cnt.get(o["eng"], 0)
            pos[i] = c
            cnt[o["eng"]] = c + 1
        waited_pos = {}
        waited_dma = {}
        need = [[] for _ in range(n)]
        signaling = [False] * n
        for i, o in enumerate(ops):
            E = o["eng"]
            for d in sorted(o["deps"]):
                kind = o["deps"][d]
                od = ops[d]
                F = od["eng"]
                if od["dma"]:
                    s = waited_dma.setdefault(E, set())
                    if d in s:
                        continue
                    s.add(d)
                    need[i].append(d)
                    signaling[d] = True
                else:
                    if F == E and not o["dma"] and not o["barrier"]:
                        if E == "pe":
                            continue
                    if pos[d] <= waited_pos.get((E, F), -1):
                        continue
                    waited_pos[(E, F)] = pos[d]
                    need[i].append(d)
                    signaling[d] = True
        engs = sorted(cnt.keys())
        self.stats = dict(cnt)
        with contextlib.ExitStack() as st:
            esem = {E: st.enter_context(nc.semaphore("s_" + E)) for E in engs}
            dsem = {}
            for E in engs:
                if any(o["dma"] and o["eng"] == E for o in ops):
                    dsem[E] = [st.enter_context(nc.semaphore("d_%s_%d" % (E, j))) for j in range(self.n_dma_sems)]
            ev = [None] * n
            ecount = {E: 0 for E in engs}
            dcount = {E: [0] * self.n_dma_sems for E in dsem}
            dnext = {E: 0 for E in dsem}
            for i, o in enumerate(ops):
                E = o["eng"]
                if o["dma"]:
                    j = dnext[E]
                    dnext[E] = (j + 1) % self.n_dma_sems
                    dcount[E][j] += 1
                    ev[i] = (dsem[E][j], 16 * dcount[E][j])
                    o["dslot"] = j
                    o["dprev"] = 16 * (dcount[E][j] - 1)
                elif signaling[i]:
                    ecount[E] += 1
                    ev[i] = (esem[E], ecount[E])
            for E in engs:
                assert ecount[E] < 60000, (E, ecount[E])
            blk = st.enter_context(nc.Block())
            handles = dict(pe=blk.tensor, act=blk.scalar, dve=blk.vector, pool=blk.gpsimd, sp=blk.sync)
            nw = [0]
            for E in engs:
                my = [i for i in range(n) if ops[i]["eng"] == E]

                def body(e, my=my, E=E):
                    dwaited = [0] * self.n_dma_sems
                    for i in my:
                        o = ops[i]
                        if o["warm"] and need[i]:
                            for _ in range(o["warm"]):
                                self.dummy(e)
                        for d in need[i]:
                            s, v = ev[d]
                            e.wait_ge(s, v)
                            nw[0] += 1
                        if o["dma"]:
                            j = o["dslot"]
                            if o["dprev"] > dwaited[j]:
                                e.wait_ge(dsem[E][j], o["dprev"])
                                dwaited[j] = o["dprev"]
                                nw[0] += 1
                        ins = o["fn"](e)
                        if ev[i] is not None:
                            s, v = ev[i]
                            ins.then_inc(s, 16 if o["dma"] else 1)
                handles[E](body)
            self.stats["waits"] = nw[0]
            self.stats["signals"] = dict(ecount)


class Rot:
    def __init__(self, name, tiles):
        self.name = name
        self.tiles = tiles
        self.i = 0

    def get(self):
        j = self.i % len(self.tiles)
        self.i += 1
        kf = getattr(self, "keyfn", None)
        return self.tiles[j], (kf(j) if kf else (self.name, j))


class B:
    def __init__(self, nc, dbg):
        self.nc = nc
        self.P = Prog(nc)
        self.dbg = dbg
        self.dbg_keys = []

    def sb(self, st, name, shape, dt):
        self.uid = getattr(self, "uid", 0) + 1
        return st.enter_context(self.nc.sbuf_tensor("%s_u%d" % (name, self.uid), list(shape), dt))

    def rot(self, st, name, shape, dt, n):
        return Rot(name, [self.sb(st, "%s%d" % (name, i), shape, dt) for i in range(n)])

    def mm(self, out, pairs, reads, wkey):
        def fn(e):
            m = len(pairs)
            ins = None
            for i, (l, r) in enumerate(pairs):
                ins = e.matmul(out, lhsT=l, rhs=r, start=(i == 0), stop=(i == m - 1))
            return ins
        self.P.op("pe", fn, reads=reads, writes=[wkey])

    def mm_acc(self, out, l, r, start, stop, reads, wkey):
        self.P.op("pe", lambda e: e.matmul(out, lhsT=l, rhs=r, start=start, stop=stop), reads=reads, writes=[wkey])

    def tr(self, out, in_, ident, reads, wkey):
        self.P.op("pe", lambda e: e.transpose(out, in_, ident), reads=reads, writes=[wkey])


def tile_stream(t):
    return 0 if t < 2 else 1


def build(dbg=None, stop_after=None):
    dbg = dbg or ()
    nc = bass.Bass("TRN2", target_bir_lowering=False)
    dram_in = lambda name, shape: nc.dram_tensor(name, list(shape), F32, kind="ExternalInput").ap()
    xin = dram_in("xin", [T, D])
    crep = dram_in("crep", [2, 128, KD * 128])
    ada_w = dram_in("ada_w", [2, D, 6 * D])
    ada_b = dram_in("ada_b", [2, 6 * D])
    w_in = dram_in("w_in", [2, D, 7888])
    w_uq = dram_in("mla_w_uq", [2, 384, 768])
    w_ukv = dram_in("mla_w_ukv", [2, 256, 1024])
    w_br = [dram_in(n, [2, 512, D]) for n in ("w_br_gdn", "w_br_mla", "w_br_ret")]
    w_out = dram_in("w_out", [2, D, D])
    w_up = dram_in("ffn_w_up", [2, D, 2 * DFF])
    w_down = dram_in("ffn_w_down", [2, DFF, D])
    NSM = 2 * 8 * 2 + 2 * 12 * 3 + 2 * 44 * 3 + 2 * 44 + 2 * 3 + 2 * 2
    smallpp = dram_in("smallpp", [128, NSM])
    NROW = 2 * 128 + 2 * 512 + 2 * 8 + 2 * 8 + 1024
    rows = dram_in("rows", [128, NROW])
    NC32 = 128 * 6 + 8
    consts = dram_in("consts", [128, NC32])
    ropem = dram_in("ropem", [2, 64, T])
    roper = dram_in("roper", [2, 128, T])
    rmats = dram_in("rmats", [128, 192])
    retc = dram_in("retc", [128, 8 * 128 * 2 + 8])
    out = nc.dram_tensor("out", [2048, D], F32, kind="ExternalOutput").ap()
    xs = nc.dram_tensor("xs", [T, D], F32).ap()
    oT_s = [nc.dram_tensor("oT%d" % b, [4, 128, T], BF16).ap() for b in range(3)]
    gates_s = nc.dram_tensor("gates_s", [T, 3 * D], BF16).ap()
    aT_s = nc.dram_tensor("aT_s", [NT, 128, NJ, 128], BF16).ap()
    dbg_out = {}
    for name, shape, dt in dbg:
        dbg_out[name] = nc.dram_tensor("dbg_" + name, list(shape), dt, kind="ExternalOutput").ap()

    bld = B(nc, dbg_out)
    P = bld.P
    with contextlib.ExitStack() as top:
        sb = lambda name, shape, dt, st=top: bld.sb(st, name, shape, dt)
        HT = sb("HT", [128, KD, T], BF16)
        c32 = sb("c32", [128, NC32], F32)
        ident32 = c32[:, 0:128]
        ones32 = c32[:, 128:256]
        Umask = [c32[:, 256:384], c32[:, 384:512]]
        NEGS = [c32[:, 512:640], c32[:, 640:768]]
        e0 = c32[:, 768:769]
        epsc = lambda i: c32[:, 769 + i:770 + i]
        identb = sb("identb", [128, 128], BF16)
        onesb = sb("onesb", [128, 128], BF16)
        onesr = sb("onesr", [128, 128], F32R)
        rm32 = sb("rm32", [128, 192], F32)
        rmb = sb("rmb", [128, 192], BF16)
        spp = sb("spp", [128, NSM], F32)
        rws = sb("rws", [128, NROW], F32)
        crs = sb("crs", [128, 2, KD * 128], F32)
        grow = sb("grow", [128, 2, 2, D], F32)
        modpp = sb("modpp", [128, 64], F32)
        AB = sb("ABpp", [128, 2, 2, 2, KD], F32)
        banks = [top.enter_context(nc.psum_tensor("bank%d" % i, [128, 512], F32)) for i in range(8)]
        PSB = Rot("psb", banks[0:3])
        jw = sb("jw", [128, 128], BF16)
        jr = sb("jr", [128, 512], BF16)
        P.op("pool", lambda e: e.memset(jw[:], 0.25), writes=["jw"])
        P.op("pool", lambda e: e.memset(jr[:], 0.5), writes=["jr"])
        P.dummy = lambda e: e.matmul(banks[3][:, 0:512], lhsT=jw[:], rhs=jr[:], start=True, stop=True)
        PSS = Rot("pss", [banks[4 + i % 3][:, ((i // 3) % 4) * 128:((i // 3) % 4 + 1) * 128] for i in range(12)])
        PSS.keyfn = lambda j: ("pssbank", j % 3)
        PSH = Rot("psh", [banks[4 + i % 3][:, ((i // 3) % 2) * 256:((i // 3) % 2 + 1) * 256] for i in range(6)])
        PSH.keyfn = lambda j: ("pssbank", j % 3)
        PSX = banks[7]

        o = 0
        def take(n):
            nonlocal o
            v = (o, o + n)
            o += n
            return v
        r_n1 = take(16); r_n2 = take(16); r_gc = take(72); r_fc = take(264); r_fb = take(88); r_qn = take(6); r_kvn = take(4)
        n1w = lambda l: spp[:, r_n1[0] + l * 8: r_n1[0] + l * 8 + 8]
        n2w = lambda l: spp[:, r_n2[0] + l * 8: r_n2[0] + l * 8 + 8]
        gconv = lambda l, ch, k: spp[:, r_gc[0] + (l * 12 + ch) * 3 + k: r_gc[0] + (l * 12 + ch) * 3 + k + 1]
        fconv = lambda l, ch, k: spp[:, r_fc[0] + (l * 44 + ch) * 3 + k: r_fc[0] + (l * 44 + ch) * 3 + k + 1]
        fconvb = lambda l, ch: spp[:, r_fb[0] + l * 44 + ch: r_fb[0] + l * 44 + ch + 1]
        qn = lambda l, k: spp[:, r_qn[0] + l * 3 + k: r_qn[0] + l * 3 + k + 1]
        kvn = lambda l, k: spp[:, r_kvn[0] + l * 2 + k: r_kvn[0] + l * 2 + k + 1]
        gnw = lambda l: rws[:, l * 128:(l + 1) * 128]
        rnw = lambda l, h: rws[:, 256 + l * 512 + h * 128: 256 + l * 512 + (h + 1) * 128]
        alog = lambda l: rws[:, 1280 + l * 8: 1280 + l * 8 + 8]
        dtb = lambda l: rws[:, 1296 + l * 8: 1296 + l * 8 + 8]
        fnw = rws[:, 1312:1312 + 1024]

        P.dma("sp", c32[:], consts, writes=["c32"])
        P.dma("sp", rm32[:], rmats, writes=["rm32"])
        P.dma("sp", spp[:], smallpp, writes=["spp"])
        P.dma("sp", rws[:], rows, writes=["rws"])
        for s in range(2):
            P.dma("sp", crs[:, s, :], crep[s], writes=[("crs", s)])
        P.op("dve", lambda e: e.tensor_copy(identb[:], ident32), reads=["c32"], writes=["identb"])
        P.op("dve", lambda e: e.tensor_copy(onesb[:], ones32), reads=["c32"], writes=["onesb"])
        P.op("dve", lambda e: e.tensor_copy(onesr[:], ones32), reads=["c32"], writes=["onesr"])
        P.op("dve", lambda e: e.tensor_copy(rmb[:], rm32[:]), reads=["rm32"], writes=["rmb"])
        for s in range(2):
            P.op("act", lambda e, s=s: e.activation(crs[:, s, :], crs[:, s, :], AF.Silu), reads=[("crs", s)], writes=[("crs", s)])
        cst = ["c32", "identb", "onesb", "onesr", "rmb", "spp", "rws"]

        def dump(name, src_ap, reads):
            if name in dbg_out:
                P.dma("sp", dbg_out[name], src_ap, reads=reads, writes=[("dbg", name)])
                bld.dbg_keys.append(("dbg", name))

        def phase_mod(l):
            with contextlib.ExitStack() as st:
                wbuf = bld.rot(st, "adaw", [128, KD, 512], F32, 2)
                bbuf = bld.rot(st, "adab", [1, 512], F32, 2)
                rowt = bld.rot(st, "modrow", [128, 512], F32, 2)
                pp_ps, pp_key = PSX, "psx"
                for nb in range(12):
                    wt, wk = wbuf.get()
                    bt, bk = bbuf.get()
                    P.dma("sp", wt[:], ada_w[l][:, nb * 512:(nb + 1) * 512].rearrange("(k p) c -> p k c", p=128), writes=[wk])
                    P.dma("sp", bt[:], ada_b[l:l + 1, nb * 512:(nb + 1) * 512], writes=[bk])
                    vec = nb // 2
                    half = nb % 2
                    for s in range(2):
                        ps, pk = PSB.get()
                        pairs = [(crs[:, s, k * 128:(k + 1) * 128], wt[:, k, :]) for k in range(KD)]
                        pairs.append((ones32[0:1, :], bt[0:1, :]))
                        bld.mm(ps[:], pairs, [wk, bk, ("crs", s), "c32"], pk)
                        if vec in (2, 5):
                            dst = grow[:, s, 0 if vec == 2 else 1, half * 512:(half + 1) * 512]
                            P.op("act", lambda e, dst=dst, ps=ps: e.copy(dst, ps[:]), reads=[pk], writes=[("grow", s, vec, half)])
                        else:
                            rt, rk = rowt.get()
                            P.op("dve", lambda e, rt=rt, ps=ps: e.tensor_copy(rt[:], ps[:]), reads=[pk], writes=[rk])
                            vi = {0: 0, 1: 1, 3: 2, 4: 3}[vec]
                            for c4 in range(4):
                                col = s * 32 + vi * 8 + half * 4 + c4
                                bld.mm(pp_ps[:, col:col + 1], [(rt[:, c4 * 128:(c4 + 1) * 128], e0)], [rk, "c32"], pp_key)
                P.op("dve", lambda e: e.tensor_copy(modpp[:], pp_ps[:, 0:64]), reads=[pp_key], writes=["modpp"])
                for s in range(2):
                    for nrm in range(2):
                        sh = modpp[:, s * 32 + (2 * nrm) * 8: s * 32 + (2 * nrm) * 8 + 8]
                        sc = modpp[:, s * 32 + (2 * nrm + 1) * 8: s * 32 + (2 * nrm + 1) * 8 + 8]
                        nw = n1w(l) if nrm == 0 else n2w(l)
                        P.op("dve", lambda e, sc=sc, nw=nw, s=s, nrm=nrm: e.scalar_tensor_tensor(out=AB[:, s, nrm, 0, :], in0=sc, scalar=1.0, in1=nw, op0=ALU.add, op1=ALU.mult),
                             reads=["modpp", "spp"], writes=[("AB", s, nrm, 0)])
                        P.op("dve", lambda e, s=s, nrm=nrm: e.tensor_scalar(out=AB[:, s, nrm, 0, :], in0=AB[:, s, nrm, 0, :], scalar1=float(math.sqrt(D)), scalar2=None, op0=ALU.mult),
                             reads=[("AB", s, nrm, 0)], writes=[("AB", s, nrm, 0)])
                        P.op("dve", lambda e, sh=sh, s=s, nrm=nrm: e.tensor_copy(AB[:, s, nrm, 1, :], sh), reads=["modpp"], writes=[("AB", s, nrm, 1)])
                P.barrier()

        def phase_norm(l, nrm, xsrc, tiles):
            with contextlib.ExitStack() as st:
                xb = bld.rot(st, "nx", [128, D], F32, 3)
                junk = bld.sb(st, "njunk", [128, D], BF16)
                xn = bld.rot(st, "nxn", [128, D], BF16, 8)
                ssb = bld.rot(st, "nss", [128, 1], F32, 8)
                groups = []
                cur = []
                for t in tiles:
                    cur.append(t)
                    if len(cur) == 4:
                        groups.append(cur); cur = []
                if cur:
                    groups.append(cur)
                for grp in groups:
                    xns = []
                    for t in grp:
                        xt, xk = xb.get()
                        P.dma("sp", xt[:], xsrc[t * 128:(t + 1) * 128, :], reads=[("xs", t)], writes=[xk])
                        ss, sk = ssb.get()
                        P.op("pool", lambda e, ss=ss: e.memset(ss[:], 0.0), writes=[sk])
                        P.op("dve", lambda e, xt=xt, ss=ss: e.scalar_tensor_tensor(out=junk[:], in0=xt[:], scalar=1.0, in1=xt[:], op0=ALU.mult, op1=ALU.mult, accum_out=ss[:]), reads=[xk, sk], writes=["njunk", sk])
                        P.op("act", lambda e, ss=ss: e.activation(ss[:], ss[:], AF.Sqrt, bias=epsc(0)), reads=[sk, "c32"], writes=[sk])
                        P.op("dve", lambda e, ss=ss: e.reciprocal(ss[:], ss[:]), reads=[sk], writes=[sk])
                        xnt, xnk = xn.get()
                        P.op("act", lambda e, xnt=xnt, xt=xt, ss=ss: e.activation(xnt[:], xt[:], AF.Copy, scale=ss[:]), reads=[xk, sk], writes=[xnk])
                        xns.append((t, xnt, xnk))
                    for k in range(KD):
                        ps, pk = PSB.get()
                        psv = ps[:].bitcast(BF16)
                        for i, (t, xnt, xnk) in enumerate(xns):
                            bld.tr(psv[:, i * 128:(i + 1) * 128], xnt[:, k * 128:(k + 1) * 128], identb[:], [xnk, "identb"], pk)
                        i = 0
                        while i < len(xns):
                            s = tile_stream(xns[i][0])
                            j = i
                            while j < len(xns) and tile_stream(xns[j][0]) == s:
                                j += 1
                            t0 = xns[i][0]
                            dst = HT[:, k, t0 * 128:(t0 + (j - i)) * 128]
                            src = psv[:, i * 128:j * 128]
                            wr = [("HT", tt) for tt in range(t0, t0 + (j - i))]
                            a_ap = AB[:, s, nrm, 0, k:k + 1]
                            b_ap = AB[:, s, nrm, 1, k:k + 1]
                            if k % 2 == 0:
                                P.op("act", lambda e, dst=dst, src=src, a_ap=a_ap, b_ap=b_ap: e.activation(dst, src, AF.Identity, bias=b_ap, scale=a_ap),
                                     reads=[pk, ("AB", s, nrm, 0), ("AB", s, nrm, 1)], writes=wr)
                            else:
                                P.op("dve", lambda e, dst=dst, src=src, a_ap=a_ap, b_ap=b_ap: e.tensor_scalar(out=dst, in0=src, scalar1=a_ap, scalar2=b_ap, op0=ALU.mult, op1=ALU.add),
                                     reads=[pk, ("AB", s, nrm, 0), ("AB", s, nrm, 1)], writes=wr)
                            i = j
                P.barrier()

        HTk = lambda ts: [("HT", t) for t in ts]
        ALLT = list(range(NT))

        def load_w(rotw, src2d, reads=()):
            wt, wk = rotw.get()
            P.dma("pool", wt[:], src2d.rearrange("(k p) c -> p k c", p=128), reads=reads, writes=[wk])
            return wt, wk

        BLKS = [(0, 512), (512, 512), (1024, 512), (1536, 512), (2048, 256)]

        def proj_fm(wt, wk, rhs_of, nk, evac, m=128, blks=BLKS, extra_reads=(), coff=0):
            for bi, (t0, n) in enumerate(blks):
                ps, pk = PSB.get()
                pairs = [(wt[:, k, coff:coff + m], rhs_of(k, t0, n)) for k in range(nk)]
                tl = list(range(t0 // 128, (t0 + n) // 128))
                bld.mm(ps[0:m, 0:n], pairs, [wk] + HTk(tl) + list(extra_reads), pk)
                evac(bi, t0, n, ps, pk)

        hT_rhs = lambda k, t0, n: HT[:, k, t0:t0 + n]

        def head_out_norm(st, l, h, Oacc, zs, nrot, b_idx, tiles, tag):
            oTh = bld.sb(st, tag + "oTh", [128, T], BF16)
            ssq = bld.sb(st, tag + "ssq", [128, NT], F32)
            junk = bld.sb(st, tag + "junk", [128, 128], BF16)
            yb = bld.rot(st, tag + "yb", [128, 128], BF16, 4)
            for t in tiles:
                P.op("act", lambda e, t=t: e.activation(junk[:], Oacc[:, t, :], AF.Square, accum_out=ssq[:, t:t + 1]), reads=[(tag + "O", t)], writes=[tag + "junk", (tag + "ssq", t)])
            t0, t1 = tiles[0], tiles[-1] + 1
            P.op("act", lambda e: e.activation(ssq[:, t0:t1], ssq[:, t0:t1], AF.Sqrt, bias=epsc(2), scale=1.0 / 128.0), reads=[(tag + "ssq", t) for t in tiles] + ["c32"], writes=[tag + "ssqall"])
            P.op("dve", lambda e: e.reciprocal(ssq[:, t0:t1], ssq[:, t0:t1]), reads=[tag + "ssqall"], writes=[tag + "ssqall"])
            grp = [tiles[i:i + 4] for i in range(0, len(tiles), 4)]
            for g in grp:
                ps, pk = PSB.get()
                psv = ps[:].bitcast(BF16)
                for i, t in enumerate(g):
                    y, yk = yb.get()
                    P.op("dve", lambda e, y=y, t=t: e.scalar_tensor_tensor(out=y[:], in0=Oacc[:, t, :], scalar=ssq[:, t:t + 1], in1=zs[:, t, :], op0=ALU.mult, op1=ALU.mult),
                         reads=[(tag + "O", t), tag + "ssqall", (tag + "zs", t)], writes=[yk])
                    bld.tr(psv[:, i * 128:(i + 1) * 128], y[:], identb[:], [yk, "identb"], pk)
                n = len(g) * 128
                P.op("act", lambda e, g=g, n=n, psv=psv: e.copy(oTh[:, g[0] * 128:g[0] * 128 + n], psv[:, 0:n]), reads=[pk], writes=[(tag + "oTh", t) for t in g])
            c0 = tiles[0] * 128
            P.dma("sp", oT_s[b_idx][h][:, c0:T], oTh[:, c0:T], reads=[(tag + "oTh", t) for t in tiles], writes=[("oTs", b_idx, h)])

        def phase_gdn(l, tiles):
            with contextlib.ExitStack() as st:
                wrot = bld.rot(st, "gw", [128, KD, 128], BF16, 3)
                wab = bld.sb(st, "gwab", [128, KD, 16], BF16)
                gbeta = bld.sb(st, "gbeta", [128, NT, 16], F32)
                tmp8 = bld.sb(st, "gtmp8", [128, NT, 8], F32)
                P.dma("pool", wab[:], w_in[l][:, GAB:GAB + 16].rearrange("(k p) c -> p k c", p=128), writes=["gwab"])
                ab_ps, ab_key = PSX, "psx"
                for t in ALLT:
                    bld.mm(ab_ps[:, t * 16:(t + 1) * 16], [(HT[:, k, t * 128:(t + 1) * 128], wab[:, k, :]) for k in range(KD)], ["gwab", ("HT", t)], ab_key)
                abv = ab_ps[:, 0:NT * 16].rearrange("p (t c) -> p t c", c=16)
                for t in ALLT:
                    P.op("dve", lambda e, t=t: e.tensor_tensor(out=tmp8[:, t, :], in0=abv[:, t, 0:8], in1=dtb(l), op=ALU.add), reads=[ab_key, "rws"], writes=[("gtmp8", t)])
                al = bld.sb(st, "galog", [128, 8], F32)
                P.op("act", lambda e: e.activation(al[:], alog(l), AF.Exp), reads=["rws"], writes=["galog"])
                t8all = [("gtmp8", t) for t in ALLT]
                P.op("act", lambda e: e.activation(tmp8[:], tmp8[:], AF.Exp), reads=t8all, writes=["gtmp8all"])
                P.op("act", lambda e: e.activation(tmp8[:], tmp8[:], AF.Ln, bias=1.0), reads=["gtmp8all"], writes=["gtmp8all"])
                for t in ALLT:
                    P.op("dve", lambda e, t=t: e.scalar_tensor_tensor(out=gbeta[:, t, 0:8], in0=tmp8[:, t, :], scalar=-1.0, in1=al[:], op0=ALU.mult, op1=ALU.mult),
                         reads=["gtmp8all", "galog"], writes=[("gb_g", t)])
                P.op("act", lambda e: e.activation(gbeta[:, :, 8:16], abv[:, :, 8:16], AF.Sigmoid), reads=[ab_key], writes=["gb_beta"])
                gbk = [("gb_g", t) for t in ALLT] + ["gb_beta"]
                P.barrier()
                for h in range(4):
                    with contextlib.ExitStack() as sh:
                        gdn_head(sh, l, h, tiles, wrot, gbeta)
                    P.barrier()

        def gdn_head(st, l, h, tiles, wrot, gbeta):
            sbh = lambda name, shape, dt: bld.sb(st, name, shape, dt)
            W = T + 3
            off = lambda t0: t0 + 1 if t0 < 256 else t0 + 2
            raw = bld.rot(st, "graw", [128, W], F32, 2)
            cv = bld.rot(st, "gcv", [128, W], F32, 2)
            qT = sbh("gqT", [128, T], BF16)
            kT = sbh("gkT", [128, T], BF16)
            vT = sbh("gvT", [128, T], BF16)
            ktok = sbh("gktok", [128, NT, 128], BF16)
            vtok = sbh("gvtok", [128, NT, 128], BF16)
            zs = sbh("gzs", [128, NT, 128], F32)
            Oacc = sbh("gO", [128, NT, 128], F32)
            sqr = bld.rot(st, "gsqr", [128, 512], F32R, 2)
            rnb = bld.rot(st, "grnb", [128, 512], F32, 2)
            for fi, (c0, dst) in enumerate(((GQ, qT), (GK, kT), (GV, vT))):
                ch = fi * 4 + h
                wt, wk = load_w(wrot, w_in[l][:, c0 + h * 128: c0 + (h + 1) * 128])
                rw, rk = raw.get()
                P.op("pool", lambda e, rw=rw: e.memset(rw[:], 0.0), writes=[rk])
                def ev(bi, t0, n, ps, pk, rw=rw, rk=rk):
                    o_ = off(t0)
                    if t0 == 0:
                        P.op("act", lambda e: e.copy(rw[:, 1:257], ps[:, 0:256]), reads=[pk], writes=[rk])
                        P.op("dve", lambda e: e.tensor_copy(rw[:, 258:514], ps[:, 256:512]), reads=[pk], writes=[rk])
                    else:
                        eng = "act" if bi % 2 else "dve"
                        if eng == "act":
                            P.op("act", lambda e: e.copy(rw[:, o_:o_ + n], ps[:, 0:n]), reads=[pk], writes=[rk])
                        else:
                            P.op("dve", lambda e: e.tensor_copy(rw[:, o_:o_ + n], ps[:, 0:n]), reads=[pk], writes=[rk])
                proj_fm(wt, wk, hT_rhs, KD, ev)
                c, ck = cv.get()
                P.op("act", lambda e, c=c, rw=rw, ch=ch: e.activation(c[:, 1:W - 1], rw[:, 1:W - 1], AF.Copy, scale=gconv(l, ch, 1)), reads=[rk, "spp"], writes=[ck])
                P.op("dve", lambda e, c=c, rw=rw, ch=ch: e.scalar_tensor_tensor(out=c[:, 1:W - 1], in0=rw[:, 0:W - 2], scalar=gconv(l, ch, 0), in1=c[:, 1:W - 1], op0=ALU.mult, op1=ALU.add), reads=[rk, ck, "spp"], writes=[ck])
                P.op("dve", lambda e, c=c, rw=rw, ch=ch: e.scalar_tensor_tensor(out=c[:, 1:W - 1], in0=rw[:, 2:W], scalar=gconv(l, ch, 2), in1=c[:, 1:W - 1], op0=ALU.mult, op1=ALU.add), reads=[rk, ck, "spp"], writes=[ck])
                P.op("act", lambda e, c=c: e.activation(c[:, 1:W - 1], c[:, 1:W - 1], AF.Silu), reads=[ck], writes=[ck])
                if fi == 2:
                    P.op("dve", lambda e, c=c: e.tensor_copy(vT[:, 0:256], c[:, 1:257]), reads=[ck], writes=[("gT", 2, 0)])
                    P.op("dve", lambda e, c=c: e.tensor_copy(vT[:, 256:T], c[:, 258:W - 1]), reads=[ck], writes=[("gT", 2, 1)])
                else:
                    for bi, (t0, n) in enumerate(BLKS):
                        segs = [(0, 256), (256, 256)] if t0 == 0 else [(t0, n)]
                        sq, sqk = sqr.get()
                        for (s0, sn) in segs:
                            P.op("act", lambda e, c=c, s0=s0, sn=sn, sq=sq, t0=t0: e.activation(sq[:, s0 - t0:s0 - t0 + sn], c[:, off(s0):off(s0) + sn], AF.Square), reads=[ck], writes=[sqk])
                        ps, pk = PSB.get()
                        bld.mm(ps[:, 0:n], [(onesr[:], sq[:, 0:n])], [sqk, "onesr"], pk)
                        rn, rnk = rnb.get()
                        P.op("act", lambda e, rn=rn, ps=ps, n=n: e.activation(rn[:, 0:n], ps[:, 0:n], AF.Sqrt, bias=epsc(2)), reads=[pk, "c32"], writes=[rnk])
                        P.op("dve", lambda e, rn=rn, n=n: e.reciprocal(rn[:, 0:n], rn[:, 0:n]), reads=[rnk], writes=[rnk])
                        scl = float(128 ** -0.5) if fi == 0 else 1.0
                        for (s0, sn) in segs:
                            P.op("dve", lambda e, c=c, s0=s0, sn=sn, rn=rn, t0=t0, dst=dst, scl=scl: e.scalar_tensor_tensor(out=dst[:, s0:s0 + sn], in0=c[:, off(s0):off(s0) + sn], scalar=scl, in1=rn[:, s0 - t0:s0 - t0 + sn], op0=ALU.mult, op1=ALU.mult),
                                 reads=[ck, rnk], writes=[("gT", fi, s0)])
            gTk = lambda fi: [("gT", fi, s0) for s0 in (0, 256, 512, 1024, 1536, 2048)] + [("gT", 2, 0), ("gT", 2, 1)]
            for (src, dstt, fi, nm) in ((kT, ktok, 1, "gktok"), (vT, vtok, 2, "gvtok")):
                for g0 in range(0, NT, 4):
                    g = list(range(g0, min(g0 + 4, NT)))
                    ps, pk = PSB.get()
                    psv = ps[:].bitcast(BF16)
                    for i, t in enumerate(g):
                        bld.tr(psv[:, i * 128:(i + 1) * 128], src[:, t * 128:(t + 1) * 128], identb[:], gTk(fi) + ["identb"], pk)
                    n = len(g) * 128
                    P.op("act" if (g0 // 4) % 2 else "dve",
                         (lambda e, g=g, n=n, psv=psv, dstt=dstt: e.copy(dstt[:, g[0]:g[0] + len(g), :], psv[:, 0:n].rearrange("p (t c) -> p t c", c=128))) if (g0 // 4) % 2 else
                         (lambda e, g=g, n=n, psv=psv, dstt=dstt: e.tensor_copy(dstt[:, g[0]:g[0] + len(g), :], psv[:, 0:n].rearrange("p (t c) -> p t c", c=128))),
                         reads=[pk], writes=[(nm, t) for t in g])
            wt, wk = load_w(wrot, w_in[l][:, GZ + h * 128: GZ + (h + 1) * 128])
            for t in tiles:
                ps, pk = PSS.get()
                bld.mm(ps, [(HT[:, k, t * 128:(t + 1) * 128], wt[:, k, :]) for k in range(KD)], [wk, ("HT", t)], pk)
                P.op("act", lambda e, t=t, ps=ps: e.activation(zs[:, t, :], ps, AF.Silu), reads=[pk], writes=[("gzs", t)])
                P.op("pool", lambda e, t=t: e.tensor_tensor(out=zs[:, t, :], in0=zs[:, t, :], in1=gnw(l), op=ALU.mult), reads=[("gzs", t), "rws"], writes=[("gzs", t)])
            f32t = lambda name, n: bld.rot(st, name, [128, 128], F32, n)
            b16t = lambda name, n: bld.rot(st, name, [128, 128], BF16, n)
            gbr = f32t("g_gb", 3); egr = f32t("g_eg", 5); dsr = f32t("g_ds", 3); dir_ = f32t("g_di", 3)
            colr = bld.rot(st, "g_col", [128, 4], F32, 5)
            Pm = f32t("g_P", 8); PTm = f32t("g_PT", 8); Rm = f32t("g_R", 4)
            TTb = b16t("g_TT", 5); atb = b16t("g_at", 5); qdb = b16t("g_qd", 5); kdb = b16t("g_kd", 5)
            rb = b16t("g_r", 3); vnb = b16t("g_vn", 3)
            S32 = [sbh("gS32_%d" % d, [128, 128], F32) for d in range(2)]
            Sb = [bld.rot(st, "gSb%d" % d, [128, 128], BF16, 2) for d in range(2)]
            order = [list(range(NT)), [1, 0] + list(range(NT - 1, 1, -1))]
            cur_Sb = [None, None]
            for d in range(2):
                P.op("pool", lambda e, d=d: e.memset(S32[d][:], 0.0), writes=[("gS32", d)])
                sbt, sbk = Sb[d].get()
                P.op("pool", lambda e, sbt=sbt: e.memset(sbt[:], 0.0), writes=[sbk])
                cur_Sb[d] = (sbt, sbk)
            visited = set()
            kT_k, qT_k = gTk(1), gTk(0)

            def precompute(d, c):
                q = d * 4 + h
                gcol = gbeta[:, c, q:q + 1]
                bcol = gbeta[:, c, 8 + q:9 + q]
                tcol = slice(c * 128, (c + 1) * 128)
                gb, gbk_ = gbr.get()
                P.op("dve", lambda e: e.tensor_scalar(out=gb[:], in0=Umask[d], scalar1=gcol, scalar2=None, op0=ALU.mult), reads=[("gb_g", c), "c32"], writes=[gbk_])
                psA, kA = PSS.get()
                bld.mm(psA, [(ones32, gb[:])], [gbk_, "c32"], kA)
                psB, kB = PSS.get()
                bld.mm(psB, [(ones32, gb[:]), (ident32, NEGS[d])], [gbk_, "c32"], kB)
                psC, kC = PSS.get()
                bld.mm(psC[:, 0:1], [(Umask[d], gcol)], [("gb_g", c), "c32"], kC)
                col, colk = colr.get()
                P.op("dve", lambda e: e.tensor_scalar(out=col[:, 1:2], in0=psC[:, 0:1], scalar1=-1.0, scalar2=None, op0=ALU.mult), reads=[kC], writes=[colk])
                P.op("act", lambda e: e.activation(col[:, 0:1], psC[:, 0:1], AF.Exp), reads=[kC], writes=[colk])
                P.op("dve", lambda e: e.tensor_scalar(out=col[:, 0:1], in0=col[:, 0:1], scalar1=-1.0, scalar2=None, op0=ALU.mult), reads=[colk], writes=[colk])
                P.op("dve", lambda e: e.tensor_scalar(out=col[:, 2:3], in0=bcol, scalar1=-1.0, scalar2=None, op0=ALU.mult), reads=["gb_beta"], writes=[colk])
                eg, egk = egr.get()
                P.op("act", lambda e: e.activation(eg[:], psA, AF.Exp), reads=[kA], writes=[egk])
                ds, dsk = dsr.get()
                P.op("act", lambda e: e.activation(ds[:], psB, AF.Exp, bias=col[:, 1:2]), reads=[kB, colk], writes=[dsk])
                di, dik = dir_.get()
                P.op("dve", lambda e: e.tensor_tensor(out=di[:], in0=ds[:], in1=ident32, op=ALU.add), reads=[dsk, "c32"], writes=[dik])
                psK, kK = PSS.get()
                bld.mm(psK, [(kT[:, tcol], kT[:, tcol])], kT_k, kK)
                psQ, kQ = PSS.get()
                bld.mm(psQ, [(kT[:, tcol], qT[:, tcol])], kT_k + qT_k, kQ)
                p0, p0k = Pm.get()
                P.op("dve", lambda e: e.scalar_tensor_tensor(out=p0[:], in0=psK, scalar=col[:, 2:3], in1=ds[:], op0=ALU.mult, op1=ALU.mult), reads=[kK, colk, dsk], writes=[p0k])
                at, atk = atb.get()
                P.op("dve", lambda e: e.tensor_tensor(out=at[:], in0=psQ, in1=di[:], op=ALU.mult), reads=[kQ, dik], writes=[atk])
                last = 127 if d == 0 else 0
                kd, kdk = kdb.get()
                P.op("pool", lambda e: e.tensor_scalar(out=kd[:], in0=ktok[:, c, :], scalar1=di[:, last:last + 1], scalar2=None, op0=ALU.mult), reads=[("gktok", c), dik], writes=[kdk])
                qd, qdk = qdb.get()
                P.op("pool", lambda e: e.tensor_tensor(out=qd[:], in0=qT[:, tcol], in1=eg[:], op=ALU.mult), reads=qT_k + [egk], writes=[qdk])
                psT, kT_ = PSS.get()
                bld.tr(psT, p0[:], ident32, [p0k, "c32"], kT_)
                pt, ptk = PTm.get()
                P.op("act", lambda e: e.copy(pt[:], psT), reads=[kT_], writes=[ptk])
                R, Rk = Rm.get()
                P.op("dve", lambda e, R=R: e.tensor_tensor(out=R[:], in0=p0[:], in1=ident32, op=ALU.add), reads=[p0k, "c32"], writes=[Rk])
                pc, pck, ptc, ptck = p0, p0k, pt, ptk
                for lev in range(6):
                    if lev < 5:
                        ps1, k1 = PSS.get()
                        bld.mm(ps1, [(ptc[:], pc[:])], [pck, ptck], k1)
                    ps2, k2 = PSS.get()
                    bld.mm(ps2, [(pc[:], ptc[:])], [pck, ptck], k2)
                    if lev >= 1:
                        ps3, k3 = PSS.get()
                        bld.mm(ps3, [(ptc[:], R[:])], [ptck, Rk], k3)
                    pn, pnk = Pm.get()
                    if lev < 5:
                        P.op("act", lambda e, pn=pn, ps1=ps1: e.copy(pn[:], ps1), reads=[k1], writes=[pnk])
                    ptn, ptnk = PTm.get()
                    P.op("act", lambda e, ptn=ptn, ps2=ps2: e.copy(ptn[:], ps2), reads=[k2], writes=[ptnk])
                    if lev >= 1:
                        Rn, Rnk = Rm.get()
                        P.op("dve", lambda e, Rn=Rn, R=R, ps3=ps3: e.tensor_tensor(out=Rn[:], in0=ps3, in1=R[:], op=ALU.add), reads=[k3, Rk], writes=[Rnk])
                        R, Rk = Rn, Rnk
                    pc, pck, ptc, ptck = pn, pnk, ptn, ptnk
                ps3, k3 = PSS.get()
                bld.mm(ps3, [(ptc[:], R[:])], [ptck, Rk], k3)
                tt, ttk = TTb.get()
                P.op("dve", lambda e, tt=tt, R=R, ps3=ps3: e.tensor_tensor(out=tt[:], in0=ps3, in1=R[:], op=ALU.add), reads=[k3, Rk], writes=[ttk])
                return dict(col=col, colk=colk, eg=eg, egk=egk, tt=tt, ttk=ttk, at=at, atk=atk, kd=kd, kdk=kdk, qd=qd, qdk=qdk, bcol=bcol, last=last)

            def step(d, c, pre):
                tcol = slice(c * 128, (c + 1) * 128)
                sbt, sbk = cur_Sb[d]
                psk, kk = PSS.get()
                bld.mm(psk, [(kT[:, tcol], sbt[:])], kT_k + [sbk], kk)
                r, rk_ = rb.get()
                P.op("dve", lambda e: e.scalar_tensor_tensor(out=r[:], in0=psk, scalar=pre["col"][:, 0:1], in1=vtok[:, c, :], op0=ALU.mult, op1=ALU.add), reads=[kk, pre["colk"], ("gvtok", c)], writes=[rk_])
                psv, kv = PSS.get()
                bld.mm(psv, [(pre["tt"][:], r[:])], [pre["ttk"], rk_], kv)
                vn, vnk = vnb.get()
                P.op("act", lambda e: e.activation(vn[:], psv, AF.Copy, scale=pre["bcol"]), reads=[kv, "gb_beta"], writes=[vnk])
                pso, ko = PSS.get()
                bld.mm(pso, [(pre["qd"][:], sbt[:]), (pre["at"][:], vn[:])], [pre["qdk"], sbk, pre["atk"], vnk], ko)
                if c in visited:
                    P.op("dve", lambda e: e.tensor_tensor(out=Oacc[:, c, :], in0=pso, in1=Oacc[:, c, :], op=ALU.add), reads=[ko, ("gO", c)], writes=[("gO", c)])
                else:
                    visited.add(c)
                    P.op("act", lambda e: e.copy(Oacc[:, c, :], pso), reads=[ko], writes=[("gO", c)])
                pss_, ks = PSS.get()
                bld.mm(pss_, [(pre["kd"][:], vn[:])], [pre["kdk"], vnk], ks)
                last = pre["last"]
                P.op("dve", lambda e: e.scalar_tensor_tensor(out=S32[d][:], in0=S32[d][:], scalar=pre["eg"][:, last:last + 1], in1=pss_, op0=ALU.mult, op1=ALU.add), reads=[("gS32", d), pre["egk"], ks], writes=[("gS32", d)])
                nsb, nsbk = Sb[d].get()
                P.op("act", lambda e: e.copy(nsb[:], S32[d][:]), reads=[("gS32", d)], writes=[nsbk])
                cur_Sb[d] = (nsb, nsbk)

            pres = {}
            P.warm = GDN_WARM
            for s_ in range(NT + 1):
                for d in range(2):
                    if s_ < NT:
                        pres[(d, s_)] = precompute(d, order[d][s_])
                for d in range(2):
                    if s_ >= 1:
                        step(d, order[d][s_ - 1], pres.pop((d, s_ - 1)))
            P.warm = 0
            head_out_norm(st, l, h, Oacc, zs, None, 0, tiles, "g")
        def rope_fm(src_bf, src_key, dst, dst_key_of, nrows, rmat, cos_t, sin_t, tabk, tmpA, tmpB):
            for bi, (t0, n) in enumerate(BLKS):
                ps, pk = PSB.get()
                bld.mm(ps[0:nrows, 0:n], [(rmat, src_bf[0:nrows, t0:t0 + n])], [src_key, "rmb"], pk)
                a, ak = tmpA.get()
                b_, bk = tmpB.get()
                P.op("dve", lambda e, a=a, ps=ps, n=n, t0=t0: e.tensor_tensor(out=a[0:nrows, 0:n], in0=ps[0:nrows, 0:n], in1=sin_t[0:nrows, t0:t0 + n], op=ALU.mult), reads=[pk, tabk], writes=[ak])
                P.op("pool", lambda e, b_=b_, n=n, t0=t0: e.tensor_tensor(out=b_[0:nrows, 0:n], in0=src_bf[0:nrows, t0:t0 + n], in1=cos_t[0:nrows, t0:t0 + n], op=ALU.mult), reads=[src_key, tabk], writes=[bk])
                P.op("dve", lambda e, a=a, b_=b_, n=n, t0=t0: e.tensor_tensor(out=dst[0:nrows, t0:t0 + n], in0=a[0:nrows, 0:n], in1=b_[0:nrows, 0:n], op=ALU.add), reads=[ak, bk], writes=[dst_key_of(bi)])

        def phase_mla(l, ctx_out):
            with contextlib.ExitStack() as st:
                sbm = lambda name, shape, dt: bld.sb(st, name, shape, dt)
                wrot = bld.rot(st, "mw", [128, KD, 128], BF16, 3)
                cqn = sbm("cqn", [128, 3, T], BF16)
                ckvn = sbm("ckvn", [128, 2, T], BF16)
                krr = sbm("krr", [64, T], BF16)
                cosm = sbm("cosm", [64, T], F32)
                sinm = sbm("sinm", [64, T], F32)
                wuq = sbm("wuq", [128, 3, 768], BF16)
                wukv = sbm("wukv", [128, 2, 1024], BF16)
                tA = bld.rot(st, "mtA", [128, 512], F32, 2)
                tB = bld.rot(st, "mtB", [128, 512], F32, 2)
                qnsc = sbm("qnsc", [128, 5], F32)
                st1 = contextlib.ExitStack()
                cqraw = bld.sb(st1, "cqraw", [128, 3, T], F32)
                ckvraw = bld.sb(st1, "ckvraw", [128, 2, T], F32)
                krb = bld.sb(st1, "krb", [64, T], BF16)
                sqr = bld.rot(st1, "msqr", [128, 512], F32R, 2)
                rnb = bld.rot(st1, "mrnb", [128, 512], F32, 2)
                P.dma("sp", cosm[:], ropem[0], writes=["ropem"])
                P.dma("sp", sinm[:], ropem[1], writes=["ropem2"])
                P.dma("pool", wuq[:], w_uq[l].rearrange("(k p) c -> p k c", p=128), writes=["wuq"])
                P.dma("pool", wukv[:], w_ukv[l].rearrange("(k p) c -> p k c", p=128), writes=["wukv"])
                for k in range(3):
                    P.op("dve", lambda e, k=k: e.tensor_scalar(out=qnsc[:, k:k + 1], in0=qn(l, k), scalar1=float(math.sqrt(384.0)), scalar2=None, op0=ALU.mult), reads=["spp"], writes=["qnsc"])
                for k in range(2):
                    P.op("dve", lambda e, k=k: e.tensor_scalar(out=qnsc[:, 3 + k:4 + k], in0=kvn(l, k), scalar1=float(math.sqrt(256.0)), scalar2=None, op0=ALU.mult), reads=["spp"], writes=["qnsc"])
                for (c0, nch, rawt, nm) in ((CQ, 3, cqraw, "cqraw"), (CKV, 2, ckvraw, "ckvraw")):
                    for ch in range(nch):
                        wt, wk = load_w(wrot, w_in[l][:, c0 + ch * 128: c0 + (ch + 1) * 128])
                        def ev(bi, t0, n, ps, pk, rawt=rawt, ch=ch, nm=nm):
                            if bi % 2:
                                P.op("act", lambda e: e.copy(rawt[:, ch, t0:t0 + n], ps[:, 0:n]), reads=[pk], writes=[(nm, ch, bi)])
                            else:
                                P.op("dve", lambda e: e.tensor_copy(rawt[:, ch, t0:t0 + n], ps[:, 0:n]), reads=[pk], writes=[(nm, ch, bi)])
                        proj_fm(wt, wk, hT_rhs, KD, ev)
                wt, wk = load_w(wrot, w_in[l][:, KR:KR + 128])
                def evk(bi, t0, n, ps, pk):
                    P.op("act", lambda e: e.copy(krb[:, t0:t0 + n], ps[0:64, 0:n]), reads=[pk], writes=[("krb", bi)])
                proj_fm(wt, wk, hT_rhs, KD, evk, m=64)
                for bi, (t0, n) in enumerate(BLKS):
                    ps, pk = PSB.get()
                    bld.mm(ps[0:64, 0:n], [(rmb[0:64, 128:192], krb[:, t0:t0 + n])], [("krb", bi), "rmb"], pk)
                    a, ak = tA.get()
                    b_, bk = tB.get()
                    P.op("dve", lambda e, a=a, ps=ps, n=n, t0=t0: e.tensor_tensor(out=a[0:64, 0:n], in0=ps[0:64, 0:n], in1=sinm[:, t0:t0 + n], op=ALU.mult), reads=[pk, "ropem2"], writes=[ak])
                    P.op("pool", lambda e, b_=b_, n=n, t0=t0: e.tensor_tensor(out=b_[0:64, 0:n], in0=krb[:, t0:t0 + n], in1=cosm[:, t0:t0 + n], op=ALU.mult), reads=[("krb", bi), "ropem"], writes=[bk])
                    P.op("dve", lambda e, a=a, b_=b_, n=n, t0=t0: e.tensor_tensor(out=krr[:, t0:t0 + n], in0=a[0:64, 0:n], in1=b_[0:64, 0:n], op=ALU.add), reads=[ak, bk], writes=[("krr", bi)])
                for (nch, rawt, nm, dstn, dnm, eps_i, q0) in ((3, cqraw, "cqraw", cqn, "cqn", 3, 0), (2, ckvraw, "ckvraw", ckvn, "ckvn", 4, 3)):
                    for bi, (t0, n) in enumerate(BLKS):
                        ps, pk = PSB.get()
                        for ch in range(nch):
                            sq, sqk = sqr.get()
                            P.op("act", lambda e, sq=sq, ch=ch, t0=t0, n=n, rawt=rawt: e.activation(sq[:, 0:n], rawt[:, ch, t0:t0 + n], AF.Square), reads=[(nm, ch, bi)], writes=[sqk])
                            bld.mm_acc(ps[:, 0:n], onesr[:], sq[:, 0:n], ch == 0, ch == nch - 1, [sqk, "onesr"], pk)
                        rn, rnk = rnb.get()
                        P.op("act", lambda e, rn=rn, ps=ps, n=n, eps_i=eps_i: e.activation(rn[:, 0:n], ps[:, 0:n], AF.Sqrt, bias=epsc(eps_i)), reads=[pk, "c32"], writes=[rnk])
                        P.op("dve", lambda e, rn=rn, n=n: e.reciprocal(rn[:, 0:n], rn[:, 0:n]), reads=[rnk], writes=[rnk])
                        for ch in range(nch):
                            P.op("dve", lambda e, ch=ch, rn=rn, t0=t0, n=n, rawt=rawt, dstn=dstn, q0=q0: e.scalar_tensor_tensor(out=dstn[:, ch, t0:t0 + n], in0=rawt[:, ch, t0:t0 + n], scalar=qnsc[:, q0 + ch:q0 + ch + 1], in1=rn[:, 0:n], op0=ALU.mult, op1=ALU.mult),
                                 reads=[(nm, ch, bi), rnk, "qnsc"], writes=[(dnm, bi)])
                P.barrier()
                st1.close()
                qnope = sbm("qnope", [128, T], BF16)
                qrb = sbm("qrb", [64, T], BF16)
                qrr = sbm("qrr", [64, T], BF16)
                knope = sbm("knope", [128, T], BF16)
                vtok = sbm("mvtok", [128, NT, 128], BF16)
                oTh = sbm("moTh", [128, T], BF16)
                pT = bld.rot(st, "mpT", [128, 512], BF16, 3)
                rden = bld.rot(st, "mrden", [128, 512], F32, 2)
                cqk = lambda: [("cqn", bi) for bi in range(5)]
                ckk = lambda: [("ckvn", bi) for bi in range(5)]
                for h in range(4):
                    for bi, (t0, n) in enumerate(BLKS):
                        ps, pk = PSB.get()
                        bld.mm(ps[:, 0:n], [(wuq[:, k, h * 192:h * 192 + 128], cqn[:, k, t0:t0 + n]) for k in range(3)], ["wuq", ("cqn", bi)], pk)
                        P.op("act", lambda e, ps=ps, t0=t0, n=n: e.activation(qnope[:, t0:t0 + n], ps[:, 0:n], AF.Copy, scale=float(MLA_SCALE)), reads=[pk], writes=[("qnope", bi)])
                        ps, pk = PSB.get()
                        bld.mm(ps[0:64, 0:n], [(wuq[:, k, h * 192 + 128:h * 192 + 192], cqn[:, k, t0:t0 + n]) for k in range(3)], ["wuq", ("cqn", bi)], pk)
                        P.op("act", lambda e, ps=ps, t0=t0, n=n: e.activation(qrb[:, t0:t0 + n], ps[0:64, 0:n], AF.Copy, scale=float(MLA_SCALE)), reads=[pk], writes=[("qrb", bi)])
                        ps, pk = PSB.get()
                        bld.mm(ps[:, 0:n], [(wukv[:, k, h * 256:h * 256 + 128], ckvn[:, k, t0:t0 + n]) for k in range(2)], ["wukv", ("ckvn", bi)], pk)
                        P.op("dve", lambda e, ps=ps, t0=t0, n=n: e.tensor_copy(knope[:, t0:t0 + n], ps[:, 0:n]), reads=[pk], writes=[("knope", bi)])
                        ps, pk = PSB.get()
                        bld.mm(ps[0:64, 0:n], [(rmb[0:64, 128:192], qrb[:, t0:t0 + n])], [("qrb", bi), "rmb"], pk)
                        a, ak = tA.get()
                        b_, bk = tB.get()
                        P.op("dve", lambda e, a=a, ps=ps, n=n, t0=t0: e.tensor_tensor(out=a[0:64, 0:n], in0=ps[0:64, 0:n], in1=sinm[:, t0:t0 + n], op=ALU.mult), reads=[pk, "ropem2"], writes=[ak])
                        P.op("pool", lambda e, b_=b_, n=n, t0=t0: e.tensor_tensor(out=b_[0:64, 0:n], in0=qrb[:, t0:t0 + n], in1=cosm[:, t0:t0 + n], op=ALU.mult), reads=[("qrb", bi), "ropem"], writes=[bk])
                        P.op("dve", lambda e, a=a, b_=b_, n=n, t0=t0: e.tensor_tensor(out=qrr[:, t0:t0 + n], in0=a[0:64, 0:n], in1=b_[0:64, 0:n], op=ALU.add), reads=[ak, bk], writes=[("qrr", bi)])
                    for t in ALLT:
                        ps, pk = PSB.get()
                        bld.mm(ps[:, 0:128], [(ckvn[:, k, t * 128:(t + 1) * 128], wukv[:, k, h * 256 + 128:h * 256 + 256]) for k in range(2)], ["wukv", ("ckvn", min(t // 4, 4))], pk)
                        P.op("act", lambda e, t=t, ps=ps: e.copy(vtok[:, t, :], ps[:, 0:128]), reads=[pk], writes=[("mvtok", t)])
                    qgroups = [(256 + g * 512, 512, ALLT) for g in range(4)]
                    if ctx_out:
                        qgroups = [(0, 256, [0, 1])] + qgroups
                    for (q0, nq, ktiles) in qgroups:
                        qb = min(q0 // 512, 4)
                        qbs = sorted(set([min(q0 // 512, 4), min((q0 + nq - 1) // 512, 4)]))
                        o_ps, o_k = banks[6], "bank6"
                        d_ps, d_k = banks[7], "psx"
                        def s_mm(kt):
                            kb = min(kt // 4, 4)
                            ps, pk = PSB.get()
                            bld.mm(ps[:, 0:nq], [(knope[:, kt * 128:(kt + 1) * 128], qnope[:, q0:q0 + nq]), (krr[:, kt * 128:(kt + 1) * 128], qrr[:, q0:q0 + nq])],
                                   [("knope", kb), ("krr", kb)] + [("qnope", b) for b in qbs] + [("qrr", b) for b in qbs], pk)
                            return ps, pk
                        pend = [s_mm(kt) for kt in ktiles[:2]]
                        for i, kt in enumerate(ktiles):
                            ps, pk = pend.pop(0)
                            p_, pkk = pT.get()
                            P.op("act", lambda e, p_=p_, ps=ps, nq=nq: e.activation(p_[:, 0:nq], ps[:, 0:nq], AF.Exp), reads=[pk], writes=[pkk])
                            if i + 2 < len(ktiles):
                                pend.append(s_mm(ktiles[i + 2]))
                            bld.mm_acc(o_ps[:, 0:nq], vtok[:, kt, :], p_[:, 0:nq], i == 0, i == len(ktiles) - 1, [("mvtok", kt), pkk], o_k)
                            bld.mm_acc(d_ps[:, 0:nq], onesb[:], p_[:, 0:nq], i == 0, i == len(ktiles) - 1, ["onesb", pkk], d_k)
                        rd, rdk = rden.get()
                        P.op("dve", lambda e, rd=rd, nq=nq: e.reciprocal(rd[:, 0:nq], d_ps[:, 0:nq]), reads=[d_k], writes=[rdk])
                        P.op("dve", lambda e, rd=rd, nq=nq, q0=q0: e.tensor_tensor(out=oTh[:, q0:q0 + nq], in0=o_ps[:, 0:nq], in1=rd[:, 0:nq], op=ALU.mult), reads=[o_k, rdk], writes=[("moTh", q0)])
                    c0 = 0 if ctx_out else 256
                    P.dma("sp", oT_s[1][h][:, c0:T], oTh[:, c0:T], reads=[("moTh", q) for q in ([0] if ctx_out else []) + [256 + g * 512 for g in range(4)]], writes=[("oTs", 1, h)])
                P.barrier()
        def phase_ret(l, tiles):
            with contextlib.ExitStack() as st:
                sbm = lambda name, shape, dt: bld.sb(st, name, shape, dt)
                wrot = bld.rot(st, "rw", [128, KD, 128], BF16, 3)
                cosr = sbm("cosr", [128, T], F32)
                sinr = sbm("sinr", [128, T], F32)
                rcs = sbm("rcs", [128, 8 * 128 * 2 + 8], F32)
                P.dma("sp", cosr[:], roper[0], writes=["roper"])
                P.dma("sp", sinr[:], roper[1], writes=["roper2"])
                P.dma("sp", rcs[:], retc, writes=["rcs"])
                DTm = lambda q: rcs[:, q * 128:(q + 1) * 128]
                GWm = lambda q: rcs[:, 1024 + q * 128:1024 + (q + 1) * 128]
                kwc = lambda q: rcs[:, 2048 + q:2049 + q]
                rawb = bld.rot(st, "rrawb", [128, T], BF16, 2)
                qT = sbm("rqT", [128, T], BF16)
                kT = sbm("rkT", [128, T], BF16)
                ktok = sbm("rktok", [128, NT, 128], BF16)
                vtok = sbm("rvtok", [128, NT, 128], BF16)
                zs = sbm("rzs", [128, NT, 128], F32)
                tA = bld.rot(st, "rtA", [128, 512], F32, 2)
                tB = bld.rot(st, "rtB", [128, 512], F32, 2)
                atb = bld.rot(st, "r_at", [128, 128], BF16, 5)
                qwb = bld.rot(st, "r_qw", [128, 128], BF16, 5)
                kwb = bld.rot(st, "r_kw", [128, 128], BF16, 5)
                for h in range(4):
                    with contextlib.ExitStack() as sh:
                        Oacc = bld.sb(sh, "rO", [128, NT, 128], F32)
                        for fi, (c0, dst, scl) in enumerate(((RQ, qT, float(128 ** -0.5)), (RK, kT, 1.0))):
                            wt, wk = load_w(wrot, w_in[l][:, c0 + h * 128:c0 + (h + 1) * 128])
                            rb_, rbk = rawb.get()
                            def ev(bi, t0, n, ps, pk, rb_=rb_, rbk=rbk, scl=scl):
                                P.op("act", lambda e: e.activation(rb_[:, t0:t0 + n], ps[:, 0:n], AF.Copy, scale=scl), reads=[pk], writes=[(rbk, bi)])
                            proj_fm(wt, wk, hT_rhs, KD, ev)
                            for bi, (t0, n) in enumerate(BLKS):
                                ps, pk = PSB.get()
                                bld.mm(ps[:, 0:n], [(rmb[:, 0:128], rb_[:, t0:t0 + n])], [(rbk, bi), "rmb"], pk)
                                a, ak = tA.get()
                                b_, bk = tB.get()
                                P.op("dve", lambda e, a=a, ps=ps, n=n, t0=t0: e.tensor_tensor(out=a[:, 0:n], in0=ps[:, 0:n], in1=sinr[:, t0:t0 + n], op=ALU.mult), reads=[pk, "roper2"], writes=[ak])
                                P.op("pool", lambda e, b_=b_, n=n, t0=t0, rb_=rb_: e.tensor_tensor(out=b_[:, 0:n], in0=rb_[:, t0:t0 + n], in1=cosr[:, t0:t0 + n], op=ALU.mult), reads=[(rbk, bi), "roper"], writes=[bk])
                                P.op("dve", lambda e, a=a, b_=b_, n=n, t0=t0, dst=dst: e.tensor_tensor(out=dst[:, t0:t0 + n], in0=a[:, 0:n], in1=b_[:, 0:n], op=ALU.add), reads=[ak, bk], writes=[("rT", fi, bi)])
                        rTk = lambda fi: [("rT", fi, bi) for bi in range(5)]
                        for g0 in range(0, NT, 4):
                            g = list(range(g0, min(g0 + 4, NT)))
                            ps, pk = PSB.get()
                            psv = ps[:].bitcast(BF16)
                            for i, t in enumerate(g):
                                bld.tr(psv[:, i * 128:(i + 1) * 128], kT[:, t * 128:(t + 1) * 128], identb[:], [("rT", 1, min(t // 4, 4)), "identb"], pk)
                            n = len(g) * 128
                            P.op("dve", lambda e, g=g, n=n, psv=psv: e.tensor_copy(ktok[:, g[0]:g[0] + len(g), :], psv[:, 0:n].rearrange("p (t c) -> p t c", c=128)), reads=[pk], writes=[("rktok", t) for t in g])
                        wt, wk = load_w(wrot, w_in[l][:, RV + h * 128:RV + (h + 1) * 128])
                        for t in ALLT:
                            ps, pk = PSS.get()
                            bld.mm(ps, [(HT[:, k, t * 128:(t + 1) * 128], wt[:, k, :]) for k in range(KD)], [wk, ("HT", t)], pk)
                            P.op("act", lambda e, t=t, ps=ps: e.copy(vtok[:, t, :], ps), reads=[pk], writes=[("rvtok", t)])
                        wt, wk = load_w(wrot, w_in[l][:, RG + h * 128:RG + (h + 1) * 128])
                        for t in tiles:
                            ps, pk = PSS.get()
                            bld.mm(ps, [(HT[:, k, t * 128:(t + 1) * 128], wt[:, k, :]) for k in range(KD)], [wk, ("HT", t)], pk)
                            P.op("act", lambda e, t=t, ps=ps: e.activation(zs[:, t, :], ps, AF.Silu), reads=[pk], writes=[("rzs", t)])
                            P.op("pool", lambda e, t=t, h=h: e.tensor_tensor(out=zs[:, t, :], in0=zs[:, t, :], in1=rnw(l, h), op=ALU.mult), reads=[("rzs", t), "rws"], writes=[("rzs", t)])
                        S32 = [bld.sb(sh, "rS32_%d" % d, [128, 128], F32) for d in range(2)]
                        Sb = [bld.rot(sh, "rSb%d" % d, [128, 128], BF16, 2) for d in range(2)]
                        order = [list(range(NT)), [1, 0] + list(range(NT - 1, 1, -1))]
                        cur = [None, None]
                        for d in range(2):
                            P.op("pool", lambda e, d=d, S32=S32: e.memset(S32[d][:], 0.0), writes=[("rS32", d)])
                            sbt, sbk = Sb[d].get()
                            P.op("pool", lambda e, sbt=sbt: e.memset(sbt[:], 0.0), writes=[sbk])
                            cur[d] = (sbt, sbk)
                        visited = set()
                        atA = bld.sb(sh, "r_atA", [128, 2 * NT, 128], BF16)
                        qwA = bld.sb(sh, "r_qwA", [128, 2 * NT, 128], BF16)
                        kwA = bld.sb(sh, "r_kwA", [128, 2 * NT, 128], BF16)
                        for c in ALLT:
                            tcol = slice(c * 128, (c + 1) * 128)
                            cb = min(c // 4, 4)
                            psQ, kQ = PSS.get()
                            bld.mm(psQ, [(kT[:, tcol], qT[:, tcol])], [("rT", 0, cb), ("rT", 1, cb)], kQ)
                            for d in range(2):
                                q = d * 4 + h
                                ix = d * NT + c
                                P.op("dve", lambda e, psQ=psQ, q=q, ix=ix, atA=atA: e.tensor_tensor(out=atA[:, ix, :], in0=psQ, in1=DTm(q), op=ALU.mult), reads=[kQ, "rcs"], writes=[("r_at", ix)])
                                P.op("dve" if d == 0 else "pool", lambda e, tcol=tcol, q=q, ix=ix, qwA=qwA: e.tensor_tensor(out=qwA[:, ix, :], in0=qT[:, tcol], in1=GWm(q), op=ALU.mult), reads=[("rT", 0, cb), "rcs"], writes=[("r_qw", ix)])
                                P.op("act", lambda e, c=c, q=q, ix=ix, kwA=kwA: e.activation(kwA[:, ix, :], ktok[:, c, :], AF.Copy, scale=kwc(q)), reads=[("rktok", c), "rcs"], writes=[("r_kw", ix)])
                        for s_ in range(NT):
                            for d in range(2):
                                c = order[d][s_]
                                q = d * 4 + h
                                ix = d * NT + c
                                sbt, sbk = cur[d]
                                pso, ko = PSS.get()
                                bld.mm(pso, [(qwA[:, ix, :], sbt[:]), (atA[:, ix, :], vtok[:, c, :])], [("r_qw", ix), sbk, ("r_at", ix), ("rvtok", c)], ko)
                                pss_, ks = PSS.get()
                                bld.mm(pss_, [(kwA[:, ix, :], vtok[:, c, :])], [("r_kw", ix), ("rvtok", c)], ks)
                                P.op("dve", lambda e, d=d, q=q, pss_=pss_, S32=S32: e.scalar_tensor_tensor(out=S32[d][:], in0=S32[d][:], scalar=float(RET_CDEC[q]), in1=pss_, op0=ALU.mult, op1=ALU.add), reads=[("rS32", d), ks], writes=[("rS32", d)])
                                nsb, nsbk = Sb[d].get()
                                P.op("act", lambda e, nsb=nsb, d=d, S32=S32: e.copy(nsb[:], S32[d][:]), reads=[("rS32", d)], writes=[nsbk])
                                cur[d] = (nsb, nsbk)
                                if c in visited:
                                    P.op("dve", lambda e, c=c, pso=pso, Oacc=Oacc: e.tensor_tensor(out=Oacc[:, c, :], in0=pso, in1=Oacc[:, c, :], op=ALU.add), reads=[ko, ("rO", c)], writes=[("rO", c)])
                                else:
                                    visited.add(c)
                                    P.op("act", lambda e, c=c, pso=pso, Oacc=Oacc: e.copy(Oacc[:, c, :], pso), reads=[ko], writes=[("rO", c)])
                        head_out_norm(sh, l, h, Oacc, zs, None, 2, tiles, "r")
                    P.barrier()
        def phase_gates(l, tiles):
            with contextlib.ExitStack() as st:
                wrot = bld.rot(st, "gtw", [128, KD, 512], BF16, 2)
                gb = bld.rot(st, "gtb", [128, 512], BF16, 4)
                for cb in range(6):
                    wt, wk = load_w(wrot, w_in[l][:, GATE + cb * 512:GATE + (cb + 1) * 512])
                    for t in tiles:
                        ps, pk = PSB.get()
                        bld.mm(ps[:], [(HT[:, k, t * 128:(t + 1) * 128], wt[:, k, :]) for k in range(KD)], [wk, ("HT", t)], pk)
                        g, gk = gb.get()
                        P.op("act", lambda e, g=g, ps=ps: e.activation(g[:], ps[:], AF.Sigmoid), reads=[pk], writes=[gk])
                        P.dma("sp", gates_s[t * 128:(t + 1) * 128, cb * 512:(cb + 1) * 512], g[:], reads=[gk], writes=[("gates", t, cb)])
                P.barrier()

        def phase_merge(l, tiles, xsrc):
            stw = contextlib.ExitStack()
            wo = bld.sb(stw, "wo", [128, KD, D], BF16)
            P.dma("pool", wo[:], w_out[l].rearrange("(k p) c -> p k c", p=128), writes=["wo"])
            with contextlib.ExitStack() as st:
                wbr = [bld.sb(st, "wbr%d" % b, [128, 4, D], BF16) for b in range(3)]
                for b in range(3):
                    P.dma("pool", wbr[b][:], w_br[b][l].rearrange("(k p) c -> p k c", p=128), writes=[("wbr", b)])
                gt = bld.rot(st, "mgt", [128, 3 * D], BF16, 2)
                ot = bld.rot(st, "mot", [128, 3, 4, 128], BF16, 2)
                t32 = bld.rot(st, "mt32", [128, 512], F32, 4)
                mb = bld.rot(st, "mmb", [128, D], BF16, 2)
                for t in tiles:
                    g, gk = gt.get()
                    P.dma("sp", g[:], gates_s[t * 128:(t + 1) * 128, :], reads=[("gates", t, cb) for cb in range(6)], writes=[gk])
                    o_, ok_ = ot.get()
                    for b in range(3):
                        P.dma("sp", o_[:, b, :, :], oT_s[b][:, :, t * 128:(t + 1) * 128].rearrange("h p c -> p h c"), reads=[("oTs", b, h) for h in range(4)], writes=[(ok_, b)])
                    m, mk = mb.get()
                    for half in range(2):
                        acc, acck = t32.get()
                        for b in range(3):
                            ps, pk = PSB.get()
                            bld.mm(ps[:], [(o_[:, b, k, :], wbr[b][:, k, half * 512:(half + 1) * 512]) for k in range(4)], [(ok_, b), ("wbr", b)], pk)
                            gsl = g[:, b * D + half * 512: b * D + (half + 1) * 512]
                            if b == 0:
                                P.op("dve", lambda e, acc=acc, ps=ps, gsl=gsl: e.tensor_tensor(out=acc[:], in0=ps[:], in1=gsl, op=ALU.mult), reads=[pk, gk], writes=[acck])
                            else:
                                tmp, tk = t32.get()
                                P.op("dve", lambda e, tmp=tmp, ps=ps, gsl=gsl: e.tensor_tensor(out=tmp[:], in0=ps[:], in1=gsl, op=ALU.mult), reads=[pk, gk], writes=[tk])
                                if b == 1:
                                    P.op("pool", lambda e, acc=acc, tmp=tmp: e.tensor_tensor(out=acc[:], in0=acc[:], in1=tmp[:], op=ALU.add), reads=[acck, tk], writes=[acck])
                                else:
                                    P.op("pool", lambda e, acc=acc, tmp=tmp, m=m, half=half: e.tensor_tensor(out=m[:, half * 512:(half + 1) * 512], in0=acc[:], in1=tmp[:], op=ALU.add), reads=[acck, tk], writes=[(mk, half)])
                    ps, pk = PSB.get()
                    psv = ps[:].bitcast(BF16)
                    for k in range(KD):
                        bld.tr(psv[:, k * 128:(k + 1) * 128], m[:, k * 128:(k + 1) * 128], identb[:], [(mk, 0), (mk, 1), "identb"], pk)
                    P.op("act", lambda e, t=t, psv=psv: e.copy(HT[:, :, t * 128:(t + 1) * 128], psv[:, 0:1024].rearrange("p (k c) -> p k c", c=128)), reads=[pk], writes=[("HT", t)])
                P.barrier()
            with contextlib.ExitStack() as st:
                residual_phase(st, l, tiles, xsrc, 0, lambda t, half: [(HT[:, k, t * 128:(t + 1) * 128], wo[:, k, half * 512:(half + 1) * 512]) for k in range(KD)],
                               lambda t: [("HT", t), "wo"])
                P.barrier()
            stw.close()

        def residual_phase(st, l, tiles, xsrc, which, pairs_of, reads_of):
            xb = bld.rot(st, "rx", [128, D], F32, 3)
            yb = bld.rot(st, "ry", [128, D], F32, 2)
            for t in tiles:
                s = tile_stream(t)
                xt, xk = xb.get()
                P.dma("sp", xt[:], xsrc[t * 128:(t + 1) * 128, :], reads=[("xs", t)], writes=[xk])
                y, yk = yb.get()
                for half in range(2):
                    ps, pk = PSB.get()
                    bld.mm(ps[:], pairs_of(t, half), reads_of(t), pk)
                    sl = slice(half * 512, (half + 1) * 512)
                    P.op("dve", lambda e, y=y, ps=ps, sl=sl, s=s: e.tensor_tensor(out=y[:, sl], in0=ps[:], in1=grow[:, s, which, sl], op=ALU.mult),
                         reads=[pk] + [("grow", s, 2 if which == 0 else 5, h_) for h_ in range(2)], writes=[(yk, half)])
                    P.op("pool", lambda e, y=y, xt=xt, sl=sl: e.tensor_tensor(out=y[:, sl], in0=y[:, sl], in1=xt[:, sl], op=ALU.add), reads=[(yk, half), xk], writes=[(yk, half)])
                P.dma("pool", xs[t * 128:(t + 1) * 128, :], y[:], reads=[(yk, 0), (yk, 1)], writes=[("xs", t)])

        def phase_ffn(l, tiles):
            t_lo = tiles[0] * 128
            stw = contextlib.ExitStack()
            wd = bld.sb(stw, "wd", [128, NJ, D], BF16)
            with contextlib.ExitStack() as st:
                wrot = bld.rot(st, "fw", [128, KD, 512], BF16, 3)
                wcur = {}
                W = T + 3
                off = lambda t0: t0 + 1 if t0 < 256 else t0 + 2
                raw = bld.rot(st, "fraw", [128, W], F32, 3)
                cv = bld.rot(st, "fcv", [128, W], F32, 2)
                aT = bld.rot(st, "faT", [128, T], BF16, 2)
                blks = BLKS if t_lo == 0 else [(256 + i * 512, 512) for i in range(4)]
                for rw_ in raw.tiles:
                    P.op("pool", lambda e, rw_=rw_: e.memset(rw_[:], 0.0), writes=[("fraw", raw.tiles.index(rw_))])
                for j in range(NJ):
                    cvs = []
                    for gi, c0 in enumerate((j * 128, DFF + j * 128)):
                        ch = c0 // 128
                        if j % 4 == 0:
                            ng = min(4, NJ - j)
                            wtf, wk = wrot.get()
                            P.dma("pool", wtf[:, :, 0:ng * 128], w_up[l][:, c0:c0 + ng * 128].rearrange("(k p) c -> p k c", p=128), writes=[wk])
                            wcur[gi] = (wtf, wk)
                        wt, wk = wcur[gi]
                        rw, rk = raw.get()
                        def ev(bi, t0, n, ps, pk, rw=rw, rk=rk):
                            if t0 == 0:
                                P.op("act", lambda e: e.copy(rw[:, 1:257], ps[:, 0:256]), reads=[pk], writes=[rk])
                                P.op("dve", lambda e: e.tensor_copy(rw[:, 258:514], ps[:, 256:512]), reads=[pk], writes=[rk])
                            elif bi % 2:
                                P.op("act", lambda e: e.copy(rw[:, off(t0):off(t0) + n], ps[:, 0:n]), reads=[pk], writes=[rk])
                            else:
                                P.op("dve", lambda e: e.tensor_copy(rw[:, off(t0):off(t0) + n], ps[:, 0:n]), reads=[pk], writes=[rk])
                        proj_fm(wt, wk, hT_rhs, KD, ev, blks=blks, coff=(j % 4) * 128)
                        c, ck = cv.get()
                        lo = off(t_lo)
                        P.op("act", lambda e, c=c, rw=rw, ch=ch: e.activation(c[:, lo:W - 1], rw[:, lo:W - 1], AF.Identity, bias=fconvb(l, ch), scale=fconv(l, ch, 1)), reads=[rk, "spp"], writes=[ck])
                        P.op("dve", lambda e, c=c, rw=rw, ch=ch: e.scalar_tensor_tensor(out=c[:, lo:W - 1], in0=rw[:, lo - 1:W - 2], scalar=fconv(l, ch, 0), in1=c[:, lo:W - 1], op0=ALU.mult, op1=ALU.add), reads=[rk, ck, "spp"], writes=[ck])
                        P.op("dve", lambda e, c=c, rw=rw, ch=ch: e.scalar_tensor_tensor(out=c[:, lo:W - 1], in0=rw[:, lo + 1:W], scalar=fconv(l, ch, 2), in1=c[:, lo:W - 1], op0=ALU.mult, op1=ALU.add), reads=[rk, ck, "spp"], writes=[ck])
                        cvs.append((c, ck))
                    (cg, cgk), (cval, cvk) = cvs
                    P.op("act", lambda e, cg=cg: e.activation(cg[:, lo:W - 1], cg[:, lo:W - 1], AF.Silu), reads=[cgk], writes=[cgk])
                    a, ak = aT.get()
                    if t_lo == 0:
                        P.op("dve", lambda e, a=a, cg=cg, cval=cval: e.tensor_tensor(out=a[:, 0:256], in0=cg[:, 1:257], in1=cval[:, 1:257], op=ALU.mult), reads=[cgk, cvk], writes=[(ak, 0)])
                    P.op("dve", lambda e, a=a, cg=cg, cval=cval: e.tensor_tensor(out=a[:, 256:T], in0=cg[:, 258:W - 1], in1=cval[:, 258:W - 1], op=ALU.mult), reads=[cgk, cvk], writes=[(ak, 1)])
                    P.dma("sp", aT_s[tiles[0]:NT, :, j, :].rearrange("t p c -> p t c"), a[:, t_lo:T].rearrange("p (t c) -> p t c", c=128), reads=[(ak, 0), (ak, 1)], writes=[("aTs", j)])
                    P.dma("pool", wd[:, j, :], w_down[l][j * 128:(j + 1) * 128, :], writes=[("wd", j)])
                P.barrier()
            with contextlib.ExitStack() as st:
                ab_ = bld.rot(st, "fab", [128, NJ, 128], BF16, 2)
                cur = {}
                def pairs_of(t, half):
                    if half == 0:
                        a, ak = ab_.get()
                        P.dma("sp", a[:], aT_s[t], reads=[("aTs", j) for j in range(NJ)], writes=[ak])
                        cur[t] = (a, ak)
                    a, ak = cur[t]
                    return [(a[:, j, :], wd[:, j, half * 512:(half + 1) * 512]) for j in range(NJ)]
                residual_phase(st, l, tiles, xs, 1, pairs_of, lambda t: [cur[t][1]] + [("wd", j) for j in range(NJ)])
                P.barrier()
            stw.close()

        def phase_final():
            with contextlib.ExitStack() as st:
                xb = bld.rot(st, "fx", [128, D], F32, 3)
                junk = bld.sb(st, "fjunk", [128, D], BF16)
                ssb = bld.rot(st, "fss", [128, 1], F32, 4)
                ob = bld.rot(st, "fo", [128, D], F32, 3)
                for t in range(2, NT):
                    xt, xk = xb.get()
                    P.dma("sp", xt[:], xs[t * 128:(t + 1) * 128, :], reads=[("xs", t)], writes=[xk])
                    ss, sk = ssb.get()
                    P.op("act", lambda e, xt=xt, ss=ss: e.activation(junk[:], xt[:], AF.Square, accum_out=ss[:]), reads=[xk], writes=["fjunk", sk])
                    P.op("act", lambda e, ss=ss: e.activation(ss[:], ss[:], AF.Sqrt, bias=epsc(0)), reads=[sk, "c32"], writes=[sk])
                    P.op("dve", lambda e, ss=ss: e.reciprocal(ss[:], ss[:]), reads=[sk], writes=[sk])
                    P.op("dve", lambda e, ss=ss: e.tensor_scalar(out=ss[:], in0=ss[:], scalar1=float(math.sqrt(D)), scalar2=None, op0=ALU.mult), reads=[sk], writes=[sk])
                    o_, ok_ = ob.get()
                    P.op("dve", lambda e, o_=o_, xt=xt, ss=ss: e.scalar_tensor_tensor(out=o_[:], in0=xt[:], scalar=ss[:], in1=fnw, op0=ALU.mult, op1=ALU.mult), reads=[xk, sk, "rws"], writes=[ok_])
                    P.dma("pool", out[(t - 2) * 128:(t - 1) * 128, :], o_[:], reads=[ok_], writes=[("out", t)])
        P.barrier()
        for l in range(2):
            ctx_out = (l == 0)
            tiles = ALLT if ctx_out else list(range(2, NT))
            xsrc = xin if l == 0 else xs
            phase_mod(l)
            if stop_after == "mod":
                dump("modpp", modpp[:], ["modpp"])
                break
            phase_norm(l, 0, xsrc, ALLT)
            if l == 0:
                dump("HT", HT[:], HTk(ALLT))
                dump("modpp", modpp[:], ["modpp"])
                dump("grow", grow[:], [("grow", s_, v_, h_) for s_ in range(2) for v_ in (2, 5) for h_ in range(2)])
            if stop_after == "norm":
                break
            if stop_after in (None, "all", "gdn", "merge", "ffn"):
                phase_gdn(l, tiles)
                if l == 0:
                    dump("oTa", oT_s[0], [("oTs", 0, h_) for h_ in range(4)])
                if stop_after == "gdn":
                    break
            if stop_after in (None, "all", "mla", "merge", "ffn"):
                phase_mla(l, ctx_out)
                if l == 0:
                    dump("oTb", oT_s[1], [("oTs", 1, h_) for h_ in range(4)])
                if stop_after == "mla":
                    break
            if stop_after in (None, "all", "ret", "merge", "ffn"):
                phase_ret(l, tiles)
                if l == 0:
                    dump("oTc", oT_s[2], [("oTs", 2, h_) for h_ in range(4)])
                if stop_after == "ret":
                    break
            phase_gates(l, tiles)
            phase_merge(l, tiles, xsrc)
            if l == 0:
                dump("xmid", xs, [("xs", t_) for t_ in ALLT])
            if stop_after == "merge":
                break
            phase_norm(l, 1, xs, tiles)
            phase_ffn(l, tiles)
            if l == 0:
                dump("xl0", xs, [("xs", t_) for t_ in ALLT])
            if stop_after == "ffn":
                break
        if stop_after in (None, "all"):
            phase_final()
        P.emit(final_reads=[("out", t) for t in range(2, NT)] + bld.dbg_keys)
    return nc, P.stats


def _rope_tables(d):
    n = 2048
    rows_ = n // 64
    r = np.repeat(np.arange(rows_, dtype=np.float32), 64)
    col = np.tile(np.arange(64, dtype=np.float32), rows_)
    quarter = d // 4
    inv = (np.float32(10000.0) ** (-np.arange(quarter, dtype=np.float32) / np.float32(quarter))).astype(np.float32)
    ang = np.concatenate([r[:, None] * inv, col[:, None] * inv], axis=-1).astype(np.float32)
    cos = np.cos(ang).astype(np.float32)
    sin = np.sin(ang).astype(np.float32)
    C = np.ones((d, T), np.float32)
    S = np.zeros((d, T), np.float32)
    C[:, 256:] = np.concatenate([cos, cos], axis=1).T
    S[:, 256:] = np.concatenate([sin, sin], axis=1).T
    return np.stack([C, S])


def _rot_mat(d):
    R = np.zeros((d, d), np.float32)
    h = d // 2
    for m in range(h):
        R[m + h, m] = -1.0
    for m in range(h, d):
        R[m - h, m] = 1.0
    return R


def _host_consts():
    c = np.zeros((128, 128 * 6 + 8), np.float32)
    idx = np.arange(128)
    k = idx[:, None]
    i = idx[None, :]
    c[:, 0:128] = np.eye(128)
    c[:, 128:256] = 1.0
    c[:, 256:384] = (k <= i)
    c[:, 384:512] = (k >= i)
    c[:, 512:640] = np.where(i > k, 0.0, NEGBIG)
    c[:, 640:768] = np.where(i < k, 0.0, NEGBIG)
    c[0, 768] = 1.0
    c[:, 769] = 1024 * EPS; c[:, 770] = 128 * EPS; c[:, 771] = EPS; c[:, 772] = 384 * EPS; c[:, 773] = 256 * EPS
    rm = np.zeros((128, 192), np.float32)
    rm[:, 0:128] = _rot_mat(128)
    rm[0:64, 128:192] = _rot_mat(64)
    hh = np.arange(4, dtype=np.float64)
    lg = np.stack([np.log1p(-(2.0 ** (-(5.0 + hh + 0.5 * d)))) for d in range(2)])
    rc = np.zeros((128, 8 * 128 * 2 + 8), np.float64)
    jj = idx[:, None].astype(np.float64)
    ii = idx[None, :].astype(np.float64)
    cdec = []
    for d in range(2):
        for h in range(4):
            g = lg[d, h]
            q = d * 4 + h
            if d == 0:
                DT = np.where(ii >= jj, np.exp(g * np.maximum(ii - jj, 0)), 0.0)
                GW = np.exp(g * (ii + 1)) * np.ones((128, 1))
                kw = np.exp(g * (127 - idx))
            else:
                DT = np.where(ii <= jj, np.exp(g * np.maximum(jj - ii, 0)), 0.0)
                GW = np.exp(g * (128 - ii)) * np.ones((128, 1))
                kw = np.exp(g * idx)
            rc[:, q * 128:(q + 1) * 128] = DT
            rc[:, 1024 + q * 128: 1024 + (q + 1) * 128] = GW
            rc[:, 2048 + q] = kw
            cdec.append(float(np.exp(g * 128)))
    return c, rm, rc.astype(np.float32), cdec


RET_CDEC = _host_consts()[3]
_NC_CACHE = {}


def _prep_common(inp):
    f = lambda a: np.ascontiguousarray(np.asarray(a, dtype=np.float32))
    pp = lambda v, nch: v.reshape(nch, 128).T
    sm = []
    n1, n2 = f(inp["norm1_w"]), f(inp["norm2_w"])
    sm.append(np.concatenate([pp(n1[l], 8) for l in range(2)], axis=1))
    sm.append(np.concatenate([pp(n2[l], 8) for l in range(2)], axis=1))
    gc = f(inp["gdn_conv_w"])
    sm.append(np.concatenate([gc[l].reshape(3, 12, 128).transpose(2, 1, 0).reshape(128, 36) for l in range(2)], axis=1))
    fc = f(inp["ffn_conv_w"])
    sm.append(np.concatenate([fc[l].reshape(3, 44, 128).transpose(2, 1, 0).reshape(128, 132) for l in range(2)], axis=1))
    fb = f(inp["ffn_conv_b"])
    sm.append(np.concatenate([pp(fb[l], 44) for l in range(2)], axis=1))
    qn_, kvn_ = f(inp["mla_q_norm"]), f(inp["mla_kv_norm"])
    sm.append(np.concatenate([pp(qn_[l], 3) for l in range(2)], axis=1))
    sm.append(np.concatenate([pp(kvn_[l], 2) for l in range(2)], axis=1))
    smallpp = np.ascontiguousarray(np.concatenate(sm, axis=1))
    rep = lambda v: np.broadcast_to(v[None, :], (128, v.shape[0]))
    rw = [rep(f(inp["gdn_norm_w"]).reshape(-1)), rep(f(inp["ret_norm_w"]).reshape(-1)),
          rep(f(inp["gdn_A_log"]).reshape(-1)), rep(f(inp["gdn_dt_bias"]).reshape(-1)), rep(f(inp["final_norm_w"]))]
    rows = np.ascontiguousarray(np.concatenate(rw, axis=1))
    c, rm, rc, _ = _host_consts()
    common = dict(smallpp=smallpp, rows=rows, consts=c, rmats=rm, retc=rc,
                  ropem=_rope_tables(64), roper=_rope_tables(128))
    for k in ("ada_w", "ada_b", "w_in", "mla_w_uq", "mla_w_ukv", "w_br_gdn", "w_br_mla", "w_br_ret", "w_out",
              "ffn_w_up", "ffn_w_down"):
        common[k] = f(inp[k])
    return common


def _prep_core(inp, b):
    f = lambda a: np.ascontiguousarray(np.asarray(a, dtype=np.float32))
    xin = np.concatenate([f(inp["ctx"][b]), f(inp["x"][b])], axis=0)
    def crep_of(v):
        return np.broadcast_to(v.reshape(8, 128).T[:, :, None], (128, 8, 128)).reshape(128, 1024)
    crep = np.stack([crep_of(f(inp["c_ctx"])), crep_of(f(inp["c"][b]))])
    return dict(xin=np.ascontiguousarray(xin), crep=np.ascontiguousarray(crep))


def kernel(**inputs):
    if "nc" not in _NC_CACHE:
        _NC_CACHE["nc"] = build()[0]
    nc = _NC_CACHE["nc"]
    common = _prep_common(inputs)
    in_maps = []
    for b in range(NCORES):
        m = dict(common)
        m.update(_prep_core(inputs, b))
        in_maps.append(m)
    res = run_bass_kernel_spmd(nc, in_maps, core_ids=list(range(NCORES)))
    return np.stack([np.asarray(r["out"], dtype=np.float32) for r in res.results], axis=0)
```

```python
import contextlib
import math
import os
import numpy as np
import concourse.bass as bass
import concourse.mybir as mybir
from concourse.bass_utils import run_bass_kernel_spmd

F32 = mybir.dt.float32
F32R = mybir.dt.float32r
BF16 = mybir.dt.bfloat16
AF = mybir.ActivationFunctionType
ALU = mybir.AluOpType

NCORES = 8
T = 2304
NT = 18
D = 1024
KD = 8
DFF = 2816
NJ = 22
EPS = 1e-6
GQ, GK, GV, GZ, GAB, CQ, CKV, KR, RQ, RK, RV, RG, GATE = 0, 512, 1024, 1536, 2048, 2064, 2448, 2704, 2768, 3280, 3792, 4304, 4816
MLA_SCALE = 192 ** -0.5
NEGBIG = -30000.0
GDN_WARM = 1


class Prog:
    def __init__(self, nc, n_dma_sems=8):
        self.nc = nc
        self.ops = []
        self.last_w = {}
        self.readers = {}
        self.n_dma_sems = n_dma_sems
        self.warm = 0
        self.dummy = None

    def op(self, eng, fn, reads=(), writes=(), dma=False, barrier=False):
        idx = len(self.ops)
        deps = {}
        reads = list(reads)
        writes = list(writes)
        if barrier:
            writes.append("__phase")
        else:
            reads.append("__phase")
        for k in reads:
            w = self.last_w.get(k)
            if w is not None:
                deps[w] = "raw"
        for k in writes:
            w = self.last_w.get(k)
            if w is not None and w not in deps:
                deps[w] = "waw"
            for r in self.readers.get(k, ()):
                if r not in deps:
                    deps[r] = "war"
        for k in reads:
            self.readers.setdefault(k, []).append(idx)
        for k in writes:
            self.last_w[k] = idx
            self.readers[k] = []
        self.ops.append(dict(eng=eng, fn=fn, deps=deps, dma=dma, barrier=barrier, warm=(self.warm if eng == "pe" else 0)))
        return idx

    def barrier(self):
        self.op("dve", lambda e: e.nop(), barrier=True)

    def dma(self, q, out, in_, reads=(), writes=()):
        return self.op(q, lambda e: e.dma_start(out=out, in_=in_), reads, writes, dma=True)

    def emit(self, final_reads=()):
        nc = self.nc
        ops = self.ops
        self.op("sp", lambda e: e.nop(), reads=final_reads)
        n = len(ops)
        pos = [0] * n
        cnt = {}
        for i, o in enumerate(ops):
            c = cnt.get(o["eng"], 0)
            pos[i] = c
            cnt[o["eng"]] = c + 1
        waited_pos = {}
        waited_dma = {}
        need = [[] for _ in range(n)]
        signaling = [False] * n
        for i, o in enumerate(ops):
            E = o["eng"]
            for d in sorted(o["deps"]):
                kind = o["deps"][d]
                od = ops[d]
                F = od["eng"]
                if od["dma"]:
                    s = waited_dma.setdefault(E, set())
                    if d in s:
                        continue
                    s.add(d)
                    need[i].append(d)
                    signaling[d] = True
                else:
                    if F == E and not o["dma"] and not o["barrier"]:
                        if E == "pe":
                            continue
                    if pos[d] <= waited_pos.get((E, F), -1):
                        continue
                    waited_pos[(E, F)] = pos[d]
                    need[i].append(d)
                    signaling[d] = True
        engs = sorted(cnt.keys())
        self.stats = dict(cnt)
        with contextlib.ExitStack() as st:
            esem = {E: st.enter_context(nc.semaphore("s_" + E)) for E in engs}
            dsem = {}
            for E in engs:
                if any(o["dma"] and o["eng"] == E for o in ops):
                    dsem[E] = [st.enter_context(nc.semaphore("d_%s_%d" % (E, j))) for j in range(self.n_dma_sems)]
            ev = [None] * n
            ecount = {E: 0 for E in engs}
            dcount = {E: [0] * self.n_dma_sems for E in dsem}
            dnext = {E: 0 for E in dsem}
            for i, o in enumerate(ops):
                E = o["eng"]
                if o["dma"]:
                    j = dnext[E]
                    dnext[E] = (j + 1) % self.n_dma_sems
                    dcount[E][j] += 1
                    ev[i] = (dsem[E][j], 16 * dcount[E][j])
                    o["dslot"] = j
                    o["dprev"] = 16 * (dcount[E][j] - 1)
                elif signaling[i]:
                    ecount[E] += 1
                    ev[i] = (esem[E], ecount[E])
            for E in engs:
                assert ecount[E] < 60000, (E, ecount[E])
            blk = st.enter_context(nc.Block())
            handles = dict(pe=blk.tensor, act=blk.scalar, dve=blk.vector, pool=blk.gpsimd, sp=blk.sync)
            nw = [0]
            for E in engs:
                my = [i for i in range(n) if ops[i]["eng"] == E]

                def body(e, my=my, E=E):
                    dwaited = [0] * self.n_dma_sems
                    for i in my:
                        o = ops[i]
                        if o["warm"] and need[i]:
                            for _ in range(o["warm"]):
                                self.dummy(e)
                        for d in need[i]:
                            s, v = ev[d]
                            e.wait_ge(s, v)
                            nw[0] += 1
                        if o["dma"]:
                            j = o["dslot"]
                            if o["dprev"] > dwaited[j]:
                                e.wait_ge(dsem[E][j], o["dprev"])
                                dwaited[j] = o["dprev"]
                                nw[0] += 1
                        ins = o["fn"](e)
                        if ev[i] is not None:
                            s, v = ev[i]
                            ins.then_inc(s, 16 if o["dma"] else 1)
                handles[E](body)
            self.stats["waits"] = nw[0]
            self.stats["signals"] = dict(ecount)


class Rot:
    def __init__(self, name, tiles):
        self.name = name
        self.tiles = tiles
        self.i = 0

    def get(self):
        j = self.i % len(self.tiles)
        self.i += 1
        kf = getattr(self, "keyfn", None)
        return self.tiles[j], (kf(j) if kf else (self.name, j))


class B:
    def __init__(self, nc, dbg):
        self.nc = nc
        self.P = Prog(nc)
        self.dbg = dbg
        self.dbg_keys = []

    def sb(self, st, name, shape, dt):
        self.uid = getattr(self, "uid", 0) + 1
        return st.enter_context(self.nc.sbuf_tensor("%s_u%d" % (name, self.uid), list(shape), dt))

    def rot(self, st, name, shape, dt, n):
        return Rot(name, [self.sb(st, "%s%d" % (name, i), shape, dt) for i in range(n)])

    def mm(self, out, pairs, reads, wkey):
        def fn(e):
            m = len(pairs)
            ins = None
            for i, (l, r) in enumerate(pairs):
                ins = e.matmul(out, lhsT=l, rhs=r, start=(i == 0), stop=(i == m - 1))
            return ins
        self.P.op("pe", fn, reads=reads, writes=[wkey])

    def mm_acc(self, out, l, r, start, stop, reads, wkey):
        self.P.op("pe", lambda e: e.matmul(out, lhsT=l, rhs=r, start=start, stop=stop), reads=reads, writes=[wkey])

    def tr(self, out, in_, ident, reads, wkey):
        self.P.op("pe", lambda e: e.transpose(out, in_, ident), reads=reads, writes=[wkey])


def tile_stream(t):
    return 0 if t < 2 else 1


def build(dbg=None, stop_after=None):
    dbg = dbg or ()
    nc = bass.Bass("TRN2", target_bir_lowering=False)
    dram_in = lambda name, shape: nc.dram_tensor(name, list(shape), F32, kind="ExternalInput").ap()
    xin = dram_in("xin", [T, D])
    crep = dram_in("crep", [2, 128, KD * 128])
    ada_w = dram_in("ada_w", [2, D, 6 * D])
    ada_b = dram_in("ada_b", [2, 6 * D])
    w_in = dram_in("w_in", [2, D, 7888])
    w_uq = dram_in("mla_w_uq", [2, 384, 768])
    w_ukv = dram_in("mla_w_ukv", [2, 256, 1024])
    w_br = [dram_in(n, [2, 512, D]) for n in ("w_br_gdn", "w_br_mla", "w_br_ret")]
    w_out = dram_in("w_out", [2, D, D])
    w_up = dram_in("ffn_w_up", [2, D, 2 * DFF])
    w_down = dram_in("ffn_w_down", [2, DFF, D])
    NSM = 2 * 8 * 2 + 2 * 12 * 3 + 2 * 44 * 3 + 2 * 44 + 2 * 3 + 2 * 2
    smallpp = dram_in("smallpp", [128, NSM])
    NROW = 2 * 128 + 2 * 512 + 2 * 8 + 2 * 8 + 1024
    rows = dram_in("rows", [128, NROW])
    NC32 = 128 * 6 + 8
    consts = dram_in("consts", [128, NC32])
    ropem = dram_in("ropem", [2, 64, T])
    roper = dram_in("roper", [2, 128, T])
    rmats = dram_in("rmats", [128, 192])
    retc = dram_in("retc", [128, 8 * 128 * 2 + 8])
    out = nc.dram_tensor("out", [2048, D], F32, kind="ExternalOutput").ap()
    xs = nc.dram_tensor("xs", [T, D], F32).ap()
    oT_s = [nc.dram_tensor("oT%d" % b, [4, 128, T], BF16).ap() for b in range(3)]
    gates_s = nc.dram_tensor("gates_s", [T, 3 * D], BF16).ap()
    aT_s = nc.dram_tensor("aT_s", [NT, 128, NJ, 128], BF16).ap()
    dbg_out = {}
    for name, shape, dt in dbg:
        dbg_out[name] = nc.dram_tensor("dbg_" + name, list(shape), dt, kind="ExternalOutput").ap()

    bld = B(nc, dbg_out)
    P = bld.P
    with contextlib.ExitStack() as top:
        sb = lambda name, shape, dt, st=top: bld.sb(st, name, shape, dt)
        HT = sb("HT", [128, KD, T], BF16)
        c32 = sb("c32", [128, NC32], F32)
        ident32 = c32[:, 0:128]
        ones32 = c32[:, 128:256]
        Umask = [c32[:, 256:384], c32[:, 384:512]]
        NEGS = [c32[:, 512:640], c32[:, 640:768]]
        e0 = c32[:, 768:769]
        epsc = lambda i: c32[:, 769 + i:770 + i]
        identb = sb("identb", [128, 128], BF16)
        onesb = sb("onesb", [128, 128], BF16)
        onesr = sb("onesr", [128, 128], F32R)
        rm32 = sb("rm32", [128, 192], F32)
        rmb = sb("rmb", [128, 192], BF16)
        spp = sb("spp", [128, NSM], F32)
        rws = sb("rws", [128, NROW], F32)
        crs = sb("crs", [128, 2, KD * 128], F32)
        grow = sb("grow", [128, 2, 2, D], F32)
        modpp = sb("modpp", [128, 64], F32)
        AB = sb("ABpp", [128, 2, 2, 2, KD], F32)
        banks = [top.enter_context(nc.psum_tensor("bank%d" % i, [128, 512], F32)) for i in range(8)]
        PSB = Rot("psb", banks[0:3])
        jw = sb("jw", [128, 128], BF16)
        jr = sb("jr", [128, 512], BF16)
        P.op("pool", lambda e: e.memset(jw[:], 0.25), writes=["jw"])
        P.op("pool", lambda e: e.memset(jr[:], 0.5), writes=["jr"])
        P.dummy = lambda e: e.matmul(banks[3][:, 0:512], lhsT=jw[:], rhs=jr[:], start=True, stop=True)
        PSS = Rot("pss", [banks[4 + i % 3][:, ((i // 3) % 4) * 128:((i // 3) % 4 + 1) * 128] for i in range(12)])
        PSS.keyfn = lambda j: ("pssbank", j % 3)
        PSH = Rot("psh", [banks[4 + i % 3][:, ((i // 3) % 2) * 256:((i // 3) % 2 + 1) * 256] for i in range(6)])
        PSH.keyfn = lambda j: ("pssbank", j % 3)
        PSX = banks[7]

        o = 0
        def take(n):
            nonlocal o
            v = (o, o + n)
            o += n
            return v
        r_n1 = take(16); r_n2 = take(16); r_gc = take(72); r_fc = take(264); r_fb = take(88); r_qn = take(6); r_kvn = take(4)
        n1w = lambda l: spp[:, r_n1[0] + l * 8: r_n1[0] + l * 8 + 8]
        n2w = lambda l: spp[:, r_n2[0] + l * 8: r_n2[0] + l * 8 + 8]
        gconv = lambda l, ch, k: spp[:, r_gc[0] + (l * 12 + ch) * 3 + k: r_gc[0] + (l * 12 + ch) * 3 + k + 1]
        fconv = lambda l, ch, k: spp[:, r_fc[0] + (l * 44 + ch) * 3 + k: r_fc[0] + (l * 44 + ch) * 3 + k + 1]
        fconvb = lambda l, ch: spp[:, r_fb[0] + l * 44 + ch: r_fb[0] + l * 44 + ch + 1]
        qn = lambda l, k: spp[:, r_qn[0] + l * 3 + k: r_qn[0] + l * 3 + k + 1]
        kvn = lambda l, k: spp[:, r_kvn[0] + l * 2 + k: r_kvn[0] + l * 2 + k + 1]
        gnw = lambda l: rws[:, l * 128:(l + 1) * 128]
        rnw = lambda l, h: rws[:, 256 + l * 512 + h * 128: 256 + l * 512 + (h + 1) * 128]
        alog = lambda l: rws[:, 1280 + l * 8: 1280 + l * 8 + 8]
        dtb = lambda l: rws[:, 1296 + l * 8: 1296 + l * 8 + 8]
        fnw = rws[:, 1312:1312 + 1024]

        P.dma("sp", c32[:], consts, writes=["c32"])
        P.dma("sp", rm32[:], rmats, writes=["rm32"])
        P.dma("sp", spp[:], smallpp, writes=["spp"])
        P.dma("sp", rws[:], rows, writes=["rws"])
        for s in range(2):
            P.dma("sp", crs[:, s, :], crep[s], writes=[("crs", s)])
        P.op("dve", lambda e: e.tensor_copy(identb[:], ident32), reads=["c32"], writes=["identb"])
        P.op("dve", lambda e: e.tensor_copy(onesb[:], ones32), reads=["c32"], writes=["onesb"])
        P.op("dve", lambda e: e.tensor_copy(onesr[:], ones32), reads=["c32"], writes=["onesr"])
        P.op("dve", lambda e: e.tensor_copy(rmb[:], rm32[:]), reads=["rm32"], writes=["rmb"])
        for s in range(2):
            P.op("act", lambda e, s=s: e.activation(crs[:, s, :], crs[:, s, :], AF.Silu), reads=[("crs", s)], writes=[("crs", s)])
        cst = ["c32", "identb", "onesb", "onesr", "rmb", "spp", "rws"]

        def dump(name, src_ap, reads):
            if name in dbg_out:
                P.dma("sp", dbg_out[name], src_ap, reads=reads, writes=[("dbg", name)])
                bld.dbg_keys.append(("dbg", name))

        def phase_mod(l):
            with contextlib.ExitStack() as st:
                wbuf = bld.rot(st, "adaw", [128, KD, 512], F32, 2)
                bbuf = bld.rot(st, "adab", [1, 512], F32, 2)
                rowt = bld.rot(st, "modrow", [128, 512], F32, 2)
                pp_ps, pp_key = PSX, "psx"
                for nb in range(12):
                    wt, wk = wbuf.get()
                    bt, bk = bbuf.get()
                    P.dma("sp", wt[:], ada_w[l][:, nb * 512:(nb + 1) * 512].rearrange("(k p) c -> p k c", p=128), writes=[wk])
                    P.dma("sp", bt[:], ada_b[l:l + 1, nb * 512:(nb + 1) * 512], writes=[bk])
                    vec = nb // 2
                    half = nb % 2
                    for s in range(2):
                        ps, pk = PSB.get()
                        pairs = [(crs[:, s, k * 128:(k + 1) * 128], wt[:, k, :]) for k in range(KD)]
                        pairs.append((ones32[0:1, :], bt[0:1, :]))
                        bld.mm(ps[:], pairs, [wk, bk, ("crs", s), "c32"], pk)
                        if vec in (2, 5):
                            dst = grow[:, s, 0 if vec == 2 else 1, half * 512:(half + 1) * 512]
                            P.op("act", lambda e, dst=dst, ps=ps: e.copy(dst, ps[:]), reads=[pk], writes=[("grow", s, vec, half)])
                        else:
                            rt, rk = rowt.get()
                            P.op("dve", lambda e, rt=rt, ps=ps: e.tensor_copy(rt[:], ps[:]), reads=[pk], writes=[rk])
                            vi = {0: 0, 1: 1, 3: 2, 4: 3}[vec]
                            for c4 in range(4):
                                col = s * 32 + vi * 8 + half * 4 + c4
                                bld.mm(pp_ps[:, col:col + 1], [(rt[:, c4 * 128:(c4 + 1) * 128], e0)], [rk, "c32"], pp_key)
                P.op("dve", lambda e: e.tensor_copy(modpp[:], pp_ps[:, 0:64]), reads=[pp_key], writes=["modpp"])
                for s in range(2):
                    for nrm in range(2):
                        sh = modpp[:, s * 32 + (2 * nrm) * 8: s * 32 + (2 * nrm) * 8 + 8]
                        sc = modpp[:, s * 32 + (2 * nrm + 1) * 8: s * 32 + (2 * nrm + 1) * 8 + 8]
                        nw = n1w(l) if nrm == 0 else n2w(l)
                        P.op("dve", lambda e, sc=sc, nw=nw, s=s, nrm=nrm: e.scalar_tensor_tensor(out=AB[:, s, nrm, 0, :], in0=sc, scalar=1.0, in1=nw, op0=ALU.add, op1=ALU.mult),
                             reads=["modpp", "spp"], writes=[("AB", s, nrm, 0)])
                        P.op("dve", lambda e, s=s, nrm=nrm: e.tensor_scalar(out=AB[:, s, nrm, 0, :], in0=AB[:, s, nrm, 0, :], scalar1=float(math.sqrt(D)), scalar2=None, op0=ALU.mult),
                             reads=[("AB", s, nrm, 0)], writes=[("AB", s, nrm, 0)])
                        P.op("dve", lambda e, sh=sh, s=s, nrm=nrm: e.tensor_copy(AB[:, s, nrm, 1, :], sh), reads=["modpp"], writes=[("AB", s, nrm, 1)])
                P.barrier()

        def phase_norm(l, nrm, xsrc, tiles):
            with contextlib.ExitStack() as st:
                xb = bld.rot(st, "nx", [128, D], F32, 3)
                junk = bld.sb(st, "njunk", [128, D], BF16)
                xn = bld.rot(st, "nxn", [128, D], BF16, 8)
                ssb = bld.rot(st, "nss", [128, 1], F32, 8)
                groups = []
                cur = []
                for t in tiles:
                    cur.append(t)
                    if len(cur) == 4:
                        groups.append(cur); cur = []
                if cur:
                    groups.append(cur)
                for grp in groups:
                    xns = []
                    for t in grp:
                        xt, xk = xb.get()
                        P.dma("sp", xt[:], xsrc[t * 128:(t + 1) * 128, :], reads=[("xs", t)], writes=[xk])
                        ss, sk = ssb.get()
                        P.op("pool", lambda e, ss=ss: e.memset(ss[:], 0.0), writes=[sk])
                        P.op("dve", lambda e, xt=xt, ss=ss: e.scalar_tensor_tensor(out=junk[:], in0=xt[:], scalar=1.0, in1=xt[:], op0=ALU.mult, op1=ALU.mult, accum_out=ss[:]), reads=[xk, sk], writes=["njunk", sk])
                        P.op("act", lambda e, ss=ss: e.activation(ss[:], ss[:], AF.Sqrt, bias=epsc(0)), reads=[sk, "c32"], writes=[sk])
                        P.op("dve", lambda e, ss=ss: e.reciprocal(ss[:], ss[:]), reads=[sk], writes=[sk])
                        xnt, xnk = xn.get()
                        P.op("act", lambda e, xnt=xnt, xt=xt, ss=ss: e.activation(xnt[:], xt[:], AF.Copy, scale=ss[:]), reads=[xk, sk], writes=[xnk])
                        xns.append((t, xnt, xnk))
                    for k in range(KD):
                        ps, pk = PSB.get()
                        psv = ps[:].bitcast(BF16)
                        for i, (t, xnt, xnk) in enumerate(xns):
                            bld.tr(psv[:, i * 128:(i + 1) * 128], xnt[:, k * 128:(k + 1) * 128], identb[:], [xnk, "identb"], pk)
                        i = 0
                        while i < len(xns):
                            s = tile_stream(xns[i][0])
                            j = i
                            while j < len(xns) and tile_stream(xns[j][0]) == s:
                                j += 1
                            t0 = xns[i][0]
                            dst = HT[:, k, t0 * 128:(t0 + (j - i)) * 128]
                            src = psv[:, i * 128:j * 128]
                            wr = [("HT", tt) for tt in range(t0, t0 + (j - i))]
                            a_ap = AB[:, s, nrm, 0, k:k + 1]
                            b_ap = AB[:, s, nrm, 1, k:k + 1]
                            if k % 2 == 0:
                                P.op("act", lambda e, dst=dst, src=src, a_ap=a_ap, b_ap=b_ap: e.activation(dst, src, AF.Identity, bias=b_ap, scale=a_ap),
                                     reads=[pk, ("AB", s, nrm, 0), ("AB", s, nrm, 1)], writes=wr)
                            else:
                                P.op("dve", lambda e, dst=dst, src=src, a_ap=a_ap, b_ap=b_ap: e.tensor_scalar(out=dst, in0=src, scalar1=a_ap, scalar2=b_ap, op0=ALU.mult, op1=ALU.add),
                                     reads=[pk, ("AB", s, nrm, 0), ("AB", s, nrm, 1)], writes=wr)
                            i = j
                P.barrier()

        HTk = lambda ts: [("HT", t) for t in ts]
        ALLT = list(range(NT))

        def load_w(rotw, src2d, reads=()):
            wt, wk = rotw.get()
            P.dma("pool", wt[:], src2d.rearrange("(k p) c -> p k c", p=128), reads=reads, writes=[wk])
            return wt, wk

        BLKS = [(0, 512), (512, 512), (1024, 512), (1536, 512), (2048, 256)]

        def proj_fm(wt, wk, rhs_of, nk, evac, m=128, blks=BLKS, extra_reads=(), coff=0):
            for bi, (t0, n) in enumerate(blks):
                ps, pk = PSB.get()
                pairs = [(wt[:, k, coff:coff + m], rhs_of(k, t0, n)) for k in range(nk)]
                tl = list(range(t0 // 128, (t0 + n) // 128))
                bld.mm(ps[0:m, 0:n], pairs, [wk] + HTk(tl) + list(extra_reads), pk)
                evac(bi, t0, n, ps, pk)

        hT_rhs = lambda k, t0, n: HT[:, k, t0:t0 + n]

        def head_out_norm(st, l, h, Oacc, zs, nrot, b_idx, tiles, tag):
            oTh = bld.sb(st, tag + "oTh", [128, T], BF16)
            ssq = bld.sb(st, tag + "ssq", [128, NT], F32)
            junk = bld.sb(st, tag + "junk", [128, 128], BF16)
            yb = bld.rot(st, tag + "yb", [128, 128], BF16, 4)
            for t in tiles:
                P.op("act", lambda e, t=t: e.activation(junk[:], Oacc[:, t, :], AF.Square, accum_out=ssq[:, t:t + 1]), reads=[(tag + "O", t)], writes=[tag + "junk", (tag + "ssq", t)])
            t0, t1 = tiles[0], tiles[-1] + 1
            P.op("act", lambda e: e.activation(ssq[:, t0:t1], ssq[:, t0:t1], AF.Sqrt, bias=epsc(2), scale=1.0 / 128.0), reads=[(tag + "ssq", t) for t in tiles] + ["c32"], writes=[tag + "ssqall"])
            P.op("dve", lambda e: e.reciprocal(ssq[:, t0:t1], ssq[:, t0:t1]), reads=[tag + "ssqall"], writes=[tag + "ssqall"])
            grp = [tiles[i:i + 4] for i in range(0, len(tiles), 4)]
            for g in grp:
                ps, pk = PSB.get()
                psv = ps[:].bitcast(BF16)
                for i, t in enumerate(g):
                    y, yk = yb.get()
                    P.op("dve", lambda e, y=y, t=t: e.scalar_tensor_tensor(out=y[:], in0=Oacc[:, t, :], scalar=ssq[:, t:t + 1], in1=zs[:, t, :], op0=ALU.mult, op1=ALU.mult),
                         reads=[(tag + "O", t), tag + "ssqall", (tag + "zs", t)], writes=[yk])
                    bld.tr(psv[:, i * 128:(i + 1) * 128], y[:], identb[:], [yk, "identb"], pk)
                n = len(g) * 128
                P.op("act", lambda e, g=g, n=n, psv=psv: e.copy(oTh[:, g[0] * 128:g[0] * 128 + n], psv[:, 0:n]), reads=[pk], writes=[(tag + "oTh", t) for t in g])
            c0 = tiles[0] * 128
            P.dma("sp", oT_s[b_idx][h][:, c0:T], oTh[:, c0:T], reads=[(tag + "oTh", t) for t in tiles], writes=[("oTs", b_idx, h)])

        def phase_gdn(l, tiles):
            with contextlib.ExitStack() as st:
                wrot = bld.rot(st, "gw", [128, KD, 128], BF16, 3)
                wab = bld.sb(st, "gwab", [128, KD, 16], BF16)
                gbeta = bld.sb(st, "gbeta", [128, NT, 16], F32)
                tmp8 = bld.sb(st, "gtmp8", [128, NT, 8], F32)
                P.dma("pool", wab[:], w_in[l][:, GAB:GAB + 16].rearrange("(k p) c -> p k c", p=128), writes=["gwab"])
                ab_ps, ab_key = PSX, "psx"
                for t in ALLT:
                    bld.mm(ab_ps[:, t * 16:(t + 1) * 16], [(HT[:, k, t * 128:(t + 1) * 128], wab[:, k, :]) for k in range(KD)], ["gwab", ("HT", t)], ab_key)
                abv = ab_ps[:, 0:NT * 16].rearrange("p (t c) -> p t c", c=16)
                for t in ALLT:
                    P.op("dve", lambda e, t=t: e.tensor_tensor(out=tmp8[:, t, :], in0=abv[:, t, 0:8], in1=dtb(l), op=ALU.add), reads=[ab_key, "rws"], writes=[("gtmp8", t)])
                al = bld.sb(st, "galog", [128, 8], F32)
                P.op("act", lambda e: e.activation(al[:], alog(l), AF.Exp), reads=["rws"], writes=["galog"])
                t8all = [("gtmp8", t) for t in ALLT]
                P.op("act", lambda e: e.activation(tmp8[:], tmp8[:], AF.Exp), reads=t8all, writes=["gtmp8all"])
                P.op("act", lambda e: e.activation(tmp8[:], tmp8[:], AF.Ln, bias=1.0), reads=["gtmp8all"], writes=["gtmp8all"])
                for t in ALLT:
                    P.op("dve", lambda e, t=t: e.scalar_tensor_tensor(out=gbeta[:, t, 0:8], in0=tmp8[:, t, :], scalar=-1.0, in1=al[:], op0=ALU.mult, op1=ALU.mult),
                         reads=["gtmp8all", "galog"], writes=[("gb_g", t)])
                P.op("act", lambda e: e.activation(gbeta[:, :, 8:16], abv[:, :, 8:16], AF.Sigmoid), reads=[ab_key], writes=["gb_beta"])
                gbk = [("gb_g", t) for t in ALLT] + ["gb_beta"]
                P.barrier()
                for h in range(4):
                    with contextlib.ExitStack() as sh:
                        gdn_head(sh, l, h, tiles, wrot, gbeta)
                    P.barrier()

        def gdn_head(st, l, h, tiles, wrot, gbeta):
            sbh = lambda name, shape, dt: bld.sb(st, name, shape, dt)
            W = T + 3
            off = lambda t0: t0 + 1 if t0 < 256 else t0 + 2
            raw = bld.rot(st, "graw", [128, W], F32, 2)
            cv = bld.rot(st, "gcv", [128, W], F32, 2)
            qT = sbh("gqT", [128, T], BF16)
            kT = sbh("gkT", [128, T], BF16)
            vT = sbh("gvT", [128, T], BF16)
            ktok = sbh("gktok", [128, NT, 128], BF16)
            vtok = sbh("gvtok", [128, NT, 128], BF16)
            zs = sbh("gzs", [128, NT, 128], F32)
            Oacc = sbh("gO", [128, NT, 128], F32)
            sqr = bld.rot(st, "gsqr", [128, 512], F32R, 2)
            rnb = bld.rot(st, "grnb", [128, 512], F32, 2)
            for fi, (c0, dst) in enumerate(((GQ, qT), (GK, kT), (GV, vT))):
                ch = fi * 4 + h
                wt, wk = load_w(wrot, w_in[l][:, c0 + h * 128: c0 + (h + 1) * 128])
                rw, rk = raw.get()
                P.op("pool", lambda e, rw=rw: e.memset(rw[:], 0.0), writes=[rk])
                def ev(bi, t0, n, ps, pk, rw=rw, rk=rk):
                    o_ = off(t0)
                    if t0 == 0:
                        P.op("act", lambda e: e.copy(rw[:, 1:257], ps[:, 0:256]), reads=[pk], writes=[rk])
                        P.op("dve", lambda e: e.tensor_copy(rw[:, 258:514], ps[:, 256:512]), reads=[pk], writes=[rk])
                    else:
                        eng = "act" if bi % 2 else "dve"
                        if eng == "act":
                            P.op("act", lambda e: e.copy(rw[:, o_:o_ + n], ps[:, 0:n]), reads=[pk], writes=[rk])
                        else:
                            P.op("dve", lambda e: e.tensor_copy(rw[:, o_:o_ + n], ps[:, 0:n]), reads=[pk], writes=[rk])
                proj_fm(wt, wk, hT_rhs, KD, ev)
                c, ck = cv.get()
                P.op("act", lambda e, c=c, rw=rw, ch=ch: e.activation(c[:, 1:W - 1], rw[:, 1:W - 1], AF.Copy, scale=gconv(l, ch, 1)), reads=[rk, "spp"], writes=[ck])
                P.op("dve", lambda e, c=c, rw=rw, ch=ch: e.scalar_tensor_tensor(out=c[:, 1:W - 1], in0=rw[:, 0:W - 2], scalar=gconv(l, ch, 0), in1=c[:, 1:W - 1], op0=ALU.mult, op1=ALU.add), reads=[rk, ck, "spp"], writes=[ck])
                P.op("dve", lambda e, c=c, rw=rw, ch=ch: e.scalar_tensor_tensor(out=c[:, 1:W - 1], in0=rw[:, 2:W], scalar=gconv(l, ch, 2), in1=c[:, 1:W - 1], op0=ALU.mult, op1=ALU.add), reads=[rk, ck, "spp"], writes=[ck])
                P.op("act", lambda e, c=c: e.activation(c[:, 1:W - 1], c[:, 1:W - 1], AF.Silu), reads=[ck], writes=[ck])
                if fi == 2:
                    P.op("dve", lambda e, c=c: e.tensor_copy(vT[:, 0:256], c[:, 1:257]), reads=[ck], writes=[("gT", 2, 0)])
                    P.op("dve", lambda e, c=c: e.tensor_copy(vT[:, 256:T], c[:, 258:W - 1]), reads=[ck], writes=[("gT", 2, 1)])
                else:
                    for bi, (t0, n) in enumerate(BLKS):
                        segs = [(0, 256), (256, 256)] if t0 == 0 else [(t0, n)]
                        sq, sqk = sqr.get()
                        for (s0, sn) in segs:
                            P.op("act", lambda e, c=c, s0=s0, sn=sn, sq=sq, t0=t0: e.activation(sq[:, s0 - t0:s0 - t0 + sn], c[:, off(s0):off(s0) + sn], AF.Square), reads=[ck], writes=[sqk])
                        ps, pk = PSB.get()
                        bld.mm(ps[:, 0:n], [(onesr[:], sq[:, 0:n])], [sqk, "onesr"], pk)
                        rn, rnk = rnb.get()
                        P.op("act", lambda e, rn=rn, ps=ps, n=n: e.activation(rn[:, 0:n], ps[:, 0:n], AF.Sqrt, bias=epsc(2)), reads=[pk, "c32"], writes=[rnk])
                        P.op("dve", lambda e, rn=rn, n=n: e.reciprocal(rn[:, 0:n], rn[:, 0:n]), reads=[rnk], writes=[rnk])
                        scl = float(128 ** -0.5) if fi == 0 else 1.0
                        for (s0, sn) in segs:
                            P.op("dve", lambda e, c=c, s0=s0, sn=sn, rn=rn, t0=t0, dst=dst, scl=scl: e.scalar_tensor_tensor(out=dst[:, s0:s0 + sn], in0=c[:, off(s0):off(s0) + sn], scalar=scl, in1=rn[:, s0 - t0:s0 - t0 + sn], op0=ALU.mult, op1=ALU.mult),
                                 reads=[ck, rnk], writes=[("gT", fi, s0)])
            gTk = lambda fi: [("gT", fi, s0) for s0 in (0, 256, 512, 1024, 1536, 2048)] + [("gT", 2, 0), ("gT", 2, 1)]
            for (src, dstt, fi, nm) in ((kT, ktok, 1, "gktok"), (vT, vtok, 2, "gvtok")):
                for g0 in range(0, NT, 4):
                    g = list(range(g0, min(g0 + 4, NT)))
                    ps, pk = PSB.get()
                    psv = ps[:].bitcast(BF16)
                    for i, t in enumerate(g):
                        bld.tr(psv[:, i * 128:(i + 1) * 128], src[:, t * 128:(t + 1) * 128], identb[:], gTk(fi) + ["identb"], pk)
                    n = len(g) * 128
                    P.op("act" if (g0 // 4) % 2 else "dve",
                         (lambda e, g=g, n=n, psv=psv, dstt=dstt: e.copy(dstt[:, g[0]:g[0] + len(g), :], psv[:, 0:n].rearrange("p (t c) -> p t c", c=128))) if (g0 // 4) % 2 else
                         (lambda e, g=g, n=n, psv=psv, dstt=dstt: e.tensor_copy(dstt[:, g[0]:g[0] + len(g), :], psv[:, 0:n].rearrange("p (t c) -> p t c", c=128))),
                         reads=[pk], writes=[(nm, t) for t in g])
            wt, wk = load_w(wrot, w_in[l][:, GZ + h * 128: GZ + (h + 1) * 128])
            for t in tiles:
                ps, pk = PSS.get()
                bld.mm(ps, [(HT[:, k, t * 128:(t + 1) * 128], wt[:, k, :]) for k in range(KD)], [wk, ("HT", t)], pk)
                P.op("act", lambda e, t=t, ps=ps: e.activation(zs[:, t, :], ps, AF.Silu), reads=[pk], writes=[("gzs", t)])
                P.op("pool", lambda e, t=t: e.tensor_tensor(out=zs[:, t, :], in0=zs[:, t, :], in1=gnw(l), op=ALU.mult), reads=[("gzs", t), "rws"], writes=[("gzs", t)])
            f32t = lambda name, n: bld.rot(st, name, [128, 128], F32, n)
            b16t = lambda name, n: bld.rot(st, name, [128, 128], BF16, n)
            gbr = f32t("g_gb", 3); egr = f32t("g_eg", 5); dsr = f32t("g_ds", 3); dir_ = f32t("g_di", 3)
            colr = bld.rot(st, "g_col", [128, 4], F32, 5)
            Pm = bld.rot(st, "g_P", [128, 256], F32, 4); PTm = bld.rot(st, "g_PT", [128, 256], F32, 4); Rm = bld.rot(st, "g_R", [128, 256], F32, 3)
            ident2 = sbh("g_id2", [128, 256], F32)
            P.op("dve", lambda e: e.tensor_copy(ident2[:, 0:128], ident32), reads=["c32"], writes=["g_id2"])
            P.op("dve", lambda e: e.tensor_copy(ident2[:, 128:256], ident32), reads=["c32"], writes=["g_id2"])
            TTb = bld.rot(st, "g_TT", [128, 256], BF16, 3); atb = b16t("g_at", 5); qdb = b16t("g_qd", 5); kdb = b16t("g_kd", 5)
            rb = b16t("g_r", 3); vnb = b16t("g_vn", 3)
            S32 = [sbh("gS32_%d" % d, [128, 128], F32) for d in range(2)]
            Sb = [bld.rot(st, "gSb%d" % d, [128, 128], BF16, 2) for d in range(2)]
            order = [list(range(NT)), [1, 0] + list(range(NT - 1, 1, -1))]
            cur_Sb = [None, None]
            for d in range(2):
                P.op("pool", lambda e, d=d: e.memset(S32[d][:], 0.0), writes=[("gS32", d)])
                sbt, sbk = Sb[d].get()
                P.op("pool", lambda e, sbt=sbt: e.memset(sbt[:], 0.0), writes=[sbk])
                cur_Sb[d] = (sbt, sbk)
            visited = set()
            kT_k, qT_k = gTk(1), gTk(0)

            def precompute(d, c, P2, P2k):
                q = d * 4 + h
                gcol = gbeta[:, c, q:q + 1]
                bcol = gbeta[:, c, 8 + q:9 + q]
                tcol = slice(c * 128, (c + 1) * 128)
                gb, gbk_ = gbr.get()
                P.op("dve", lambda e: e.tensor_scalar(out=gb[:], in0=Umask[d], scalar1=gcol, scalar2=None, op0=ALU.mult), reads=[("gb_g", c), "c32"], writes=[gbk_])
                psA, kA = PSS.get()
                bld.mm(psA, [(ones32, gb[:])], [gbk_, "c32"], kA)
                psB, kB = PSS.get()
                bld.mm(psB, [(ones32, gb[:]), (ident32, NEGS[d])], [gbk_, "c32"], kB)
                psC, kC = PSS.get()
                bld.mm(psC[:, 0:1], [(Umask[d], gcol)], [("gb_g", c), "c32"], kC)
                col, colk = colr.get()
                P.op("dve", lambda e: e.tensor_scalar(out=col[:, 1:2], in0=psC[:, 0:1], scalar1=-1.0, scalar2=None, op0=ALU.mult), reads=[kC], writes=[colk])
                P.op("act", lambda e: e.activation(col[:, 0:1], psC[:, 0:1], AF.Exp), reads=[kC], writes=[colk])
                P.op("dve", lambda e: e.tensor_scalar(out=col[:, 0:1], in0=col[:, 0:1], scalar1=-1.0, scalar2=None, op0=ALU.mult), reads=[colk], writes=[colk])
                P.op("dve", lambda e: e.tensor_scalar(out=col[:, 2:3], in0=bcol, scalar1=-1.0, scalar2=None, op0=ALU.mult), reads=["gb_beta"], writes=[colk])
                eg, egk = egr.get()
                P.op("act", lambda e: e.activation(eg[:], psA, AF.Exp), reads=[kA], writes=[egk])
                ds, dsk = dsr.get()
                P.op("act", lambda e: e.activation(ds[:], psB, AF.Exp, bias=col[:, 1:2]), reads=[kB, colk], writes=[dsk])
                di, dik = dir_.get()
                P.op("dve", lambda e: e.tensor_tensor(out=di[:], in0=ds[:], in1=ident32, op=ALU.add), reads=[dsk, "c32"], writes=[dik])
                psK, kK = PSS.get()
                bld.mm(psK, [(kT[:, tcol], kT[:, tcol])], kT_k, kK)
                psQ, kQ = PSS.get()
                bld.mm(psQ, [(kT[:, tcol], qT[:, tcol])], kT_k + qT_k, kQ)
                p0 = P2[:, d * 128:(d + 1) * 128]
                p0k = (P2k, d)
                P.op("dve", lambda e: e.scalar_tensor_tensor(out=p0, in0=psK, scalar=col[:, 2:3], in1=ds[:], op0=ALU.mult, op1=ALU.mult), reads=[kK, colk, dsk], writes=[p0k])
                at, atk = atb.get()
                P.op("dve", lambda e: e.tensor_tensor(out=at[:], in0=psQ, in1=di[:], op=ALU.mult), reads=[kQ, dik], writes=[atk])
                last = 127 if d == 0 else 0
                kd, kdk = kdb.get()
                P.op("pool", lambda e: e.tensor_scalar(out=kd[:], in0=ktok[:, c, :], scalar1=di[:, last:last + 1], scalar2=None, op0=ALU.mult), reads=[("gktok", c), dik], writes=[kdk])
                qd, qdk = qdb.get()
                P.op("pool", lambda e: e.tensor_tensor(out=qd[:], in0=qT[:, tcol], in1=eg[:], op=ALU.mult), reads=qT_k + [egk], writes=[qdk])
                return dict(col=col, colk=colk, eg=eg, egk=egk, at=at, atk=atk, kd=kd, kdk=kdk, qd=qd, qdk=qdk, bcol=bcol, last=last)

            def step(d, c, pre):
                tcol = slice(c * 128, (c + 1) * 128)
                sbt, sbk = cur_Sb[d]
                psk, kk = PSS.get()
                bld.mm(psk, [(kT[:, tcol], sbt[:])], kT_k + [sbk], kk)
                r, rk_ = rb.get()
                P.op("dve", lambda e: e.scalar_tensor_tensor(out=r[:], in0=psk, scalar=pre["col"][:, 0:1], in1=vtok[:, c, :], op0=ALU.mult, op1=ALU.add), reads=[kk, pre["colk"], ("gvtok", c)], writes=[rk_])
                psv, kv = PSS.get()
                bld.mm(psv, [(pre["tt"], r[:])], [pre["ttk"], rk_], kv)
                vn, vnk = vnb.get()
                P.op("act", lambda e: e.activation(vn[:], psv, AF.Copy, scale=pre["bcol"]), reads=[kv, "gb_beta"], writes=[vnk])
                pso, ko = PSS.get()
                bld.mm(pso, [(pre["qd"][:], sbt[:]), (pre["at"][:], vn[:])], [pre["qdk"], sbk, pre["atk"], vnk], ko)
                if c in visited:
                    P.op("dve", lambda e: e.tensor_tensor(out=Oacc[:, c, :], in0=pso, in1=Oacc[:, c, :], op=ALU.add), reads=[ko, ("gO", c)], writes=[("gO", c)])
                else:
                    visited.add(c)
                    P.op("act", lambda e: e.copy(Oacc[:, c, :], pso), reads=[ko], writes=[("gO", c)])
                pss_, ks = PSS.get()
                bld.mm(pss_, [(pre["kd"][:], vn[:])], [pre["kdk"], vnk], ks)
                last = pre["last"]
                P.op("dve", lambda e: e.scalar_tensor_tensor(out=S32[d][:], in0=S32[d][:], scalar=pre["eg"][:, last:last + 1], in1=pss_, op0=ALU.mult, op1=ALU.add), reads=[("gS32", d), pre["egk"], ks], writes=[("gS32", d)])
                nsb, nsbk = Sb[d].get()
                P.op("act", lambda e: e.copy(nsb[:], S32[d][:]), reads=[("gS32", d)], writes=[nsbk])
                cur_Sb[d] = (nsb, nsbk)

            def neumann2(P2, P2k):
                pk_all = [(P2k, 0), (P2k, 1)]
                psT, kT_ = PSH.get()
                for x in range(2):
                    bld.tr(psT[:, x * 128:(x + 1) * 128], P2[:, x * 128:(x + 1) * 128], ident32, pk_all + ["c32"], kT_)
                PT2, PT2k = PTm.get()
                P.op("act", lambda e, PT2=PT2: e.copy(PT2[:], psT), reads=[kT_], writes=[PT2k])
                R2, R2k = Rm.get()
                P.op("dve", lambda e, R2=R2: e.tensor_tensor(out=R2[:], in0=P2[:], in1=ident2[:], op=ALU.add), reads=pk_all + ["g_id2"], writes=[R2k])
                pc, pck, ptc, ptck = P2, pk_all, PT2, [PT2k]
                sl = lambda t_, x: t_[:, x * 128:(x + 1) * 128]
                for lev in range(6):
                    if lev < 5:
                        S1, k1 = PSH.get()
                        for x in range(2):
                            bld.mm(sl(S1, x), [(sl(ptc, x), sl(pc, x))], pck + ptck, k1)
                    S2, k2 = PSH.get()
                    for x in range(2):
                        bld.mm(sl(S2, x), [(sl(pc, x), sl(ptc, x))], pck + ptck, k2)
                    if lev >= 1:
                        S3, k3 = PSH.get()
                        for x in range(2):
                            bld.mm(sl(S3, x), [(sl(ptc, x), sl(R2, x))], ptck + [R2k], k3)
                    Pn, Pnk = Pm.get()
                    if lev < 5:
                        P.op("act", lambda e, Pn=Pn, S1=S1: e.copy(Pn[:], S1), reads=[k1], writes=[Pnk])
                    PTn, PTnk = PTm.get()
                    P.op("dve" if lev < 1 else "act", (lambda e, PTn=PTn, S2=S2: e.tensor_copy(PTn[:], S2)) if lev < 1 else (lambda e, PTn=PTn, S2=S2: e.copy(PTn[:], S2)), reads=[k2], writes=[PTnk])
                    if lev >= 1:
                        Rn, Rnk = Rm.get()
                        P.op("dve", lambda e, Rn=Rn, R2=R2, S3=S3: e.tensor_tensor(out=Rn[:], in0=S3, in1=R2[:], op=ALU.add), reads=[k3, R2k], writes=[Rnk])
                        R2, R2k = Rn, Rnk
                    pc, pck, ptc, ptck = Pn, [Pnk], PTn, [PTnk]
                S3, k3 = PSH.get()
                for x in range(2):
                    bld.mm(sl(S3, x), [(sl(ptc, x), sl(R2, x))], ptck + [R2k], k3)
                tt2, tt2k = TTb.get()
                P.op("dve", lambda e, tt2=tt2, R2=R2, S3=S3: e.tensor_tensor(out=tt2[:], in0=S3, in1=R2[:], op=ALU.add), reads=[k3, R2k], writes=[tt2k])
                return tt2, tt2k

            pres = {}
            P.warm = GDN_WARM
            for s_ in range(NT + 1):
                if s_ < NT:
                    P2, P2k = Pm.get()
                    pr = [precompute(d, order[d][s_], P2, P2k) for d in range(2)]
                    tt2, tt2k = neumann2(P2, P2k)
                    for d in range(2):
                        pr[d]["tt"] = tt2[:, d * 128:(d + 1) * 128]
                        pr[d]["ttk"] = tt2k
                        pres[(d, order[d][s_])] = pr[d]
                for d in range(2):
                    if s_ >= 1:
                        c_ = order[d][s_ - 1]
                        step(d, c_, pres.pop((d, c_)))
            P.warm = 0
            head_out_norm(st, l, h, Oacc, zs, None, 0, tiles, "g")
        def rope_fm(src_bf, src_key, dst, dst_key_of, nrows, rmat, cos_t, sin_t, tabk, tmpA, tmpB):
            for bi, (t0, n) in enumerate(BLKS):
                ps, pk = PSB.get()
                bld.mm(ps[0:nrows, 0:n], [(rmat, src_bf[0:nrows, t0:t0 + n])], [src_key, "rmb"], pk)
                a, ak = tmpA.get()
                b_, bk = tmpB.get()
                P.op("dve", lambda e, a=a, ps=ps, n=n, t0=t0: e.tensor_tensor(out=a[0:nrows, 0:n], in0=ps[0:nrows, 0:n], in1=sin_t[0:nrows, t0:t0 + n], op=ALU.mult), reads=[pk, tabk], writes=[ak])
                P.op("pool", lambda e, b_=b_, n=n, t0=t0: e.tensor_tensor(out=b_[0:nrows, 0:n], in0=src_bf[0:nrows, t0:t0 + n], in1=cos_t[0:nrows, t0:t0 + n], op=ALU.mult), reads=[src_key, tabk], writes=[bk])
                P.op("dve", lambda e, a=a, b_=b_, n=n, t0=t0: e.tensor_tensor(out=dst[0:nrows, t0:t0 + n], in0=a[0:nrows, 0:n], in1=b_[0:nrows, 0:n], op=ALU.add), reads=[ak, bk], writes=[dst_key_of(bi)])

        def phase_mla(l, ctx_out):
            with contextlib.ExitStack() as st:
                sbm = lambda name, shape, dt: bld.sb(st, name, shape, dt)
                wrot = bld.rot(st, "mw", [128, KD, 128], BF16, 3)
                cqn = sbm("cqn", [128, 3, T], BF16)
                ckvn = sbm("ckvn", [128, 2, T], BF16)
                krr = sbm("krr", [64, T], BF16)
                cosm = sbm("cosm", [64, T], F32)
                sinm = sbm("sinm", [64, T], F32)
                wuq = sbm("wuq", [128, 3, 768], BF16)
                wukv = sbm("wukv", [128, 2, 1024], BF16)
                tA = bld.rot(st, "mtA", [128, 512], F32, 2)
                tB = bld.rot(st, "mtB", [128, 512], F32, 2)
                qnsc = sbm("qnsc", [128, 5], F32)
                st1 = contextlib.ExitStack()
                cqraw = bld.sb(st1, "cqraw", [128, 3, T], F32)
                ckvraw = bld.sb(st1, "ckvraw", [128, 2, T], F32)
                krb = bld.sb(st1, "krb", [64, T], BF16)
                sqr = bld.rot(st1, "msqr", [128, 512], F32R, 2)
                rnb = bld.rot(st1, "mrnb", [128, 512], F32, 2)
                P.dma("sp", cosm[:], ropem[0], writes=["ropem"])
                P.dma("sp", sinm[:], ropem[1], writes=["ropem2"])
                P.dma("pool", wuq[:], w_uq[l].rearrange("(k p) c -> p k c", p=128), writes=["wuq"])
                P.dma("pool", wukv[:], w_ukv[l].rearrange("(k p) c -> p k c", p=128), writes=["wukv"])
                for k in range(3):
                    P.op("dve", lambda e, k=k: e.tensor_scalar(out=qnsc[:, k:k + 1], in0=qn(l, k), scalar1=float(math.sqrt(384.0)), scalar2=None, op0=ALU.mult), reads=["spp"], writes=["qnsc"])
                for k in range(2):
                    P.op("dve", lambda e, k=k: e.tensor_scalar(out=qnsc[:, 3 + k:4 + k], in0=kvn(l, k), scalar1=float(math.sqrt(256.0)), scalar2=None, op0=ALU.mult), reads=["spp"], writes=["qnsc"])
                for (c0, nch, rawt, nm) in ((CQ, 3, cqraw, "cqraw"), (CKV, 2, ckvraw, "ckvraw")):
                    for ch in range(nch):
                        wt, wk = load_w(wrot, w_in[l][:, c0 + ch * 128: c0 + (ch + 1) * 128])
                        def ev(bi, t0, n, ps, pk, rawt=rawt, ch=ch, nm=nm):
                            if bi % 2:
                                P.op("act", lambda e: e.copy(rawt[:, ch, t0:t0 + n], ps[:, 0:n]), reads=[pk], writes=[(nm, ch, bi)])
                            else:
                                P.op("dve", lambda e: e.tensor_copy(rawt[:, ch, t0:t0 + n], ps[:, 0:n]), reads=[pk], writes=[(nm, ch, bi)])
                        proj_fm(wt, wk, hT_rhs, KD, ev)
                wt, wk = load_w(wrot, w_in[l][:, KR:KR + 128])
                def evk(bi, t0, n, ps, pk):
                    P.op("act", lambda e: e.copy(krb[:, t0:t0 + n], ps[0:64, 0:n]), reads=[pk], writes=[("krb", bi)])
                proj_fm(wt, wk, hT_rhs, KD, evk, m=64)
                for bi, (t0, n) in enumerate(BLKS):
                    ps, pk = PSB.get()
                    bld.mm(ps[0:64, 0:n], [(rmb[0:64, 128:192], krb[:, t0:t0 + n])], [("krb", bi), "rmb"], pk)
                    a, ak = tA.get()
                    b_, bk = tB.get()
                    P.op("dve", lambda e, a=a, ps=ps, n=n, t0=t0: e.tensor_tensor(out=a[0:64, 0:n], in0=ps[0:64, 0:n], in1=sinm[:, t0:t0 + n], op=ALU.mult), reads=[pk, "ropem2"], writes=[ak])
                    P.op("pool", lambda e, b_=b_, n=n, t0=t0: e.tensor_tensor(out=b_[0:64, 0:n], in0=krb[:, t0:t0 + n], in1=cosm[:, t0:t0 + n], op=ALU.mult), reads=[("krb", bi), "ropem"], writes=[bk])
                    P.op("dve", lambda e, a=a, b_=b_, n=n, t0=t0: e.tensor_tensor(out=krr[:, t0:t0 + n], in0=a[0:64, 0:n], in1=b_[0:64, 0:n], op=ALU.add), reads=[ak, bk], writes=[("krr", bi)])
                for (nch, rawt, nm, dstn, dnm, eps_i, q0) in ((3, cqraw, "cqraw", cqn, "cqn", 3, 0), (2, ckvraw, "ckvraw", ckvn, "ckvn", 4, 3)):
                    for bi, (t0, n) in enumerate(BLKS):
                        ps, pk = PSB.get()
                        for ch in range(nch):
                            sq, sqk = sqr.get()
                            P.op("act", lambda e, sq=sq, ch=ch, t0=t0, n=n, rawt=rawt: e.activation(sq[:, 0:n], rawt[:, ch, t0:t0 + n], AF.Square), reads=[(nm, ch, bi)], writes=[sqk])
                            bld.mm_acc(ps[:, 0:n], onesr[:], sq[:, 0:n], ch == 0, ch == nch - 1, [sqk, "onesr"], pk)
                        rn, rnk = rnb.get()
                        P.op("act", lambda e, rn=rn, ps=ps, n=n, eps_i=eps_i: e.activation(rn[:, 0:n], ps[:, 0:n], AF.Sqrt, bias=epsc(eps_i)), reads=[pk, "c32"], writes=[rnk])
                        P.op("dve", lambda e, rn=rn, n=n: e.reciprocal(rn[:, 0:n], rn[:, 0:n]), reads=[rnk], writes=[rnk])
                        for ch in range(nch):
                            P.op("dve", lambda e, ch=ch, rn=rn, t0=t0, n=n, rawt=rawt, dstn=dstn, q0=q0: e.scalar_tensor_tensor(out=dstn[:, ch, t0:t0 + n], in0=rawt[:, ch, t0:t0 + n], scalar=qnsc[:, q0 + ch:q0 + ch + 1], in1=rn[:, 0:n], op0=ALU.mult, op1=ALU.mult),
                                 reads=[(nm, ch, bi), rnk, "qnsc"], writes=[(dnm, bi)])
                P.barrier()
                st1.close()
                qnope = sbm("qnope", [128, T], BF16)
                qrb = sbm("qrb", [64, T], BF16)
                qrr = sbm("qrr", [64, T], BF16)
                knope = sbm("knope", [128, T], BF16)
                vtok = sbm("mvtok", [128, NT, 128], BF16)
                oTh = sbm("moTh", [128, T], BF16)
                pT = bld.rot(st, "mpT", [128, 512], BF16, 3)
                rden = bld.rot(st, "mrden", [128, 512], F32, 2)
                cqk = lambda: [("cqn", bi) for bi in range(5)]
                ckk = lambda: [("ckvn", bi) for bi in range(5)]
                for h in range(4):
                    for bi, (t0, n) in enumerate(BLKS):
                        ps, pk = PSB.get()
                        bld.mm(ps[:, 0:n], [(wuq[:, k, h * 192:h * 192 + 128], cqn[:, k, t0:t0 + n]) for k in range(3)], ["wuq", ("cqn", bi)], pk)
                        P.op("act", lambda e, ps=ps, t0=t0, n=n: e.activation(qnope[:, t0:t0 + n], ps[:, 0:n], AF.Copy, scale=float(MLA_SCALE)), reads=[pk], writes=[("qnope", bi)])
                        ps, pk = PSB.get()
                        bld.mm(ps[0:64, 0:n], [(wuq[:, k, h * 192 + 128:h * 192 + 192], cqn[:, k, t0:t0 + n]) for k in range(3)], ["wuq", ("cqn", bi)], pk)
                        P.op("act", lambda e, ps=ps, t0=t0, n=n: e.activation(qrb[:, t0:t0 + n], ps[0:64, 0:n], AF.Copy, scale=float(MLA_SCALE)), reads=[pk], writes=[("qrb", bi)])
                        ps, pk = PSB.get()
                        bld.mm(ps[:, 0:n], [(wukv[:, k, h * 256:h * 256 + 128], ckvn[:, k, t0:t0 + n]) for k in range(2)], ["wukv", ("ckvn", bi)], pk)
                        P.op("dve", lambda e, ps=ps, t0=t0, n=n: e.tensor_copy(knope[:, t0:t0 + n], ps[:, 0:n]), reads=[pk], writes=[("knope", bi)])
                        ps, pk = PSB.get()
                        bld.mm(ps[0:64, 0:n], [(rmb[0:64, 128:192], qrb[:, t0:t0 + n])], [("qrb", bi), "rmb"], pk)
                        a, ak = tA.get()
                        b_, bk = tB.get()
                        P.op("dve", lambda e, a=a, ps=ps, n=n, t0=t0: e.tensor_tensor(out=a[0:64, 0:n], in0=ps[0:64, 0:n], in1=sinm[:, t0:t0 + n], op=ALU.mult), reads=[pk, "ropem2"], writes=[ak])
                        P.op("pool", lambda e, b_=b_, n=n, t0=t0: e.tensor_tensor(out=b_[0:64, 0:n], in0=qrb[:, t0:t0 + n], in1=cosm[:, t0:t0 + n], op=ALU.mult), reads=[("qrb", bi), "ropem"], writes=[bk])
                        P.op("dve", lambda e, a=a, b_=b_, n=n, t0=t0: e.tensor_tensor(out=qrr[:, t0:t0 + n], in0=a[0:64, 0:n], in1=b_[0:64, 0:n], op=ALU.add), reads=[ak, bk], writes=[("qrr", bi)])
                    for t in ALLT:
                        ps, pk = PSB.get()
                        bld.mm(ps[:, 0:128], [(ckvn[:, k, t * 128:(t + 1) * 128], wukv[:, k, h * 256 + 128:h * 256 + 256]) for k in range(2)], ["wukv", ("ckvn", min(t // 4, 4))], pk)
                        P.op("act", lambda e, t=t, ps=ps: e.copy(vtok[:, t, :], ps[:, 0:128]), reads=[pk], writes=[("mvtok", t)])
                    qgroups = [(256 + g * 512, 512, ALLT) for g in range(4)]
                    if ctx_out:
                        qgroups = [(0, 256, [0, 1])] + qgroups
                    for (q0, nq, ktiles) in qgroups:
                        qb = min(q0 // 512, 4)
                        qbs = sorted(set([min(q0 // 512, 4), min((q0 + nq - 1) // 512, 4)]))
                        o_ps, o_k = banks[6], "bank6"
                        d_ps, d_k = banks[7], "psx"
                        def s_mm(kt):
                            kb = min(kt // 4, 4)
                            ps, pk = PSB.get()
                            bld.mm(ps[:, 0:nq], [(knope[:, kt * 128:(kt + 1) * 128], qnope[:, q0:q0 + nq]), (krr[:, kt * 128:(kt + 1) * 128], qrr[:, q0:q0 + nq])],
                                   [("knope", kb), ("krr", kb)] + [("qnope", b) for b in qbs] + [("qrr", b) for b in qbs], pk)
                            return ps, pk
                        pend = [s_mm(kt) for kt in ktiles[:2]]
                        for i, kt in enumerate(ktiles):
                            ps, pk = pend.pop(0)
                            p_, pkk = pT.get()
                            P.op("act", lambda e, p_=p_, ps=ps, nq=nq: e.activation(p_[:, 0:nq], ps[:, 0:nq], AF.Exp), reads=[pk], writes=[pkk])
                            if i + 2 < len(ktiles):
                                pend.append(s_mm(ktiles[i + 2]))
                            bld.mm_acc(o_ps[:, 0:nq], vtok[:, kt, :], p_[:, 0:nq], i == 0, i == len(ktiles) - 1, [("mvtok", kt), pkk], o_k)
                            bld.mm_acc(d_ps[:, 0:nq], onesb[:], p_[:, 0:nq], i == 0, i == len(ktiles) - 1, ["onesb", pkk], d_k)
                        rd, rdk = rden.get()
                        P.op("dve", lambda e, rd=rd, nq=nq: e.reciprocal(rd[:, 0:nq], d_ps[:, 0:nq]), reads=[d_k], writes=[rdk])
                        P.op("dve", lambda e, rd=rd, nq=nq, q0=q0: e.tensor_tensor(out=oTh[:, q0:q0 + nq], in0=o_ps[:, 0:nq], in1=rd[:, 0:nq], op=ALU.mult), reads=[o_k, rdk], writes=[("moTh", q0)])
                    c0 = 0 if ctx_out else 256
                    P.dma("sp", oT_s[1][h][:, c0:T], oTh[:, c0:T], reads=[("moTh", q) for q in ([0] if ctx_out else []) + [256 + g * 512 for g in range(4)]], writes=[("oTs", 1, h)])
                P.barrier()
        def phase_ret(l, tiles):
            with contextlib.ExitStack() as st:
                sbm = lambda name, shape, dt: bld.sb(st, name, shape, dt)
                wrot = bld.rot(st, "rw", [128, KD, 128], BF16, 3)
                cosr = sbm("cosr", [128, T], F32)
                sinr = sbm("sinr", [128, T], F32)
                rcs = sbm("rcs", [128, 8 * 128 * 2 + 8], F32)
                P.dma("sp", cosr[:], roper[0], writes=["roper"])
                P.dma("sp", sinr[:], roper[1], writes=["roper2"])
                P.dma("sp", rcs[:], retc, writes=["rcs"])
                DTm = lambda q: rcs[:, q * 128:(q + 1) * 128]
                GWm = lambda q: rcs[:, 1024 + q * 128:1024 + (q + 1) * 128]
                kwc = lambda q: rcs[:, 2048 + q:2049 + q]
                rawb = bld.rot(st, "rrawb", [128, T], BF16, 2)
                qT = sbm("rqT", [128, T], BF16)
                kT = sbm("rkT", [128, T], BF16)
                ktok = sbm("rktok", [128, NT, 128], BF16)
                vtok = sbm("rvtok", [128, NT, 128], BF16)
                zs = sbm("rzs", [128, NT, 128], F32)
                tA = bld.rot(st, "rtA", [128, 512], F32, 2)
                tB = bld.rot(st, "rtB", [128, 512], F32, 2)
                atb = bld.rot(st, "r_at", [128, 128], BF16, 5)
                qwb = bld.rot(st, "r_qw", [128, 128], BF16, 5)
                kwb = bld.rot(st, "r_kw", [128, 128], BF16, 5)
                for h in range(4):
                    with contextlib.ExitStack() as sh:
                        Oacc = bld.sb(sh, "rO", [128, NT, 128], F32)
                        for fi, (c0, dst, scl) in enumerate(((RQ, qT, float(128 ** -0.5)), (RK, kT, 1.0))):
                            wt, wk = load_w(wrot, w_in[l][:, c0 + h * 128:c0 + (h + 1) * 128])
                            rb_, rbk = rawb.get()
                            def ev(bi, t0, n, ps, pk, rb_=rb_, rbk=rbk, scl=scl):
                                P.op("act", lambda e: e.activation(rb_[:, t0:t0 + n], ps[:, 0:n], AF.Copy, scale=scl), reads=[pk], writes=[(rbk, bi)])
                            proj_fm(wt, wk, hT_rhs, KD, ev)
                            for bi, (t0, n) in enumerate(BLKS):
                                ps, pk = PSB.get()
                                bld.mm(ps[:, 0:n], [(rmb[:, 0:128], rb_[:, t0:t0 + n])], [(rbk, bi), "rmb"], pk)
                                a, ak = tA.get()
                                b_, bk = tB.get()
                                P.op("dve", lambda e, a=a, ps=ps, n=n, t0=t0: e.tensor_tensor(out=a[:, 0:n], in0=ps[:, 0:n], in1=sinr[:, t0:t0 + n], op=ALU.mult), reads=[pk, "roper2"], writes=[ak])
                                P.op("pool", lambda e, b_=b_, n=n, t0=t0, rb_=rb_: e.tensor_tensor(out=b_[:, 0:n], in0=rb_[:, t0:t0 + n], in1=cosr[:, t0:t0 + n], op=ALU.mult), reads=[(rbk, bi), "roper"], writes=[bk])
                                P.op("dve", lambda e, a=a, b_=b_, n=n, t0=t0, dst=dst: e.tensor_tensor(out=dst[:, t0:t0 + n], in0=a[:, 0:n], in1=b_[:, 0:n], op=ALU.add), reads=[ak, bk], writes=[("rT", fi, bi)])
                        rTk = lambda fi: [("rT", fi, bi) for bi in range(5)]
                        for g0 in range(0, NT, 4):
                            g = list(range(g0, min(g0 + 4, NT)))
                            ps, pk = PSB.get()
                            psv = ps[:].bitcast(BF16)
                            for i, t in enumerate(g):
                                bld.tr(psv[:, i * 128:(i + 1) * 128], kT[:, t * 128:(t + 1) * 128], identb[:], [("rT", 1, min(t // 4, 4)), "identb"], pk)
                            n = len(g) * 128
                            P.op("dve", lambda e, g=g, n=n, psv=psv: e.tensor_copy(ktok[:, g[0]:g[0] + len(g), :], psv[:, 0:n].rearrange("p (t c) -> p t c", c=128)), reads=[pk], writes=[("rktok", t) for t in g])
                        wt, wk = load_w(wrot, w_in[l][:, RV + h * 128:RV + (h + 1) * 128])
                        for t in ALLT:
                            ps, pk = PSS.get()
                            bld.mm(ps, [(HT[:, k, t * 128:(t + 1) * 128], wt[:, k, :]) for k in range(KD)], [wk, ("HT", t)], pk)
                            P.op("act", lambda e, t=t, ps=ps: e.copy(vtok[:, t, :], ps), reads=[pk], writes=[("rvtok", t)])
                        wt, wk = load_w(wrot, w_in[l][:, RG + h * 128:RG + (h + 1) * 128])
                        for t in tiles:
                            ps, pk = PSS.get()
                            bld.mm(ps, [(HT[:, k, t * 128:(t + 1) * 128], wt[:, k, :]) for k in range(KD)], [wk, ("HT", t)], pk)
                            P.op("act", lambda e, t=t, ps=ps: e.activation(zs[:, t, :], ps, AF.Silu), reads=[pk], writes=[("rzs", t)])
                            P.op("pool", lambda e, t=t, h=h: e.tensor_tensor(out=zs[:, t, :], in0=zs[:, t, :], in1=rnw(l, h), op=ALU.mult), reads=[("rzs", t), "rws"], writes=[("rzs", t)])
                        S32 = [bld.sb(sh, "rS32_%d" % d, [128, 128], F32) for d in range(2)]
                        Sb = [bld.rot(sh, "rSb%d" % d, [128, 128], BF16, 2) for d in range(2)]
                        order = [list(range(NT)), [1, 0] + list(range(NT - 1, 1, -1))]
                        cur = [None, None]
                        for d in range(2):
                            P.op("pool", lambda e, d=d, S32=S32: e.memset(S32[d][:], 0.0), writes=[("rS32", d)])
                            sbt, sbk = Sb[d].get()
                            P.op("pool", lambda e, sbt=sbt: e.memset(sbt[:], 0.0), writes=[sbk])
                            cur[d] = (sbt, sbk)
                        visited = set()
                        atA = bld.sb(sh, "r_atA", [128, 2 * NT, 128], BF16)
                        qwA = bld.sb(sh, "r_qwA", [128, 2 * NT, 128], BF16)
                        kwA = bld.sb(sh, "r_kwA", [128, 2 * NT, 128], BF16)
                        for c in ALLT:
                            tcol = slice(c * 128, (c + 1) * 128)
                            cb = min(c // 4, 4)
                            psQ, kQ = PSS.get()
                            bld.mm(psQ, [(kT[:, tcol], qT[:, tcol])], [("rT", 0, cb), ("rT", 1, cb)], kQ)
                            for d in range(2):
                                q = d * 4 + h
                                ix = d * NT + c
                                P.op("dve", lambda e, psQ=psQ, q=q, ix=ix, atA=atA: e.tensor_tensor(out=atA[:, ix, :], in0=psQ, in1=DTm(q), op=ALU.mult), reads=[kQ, "rcs"], writes=[("r_at", ix)])
                                P.op("dve" if d == 0 else "pool", lambda e, tcol=tcol, q=q, ix=ix, qwA=qwA: e.tensor_tensor(out=qwA[:, ix, :], in0=qT[:, tcol], in1=GWm(q), op=ALU.mult), reads=[("rT", 0, cb), "rcs"], writes=[("r_qw", ix)])
                                P.op("act", lambda e, c=c, q=q, ix=ix, kwA=kwA: e.activation(kwA[:, ix, :], ktok[:, c, :], AF.Copy, scale=kwc(q)), reads=[("rktok", c), "rcs"], writes=[("r_kw", ix)])
                        for s_ in range(NT):
                            for d in range(2):
                                c = order[d][s_]
                                q = d * 4 + h
                                ix = d * NT + c
                                sbt, sbk = cur[d]
                                pso, ko = PSS.get()
                                bld.mm(pso, [(qwA[:, ix, :], sbt[:]), (atA[:, ix, :], vtok[:, c, :])], [("r_qw", ix), sbk, ("r_at", ix), ("rvtok", c)], ko)
                                pss_, ks = PSS.get()
                                bld.mm(pss_, [(kwA[:, ix, :], vtok[:, c, :])], [("r_kw", ix), ("rvtok", c)], ks)
                                P.op("dve", lambda e, d=d, q=q, pss_=pss_, S32=S32: e.scalar_tensor_tensor(out=S32[d][:], in0=S32[d][:], scalar=float(RET_CDEC[q]), in1=pss_, op0=ALU.mult, op1=ALU.add), reads=[("rS32", d), ks], writes=[("rS32", d)])
                                nsb, nsbk = Sb[d].get()
                                P.op("act", lambda e, nsb=nsb, d=d, S32=S32: e.copy(nsb[:], S32[d][:]), reads=[("rS32", d)], writes=[nsbk])
                                cur[d] = (nsb, nsbk)
                                if c in visited:
                                    P.op("dve", lambda e, c=c, pso=pso, Oacc=Oacc: e.tensor_tensor(out=Oacc[:, c, :], in0=pso, in1=Oacc[:, c, :], op=ALU.add), reads=[ko, ("rO", c)], writes=[("rO", c)])
                                else:
                                    visited.add(c)
                                    P.op("act", lambda e, c=c, pso=pso, Oacc=Oacc: e.copy(Oacc[:, c, :], pso), reads=[ko], writes=[("rO", c)])
                        head_out_norm(sh, l, h, Oacc, zs, None, 2, tiles, "r")
                    P.barrier()
        def phase_gates(l, tiles):
            with contextlib.ExitStack() as st:
                wrot = bld.rot(st, "gtw", [128, KD, 512], BF16, 2)
                gb = bld.rot(st, "gtb", [128, 512], BF16, 4)
                for cb in range(6):
                    wt, wk = load_w(wrot, w_in[l][:, GATE + cb * 512:GATE + (cb + 1) * 512])
                    for t in tiles:
                        ps, pk = PSB.get()
                        bld.mm(ps[:], [(HT[:, k, t * 128:(t + 1) * 128], wt[:, k, :]) for k in range(KD)], [wk, ("HT", t)], pk)
                        g, gk = gb.get()
                        P.op("act", lambda e, g=g, ps=ps: e.activation(g[:], ps[:], AF.Sigmoid), reads=[pk], writes=[gk])
                        P.dma("sp", gates_s[t * 128:(t + 1) * 128, cb * 512:(cb + 1) * 512], g[:], reads=[gk], writes=[("gates", t, cb)])
                P.barrier()

        def phase_merge(l, tiles, xsrc):
            stw = contextlib.ExitStack()
            wo = bld.sb(stw, "wo", [128, KD, D], BF16)
            P.dma("pool", wo[:], w_out[l].rearrange("(k p) c -> p k c", p=128), writes=["wo"])
            with contextlib.ExitStack() as st:
                wbr = [bld.sb(st, "wbr%d" % b, [128, 4, D], BF16) for b in range(3)]
                for b in range(3):
                    P.dma("pool", wbr[b][:], w_br[b][l].rearrange("(k p) c -> p k c", p=128), writes=[("wbr", b)])
                gt = bld.rot(st, "mgt", [128, 3 * D], BF16, 2)
                ot = bld.rot(st, "mot", [128, 3, 4, 128], BF16, 2)
                t32 = bld.rot(st, "mt32", [128, 512], F32, 4)
                mb = bld.rot(st, "mmb", [128, D], BF16, 2)
                for t in tiles:
                    g, gk = gt.get()
                    P.dma("sp", g[:], gates_s[t * 128:(t + 1) * 128, :], reads=[("gates", t, cb) for cb in range(6)], writes=[gk])
                    o_, ok_ = ot.get()
                    for b in range(3):
                        P.dma("sp", o_[:, b, :, :], oT_s[b][:, :, t * 128:(t + 1) * 128].rearrange("h p c -> p h c"), reads=[("oTs", b, h) for h in range(4)], writes=[(ok_, b)])
                    m, mk = mb.get()
                    for half in range(2):
                        acc, acck = t32.get()
                        for b in range(3):
                            ps, pk = PSB.get()
                            bld.mm(ps[:], [(o_[:, b, k, :], wbr[b][:, k, half * 512:(half + 1) * 512]) for k in range(4)], [(ok_, b), ("wbr", b)], pk)
                            gsl = g[:, b * D + half * 512: b * D + (half + 1) * 512]
                            if b == 0:
                                P.op("dve", lambda e, acc=acc, ps=ps, gsl=gsl: e.tensor_tensor(out=acc[:], in0=ps[:], in1=gsl, op=ALU.mult), reads=[pk, gk], writes=[acck])
                            else:
                                tmp, tk = t32.get()
                                P.op("dve", lambda e, tmp=tmp, ps=ps, gsl=gsl: e.tensor_tensor(out=tmp[:], in0=ps[:], in1=gsl, op=ALU.mult), reads=[pk, gk], writes=[tk])
                                if b == 1:
                                    P.op("pool", lambda e, acc=acc, tmp=tmp: e.tensor_tensor(out=acc[:], in0=acc[:], in1=tmp[:], op=ALU.add), reads=[acck, tk], writes=[acck])
                                else:
                                    P.op("pool", lambda e, acc=acc, tmp=tmp, m=m, half=half: e.tensor_tensor(out=m[:, half * 512:(half + 1) * 512], in0=acc[:], in1=tmp[:], op=ALU.add), reads=[acck, tk], writes=[(mk, half)])
                    ps, pk = PSB.get()
                    psv = ps[:].bitcast(BF16)
                    for k in range(KD):
                        bld.tr(psv[:, k * 128:(k + 1) * 128], m[:, k * 128:(k + 1) * 128], identb[:], [(mk, 0), (mk, 1), "identb"], pk)
                    P.op("act", lambda e, t=t, psv=psv: e.copy(HT[:, :, t * 128:(t + 1) * 128], psv[:, 0:1024].rearrange("p (k c) -> p k c", c=128)), reads=[pk], writes=[("HT", t)])
                P.barrier()
            with contextlib.ExitStack() as st:
                residual_phase(st, l, tiles, xsrc, 0, lambda t, half: [(HT[:, k, t * 128:(t + 1) * 128], wo[:, k, half * 512:(half + 1) * 512]) for k in range(KD)],
                               lambda t: [("HT", t), "wo"])
                P.barrier()
            stw.close()

        def residual_phase(st, l, tiles, xsrc, which, pairs_of, reads_of):
            xb = bld.rot(st, "rx", [128, D], F32, 3)
            yb = bld.rot(st, "ry", [128, D], F32, 2)
            for t in tiles:
                s = tile_stream(t)
                xt, xk = xb.get()
                P.dma("sp", xt[:], xsrc[t * 128:(t + 1) * 128, :], reads=[("xs", t)], writes=[xk])
                y, yk = yb.get()
                for half in range(2):
                    ps, pk = PSB.get()
                    bld.mm(ps[:], pairs_of(t, half), reads_of(t), pk)
                    sl = slice(half * 512, (half + 1) * 512)
                    P.op("dve", lambda e, y=y, ps=ps, sl=sl, s=s: e.tensor_tensor(out=y[:, sl], in0=ps[:], in1=grow[:, s, which, sl], op=ALU.mult),
                         reads=[pk] + [("grow", s, 2 if which == 0 else 5, h_) for h_ in range(2)], writes=[(yk, half)])
                    P.op("pool", lambda e, y=y, xt=xt, sl=sl: e.tensor_tensor(out=y[:, sl], in0=y[:, sl], in1=xt[:, sl], op=ALU.add), reads=[(yk, half), xk], writes=[(yk, half)])
                P.dma("pool", xs[t * 128:(t + 1) * 128, :], y[:], reads=[(yk, 0), (yk, 1)], writes=[("xs", t)])

        def phase_ffn(l, tiles):
            t_lo = tiles[0] * 128
            stw = contextlib.ExitStack()
            wd = bld.sb(stw, "wd", [128, NJ, D], BF16)
            with contextlib.ExitStack() as st:
                wrot = bld.rot(st, "fw", [128, KD, 512], BF16, 3)
                wcur = {}
                W = T + 3
                off = lambda t0: t0 + 1 if t0 < 256 else t0 + 2
                raw = bld.rot(st, "fraw", [128, W], F32, 3)
                cv = bld.rot(st, "fcv", [128, W], F32, 2)
                aT = bld.rot(st, "faT", [128, T], BF16, 2)
                blks = BLKS if t_lo == 0 else [(256 + i * 512, 512) for i in range(4)]
                for rw_ in raw.tiles:
                    P.op("pool", lambda e, rw_=rw_: e.memset(rw_[:], 0.0), writes=[("fraw", raw.tiles.index(rw_))])
                for j in range(NJ):
                    cvs = []
                    for gi, c0 in enumerate((j * 128, DFF + j * 128)):
                        ch = c0 // 128
                        if j % 4 == 0:
                            ng = min(4, NJ - j)
                            wtf, wk = wrot.get()
                            P.dma("pool", wtf[:, :, 0:ng * 128], w_up[l][:, c0:c0 + ng * 128].rearrange("(k p) c -> p k c", p=128), writes=[wk])
                            wcur[gi] = (wtf, wk)
                        wt, wk = wcur[gi]
                        rw, rk = raw.get()
                        def ev(bi, t0, n, ps, pk, rw=rw, rk=rk):
                            if t0 == 0:
                                P.op("act", lambda e: e.copy(rw[:, 1:257], ps[:, 0:256]), reads=[pk], writes=[rk])
                                P.op("dve", lambda e: e.tensor_copy(rw[:, 258:514], ps[:, 256:512]), reads=[pk], writes=[rk])
                            elif bi % 2:
                                P.op("act", lambda e: e.copy(rw[:, off(t0):off(t0) + n], ps[:, 0:n]), reads=[pk], writes=[rk])
                            else:
                                P.op("dve", lambda e: e.tensor_copy(rw[:, off(t0):off(t0) + n], ps[:, 0:n]), reads=[pk], writes=[rk])
                        proj_fm(wt, wk, hT_rhs, KD, ev, blks=blks, coff=(j % 4) * 128)
                        c, ck = cv.get()
                        lo = off(t_lo)
                        P.op("act", lambda e, c=c, rw=rw, ch=ch: e.activation(c[:, lo:W - 1], rw[:, lo:W - 1], AF.Identity, bias=fconvb(l, ch), scale=fconv(l, ch, 1)), reads=[rk, "spp"], writes=[ck])
                        P.op("dve", lambda e, c=c, rw=rw, ch=ch: e.scalar_tensor_tensor(out=c[:, lo:W - 1], in0=rw[:, lo - 1:W - 2], scalar=fconv(l, ch, 0), in1=c[:, lo:W - 1], op0=ALU.mult, op1=ALU.add), reads=[rk, ck, "spp"], writes=[ck])
                        P.op("dve", lambda e, c=c, rw=rw, ch=ch: e.scalar_tensor_tensor(out=c[:, lo:W - 1], in0=rw[:, lo + 1:W], scalar=fconv(l, ch, 2), in1=c[:, lo:W - 1], op0=ALU.mult, op1=ALU.add), reads=[rk, ck, "spp"], writes=[ck])
                        cvs.append((c, ck))
                    (cg, cgk), (cval, cvk) = cvs
                    P.op("act", lambda e, cg=cg: e.activation(cg[:, lo:W - 1], cg[:, lo:W - 1], AF.Silu), reads=[cgk], writes=[cgk])
                    a, ak = aT.get()
                    if t_lo == 0:
                        P.op("dve", lambda e, a=a, cg=cg, cval=cval: e.tensor_tensor(out=a[:, 0:256], in0=cg[:, 1:257], in1=cval[:, 1:257], op=ALU.mult), reads=[cgk, cvk], writes=[(ak, 0)])
                    P.op("dve", lambda e, a=a, cg=cg, cval=cval: e.tensor_tensor(out=a[:, 256:T], in0=cg[:, 258:W - 1], in1=cval[:, 258:W - 1], op=ALU.mult), reads=[cgk, cvk], writes=[(ak, 1)])
                    P.dma("sp", aT_s[tiles[0]:NT, :, j, :].rearrange("t p c -> p t c"), a[:, t_lo:T].rearrange("p (t c) -> p t c", c=128), reads=[(ak, 0), (ak, 1)], writes=[("aTs", j)])
                    P.dma("pool", wd[:, j, :], w_down[l][j * 128:(j + 1) * 128, :], writes=[("wd", j)])
                P.barrier()
            with contextlib.ExitStack() as st:
                ab_ = bld.rot(st, "fab", [128, NJ, 128], BF16, 2)
                cur = {}
                def pairs_of(t, half):
                    if half == 0:
                        a, ak = ab_.get()
                        P.dma("sp", a[:], aT_s[t], reads=[("aTs", j) for j in range(NJ)], writes=[ak])
                        cur[t] = (a, ak)
                    a, ak = cur[t]
                    return [(a[:, j, :], wd[:, j, half * 512:(half + 1) * 512]) for j in range(NJ)]
                residual_phase(st, l, tiles, xs, 1, pairs_of, lambda t: [cur[t][1]] + [("wd", j) for j in range(NJ)])
                P.barrier()
            stw.close()

        def phase_final():
            with contextlib.ExitStack() as st:
                xb = bld.rot(st, "fx", [128, D], F32, 3)
                junk = bld.sb(st, "fjunk", [128, D], BF16)
                ssb = bld.rot(st, "fss", [128, 1], F32, 4)
                ob = bld.rot(st, "fo", [128, D], F32, 3)
                for t in range(2, NT):
                    xt, xk = xb.get()
                    P.dma("sp", xt[:], xs[t * 128:(t + 1) * 128, :], reads=[("xs", t)], writes=[xk])
                    ss, sk = ssb.get()
                    P.op("act", lambda e, xt=xt, ss=ss: e.activation(junk[:], xt[:], AF.Square, accum_out=ss[:]), reads=[xk], writes=["fjunk", sk])
                    P.op("act", lambda e, ss=ss: e.activation(ss[:], ss[:], AF.Sqrt, bias=epsc(0)), reads=[sk, "c32"], writes=[sk])
                    P.op("dve", lambda e, ss=ss: e.reciprocal(ss[:], ss[:]), reads=[sk], writes=[sk])
                    P.op("dve", lambda e, ss=ss: e.tensor_scalar(out=ss[:], in0=ss[:], scalar1=float(math.sqrt(D)), scalar2=None, op0=ALU.mult), reads=[sk], writes=[sk])
                    o_, ok_ = ob.get()
                    P.op("dve", lambda e, o_=o_, xt=xt, ss=ss: e.scalar_tensor_tensor(out=o_[:], in0=xt[:], scalar=ss[:], in1=fnw, op0=ALU.mult, op1=ALU.mult), reads=[xk, sk, "rws"], writes=[ok_])
                    P.dma("pool", out[(t - 2) * 128:(t - 1) * 128, :], o_[:], reads=[ok_], writes=[("out", t)])
        P.barrier()
        for l in range(2):
            ctx_out = (l == 0)
            tiles = ALLT if ctx_out else list(range(2, NT))
            xsrc = xin if l == 0 else xs
            phase_mod(l)
            if stop_after == "mod":
                dump("modpp", modpp[:], ["modpp"])
                break
            phase_norm(l, 0, xsrc, ALLT)
            if l == 0:
                dump("HT", HT[:], HTk(ALLT))
                dump("modpp", modpp[:], ["modpp"])
                dump("grow", grow[:], [("grow", s_, v_, h_) for s_ in range(2) for v_ in (2, 5) for h_ in range(2)])
            if stop_after == "norm":
                break
            if stop_after in (None, "all", "gdn", "merge", "ffn"):
                phase_gdn(l, tiles)
                if l == 0:
                    dump("oTa", oT_s[0], [("oTs", 0, h_) for h_ in range(4)])
                if stop_after == "gdn":
                    break
            if stop_after in (None, "all", "mla", "merge", "ffn"):
                phase_mla(l, ctx_out)
                if l == 0:
                    dump("oTb", oT_s[1], [("oTs", 1, h_) for h_ in range(4)])
                if stop_after == "mla":
                    break
            if stop_after in (None, "all", "ret", "merge", "ffn"):
                phase_ret(l, tiles)
                if l == 0:
                    dump("oTc", oT_s[2], [("oTs", 2, h_) for h_ in range(4)])
                if stop_after == "ret":
                    break
            phase_gates(l, tiles)
            phase_merge(l, tiles, xsrc)
            if l == 0:
                dump("xmid", xs, [("xs", t_) for t_ in ALLT])
            if stop_after == "merge":
                break
            phase_norm(l, 1, xs, tiles)
            phase_ffn(l, tiles)
            if l == 0:
                dump("xl0", xs, [("xs", t_) for t_ in ALLT])
            if stop_after == "ffn":
                break
        if stop_after in (None, "all"):
            phase_final()
        P.emit(final_reads=[("out", t) for t in range(2, NT)] + bld.dbg_keys)
    return nc, P.stats


def _rope_tables(d):
    n = 2048
    rows_ = n // 64
    r = np.repeat(np.arange(rows_, dtype=np.float32), 64)
    col = np.tile(np.arange(64, dtype=np.float32), rows_)
    quarter = d // 4
    inv = (np.float32(10000.0) ** (-np.arange(quarter, dtype=np.float32) / np.float32(quarter))).astype(np.float32)
    ang = np.concatenate([r[:, None] * inv, col[:, None] * inv], axis=-1).astype(np.float32)
    cos = np.cos(ang).astype(np.float32)
    sin = np.sin(ang).astype(np.float32)
    C = np.ones((d, T), np.float32)
    S = np.zeros((d, T), np.float32)
    C[:, 256:] = np.concatenate([cos, cos], axis=1).T
    S[:, 256:] = np.concatenate([sin, sin], axis=1).T
    return np.stack([C, S])


def _rot_mat(d):
    R = np.zeros((d, d), np.float32)
    h = d // 2
    for m in range(h):
        R[m + h, m] = -1.0
    for m in range(h, d):
        R[m - h, m] = 1.0
    return R


def _host_consts():
    c = np.zeros((128, 128 * 6 + 8), np.float32)
    idx = np.arange(128)
    k = idx[:, None]
    i = idx[None, :]
    c[:, 0:128] = np.eye(128)
    c[:, 128:256] = 1.0
    c[:, 256:384] = (k <= i)
    c[:, 384:512] = (k >= i)
    c[:, 512:640] = np.where(i > k, 0.0, NEGBIG)
    c[:, 640:768] = np.where(i < k, 0.0, NEGBIG)
    c[0, 768] = 1.0
    c[:, 769] = 1024 * EPS; c[:, 770] = 128 * EPS; c[:, 771] = EPS; c[:, 772] = 384 * EPS; c[:, 773] = 256 * EPS
    rm = np.zeros((128, 192), np.float32)
    rm[:, 0:128] = _rot_mat(128)
    rm[0:64, 128:192] = _rot_mat(64)
    hh = np.arange(4, dtype=np.float64)
    lg = np.stack([np.log1p(-(2.0 ** (-(5.0 + hh + 0.5 * d)))) for d in range(2)])
    rc = np.zeros((128, 8 * 128 * 2 + 8), np.float64)
    jj = idx[:, None].astype(np.float64)
    ii = idx[None, :].astype(np.float64)
    cdec = []
    for d in range(2):
        for h in range(4):
            g = lg[d, h]
            q = d * 4 + h
            if d == 0:
                DT = np.where(ii >= jj, np.exp(g * np.maximum(ii - jj, 0)), 0.0)
                GW = np.exp(g * (ii + 1)) * np.ones((128, 1))
                kw = np.exp(g * (127 - idx))
            else:
                DT = np.where(ii <= jj, np.exp(g * np.maximum(jj - ii, 0)), 0.0)
                GW = np.exp(g * (128 - ii)) * np.ones((128, 1))
                kw = np.exp(g * idx)
            rc[:, q * 128:(q + 1) * 128] = DT
            rc[:, 1024 + q * 128: 1024 + (q + 1) * 128] = GW
            rc[:, 2048 + q] = kw
            cdec.append(float(np.exp(g * 128)))
    return c, rm, rc.astype(np.float32), cdec


RET_CDEC = _host_consts()[3]
_NC_CACHE = {}


def _prep_common(inp):
    f = lambda a: np.ascontiguousarray(np.asarray(a, dtype=np.float32))
    pp = lambda v, nch: v.reshape(nch, 128).T
    sm = []
    n1, n2 = f(inp["norm1_w"]), f(inp["norm2_w"])
    sm.append(np.concatenate([pp(n1[l], 8) for l in range(2)], axis=1))
    sm.append(np.concatenate([pp(n2[l], 8) for l in range(2)], axis=1))
    gc = f(inp["gdn_conv_w"])
    sm.append(np.concatenate([gc[l].reshape(3, 12, 128).transpose(2, 1, 0).reshape(128, 36) for l in range(2)], axis=1))
    fc = f(inp["ffn_conv_w"])
    sm.append(np.concatenate([fc[l].reshape(3, 44, 128).transpose(2, 1, 0).reshape(128, 132) for l in range(2)], axis=1))
    fb = f(inp["ffn_conv_b"])
    sm.append(np.concatenate([pp(fb[l], 44) for l in range(2)], axis=1))
    qn_, kvn_ = f(inp["mla_q_norm"]), f(inp["mla_kv_norm"])
    sm.append(np.concatenate([pp(qn_[l], 3) for l in range(2)], axis=1))
    sm.append(np.concatenate([pp(kvn_[l], 2) for l in range(2)], axis=1))
    smallpp = np.ascontiguousarray(np.concatenate(sm, axis=1))
    rep = lambda v: np.broadcast_to(v[None, :], (128, v.shape[0]))
    rw = [rep(f(inp["gdn_norm_w"]).reshape(-1)), rep(f(inp["ret_norm_w"]).reshape(-1)),
          rep(f(inp["gdn_A_log"]).reshape(-1)), rep(f(inp["gdn_dt_bias"]).reshape(-1)), rep(f(inp["final_norm_w"]))]
    rows = np.ascontiguousarray(np.concatenate(rw, axis=1))
    c, rm, rc, _ = _host_consts()
    common = dict(smallpp=smallpp, rows=rows, consts=c, rmats=rm, retc=rc,
                  ropem=_rope_tables(64), roper=_rope_tables(128))
    for k in ("ada_w", "ada_b", "w_in", "mla_w_uq", "mla_w_ukv", "w_br_gdn", "w_br_mla", "w_br_ret", "w_out",
              "ffn_w_up", "ffn_w_down"):
        common[k] = f(inp[k])
    return common


def _prep_core(inp, b):
    f = lambda a: np.ascontiguousarray(np.asarray(a, dtype=np.float32))
    xin = np.concatenate([f(inp["ctx"][b]), f(inp["x"][b])], axis=0)
    def crep_of(v):
        return np.broadcast_to(v.reshape(8, 128).T[:, :, None], (128, 8, 128)).reshape(128, 1024)
    crep = np.stack([crep_of(f(inp["c_ctx"])), crep_of(f(inp["c"][b]))])
    return dict(xin=np.ascontiguousarray(xin), crep=np.ascontiguousarray(crep))


def kernel(**inputs):
    if "nc" not in _NC_CACHE:
        _NC_CACHE["nc"] = build()[0]
    nc = _NC_CACHE["nc"]
    common = _prep_common(inputs)
    in_maps = []
    for b in range(NCORES):
        m = dict(common)
        m.update(_prep_core(inputs, b))
        in_maps.append(m)
    res = run_bass_kernel_spmd(nc, in_maps, core_ids=list(range(NCORES)))
    return np.stack([np.asarray(r["out"], dtype=np.float32) for r in res.results], axis=0)
```

```python
import contextlib
import math
import os
import numpy as np
import concourse.bass as bass
import concourse.mybir as mybir
from concourse.bass_utils import run_bass_kernel_spmd

F32 = mybir.dt.float32
F32R = mybir.dt.float32r
BF16 = mybir.dt.bfloat16
AF = mybir.ActivationFunctionType
ALU = mybir.AluOpType

NCORES = 8
T = 2304
NT = 18
D = 1024
KD = 8
DFF = 2816
NJ = 22
EPS = 1e-6
GQ, GK, GV, GZ, GAB, CQ, CKV, KR, RQ, RK, RV, RG, GATE = 0, 512, 1024, 1536, 2048, 2064, 2448, 2704, 2768, 3280, 3792, 4304, 4816
MLA_SCALE = 192 ** -0.5
NEGBIG = -30000.0
GDN_WARM = 1


class Prog:
    def __init__(self, nc, n_dma_sems=8):
        self.nc = nc
        self.ops = []
        self.last_w = {}
        self.readers = {}
        self.n_dma_sems = n_dma_sems
        self.warm = 0
        self.dummy = None

    def op(self, eng, fn, reads=(), writes=(), dma=False, barrier=False):
        idx = len(self.ops)
        deps = {}
        reads = list(reads)
        writes = list(writes)
        if barrier:
            writes.append("__phase")
        else:
            reads.append("__phase")
        for k in reads:
            w = self.last_w.get(k)
            if w is not None:
                deps[w] = "raw"
        for k in writes:
            w = self.last_w.get(k)
            if w is not None and w not in deps:
                deps[w] = "waw"
            for r in self.readers.get(k, ()):
                if r not in deps:
                    deps[r] = "war"
        for k in reads:
            self.readers.setdefault(k, []).append(idx)
        for k in writes:
            self.last_w[k] = idx
            self.readers[k] = []
        self.ops.append(dict(eng=eng, fn=fn, deps=deps, dma=dma, barrier=barrier, warm=(self.warm if eng == "pe" else 0)))
        return idx

    def barrier(self):
        self.op("dve", lambda e: e.nop(), barrier=True)

    def dma(self, q, out, in_, reads=(), writes=()):
        return self.op(q, lambda e: e.dma_start(out=out, in_=in_), reads, writes, dma=True)

    def emit(self, final_reads=()):
        nc = self.nc
        ops = self.ops
        self.op("sp", lambda e: e.nop(), reads=final_reads)
        n = len(ops)
        pos = [0] * n
        cnt = {}
        for i, o in enumerate(ops):
            c = cnt.get(o["eng"], 0)
            pos[i] = c
            cnt[o["eng"]] = c + 1
        waited_pos = {}
        waited_dma = {}
        need = [[] for _ in range(n)]
        signaling = [False] * n
        for i, o in enumerate(ops):
            E = o["eng"]
            for d in sorted(o["deps"]):
                kind = o["deps"][d]
                od = ops[d]
                F = od["eng"]
                if od["dma"]:
                    s = waited_dma.setdefault(E, set())
                    if d in s:
                        continue
                    s.add(d)
                    need[i].append(d)
                    signaling[d] = True
                else:
                    if F == E and not o["dma"] and not o["barrier"]:
                        if E == "pe":
                            continue
                    if pos[d] <= waited_pos.get((E, F), -1):
                        continue
                    waited_pos[(E, F)] = pos[d]
                    need[i].append(d)
                    signaling[d] = True
        engs = sorted(cnt.keys())
        self.stats = dict(cnt)
        with contextlib.ExitStack() as st:
            esem = {E: st.enter_context(nc.semaphore("s_" + E)) for E in engs}
            dsem = {}
            for E in engs:
                if any(o["dma"] and o["eng"] == E for o in ops):
                    dsem[E] = [st.enter_context(nc.semaphore("d_%s_%d" % (E, j))) for j in range(self.n_dma_sems)]
            ev = [None] * n
            ecount = {E: 0 for E in engs}
            dcount = {E: [0] * self.n_dma_sems for E in dsem}
            dnext = {E: 0 for E in dsem}
            for i, o in enumerate(ops):
                E = o["eng"]
                if o["dma"]:
                    j = dnext[E]
                    dnext[E] = (j + 1) % self.n_dma_sems
                    dcount[E][j] += 1
                    ev[i] = (dsem[E][j], 16 * dcount[E][j])
                    o["dslot"] = j
                    o["dprev"] = 16 * (dcount[E][j] - 1)
                elif signaling[i]:
                    ecount[E] += 1
                    ev[i] = (esem[E], ecount[E])
            for E in engs:
                assert ecount[E] < 60000, (E, ecount[E])
            blk = st.enter_context(nc.Block())
            handles = dict(pe=blk.tensor, act=blk.scalar, dve=blk.vector, pool=blk.gpsimd, sp=blk.sync)
            nw = [0]
            for E in engs:
                my = [i for i in range(n) if ops[i]["eng"] == E]

                def body(e, my=my, E=E):
                    dwaited = [0] * self.n_dma_sems
                    for i in my:
                        o = ops[i]
                        if o["warm"] and need[i]:
                            for _ in range(o["warm"]):
                                self.dummy(e)
                        for d in need[i]:
                            s, v = ev[d]
                            e.wait_ge(s, v)
                            nw[0] += 1
                        if o["dma"]:
                            j = o["dslot"]
                            if o["dprev"] > dwaited[j]:
                                e.wait_ge(dsem[E][j], o["dprev"])
                                dwaited[j] = o["dprev"]
                                nw[0] += 1
                        ins = o["fn"](e)
                        if ev[i] is not None:
                            s, v = ev[i]
                            ins.then_inc(s, 16 if o["dma"] else 1)
                handles[E](body)
            self.stats["waits"] = nw[0]
            self.stats["signals"] = dict(ecount)


class Rot:
    def __init__(self, name, tiles):
        self.name = name
        self.tiles = tiles
        self.i = 0

    def get(self):
        j = self.i % len(self.tiles)
        self.i += 1
        kf = getattr(self, "keyfn", None)
        return self.tiles[j], (kf(j) if kf else (self.name, j))


class B:
    def __init__(self, nc, dbg):
        self.nc = nc
        self.P = Prog(nc)
        self.dbg = dbg
        self.dbg_keys = []

    def sb(self, st, name, shape, dt):
        self.uid = getattr(self, "uid", 0) + 1
        return st.enter_context(self.nc.sbuf_tensor("%s_u%d" % (name, self.uid), list(shape), dt))

    def rot(self, st, name, shape, dt, n):
        return Rot(name, [self.sb(st, "%s%d" % (name, i), shape, dt) for i in range(n)])

    def mm(self, out, pairs, reads, wkey):
        def fn(e):
            m = len(pairs)
            ins = None
            for i, (l, r) in enumerate(pairs):
                ins = e.matmul(out, lhsT=l, rhs=r, start=(i == 0), stop=(i == m - 1))
            return ins
        self.P.op("pe", fn, reads=reads, writes=[wkey])

    def mm_acc(self, out, l, r, start, stop, reads, wkey):
        self.P.op("pe", lambda e: e.matmul(out, lhsT=l, rhs=r, start=start, stop=stop), reads=reads, writes=[wkey])

    def tr(self, out, in_, ident, reads, wkey):
        self.P.op("pe", lambda e: e.transpose(out, in_, ident), reads=reads, writes=[wkey])


def tile_stream(t):
    return 0 if t < 2 else 1


def build(dbg=None, stop_after=None):
    dbg = dbg or ()
    nc = bass.Bass("TRN2", target_bir_lowering=False)
    dram_in = lambda name, shape: nc.dram_tensor(name, list(shape), F32, kind="ExternalInput").ap()
    xin = dram_in("xin", [T, D])
    crep = dram_in("crep", [2, 128, KD * 128])
    ada_w = dram_in("ada_w", [2, D, 6 * D])
    ada_b = dram_in("ada_b", [2, 6 * D])
    w_in = dram_in("w_in", [2, D, 7888])
    w_uq = dram_in("mla_w_uq", [2, 384, 768])
    w_ukv = dram_in("mla_w_ukv", [2, 256, 1024])
    w_br = [dram_in(n, [2, 512, D]) for n in ("w_br_gdn", "w_br_mla", "w_br_ret")]
    w_out = dram_in("w_out", [2, D, D])
    w_up = dram_in("ffn_w_up", [2, D, 2 * DFF])
    w_down = dram_in("ffn_w_down", [2, DFF, D])
    NSM = 2 * 8 * 2 + 2 * 12 * 3 + 2 * 44 * 3 + 2 * 44 + 2 * 3 + 2 * 2
    smallpp = dram_in("smallpp", [128, NSM])
    NROW = 2 * 128 + 2 * 512 + 2 * 8 + 2 * 8 + 1024
    rows = dram_in("rows", [128, NROW])
    NC32 = 128 * 6 + 8
    consts = dram_in("consts", [128, NC32])
    ropem = dram_in("ropem", [2, 64, T])
    roper = dram_in("roper", [2, 128, T])
    rmats = dram_in("rmats", [128, 192])
    retc = dram_in("retc", [128, 8 * 128 * 2 + 8])
    out = nc.dram_tensor("out", [2048, D], F32, kind="ExternalOutput").ap()
    xs = nc.dram_tensor("xs", [T, D], F32).ap()
    oT_s = [nc.dram_tensor("oT%d" % b, [4, 128, T], BF16).ap() for b in range(3)]
    gates_s = nc.dram_tensor("gates_s", [T, 3 * D], BF16).ap()
    aT_s = nc.dram_tensor("aT_s", [NT, 128, NJ, 128], BF16).ap()
    dbg_out = {}
    for name, shape, dt in dbg:
        dbg_out[name] = nc.dram_tensor("dbg_" + name, list(shape), dt, kind="ExternalOutput").ap()

    bld = B(nc, dbg_out)
    P = bld.P
    with contextlib.ExitStack() as top:
        sb = lambda name, shape, dt, st=top: bld.sb(st, name, shape, dt)
        HT = sb("HT", [128, KD, T], BF16)
        c32 = sb("c32", [128, NC32], F32)
        ident32 = c32[:, 0:128]
        ones32 = c32[:, 128:256]
        Umask = [c32[:, 256:384], c32[:, 384:512]]
        NEGS = [c32[:, 512:640], c32[:, 640:768]]
        e0 = c32[:, 768:769]
        epsc = lambda i: c32[:, 769 + i:770 + i]
        identb = sb("identb", [128, 128], BF16)
        onesb = sb("onesb", [128, 128], BF16)
        onesr = sb("onesr", [128, 128], F32R)
        rm32 = sb("rm32", [128, 192], F32)
        rmb = sb("rmb", [128, 192], BF16)
        spp = sb("spp", [128, NSM], F32)
        rws = sb("rws", [128, NROW], F32)
        crs = sb("crs", [128, 2, KD * 128], F32)
        grow = sb("grow", [128, 2, 2, D], F32)
        modpp = sb("modpp", [128, 64], F32)
        AB = sb("ABpp", [128, 2, 2, 2, KD], F32)
        banks = [top.enter_context(nc.psum_tensor("bank%d" % i, [128, 512], F32)) for i in range(8)]
        PSB = Rot("psb", banks[0:3])
        jw = sb("jw", [128, 128], BF16)
        jr = sb("jr", [128, 512], BF16)
        P.op("pool", lambda e: e.memset(jw[:], 0.25), writes=["jw"])
        P.op("pool", lambda e: e.memset(jr[:], 0.5), writes=["jr"])
        P.dummy = lambda e: e.matmul(banks[3][:, 0:512], lhsT=jw[:], rhs=jr[:], start=True, stop=True)
        PSS = Rot("pss", [banks[4 + i % 3][:, ((i // 3) % 4) * 128:((i // 3) % 4 + 1) * 128] for i in range(12)])
        PSS.keyfn = lambda j: ("pssbank", j % 3)
        PSH = Rot("psh", [banks[4 + i % 3][:, ((i // 3) % 2) * 256:((i // 3) % 2 + 1) * 256] for i in range(6)])
        PSH.keyfn = lambda j: ("pssbank", j % 3)
        PSX = banks[7]

        o = 0
        def take(n):
            nonlocal o
            v = (o, o + n)
            o += n
            return v
        r_n1 = take(16); r_n2 = take(16); r_gc = take(72); r_fc = take(264); r_fb = take(88); r_qn = take(6); r_kvn = take(4)
        n1w = lambda l: spp[:, r_n1[0] + l * 8: r_n1[0] + l * 8 + 8]
        n2w = lambda l: spp[:, r_n2[0] + l * 8: r_n2[0] + l * 8 + 8]
        gconv = lambda l, ch, k: spp[:, r_gc[0] + (l * 12 + ch) * 3 + k: r_gc[0] + (l * 12 + ch) * 3 + k + 1]
        fconv = lambda l, ch, k: spp[:, r_fc[0] + (l * 44 + ch) * 3 + k: r_fc[0] + (l * 44 + ch) * 3 + k + 1]
        fconvb = lambda l, ch: spp[:, r_fb[0] + l * 44 + ch: r_fb[0] + l * 44 + ch + 1]
        qn = lambda l, k: spp[:, r_qn[0] + l * 3 + k: r_qn[0] + l * 3 + k + 1]
        kvn = lambda l, k: spp[:, r_kvn[0] + l * 2 + k: r_kvn[0] + l * 2 + k + 1]
        gnw = lambda l: rws[:, l * 128:(l + 1) * 128]
        rnw = lambda l, h: rws[:, 256 + l * 512 + h * 128: 256 + l * 512 + (h + 1) * 128]
        alog = lambda l: rws[:, 1280 + l * 8: 1280 + l * 8 + 8]
        dtb = lambda l: rws[:, 1296 + l * 8: 1296 + l * 8 + 8]
        fnw = rws[:, 1312:1312 + 1024]

        P.dma("sp", c32[:], consts, writes=["c32"])
        P.dma("sp", rm32[:], rmats, writes=["rm32"])
        P.dma("sp", spp[:], smallpp, writes=["spp"])
        P.dma("sp", rws[:], rows, writes=["rws"])
        for s in range(2):
            P.dma("sp", crs[:, s, :], crep[s], writes=[("crs", s)])
        P.op("dve", lambda e: e.tensor_copy(identb[:], ident32), reads=["c32"], writes=["identb"])
        P.op("dve", lambda e: e.tensor_copy(onesb[:], ones32), reads=["c32"], writes=["onesb"])
        P.op("dve", lambda e: e.tensor_copy(onesr[:], ones32), reads=["c32"], writes=["onesr"])
        P.op("dve", lambda e: e.tensor_copy(rmb[:], rm32[:]), reads=["rm32"], writes=["rmb"])
        for s in range(2):
            P.op("act", lambda e, s=s: e.activation(crs[:, s, :], crs[:, s, :], AF.Silu), reads=[("crs", s)], writes=[("crs", s)])
        cst = ["c32", "identb", "onesb", "onesr", "rmb", "spp", "rws"]

        def dump(name, src_ap, reads):
            if name in dbg_out:
                P.dma("sp", dbg_out[name], src_ap, reads=reads, writes=[("dbg", name)])
                bld.dbg_keys.append(("dbg", name))

        def phase_mod(l):
            with contextlib.ExitStack() as st:
                wbuf = bld.rot(st, "adaw", [128, KD, 512], F32, 2)
                bbuf = bld.rot(st, "adab", [1, 512], F32, 2)
                rowt = bld.rot(st, "modrow", [128, 512], F32, 2)
                pp_ps, pp_key = PSX, "psx"
                for nb in range(12):
                    wt, wk = wbuf.get()
                    bt, bk = bbuf.get()
                    P.dma("sp", wt[:], ada_w[l][:, nb * 512:(nb + 1) * 512].rearrange("(k p) c -> p k c", p=128), writes=[wk])
                    P.dma("sp", bt[:], ada_b[l:l + 1, nb * 512:(nb + 1) * 512], writes=[bk])
                    vec = nb // 2
                    half = nb % 2
                    for s in range(2):
                        ps, pk = PSB.get()
                        pairs = [(crs[:, s, k * 128:(k + 1) * 128], wt[:, k, :]) for k in range(KD)]
                        pairs.append((ones32[0:1, :], bt[0:1, :]))
                        bld.mm(ps[:], pairs, [wk, bk, ("crs", s), "c32"], pk)
                        if vec in (2, 5):
                            dst = grow[:, s, 0 if vec == 2 else 1, half * 512:(half + 1) * 512]
                            P.op("act", lambda e, dst=dst, ps=ps: e.copy(dst, ps[:]), reads=[pk], writes=[("grow", s, vec, half)])
                        else:
                            rt, rk = rowt.get()
                            P.op("dve", lambda e, rt=rt, ps=ps: e.tensor_copy(rt[:], ps[:]), reads=[pk], writes=[rk])
                            vi = {0: 0, 1: 1, 3: 2, 4: 3}[vec]
                            for c4 in range(4):
                                col = s * 32 + vi * 8 + half * 4 + c4
                                bld.mm(pp_ps[:, col:col + 1], [(rt[:, c4 * 128:(c4 + 1) * 128], e0)], [rk, "c32"], pp_key)
                P.op("dve", lambda e: e.tensor_copy(modpp[:], pp_ps[:, 0:64]), reads=[pp_key], writes=["modpp"])
                for s in range(2):
                    for nrm in range(2):
                        sh = modpp[:, s * 32 + (2 * nrm) * 8: s * 32 + (2 * nrm) * 8 + 8]
                        sc = modpp[:, s * 32 + (2 * nrm + 1) * 8: s * 32 + (2 * nrm + 1) * 8 + 8]
                        nw = n1w(l) if nrm == 0 else n2w(l)
                        P.op("dve", lambda e, sc=sc, nw=nw, s=s, nrm=nrm: e.scalar_tensor_tensor(out=AB[:, s, nrm, 0, :], in0=sc, scalar=1.0, in1=nw, op0=ALU.add, op1=ALU.mult),
                             reads=["modpp", "spp"], writes=[("AB", s, nrm, 0)])
                        P.op("dve", lambda e, s=s, nrm=nrm: e.tensor_scalar(out=AB[:, s, nrm, 0, :], in0=AB[:, s, nrm, 0, :], scalar1=float(math.sqrt(D)), scalar2=None, op0=ALU.mult),
                             reads=[("AB", s, nrm, 0)], writes=[("AB", s, nrm, 0)])
                        P.op("dve", lambda e, sh=sh, s=s, nrm=nrm: e.tensor_copy(AB[:, s, nrm, 1, :], sh), reads=["modpp"], writes=[("AB", s, nrm, 1)])
                P.barrier()

        def phase_norm(l, nrm, xsrc, tiles):
            with contextlib.ExitStack() as st:
                xb = bld.rot(st, "nx", [128, D], F32, 3)
                junk = bld.sb(st, "njunk", [128, D], BF16)
                xn = bld.rot(st, "nxn", [128, D], BF16, 8)
                ssb = bld.rot(st, "nss", [128, 1], F32, 8)
                groups = []
                cur = []
                for t in tiles:
                    cur.append(t)
                    if len(cur) == 4:
                        groups.append(cur); cur = []
                if cur:
                    groups.append(cur)
                for grp in groups:
                    xns = []
                    for t in grp:
                        xt, xk = xb.get()
                        P.dma("sp", xt[:], xsrc[t * 128:(t + 1) * 128, :], reads=[("xs", t)], writes=[xk])
                        ss, sk = ssb.get()
                        P.op("pool", lambda e, ss=ss: e.memset(ss[:], 0.0), writes=[sk])
                        P.op("dve", lambda e, xt=xt, ss=ss: e.scalar_tensor_tensor(out=junk[:], in0=xt[:], scalar=1.0, in1=xt[:], op0=ALU.mult, op1=ALU.mult, accum_out=ss[:]), reads=[xk, sk], writes=["njunk", sk])
                        P.op("act", lambda e, ss=ss: e.activation(ss[:], ss[:], AF.Sqrt, bias=epsc(0)), reads=[sk, "c32"], writes=[sk])
                        P.op("dve", lambda e, ss=ss: e.reciprocal(ss[:], ss[:]), reads=[sk], writes=[sk])
                        xnt, xnk = xn.get()
                        P.op("act", lambda e, xnt=xnt, xt=xt, ss=ss: e.activation(xnt[:], xt[:], AF.Copy, scale=ss[:]), reads=[xk, sk], writes=[xnk])
                        xns.append((t, xnt, xnk))
                    for k in range(KD):
                        ps, pk = PSB.get()
                        psv = ps[:].bitcast(BF16)
                        for i, (t, xnt, xnk) in enumerate(xns):
                            bld.tr(psv[:, i * 128:(i + 1) * 128], xnt[:, k * 128:(k + 1) * 128], identb[:], [xnk, "identb"], pk)
                        i = 0
                        while i < len(xns):
                            s = tile_stream(xns[i][0])
                            j = i
                            while j < len(xns) and tile_stream(xns[j][0]) == s:
                                j += 1
                            t0 = xns[i][0]
                            dst = HT[:, k, t0 * 128:(t0 + (j - i)) * 128]
                            src = psv[:, i * 128:j * 128]
                            wr = [("HT", tt) for tt in range(t0, t0 + (j - i))]
                            a_ap = AB[:, s, nrm, 0, k:k + 1]
                            b_ap = AB[:, s, nrm, 1, k:k + 1]
                            if k % 2 == 0:
                                P.op("act", lambda e, dst=dst, src=src, a_ap=a_ap, b_ap=b_ap: e.activation(dst, src, AF.Identity, bias=b_ap, scale=a_ap),
                                     reads=[pk, ("AB", s, nrm, 0), ("AB", s, nrm, 1)], writes=wr)
                            else:
                                P.op("dve", lambda e, dst=dst, src=src, a_ap=a_ap, b_ap=b_ap: e.tensor_scalar(out=dst, in0=src, scalar1=a_ap, scalar2=b_ap, op0=ALU.mult, op1=ALU.add),
                                     reads=[pk, ("AB", s, nrm, 0), ("AB", s, nrm, 1)], writes=wr)
                            i = j
                P.barrier()

        HTk = lambda ts: [("HT", t) for t in ts]
        ALLT = list(range(NT))

        def load_w(rotw, src2d, reads=()):
            wt, wk = rotw.get()
            P.dma("pool", wt[:], src2d.rearrange("(k p) c -> p k c", p=128), reads=reads, writes=[wk])
            return wt, wk

        BLKS = [(0, 512), (512, 512), (1024, 512), (1536, 512), (2048, 256)]

        def proj_fm(wt, wk, rhs_of, nk, evac, m=128, blks=BLKS, extra_reads=(), coff=0):
            for bi, (t0, n) in enumerate(blks):
                ps, pk = PSB.get()
                pairs = [(wt[:, k, coff:coff + m], rhs_of(k, t0, n)) for k in range(nk)]
                tl = list(range(t0 // 128, (t0 + n) // 128))
                bld.mm(ps[0:m, 0:n], pairs, [wk] + HTk(tl) + list(extra_reads), pk)
                evac(bi, t0, n, ps, pk)

        hT_rhs = lambda k, t0, n: HT[:, k, t0:t0 + n]

        def head_out_norm(st, l, h, Oacc, zs, nrot, b_idx, tiles, tag):
            oTh = bld.sb(st, tag + "oTh", [128, T], BF16)
            ssq = bld.sb(st, tag + "ssq", [128, NT], F32)
            junk = bld.sb(st, tag + "junk", [128, 128], BF16)
            yb = bld.rot(st, tag + "yb", [128, 128], BF16, 4)
            for t in tiles:
                P.op("act", lambda e, t=t: e.activation(junk[:], Oacc[:, t, :], AF.Square, accum_out=ssq[:, t:t + 1]), reads=[(tag + "O", t)], writes=[tag + "junk", (tag + "ssq", t)])
            t0, t1 = tiles[0], tiles[-1] + 1
            P.op("act", lambda e: e.activation(ssq[:, t0:t1], ssq[:, t0:t1], AF.Sqrt, bias=epsc(2), scale=1.0 / 128.0), reads=[(tag + "ssq", t) for t in tiles] + ["c32"], writes=[tag + "ssqall"])
            P.op("dve", lambda e: e.reciprocal(ssq[:, t0:t1], ssq[:, t0:t1]), reads=[tag + "ssqall"], writes=[tag + "ssqall"])
            grp = [tiles[i:i + 4] for i in range(0, len(tiles), 4)]
            for g in grp:
                ps, pk = PSB.get()
                psv = ps[:].bitcast(BF16)
                for i, t in enumerate(g):
                    y, yk = yb.get()
                    P.op("dve", lambda e, y=y, t=t: e.scalar_tensor_tensor(out=y[:], in0=Oacc[:, t, :], scalar=ssq[:, t:t + 1], in1=zs[:, t, :], op0=ALU.mult, op1=ALU.mult),
                         reads=[(tag + "O", t), tag + "ssqall", (tag + "zs", t)], writes=[yk])
                    bld.tr(psv[:, i * 128:(i + 1) * 128], y[:], identb[:], [yk, "identb"], pk)
                n = len(g) * 128
                P.op("act", lambda e, g=g, n=n, psv=psv: e.copy(oTh[:, g[0] * 128:g[0] * 128 + n], psv[:, 0:n]), reads=[pk], writes=[(tag + "oTh", t) for t in g])
            c0 = tiles[0] * 128
            P.dma("sp", oT_s[b_idx][h][:, c0:T], oTh[:, c0:T], reads=[(tag + "oTh", t) for t in tiles], writes=[("oTs", b_idx, h)])

        def phase_gdn(l, tiles):
            with contextlib.ExitStack() as st:
                wrot = bld.rot(st, "gw", [128, KD, 128], BF16, 3)
                wab = bld.sb(st, "gwab", [128, KD, 16], BF16)
                gbeta = bld.sb(st, "gbeta", [128, NT, 16], F32)
                tmp8 = bld.sb(st, "gtmp8", [128, NT, 8], F32)
                P.dma("pool", wab[:], w_in[l][:, GAB:GAB + 16].rearrange("(k p) c -> p k c", p=128), writes=["gwab"])
                ab_ps, ab_key = PSX, "psx"
                for t in ALLT:
                    bld.mm(ab_ps[:, t * 16:(t + 1) * 16], [(HT[:, k, t * 128:(t + 1) * 128], wab[:, k, :]) for k in range(KD)], ["gwab", ("HT", t)], ab_key)
                abv = ab_ps[:, 0:NT * 16].rearrange("p (t c) -> p t c", c=16)
                for t in ALLT:
                    P.op("dve", lambda e, t=t: e.tensor_tensor(out=tmp8[:, t, :], in0=abv[:, t, 0:8], in1=dtb(l), op=ALU.add), reads=[ab_key, "rws"], writes=[("gtmp8", t)])
                al = bld.sb(st, "galog", [128, 8], F32)
                P.op("act", lambda e: e.activation(al[:], alog(l), AF.Exp), reads=["rws"], writes=["galog"])
                t8all = [("gtmp8", t) for t in ALLT]
                P.op("act", lambda e: e.activation(tmp8[:], tmp8[:], AF.Exp), reads=t8all, writes=["gtmp8all"])
                P.op("act", lambda e: e.activation(tmp8[:], tmp8[:], AF.Ln, bias=1.0), reads=["gtmp8all"], writes=["gtmp8all"])
                for t in ALLT:
                    P.op("dve", lambda e, t=t: e.scalar_tensor_tensor(out=gbeta[:, t, 0:8], in0=tmp8[:, t, :], scalar=-1.0, in1=al[:], op0=ALU.mult, op1=ALU.mult),
                         reads=["gtmp8all", "galog"], writes=[("gb_g", t)])
                P.op("act", lambda e: e.activation(gbeta[:, :, 8:16], abv[:, :, 8:16], AF.Sigmoid), reads=[ab_key], writes=["gb_beta"])
                gbk = [("gb_g", t) for t in ALLT] + ["gb_beta"]
                P.barrier()
                for h in range(4):
                    with contextlib.ExitStack() as sh:
                        gdn_head(sh, l, h, tiles, wrot, gbeta)
                    P.barrier()

        def gdn_head(st, l, h, tiles, wrot, gbeta):
            sbh = lambda name, shape, dt: bld.sb(st, name, shape, dt)
            W = T + 3
            off = lambda t0: t0 + 1 if t0 < 256 else t0 + 2
            raw = bld.rot(st, "graw", [128, W], F32, 2)
            cv = bld.rot(st, "gcv", [128, W], F32, 2)
            qT = sbh("gqT", [128, T], BF16)
            kT = sbh("gkT", [128, T], BF16)
            vT = sbh("gvT", [128, T], BF16)
            ktok = sbh("gktok", [128, NT, 128], BF16)
            vtok = sbh("gvtok", [128, NT, 128], BF16)
            zs = sbh("gzs", [128, NT, 128], F32)
            Oacc = sbh("gO", [128, NT, 128], F32)
            sqr = bld.rot(st, "gsqr", [128, 512], F32R, 2)
            rnb = bld.rot(st, "grnb", [128, 512], F32, 2)
            for fi, (c0, dst) in enumerate(((GQ, qT), (GK, kT), (GV, vT))):
                ch = fi * 4 + h
                wt, wk = load_w(wrot, w_in[l][:, c0 + h * 128: c0 + (h + 1) * 128])
                rw, rk = raw.get()
                P.op("pool", lambda e, rw=rw: e.memset(rw[:], 0.0), writes=[rk])
                def ev(bi, t0, n, ps, pk, rw=rw, rk=rk):
                    o_ = off(t0)
                    if t0 == 0:
                        P.op("act", lambda e: e.copy(rw[:, 1:257], ps[:, 0:256]), reads=[pk], writes=[rk])
                        P.op("dve", lambda e: e.tensor_copy(rw[:, 258:514], ps[:, 256:512]), reads=[pk], writes=[rk])
                    else:
                        eng = "act" if bi % 2 else "dve"
                        if eng == "act":
                            P.op("act", lambda e: e.copy(rw[:, o_:o_ + n], ps[:, 0:n]), reads=[pk], writes=[rk])
                        else:
                            P.op("dve", lambda e: e.tensor_copy(rw[:, o_:o_ + n], ps[:, 0:n]), reads=[pk], writes=[rk])
                proj_fm(wt, wk, hT_rhs, KD, ev)
                c, ck = cv.get()
                P.op("act", lambda e, c=c, rw=rw, ch=ch: e.activation(c[:, 1:W - 1], rw[:, 1:W - 1], AF.Copy, scale=gconv(l, ch, 1)), reads=[rk, "spp"], writes=[ck])
                P.op("dve", lambda e, c=c, rw=rw, ch=ch: e.scalar_tensor_tensor(out=c[:, 1:W - 1], in0=rw[:, 0:W - 2], scalar=gconv(l, ch, 0), in1=c[:, 1:W - 1], op0=ALU.mult, op1=ALU.add), reads=[rk, ck, "spp"], writes=[ck])
                P.op("dve", lambda e, c=c, rw=rw, ch=ch: e.scalar_tensor_tensor(out=c[:, 1:W - 1], in0=rw[:, 2:W], scalar=gconv(l, ch, 2), in1=c[:, 1:W - 1], op0=ALU.mult, op1=ALU.add), reads=[rk, ck, "spp"], writes=[ck])
                P.op("act", lambda e, c=c: e.activation(c[:, 1:W - 1], c[:, 1:W - 1], AF.Silu), reads=[ck], writes=[ck])
                if fi == 2:
                    P.op("dve", lambda e, c=c: e.tensor_copy(vT[:, 0:256], c[:, 1:257]), reads=[ck], writes=[("gT", 2, 0)])
                    P.op("dve", lambda e, c=c: e.tensor_copy(vT[:, 256:T], c[:, 258:W - 1]), reads=[ck], writes=[("gT", 2, 1)])
                else:
                    for bi, (t0, n) in enumerate(BLKS):
                        segs = [(0, 256), (256, 256)] if t0 == 0 else [(t0, n)]
                        sq, sqk = sqr.get()
                        for (s0, sn) in segs:
                            P.op("act", lambda e, c=c, s0=s0, sn=sn, sq=sq, t0=t0: e.activation(sq[:, s0 - t0:s0 - t0 + sn], c[:, off(s0):off(s0) + sn], AF.Square), reads=[ck], writes=[sqk])
                        ps, pk = PSB.get()
                        bld.mm(ps[:, 0:n], [(onesr[:], sq[:, 0:n])], [sqk, "onesr"], pk)
                        rn, rnk = rnb.get()
                        P.op("act", lambda e, rn=rn, ps=ps, n=n: e.activation(rn[:, 0:n], ps[:, 0:n], AF.Sqrt, bias=epsc(2)), reads=[pk, "c32"], writes=[rnk])
                        P.op("dve", lambda e, rn=rn, n=n: e.reciprocal(rn[:, 0:n], rn[:, 0:n]), reads=[rnk], writes=[rnk])
                        scl = float(128 ** -0.5) if fi == 0 else 1.0
                        for (s0, sn) in segs:
                            P.op("dve", lambda e, c=c, s0=s0, sn=sn, rn=rn, t0=t0, dst=dst, scl=scl: e.scalar_tensor_tensor(out=dst[:, s0:s0 + sn], in0=c[:, off(s0):off(s0) + sn], scalar=scl, in1=rn[:, s0 - t0:s0 - t0 + sn], op0=ALU.mult, op1=ALU.mult),
                                 reads=[ck, rnk], writes=[("gT", fi, s0)])
            gTk = lambda fi: [("gT", fi, s0) for s0 in (0, 256, 512, 1024, 1536, 2048)] + [("gT", 2, 0), ("gT", 2, 1)]
            for (src, dstt, fi, nm) in ((kT, ktok, 1, "gktok"), (vT, vtok, 2, "gvtok")):
                for g0 in range(0, NT, 4):
                    g = list(range(g0, min(g0 + 4, NT)))
                    ps, pk = PSB.get()
                    psv = ps[:].bitcast(BF16)
                    for i, t in enumerate(g):
                        bld.tr(psv[:, i * 128:(i + 1) * 128], src[:, t * 128:(t + 1) * 128], identb[:], gTk(fi) + ["identb"], pk)
                    n = len(g) * 128
                    P.op("act" if (g0 // 4) % 2 else "dve",
                         (lambda e, g=g, n=n, psv=psv, dstt=dstt: e.copy(dstt[:, g[0]:g[0] + len(g), :], psv[:, 0:n].rearrange("p (t c) -> p t c", c=128))) if (g0 // 4) % 2 else
                         (lambda e, g=g, n=n, psv=psv, dstt=dstt: e.tensor_copy(dstt[:, g[0]:g[0] + len(g), :], psv[:, 0:n].rearrange("p (t c) -> p t c", c=128))),
                         reads=[pk], writes=[(nm, t) for t in g])
            wt, wk = load_w(wrot, w_in[l][:, GZ + h * 128: GZ + (h + 1) * 128])
            for t in tiles:
                ps, pk = PSS.get()
                bld.mm(ps, [(HT[:, k, t * 128:(t + 1) * 128], wt[:, k, :]) for k in range(KD)], [wk, ("HT", t)], pk)
                P.op("act", lambda e, t=t, ps=ps: e.activation(zs[:, t, :], ps, AF.Silu), reads=[pk], writes=[("gzs", t)])
                P.op("pool", lambda e, t=t: e.tensor_tensor(out=zs[:, t, :], in0=zs[:, t, :], in1=gnw(l), op=ALU.mult), reads=[("gzs", t), "rws"], writes=[("gzs", t)])
            f32t = lambda name, n: bld.rot(st, name, [128, 128], F32, n)
            b16t = lambda name, n: bld.rot(st, name, [128, 128], BF16, n)
            gbr = f32t("g_gb", 3); egr = f32t("g_eg", 5); dsr = f32t("g_ds", 3); dir_ = f32t("g_di", 3)
            colr = bld.rot(st, "g_col", [128, 4], F32, 5)
            Pm = bld.rot(st, "g_P", [128, 256], F32, 4); PTm = bld.rot(st, "g_PT", [128, 256], F32, 4); Rm = bld.rot(st, "g_R", [128, 256], F32, 3)
            ident2 = sbh("g_id2", [128, 256], F32)
            P.op("dve", lambda e: e.tensor_copy(ident2[:, 0:128], ident32), reads=["c32"], writes=["g_id2"])
            P.op("dve", lambda e: e.tensor_copy(ident2[:, 128:256], ident32), reads=["c32"], writes=["g_id2"])
            TTb = bld.rot(st, "g_TT", [128, 256], BF16, 3); atb = b16t("g_at", 5); qdb = b16t("g_qd", 5); kdb = b16t("g_kd", 5)
            rb = b16t("g_r", 3); vnb = b16t("g_vn", 3)
            S32 = [sbh("gS32_%d" % d, [128, 128], F32) for d in range(2)]
            Sb = [bld.rot(st, "gSb%d" % d, [128, 128], BF16, 2) for d in range(2)]
            order = [list(range(NT)), [1, 0] + list(range(NT - 1, 1, -1))]
            cur_Sb = [None, None]
            for d in range(2):
                P.op("pool", lambda e, d=d: e.memset(S32[d][:], 0.0), writes=[("gS32", d)])
                sbt, sbk = Sb[d].get()
                P.op("pool", lambda e, sbt=sbt: e.memset(sbt[:], 0.0), writes=[sbk])
                cur_Sb[d] = (sbt, sbk)
            visited = set()
            kT_k, qT_k = gTk(1), gTk(0)

            def precompute(d, c, P2, P2k):
                q = d * 4 + h
                gcol = gbeta[:, c, q:q + 1]
                bcol = gbeta[:, c, 8 + q:9 + q]
                tcol = slice(c * 128, (c + 1) * 128)
                gb, gbk_ = gbr.get()
                P.op("dve", lambda e: e.tensor_scalar(out=gb[:], in0=Umask[d], scalar1=gcol, scalar2=None, op0=ALU.mult), reads=[("gb_g", c), "c32"], writes=[gbk_])
                psA, kA = PSS.get()
                bld.mm(psA, [(ones32, gb[:])], [gbk_, "c32"], kA)
                psB, kB = PSS.get()
                bld.mm(psB, [(ones32, gb[:]), (ident32, NEGS[d])], [gbk_, "c32"], kB)
                psC, kC = PSS.get()
                bld.mm(psC[:, 0:1], [(Umask[d], gcol)], [("gb_g", c), "c32"], kC)
                yield
                col, colk = colr.get()
                P.op("dve", lambda e: e.tensor_scalar(out=col[:, 1:2], in0=psC[:, 0:1], scalar1=-1.0, scalar2=None, op0=ALU.mult), reads=[kC], writes=[colk])
                P.op("act", lambda e: e.activation(col[:, 0:1], psC[:, 0:1], AF.Exp), reads=[kC], writes=[colk])
                P.op("dve", lambda e: e.tensor_scalar(out=col[:, 0:1], in0=col[:, 0:1], scalar1=-1.0, scalar2=None, op0=ALU.mult), reads=[colk], writes=[colk])
                P.op("dve", lambda e: e.tensor_scalar(out=col[:, 2:3], in0=bcol, scalar1=-1.0, scalar2=None, op0=ALU.mult), reads=["gb_beta"], writes=[colk])
                eg, egk = egr.get()
                P.op("act", lambda e: e.activation(eg[:], psA, AF.Exp), reads=[kA], writes=[egk])
                ds, dsk = dsr.get()
                P.op("act", lambda e: e.activation(ds[:], psB, AF.Exp, bias=col[:, 1:2]), reads=[kB, colk], writes=[dsk])
                yield
                di, dik = dir_.get()
                P.op("dve", lambda e: e.tensor_tensor(out=di[:], in0=ds[:], in1=ident32, op=ALU.add), reads=[dsk, "c32"], writes=[dik])
                psK, kK = PSS.get()
                bld.mm(psK, [(kT[:, tcol], kT[:, tcol])], kT_k, kK)
                psQ, kQ = PSS.get()
                bld.mm(psQ, [(kT[:, tcol], qT[:, tcol])], kT_k + qT_k, kQ)
                yield
                p0 = P2[:, d * 128:(d + 1) * 128]
                p0k = (P2k, d)
                P.op("dve", lambda e: e.scalar_tensor_tensor(out=p0, in0=psK, scalar=col[:, 2:3], in1=ds[:], op0=ALU.mult, op1=ALU.mult), reads=[kK, colk, dsk], writes=[p0k])
                at, atk = atb.get()
                P.op("dve", lambda e: e.tensor_tensor(out=at[:], in0=psQ, in1=di[:], op=ALU.mult), reads=[kQ, dik], writes=[atk])
                last = 127 if d == 0 else 0
                kd, kdk = kdb.get()
                P.op("pool", lambda e: e.tensor_scalar(out=kd[:], in0=ktok[:, c, :], scalar1=di[:, last:last + 1], scalar2=None, op0=ALU.mult), reads=[("gktok", c), dik], writes=[kdk])
                qd, qdk = qdb.get()
                P.op("pool", lambda e: e.tensor_tensor(out=qd[:], in0=qT[:, tcol], in1=eg[:], op=ALU.mult), reads=qT_k + [egk], writes=[qdk])
                pr_out[d] = dict(col=col, colk=colk, eg=eg, egk=egk, at=at, atk=atk, kd=kd, kdk=kdk, qd=qd, qdk=qdk, bcol=bcol, last=last)

            def step(d, c, pre):
                tcol = slice(c * 128, (c + 1) * 128)
                sbt, sbk = cur_Sb[d]
                psk, kk = PSS.get()
                bld.mm(psk, [(kT[:, tcol], sbt[:])], kT_k + [sbk], kk)
                r, rk_ = rb.get()
                P.op("dve", lambda e: e.scalar_tensor_tensor(out=r[:], in0=psk, scalar=pre["col"][:, 0:1], in1=vtok[:, c, :], op0=ALU.mult, op1=ALU.add), reads=[kk, pre["colk"], ("gvtok", c)], writes=[rk_])
                psv, kv = PSS.get()
                bld.mm(psv, [(pre["tt"], r[:])], [pre["ttk"], rk_], kv)
                vn, vnk = vnb.get()
                P.op("act", lambda e: e.activation(vn[:], psv, AF.Copy, scale=pre["bcol"]), reads=[kv, "gb_beta"], writes=[vnk])
                pso, ko = PSS.get()
                bld.mm(pso, [(pre["qd"][:], sbt[:]), (pre["at"][:], vn[:])], [pre["qdk"], sbk, pre["atk"], vnk], ko)
                if c in visited:
                    P.op("dve", lambda e: e.tensor_tensor(out=Oacc[:, c, :], in0=pso, in1=Oacc[:, c, :], op=ALU.add), reads=[ko, ("gO", c)], writes=[("gO", c)])
                else:
                    visited.add(c)
                    P.op("act", lambda e: e.copy(Oacc[:, c, :], pso), reads=[ko], writes=[("gO", c)])
                pss_, ks = PSS.get()
                bld.mm(pss_, [(pre["kd"][:], vn[:])], [pre["kdk"], vnk], ks)
                last = pre["last"]
                P.op("dve", lambda e: e.scalar_tensor_tensor(out=S32[d][:], in0=S32[d][:], scalar=pre["eg"][:, last:last + 1], in1=pss_, op0=ALU.mult, op1=ALU.add), reads=[("gS32", d), pre["egk"], ks], writes=[("gS32", d)])
                nsb, nsbk = Sb[d].get()
                P.op("act", lambda e: e.copy(nsb[:], S32[d][:]), reads=[("gS32", d)], writes=[nsbk])
                cur_Sb[d] = (nsb, nsbk)

            def neumann2(P2, P2k):
                pk_all = [(P2k, 0), (P2k, 1)]
                psT, kT_ = PSH.get()
                for x in range(2):
                    bld.tr(psT[:, x * 128:(x + 1) * 128], P2[:, x * 128:(x + 1) * 128], ident32, pk_all + ["c32"], kT_)
                PT2, PT2k = PTm.get()
                P.op("act", lambda e, PT2=PT2: e.copy(PT2[:], psT), reads=[kT_], writes=[PT2k])
                R2, R2k = Rm.get()
                P.op("dve", lambda e, R2=R2: e.tensor_tensor(out=R2[:], in0=P2[:], in1=ident2[:], op=ALU.add), reads=pk_all + ["g_id2"], writes=[R2k])
                pc, pck, ptc, ptck = P2, pk_all, PT2, [PT2k]
                sl = lambda t_, x: t_[:, x * 128:(x + 1) * 128]
                for lev in range(6):
                    if lev < 5:
                        S1, k1 = PSH.get()
                        for x in range(2):
                            bld.mm(sl(S1, x), [(sl(ptc, x), sl(pc, x))], pck + ptck, k1)
                    S2, k2 = PSH.get()
                    for x in range(2):
                        bld.mm(sl(S2, x), [(sl(pc, x), sl(ptc, x))], pck + ptck, k2)
                    if lev >= 1:
                        S3, k3 = PSH.get()
                        for x in range(2):
                            bld.mm(sl(S3, x), [(sl(ptc, x), sl(R2, x))], ptck + [R2k], k3)
                    Pn, Pnk = Pm.get()
                    if lev < 5:
                        P.op("act", lambda e, Pn=Pn, S1=S1: e.copy(Pn[:], S1), reads=[k1], writes=[Pnk])
                    PTn, PTnk = PTm.get()
                    P.op("dve" if lev < 1 else "act", (lambda e, PTn=PTn, S2=S2: e.tensor_copy(PTn[:], S2)) if lev < 1 else (lambda e, PTn=PTn, S2=S2: e.copy(PTn[:], S2)), reads=[k2], writes=[PTnk])
                    if lev >= 1:
                        Rn, Rnk = Rm.get()
                        P.op("dve", lambda e, Rn=Rn, R2=R2, S3=S3: e.tensor_tensor(out=Rn[:], in0=S3, in1=R2[:], op=ALU.add), reads=[k3, R2k], writes=[Rnk])
                        R2, R2k = Rn, Rnk
                    pc, pck, ptc, ptck = Pn, [Pnk], PTn, [PTnk]
                S3, k3 = PSH.get()
                for x in range(2):
                    bld.mm(sl(S3, x), [(sl(ptc, x), sl(R2, x))], ptck + [R2k], k3)
                tt2, tt2k = TTb.get()
                P.op("dve", lambda e, tt2=tt2, R2=R2, S3=S3: e.tensor_tensor(out=tt2[:], in0=S3, in1=R2[:], op=ALU.add), reads=[k3, R2k], writes=[tt2k])
                return tt2, tt2k

            pres = {}
            pr_out = {}
            P.warm = GDN_WARM
            for s_ in range(NT + 1):
                if s_ < NT:
                    P2, P2k = Pm.get()
                    pr_out.clear()
                    gens = [precompute(d, order[d][s_], P2, P2k) for d in range(2)]
                    while gens:
                        for g_ in list(gens):
                            try:
                                next(g_)
                            except StopIteration:
                                gens.remove(g_)
                    pr = [pr_out[0], pr_out[1]]
                    tt2, tt2k = neumann2(P2, P2k)
                    for d in range(2):
                        pr[d]["tt"] = tt2[:, d * 128:(d + 1) * 128]
                        pr[d]["ttk"] = tt2k
                        pres[(d, order[d][s_])] = pr[d]
                for d in range(2):
                    if s_ >= 1:
                        c_ = order[d][s_ - 1]
                        step(d, c_, pres.pop((d, c_)))
            P.warm = 0
            head_out_norm(st, l, h, Oacc, zs, None, 0, tiles, "g")
        def rope_fm(src_bf, src_key, dst, dst_key_of, nrows, rmat, cos_t, sin_t, tabk, tmpA, tmpB):
            for bi, (t0, n) in enumerate(BLKS):
                ps, pk = PSB.get()
                bld.mm(ps[0:nrows, 0:n], [(rmat, src_bf[0:nrows, t0:t0 + n])], [src_key, "rmb"], pk)
                a, ak = tmpA.get()
                b_, bk = tmpB.get()
                P.op("dve", lambda e, a=a, ps=ps, n=n, t0=t0: e.tensor_tensor(out=a[0:nrows, 0:n], in0=ps[0:nrows, 0:n], in1=sin_t[0:nrows, t0:t0 + n], op=ALU.mult), reads=[pk, tabk], writes=[ak])
                P.op("pool", lambda e, b_=b_, n=n, t0=t0: e.tensor_tensor(out=b_[0:nrows, 0:n], in0=src_bf[0:nrows, t0:t0 + n], in1=cos_t[0:nrows, t0:t0 + n], op=ALU.mult), reads=[src_key, tabk], writes=[bk])
                P.op("dve", lambda e, a=a, b_=b_, n=n, t0=t0: e.tensor_tensor(out=dst[0:nrows, t0:t0 + n], in0=a[0:nrows, 0:n], in1=b_[0:nrows, 0:n], op=ALU.add), reads=[ak, bk], writes=[dst_key_of(bi)])

        def phase_mla(l, ctx_out):
            with contextlib.ExitStack() as st:
                sbm = lambda name, shape, dt: bld.sb(st, name, shape, dt)
                wrot = bld.rot(st, "mw", [128, KD, 128], BF16, 3)
                cqn = sbm("cqn", [128, 3, T], BF16)
                ckvn = sbm("ckvn", [128, 2, T], BF16)
                krr = sbm("krr", [64, T], BF16)
                cosm = sbm("cosm", [64, T], F32)
                sinm = sbm("sinm", [64, T], F32)
                wuq = sbm("wuq", [128, 3, 768], BF16)
                wukv = sbm("wukv", [128, 2, 1024], BF16)
                tA = bld.rot(st, "mtA", [128, 512], F32, 2)
                tB = bld.rot(st, "mtB", [128, 512], F32, 2)
                qnsc = sbm("qnsc", [128, 5], F32)
                st1 = contextlib.ExitStack()
                cqraw = bld.sb(st1, "cqraw", [128, 3, T], F32)
                ckvraw = bld.sb(st1, "ckvraw", [128, 2, T], F32)
                krb = bld.sb(st1, "krb", [64, T], BF16)
                sqr = bld.rot(st1, "msqr", [128, 512], F32R, 2)
                rnb = bld.rot(st1, "mrnb", [128, 512], F32, 2)
                P.dma("sp", cosm[:], ropem[0], writes=["ropem"])
                P.dma("sp", sinm[:], ropem[1], writes=["ropem2"])
                P.dma("pool", wuq[:], w_uq[l].rearrange("(k p) c -> p k c", p=128), writes=["wuq"])
                P.dma("pool", wukv[:], w_ukv[l].rearrange("(k p) c -> p k c", p=128), writes=["wukv"])
                for k in range(3):
                    P.op("dve", lambda e, k=k: e.tensor_scalar(out=qnsc[:, k:k + 1], in0=qn(l, k), scalar1=float(math.sqrt(384.0)), scalar2=None, op0=ALU.mult), reads=["spp"], writes=["qnsc"])
                for k in range(2):
                    P.op("dve", lambda e, k=k: e.tensor_scalar(out=qnsc[:, 3 + k:4 + k], in0=kvn(l, k), scalar1=float(math.sqrt(256.0)), scalar2=None, op0=ALU.mult), reads=["spp"], writes=["qnsc"])
                for (c0, nch, rawt, nm) in ((CQ, 3, cqraw, "cqraw"), (CKV, 2, ckvraw, "ckvraw")):
                    for ch in range(nch):
                        wt, wk = load_w(wrot, w_in[l][:, c0 + ch * 128: c0 + (ch + 1) * 128])
                        def ev(bi, t0, n, ps, pk, rawt=rawt, ch=ch, nm=nm):
                            if bi % 2:
                                P.op("act", lambda e: e.copy(rawt[:, ch, t0:t0 + n], ps[:, 0:n]), reads=[pk], writes=[(nm, ch, bi)])
                            else:
                                P.op("dve", lambda e: e.tensor_copy(rawt[:, ch, t0:t0 + n], ps[:, 0:n]), reads=[pk], writes=[(nm, ch, bi)])
                        proj_fm(wt, wk, hT_rhs, KD, ev)
                wt, wk = load_w(wrot, w_in[l][:, KR:KR + 128])
                def evk(bi, t0, n, ps, pk):
                    P.op("act", lambda e: e.copy(krb[:, t0:t0 + n], ps[0:64, 0:n]), reads=[pk], writes=[("krb", bi)])
                proj_fm(wt, wk, hT_rhs, KD, evk, m=64)
                for bi, (t0, n) in enumerate(BLKS):
                    ps, pk = PSB.get()
                    bld.mm(ps[0:64, 0:n], [(rmb[0:64, 128:192], krb[:, t0:t0 + n])], [("krb", bi), "rmb"], pk)
                    a, ak = tA.get()
                    b_, bk = tB.get()
                    P.op("dve", lambda e, a=a, ps=ps, n=n, t0=t0: e.tensor_tensor(out=a[0:64, 0:n], in0=ps[0:64, 0:n], in1=sinm[:, t0:t0 + n], op=ALU.mult), reads=[pk, "ropem2"], writes=[ak])
                    P.op("pool", lambda e, b_=b_, n=n, t0=t0: e.tensor_tensor(out=b_[0:64, 0:n], in0=krb[:, t0:t0 + n], in1=cosm[:, t0:t0 + n], op=ALU.mult), reads=[("krb", bi), "ropem"], writes=[bk])
                    P.op("dve", lambda e, a=a, b_=b_, n=n, t0=t0: e.tensor_tensor(out=krr[:, t0:t0 + n], in0=a[0:64, 0:n], in1=b_[0:64, 0:n], op=ALU.add), reads=[ak, bk], writes=[("krr", bi)])
                for (nch, rawt, nm, dstn, dnm, eps_i, q0) in ((3, cqraw, "cqraw", cqn, "cqn", 3, 0), (2, ckvraw, "ckvraw", ckvn, "ckvn", 4, 3)):
                    for bi, (t0, n) in enumerate(BLKS):
                        ps, pk = PSB.get()
                        for ch in range(nch):
                            sq, sqk = sqr.get()
                            P.op("act", lambda e, sq=sq, ch=ch, t0=t0, n=n, rawt=rawt: e.activation(sq[:, 0:n], rawt[:, ch, t0:t0 + n], AF.Square), reads=[(nm, ch, bi)], writes=[sqk])
                            bld.mm_acc(ps[:, 0:n], onesr[:], sq[:, 0:n], ch == 0, ch == nch - 1, [sqk, "onesr"], pk)
                        rn, rnk = rnb.get()
                        P.op("act", lambda e, rn=rn, ps=ps, n=n, eps_i=eps_i: e.activation(rn[:, 0:n], ps[:, 0:n], AF.Sqrt, bias=epsc(eps_i)), reads=[pk, "c32"], writes=[rnk])
                        P.op("dve", lambda e, rn=rn, n=n: e.reciprocal(rn[:, 0:n], rn[:, 0:n]), reads=[rnk], writes=[rnk])
                        for ch in range(nch):
                            P.op("dve", lambda e, ch=ch, rn=rn, t0=t0, n=n, rawt=rawt, dstn=dstn, q0=q0: e.scalar_tensor_tensor(out=dstn[:, ch, t0:t0 + n], in0=rawt[:, ch, t0:t0 + n], scalar=qnsc[:, q0 + ch:q0 + ch + 1], in1=rn[:, 0:n], op0=ALU.mult, op1=ALU.mult),
                                 reads=[(nm, ch, bi), rnk, "qnsc"], writes=[(dnm, bi)])
                P.barrier()
                st1.close()
                qnope = sbm("qnope", [128, T], BF16)
                qrb = sbm("qrb", [64, T], BF16)
                qrr = sbm("qrr", [64, T], BF16)
                knope = sbm("knope", [128, T], BF16)
                vtok = sbm("mvtok", [128, NT, 128], BF16)
                oTh = sbm("moTh", [128, T], BF16)
                pT = bld.rot(st, "mpT", [128, 512], BF16, 3)
                rden = bld.rot(st, "mrden", [128, 512], F32, 2)
                cqk = lambda: [("cqn", bi) for bi in range(5)]
                ckk = lambda: [("ckvn", bi) for bi in range(5)]
                for h in range(4):
                    for bi, (t0, n) in enumerate(BLKS):
                        ps, pk = PSB.get()
                        bld.mm(ps[:, 0:n], [(wuq[:, k, h * 192:h * 192 + 128], cqn[:, k, t0:t0 + n]) for k in range(3)], ["wuq", ("cqn", bi)], pk)
                        P.op("act", lambda e, ps=ps, t0=t0, n=n: e.activation(qnope[:, t0:t0 + n], ps[:, 0:n], AF.Copy, scale=float(MLA_SCALE)), reads=[pk], writes=[("qnope", bi)])
                        ps, pk = PSB.get()
                        bld.mm(ps[0:64, 0:n], [(wuq[:, k, h * 192 + 128:h * 192 + 192], cqn[:, k, t0:t0 + n]) for k in range(3)], ["wuq", ("cqn", bi)], pk)
                        P.op("act", lambda e, ps=ps, t0=t0, n=n: e.activation(qrb[:, t0:t0 + n], ps[0:64, 0:n], AF.Copy, scale=float(MLA_SCALE)), reads=[pk], writes=[("qrb", bi)])
                        ps, pk = PSB.get()
                        bld.mm(ps[:, 0:n], [(wukv[:, k, h * 256:h * 256 + 128], ckvn[:, k, t0:t0 + n]) for k in range(2)], ["wukv", ("ckvn", bi)], pk)
                        P.op("dve", lambda e, ps=ps, t0=t0, n=n: e.tensor_copy(knope[:, t0:t0 + n], ps[:, 0:n]), reads=[pk], writes=[("knope", bi)])
                        ps, pk = PSB.get()
                        bld.mm(ps[0:64, 0:n], [(rmb[0:64, 128:192], qrb[:, t0:t0 + n])], [("qrb", bi), "rmb"], pk)
                        a, ak = tA.get()
                        b_, bk = tB.get()
                        P.op("dve", lambda e, a=a, ps=ps, n=n, t0=t0: e.tensor_tensor(out=a[0:64, 0:n], in0=ps[0:64, 0:n], in1=sinm[:, t0:t0 + n], op=ALU.mult), reads=[pk, "ropem2"], writes=[ak])
                        P.op("pool", lambda e, b_=b_, n=n, t0=t0: e.tensor_tensor(out=b_[0:64, 0:n], in0=qrb[:, t0:t0 + n], in1=cosm[:, t0:t0 + n], op=ALU.mult), reads=[("qrb", bi), "ropem"], writes=[bk])
                        P.op("dve", lambda e, a=a, b_=b_, n=n, t0=t0: e.tensor_tensor(out=qrr[:, t0:t0 + n], in0=a[0:64, 0:n], in1=b_[0:64, 0:n], op=ALU.add), reads=[ak, bk], writes=[("qrr", bi)])
                    for t in ALLT:
                        ps, pk = PSB.get()
                        bld.mm(ps[:, 0:128], [(ckvn[:, k, t * 128:(t + 1) * 128], wukv[:, k, h * 256 + 128:h * 256 + 256]) for k in range(2)], ["wukv", ("ckvn", min(t // 4, 4))], pk)
                        P.op("act", lambda e, t=t, ps=ps: e.copy(vtok[:, t, :], ps[:, 0:128]), reads=[pk], writes=[("mvtok", t)])
                    qgroups = [(256 + g * 512, 512, ALLT) for g in range(4)]
                    if ctx_out:
                        qgroups = [(0, 256, [0, 1])] + qgroups
                    for (q0, nq, ktiles) in qgroups:
                        qb = min(q0 // 512, 4)
                        qbs = sorted(set([min(q0 // 512, 4), min((q0 + nq - 1) // 512, 4)]))
                        o_ps, o_k = banks[6], "bank6"
                        d_ps, d_k = banks[7], "psx"
                        def s_mm(kt):
                            kb = min(kt // 4, 4)
                            ps, pk = PSB.get()
                            bld.mm(ps[:, 0:nq], [(knope[:, kt * 128:(kt + 1) * 128], qnope[:, q0:q0 + nq]), (krr[:, kt * 128:(kt + 1) * 128], qrr[:, q0:q0 + nq])],
                                   [("knope", kb), ("krr", kb)] + [("qnope", b) for b in qbs] + [("qrr", b) for b in qbs], pk)
                            return ps, pk
                        pend = [s_mm(kt) for kt in ktiles[:2]]
                        for i, kt in enumerate(ktiles):
                            ps, pk = pend.pop(0)
                            p_, pkk = pT.get()
                            P.op("act", lambda e, p_=p_, ps=ps, nq=nq: e.activation(p_[:, 0:nq], ps[:, 0:nq], AF.Exp), reads=[pk], writes=[pkk])
                            if i + 2 < len(ktiles):
                                pend.append(s_mm(ktiles[i + 2]))
                            bld.mm_acc(o_ps[:, 0:nq], vtok[:, kt, :], p_[:, 0:nq], i == 0, i == len(ktiles) - 1, [("mvtok", kt), pkk], o_k)
                            bld.mm_acc(d_ps[:, 0:nq], onesb[:], p_[:, 0:nq], i == 0, i == len(ktiles) - 1, ["onesb", pkk], d_k)
                        rd, rdk = rden.get()
                        P.op("dve", lambda e, rd=rd, nq=nq: e.reciprocal(rd[:, 0:nq], d_ps[:, 0:nq]), reads=[d_k], writes=[rdk])
                        P.op("dve", lambda e, rd=rd, nq=nq, q0=q0: e.tensor_tensor(out=oTh[:, q0:q0 + nq], in0=o_ps[:, 0:nq], in1=rd[:, 0:nq], op=ALU.mult), reads=[o_k, rdk], writes=[("moTh", q0)])
                    c0 = 0 if ctx_out else 256
                    P.dma("sp", oT_s[1][h][:, c0:T], oTh[:, c0:T], reads=[("moTh", q) for q in ([0] if ctx_out else []) + [256 + g * 512 for g in range(4)]], writes=[("oTs", 1, h)])
                P.barrier()
        def phase_ret(l, tiles):
            with contextlib.ExitStack() as st:
                sbm = lambda name, shape, dt: bld.sb(st, name, shape, dt)
                wrot = bld.rot(st, "rw", [128, KD, 128], BF16, 3)
                cosr = sbm("cosr", [128, T], F32)
                sinr = sbm("sinr", [128, T], F32)
                rcs = sbm("rcs", [128, 8 * 128 * 2 + 8], F32)
                P.dma("sp", cosr[:], roper[0], writes=["roper"])
                P.dma("sp", sinr[:], roper[1], writes=["roper2"])
                P.dma("sp", rcs[:], retc, writes=["rcs"])
                DTm = lambda q: rcs[:, q * 128:(q + 1) * 128]
                GWm = lambda q: rcs[:, 1024 + q * 128:1024 + (q + 1) * 128]
                kwc = lambda q: rcs[:, 2048 + q:2049 + q]
                rawb = bld.rot(st, "rrawb", [128, T], BF16, 2)
                qT = sbm("rqT", [128, T], BF16)
                kT = sbm("rkT", [128, T], BF16)
                ktok = sbm("rktok", [128, NT, 128], BF16)
                vtok = sbm("rvtok", [128, NT, 128], BF16)
                zs = sbm("rzs", [128, NT, 128], F32)
                tA = bld.rot(st, "rtA", [128, 512], F32, 2)
                tB = bld.rot(st, "rtB", [128, 512], F32, 2)
                atb = bld.rot(st, "r_at", [128, 128], BF16, 5)
                qwb = bld.rot(st, "r_qw", [128, 128], BF16, 5)
                kwb = bld.rot(st, "r_kw", [128, 128], BF16, 5)
                for h in range(4):
                    with contextlib.ExitStack() as sh:
                        Oacc = bld.sb(sh, "rO", [128, NT, 128], F32)
                        for fi, (c0, dst, scl) in enumerate(((RQ, qT, float(128 ** -0.5)), (RK, kT, 1.0))):
                            wt, wk = load_w(wrot, w_in[l][:, c0 + h * 128:c0 + (h + 1) * 128])
                            rb_, rbk = rawb.get()
                            def ev(bi, t0, n, ps, pk, rb_=rb_, rbk=rbk, scl=scl):
                                P.op("act", lambda e: e.activation(rb_[:, t0:t0 + n], ps[:, 0:n], AF.Copy, scale=scl), reads=[pk], writes=[(rbk, bi)])
                            proj_fm(wt, wk, hT_rhs, KD, ev)
                            for bi, (t0, n) in enumerate(BLKS):
                                ps, pk = PSB.get()
                                bld.mm(ps[:, 0:n], [(rmb[:, 0:128], rb_[:, t0:t0 + n])], [(rbk, bi), "rmb"], pk)
                                a, ak = tA.get()
                                b_, bk = tB.get()
                                P.op("dve", lambda e, a=a, ps=ps, n=n, t0=t0: e.tensor_tensor(out=a[:, 0:n], in0=ps[:, 0:n], in1=sinr[:, t0:t0 + n], op=ALU.mult), reads=[pk, "roper2"], writes=[ak])
                                P.op("pool", lambda e, b_=b_, n=n, t0=t0, rb_=rb_: e.tensor_tensor(out=b_[:, 0:n], in0=rb_[:, t0:t0 + n], in1=cosr[:, t0:t0 + n], op=ALU.mult), reads=[(rbk, bi), "roper"], writes=[bk])
                                P.op("dve", lambda e, a=a, b_=b_, n=n, t0=t0, dst=dst: e.tensor_tensor(out=dst[:, t0:t0 + n], in0=a[:, 0:n], in1=b_[:, 0:n], op=ALU.add), reads=[ak, bk], writes=[("rT", fi, bi)])
                        rTk = lambda fi: [("rT", fi, bi) for bi in range(5)]
                        for g0 in range(0, NT, 4):
                            g = list(range(g0, min(g0 + 4, NT)))
                            ps, pk = PSB.get()
                            psv = ps[:].bitcast(BF16)
                            for i, t in enumerate(g):
                                bld.tr(psv[:, i * 128:(i + 1) * 128], kT[:, t * 128:(t + 1) * 128], identb[:], [("rT", 1, min(t // 4, 4)), "identb"], pk)
                            n = len(g) * 128
                            P.op("dve", lambda e, g=g, n=n, psv=psv: e.tensor_copy(ktok[:, g[0]:g[0] + len(g), :], psv[:, 0:n].rearrange("p (t c) -> p t c", c=128)), reads=[pk], writes=[("rktok", t) for t in g])
                        wt, wk = load_w(wrot, w_in[l][:, RV + h * 128:RV + (h + 1) * 128])
                        for t in ALLT:
                            ps, pk = PSS.get()
                            bld.mm(ps, [(HT[:, k, t * 128:(t + 1) * 128], wt[:, k, :]) for k in range(KD)], [wk, ("HT", t)], pk)
                            P.op("act", lambda e, t=t, ps=ps: e.copy(vtok[:, t, :], ps), reads=[pk], writes=[("rvtok", t)])
                        wt, wk = load_w(wrot, w_in[l][:, RG + h * 128:RG + (h + 1) * 128])
                        for t in tiles:
                            ps, pk = PSS.get()
                            bld.mm(ps, [(HT[:, k, t * 128:(t + 1) * 128], wt[:, k, :]) for k in range(KD)], [wk, ("HT", t)], pk)
                            P.op("act", lambda e, t=t, ps=ps: e.activation(zs[:, t, :], ps, AF.Silu), reads=[pk], writes=[("rzs", t)])
                            P.op("pool", lambda e, t=t, h=h: e.tensor_tensor(out=zs[:, t, :], in0=zs[:, t, :], in1=rnw(l, h), op=ALU.mult), reads=[("rzs", t), "rws"], writes=[("rzs", t)])
                        S32 = [bld.sb(sh, "rS32_%d" % d, [128, 128], F32) for d in range(2)]
                        Sb = [bld.rot(sh, "rSb%d" % d, [128, 128], BF16, 2) for d in range(2)]
                        order = [list(range(NT)), [1, 0] + list(range(NT - 1, 1, -1))]
                        cur = [None, None]
                        for d in range(2):
                            P.op("pool", lambda e, d=d, S32=S32: e.memset(S32[d][:], 0.0), writes=[("rS32", d)])
                            sbt, sbk = Sb[d].get()
                            P.op("pool", lambda e, sbt=sbt: e.memset(sbt[:], 0.0), writes=[sbk])
                            cur[d] = (sbt, sbk)
                        visited = set()
                        atA = bld.sb(sh, "r_atA", [128, 2 * NT, 128], BF16)
                        qwA = bld.sb(sh, "r_qwA", [128, 2 * NT, 128], BF16)
                        kwA = bld.sb(sh, "r_kwA", [128, 2 * NT, 128], BF16)
                        for c in ALLT:
                            tcol = slice(c * 128, (c + 1) * 128)
                            cb = min(c // 4, 4)
                            psQ, kQ = PSS.get()
                            bld.mm(psQ, [(kT[:, tcol], qT[:, tcol])], [("rT", 0, cb), ("rT", 1, cb)], kQ)
                            for d in range(2):
                                q = d * 4 + h
                                ix = d * NT + c
                                P.op("dve", lambda e, psQ=psQ, q=q, ix=ix, atA=atA: e.tensor_tensor(out=atA[:, ix, :], in0=psQ, in1=DTm(q), op=ALU.mult), reads=[kQ, "rcs"], writes=[("r_at", ix)])
                                P.op("dve" if d == 0 else "pool", lambda e, tcol=tcol, q=q, ix=ix, qwA=qwA: e.tensor_tensor(out=qwA[:, ix, :], in0=qT[:, tcol], in1=GWm(q), op=ALU.mult), reads=[("rT", 0, cb), "rcs"], writes=[("r_qw", ix)])
                                P.op("act", lambda e, c=c, q=q, ix=ix, kwA=kwA: e.activation(kwA[:, ix, :], ktok[:, c, :], AF.Copy, scale=kwc(q)), reads=[("rktok", c), "rcs"], writes=[("r_kw", ix)])
                        for s_ in range(NT):
                            for d in range(2):
                                c = order[d][s_]
                                q = d * 4 + h
                                ix = d * NT + c
                                sbt, sbk = cur[d]
                                pso, ko = PSS.get()
                                bld.mm(pso, [(qwA[:, ix, :], sbt[:]), (atA[:, ix, :], vtok[:, c, :])], [("r_qw", ix), sbk, ("r_at", ix), ("rvtok", c)], ko)
                                pss_, ks = PSS.get()
                                bld.mm(pss_, [(kwA[:, ix, :], vtok[:, c, :])], [("r_kw", ix), ("rvtok", c)], ks)
                                P.op("dve", lambda e, d=d, q=q, pss_=pss_, S32=S32: e.scalar_tensor_tensor(out=S32[d][:], in0=S32[d][:], scalar=float(RET_CDEC[q]), in1=pss_, op0=ALU.mult, op1=ALU.add), reads=[("rS32", d), ks], writes=[("rS32", d)])
                                nsb, nsbk = Sb[d].get()
                                P.op("act", lambda e, nsb=nsb, d=d, S32=S32: e.copy(nsb[:], S32[d][:]), reads=[("rS32", d)], writes=[nsbk])
                                cur[d] = (nsb, nsbk)
                                if c in visited:
                                    P.op("dve", lambda e, c=c, pso=pso, Oacc=Oacc: e.tensor_tensor(out=Oacc[:, c, :], in0=pso, in1=Oacc[:, c, :], op=ALU.add), reads=[ko, ("rO", c)], writes=[("rO", c)])
                                else:
                                    visited.add(c)
                                    P.op("act", lambda e, c=c, pso=pso, Oacc=Oacc: e.copy(Oacc[:, c, :], pso), reads=[ko], writes=[("rO", c)])
                        head_out_norm(sh, l, h, Oacc, zs, None, 2, tiles, "r")
                    P.barrier()
        def phase_gates(l, tiles):
            with contextlib.ExitStack() as st:
                wrot = bld.rot(st, "gtw", [128, KD, 512], BF16, 2)
                gb = bld.rot(st, "gtb", [128, 512], BF16, 4)
                for cb in range(6):
                    wt, wk = load_w(wrot, w_in[l][:, GATE + cb * 512:GATE + (cb + 1) * 512])
                    for t in tiles:
                        ps, pk = PSB.get()
                        bld.mm(ps[:], [(HT[:, k, t * 128:(t + 1) * 128], wt[:, k, :]) for k in range(KD)], [wk, ("HT", t)], pk)
                        g, gk = gb.get()
                        P.op("act", lambda e, g=g, ps=ps: e.activation(g[:], ps[:], AF.Sigmoid), reads=[pk], writes=[gk])
                        P.dma("sp", gates_s[t * 128:(t + 1) * 128, cb * 512:(cb + 1) * 512], g[:], reads=[gk], writes=[("gates", t, cb)])
                P.barrier()

        def phase_merge(l, tiles, xsrc):
            stw = contextlib.ExitStack()
            wo = bld.sb(stw, "wo", [128, KD, D], BF16)
            P.dma("pool", wo[:], w_out[l].rearrange("(k p) c -> p k c", p=128), writes=["wo"])
            with contextlib.ExitStack() as st:
                wbr = [bld.sb(st, "wbr%d" % b, [128, 4, D], BF16) for b in range(3)]
                for b in range(3):
                    P.dma("pool", wbr[b][:], w_br[b][l].rearrange("(k p) c -> p k c", p=128), writes=[("wbr", b)])
                gt = bld.rot(st, "mgt", [128, 3 * D], BF16, 2)
                ot = bld.rot(st, "mot", [128, 3, 4, 128], BF16, 2)
                t32 = bld.rot(st, "mt32", [128, 512], F32, 4)
                mb = bld.rot(st, "mmb", [128, D], BF16, 2)
                for t in tiles:
                    g, gk = gt.get()
                    P.dma("sp", g[:], gates_s[t * 128:(t + 1) * 128, :], reads=[("gates", t, cb) for cb in range(6)], writes=[gk])
                    o_, ok_ = ot.get()
                    for b in range(3):
                        P.dma("sp", o_[:, b, :, :], oT_s[b][:, :, t * 128:(t + 1) * 128].rearrange("h p c -> p h c"), reads=[("oTs", b, h) for h in range(4)], writes=[(ok_, b)])
                    m, mk = mb.get()
                    for half in range(2):
                        acc, acck = t32.get()
                        for b in range(3):
                            ps, pk = PSB.get()
                            bld.mm(ps[:], [(o_[:, b, k, :], wbr[b][:, k, half * 512:(half + 1) * 512]) for k in range(4)], [(ok_, b), ("wbr", b)], pk)
                            gsl = g[:, b * D + half * 512: b * D + (half + 1) * 512]
                            if b == 0:
                                P.op("dve", lambda e, acc=acc, ps=ps, gsl=gsl: e.tensor_tensor(out=acc[:], in0=ps[:], in1=gsl, op=ALU.mult), reads=[pk, gk], writes=[acck])
                            else:
                                tmp, tk = t32.get()
                                P.op("dve", lambda e, tmp=tmp, ps=ps, gsl=gsl: e.tensor_tensor(out=tmp[:], in0=ps[:], in1=gsl, op=ALU.mult), reads=[pk, gk], writes=[tk])
                                if b == 1:
                                    P.op("pool", lambda e, acc=acc, tmp=tmp: e.tensor_tensor(out=acc[:], in0=acc[:], in1=tmp[:], op=ALU.add), reads=[acck, tk], writes=[acck])
                                else:
                                    P.op("pool", lambda e, acc=acc, tmp=tmp, m=m, half=half: e.tensor_tensor(out=m[:, half * 512:(half + 1) * 512], in0=acc[:], in1=tmp[:], op=ALU.add), reads=[acck, tk], writes=[(mk, half)])
                    ps, pk = PSB.get()
                    psv = ps[:].bitcast(BF16)
                    for k in range(KD):
                        bld.tr(psv[:, k * 128:(k + 1) * 128], m[:, k * 128:(k + 1) * 128], identb[:], [(mk, 0), (mk, 1), "identb"], pk)
                    P.op("act", lambda e, t=t, psv=psv: e.copy(HT[:, :, t * 128:(t + 1) * 128], psv[:, 0:1024].rearrange("p (k c) -> p k c", c=128)), reads=[pk], writes=[("HT", t)])
                P.barrier()
            with contextlib.ExitStack() as st:
                residual_phase(st, l, tiles, xsrc, 0, lambda t, half: [(HT[:, k, t * 128:(t + 1) * 128], wo[:, k, half * 512:(half + 1) * 512]) for k in range(KD)],
                               lambda t: [("HT", t), "wo"])
                P.barrier()
            stw.close()

        def residual_phase(st, l, tiles, xsrc, which, pairs_of, reads_of):
            xb = bld.rot(st, "rx", [128, D], F32, 3)
            yb = bld.rot(st, "ry", [128, D], F32, 2)
            for t in tiles:
                s = tile_stream(t)
                xt, xk = xb.get()
                P.dma("sp", xt[:], xsrc[t * 128:(t + 1) * 128, :], reads=[("xs", t)], writes=[xk])
                y, yk = yb.get()
                for half in range(2):
                    ps, pk = PSB.get()
                    bld.mm(ps[:], pairs_of(t, half), reads_of(t), pk)
                    sl = slice(half * 512, (half + 1) * 512)
                    P.op("dve", lambda e, y=y, ps=ps, sl=sl, s=s: e.tensor_tensor(out=y[:, sl], in0=ps[:], in1=grow[:, s, which, sl], op=ALU.mult),
                         reads=[pk] + [("grow", s, 2 if which == 0 else 5, h_) for h_ in range(2)], writes=[(yk, half)])
                    P.op("pool", lambda e, y=y, xt=xt, sl=sl: e.tensor_tensor(out=y[:, sl], in0=y[:, sl], in1=xt[:, sl], op=ALU.add), reads=[(yk, half), xk], writes=[(yk, half)])
                P.dma("pool", xs[t * 128:(t + 1) * 128, :], y[:], reads=[(yk, 0), (yk, 1)], writes=[("xs", t)])

        def phase_ffn(l, tiles):
            t_lo = tiles[0] * 128
            stw = contextlib.ExitStack()
            wd = bld.sb(stw, "wd", [128, NJ, D], BF16)
            with contextlib.ExitStack() as st:
                wrot = bld.rot(st, "fw", [128, KD, 512], BF16, 3)
                wcur = {}
                W = T + 3
                off = lambda t0: t0 + 1 if t0 < 256 else t0 + 2
                raw = bld.rot(st, "fraw", [128, W], F32, 3)
                cv = bld.rot(st, "fcv", [128, W], F32, 2)
                aT = bld.rot(st, "faT", [128, T], BF16, 2)
                blks = BLKS if t_lo == 0 else [(256 + i * 512, 512) for i in range(4)]
                for rw_ in raw.tiles:
                    P.op("pool", lambda e, rw_=rw_: e.memset(rw_[:], 0.0), writes=[("fraw", raw.tiles.index(rw_))])
                for j in range(NJ):
                    cvs = []
                    for gi, c0 in enumerate((j * 128, DFF + j * 128)):
                        ch = c0 // 128
                        if j % 4 == 0:
                            ng = min(4, NJ - j)
                            wtf, wk = wrot.get()
                            P.dma("pool", wtf[:, :, 0:ng * 128], w_up[l][:, c0:c0 + ng * 128].rearrange("(k p) c -> p k c", p=128), writes=[wk])
                            wcur[gi] = (wtf, wk)
                        wt, wk = wcur[gi]
                        rw, rk = raw.get()
                        def ev(bi, t0, n, ps, pk, rw=rw, rk=rk):
                            if t0 == 0:
                                P.op("act", lambda e: e.copy(rw[:, 1:257], ps[:, 0:256]), reads=[pk], writes=[rk])
                                P.op("dve", lambda e: e.tensor_copy(rw[:, 258:514], ps[:, 256:512]), reads=[pk], writes=[rk])
                            elif bi % 2:
                                P.op("act", lambda e: e.copy(rw[:, off(t0):off(t0) + n], ps[:, 0:n]), reads=[pk], writes=[rk])
                            else:
                                P.op("dve", lambda e: e.tensor_copy(rw[:, off(t0):off(t0) + n], ps[:, 0:n]), reads=[pk], writes=[rk])
                        proj_fm(wt, wk, hT_rhs, KD, ev, blks=blks, coff=(j % 4) * 128)
                        c, ck = cv.get()
                        lo = off(t_lo)
                        P.op("act", lambda e, c=c, rw=rw, ch=ch: e.activation(c[:, lo:W - 1], rw[:, lo:W - 1], AF.Identity, bias=fconvb(l, ch), scale=fconv(l, ch, 1)), reads=[rk, "spp"], writes=[ck])
                        P.op("dve", lambda e, c=c, rw=rw, ch=ch: e.scalar_tensor_tensor(out=c[:, lo:W - 1], in0=rw[:, lo - 1:W - 2], scalar=fconv(l, ch, 0), in1=c[:, lo:W - 1], op0=ALU.mult, op1=ALU.add), reads=[rk, ck, "spp"], writes=[ck])
                        P.op("dve", lambda e, c=c, rw=rw, ch=ch: e.scalar_tensor_tensor(out=c[:, lo:W - 1], in0=rw[:, lo + 1:W], scalar=fconv(l, ch, 2), in1=c[:, lo:W - 1], op0=ALU.mult, op1=ALU.add), reads=[rk, ck, "spp"], writes=[ck])
                        cvs.append((c, ck))
                    (cg, cgk), (cval, cvk) = cvs
                    P.op("act", lambda e, cg=cg: e.activation(cg[:, lo:W - 1], cg[:, lo:W - 1], AF.Silu), reads=[cgk], writes=[cgk])
                    a, ak = aT.get()
                    if t_lo == 0:
                        P.op("dve", lambda e, a=a, cg=cg, cval=cval: e.tensor_tensor(out=a[:, 0:256], in0=cg[:, 1:257], in1=cval[:, 1:257], op=ALU.mult), reads=[cgk, cvk], writes=[(ak, 0)])
                    P.op("dve", lambda e, a=a, cg=cg, cval=cval: e.tensor_tensor(out=a[:, 256:T], in0=cg[:, 258:W - 1], in1=cval[:, 258:W - 1], op=ALU.mult), reads=[cgk, cvk], writes=[(ak, 1)])
                    P.dma("sp", aT_s[tiles[0]:NT, :, j, :].rearrange("t p c -> p t c"), a[:, t_lo:T].rearrange("p (t c) -> p t c", c=128), reads=[(ak, 0), (ak, 1)], writes=[("aTs", j)])
                    P.dma("pool", wd[:, j, :], w_down[l][j * 128:(j + 1) * 128, :], writes=[("wd", j)])
                P.barrier()
            with contextlib.ExitStack() as st:
                ab_ = bld.rot(st, "fab", [128, NJ, 128], BF16, 2)
                cur = {}
                def pairs_of(t, half):
                    if half == 0:
                        a, ak = ab_.get()
                        P.dma("sp", a[:], aT_s[t], reads=[("aTs", j) for j in range(NJ)], writes=[ak])
                        cur[t] = (a, ak)
                    a, ak = cur[t]
                    return [(a[:, j, :], wd[:, j, half * 512:(half + 1) * 512]) for j in range(NJ)]
                residual_phase(st, l, tiles, xs, 1, pairs_of, lambda t: [cur[t][1]] + [("wd", j) for j in range(NJ)])
                P.barrier()
            stw.close()

        def phase_final():
            with contextlib.ExitStack() as st:
                xb = bld.rot(st, "fx", [128, D], F32, 3)
                junk = bld.sb(st, "fjunk", [128, D], BF16)
                ssb = bld.rot(st, "fss", [128, 1], F32, 4)
                ob = bld.rot(st, "fo", [128, D], F32, 3)
                for t in range(2, NT):
                    xt, xk = xb.get()
                    P.dma("sp", xt[:], xs[t * 128:(t + 1) * 128, :], reads=[("xs", t)], writes=[xk])
                    ss, sk = ssb.get()
                    P.op("act", lambda e, xt=xt, ss=ss: e.activation(junk[:], xt[:], AF.Square, accum_out=ss[:]), reads=[xk], writes=["fjunk", sk])
                    P.op("act", lambda e, ss=ss: e.activation(ss[:], ss[:], AF.Sqrt, bias=epsc(0)), reads=[sk, "c32"], writes=[sk])
                    P.op("dve", lambda e, ss=ss: e.reciprocal(ss[:], ss[:]), reads=[sk], writes=[sk])
                    P.op("dve", lambda e, ss=ss: e.tensor_scalar(out=ss[:], in0=ss[:], scalar1=float(math.sqrt(D)), scalar2=None, op0=ALU.mult), reads=[sk], writes=[sk])
                    o_, ok_ = ob.get()
                    P.op("dve", lambda e, o_=o_, xt=xt, ss=ss: e.scalar_tensor_tensor(out=o_[:], in0=xt[:], scalar=ss[:], in1=fnw, op0=ALU.mult, op1=ALU.mult), reads=[xk, sk, "rws"], writes=[ok_])
                    P.dma("pool", out[(t - 2) * 128:(t - 1) * 128, :], o_[:], reads=[ok_], writes=[("out", t)])
        P.barrier()
        for l in range(2):
            ctx_out = (l == 0)
            tiles = ALLT if ctx_out else list(range(2, NT))
            xsrc = xin if l == 0 else xs
            phase_mod(l)
            if stop_after == "mod":
                dump("modpp", modpp[:], ["modpp"])
                break
            phase_norm(l, 0, xsrc, ALLT)
            if l == 0:
                dump("HT", HT[:], HTk(ALLT))
                dump("modpp", modpp[:], ["modpp"])
                dump("grow", grow[:], [("grow", s_, v_, h_) for s_ in range(2) for v_ in (2, 5) for h_ in range(2)])
            if stop_after == "norm":
                break
            if stop_after in (None, "all", "gdn", "merge", "ffn"):
                phase_gdn(l, tiles)
                if l == 0:
                    dump("oTa", oT_s[0], [("oTs", 0, h_) for h_ in range(4)])
                if stop_after == "gdn":
                    break
            if stop_after in (None, "all", "mla", "merge", "ffn"):
                phase_mla(l, ctx_out)
                if l == 0:
                    dump("oTb", oT_s[1], [("oTs", 1, h_) for h_ in range(4)])
                if stop_after == "mla":
                    break
            if stop_after in (None, "all", "ret", "merge", "ffn"):
                phase_ret(l, tiles)
                if l == 0:
                    dump("oTc", oT_s[2], [("oTs", 2, h_) for h_ in range(4)])
                if stop_after == "ret":
                    break
            phase_gates(l, tiles)
            phase_merge(l, tiles, xsrc)
            if l == 0:
                dump("xmid", xs, [("xs", t_) for t_ in ALLT])
            if stop_after == "merge":
                break
            phase_norm(l, 1, xs, tiles)
            phase_ffn(l, tiles)
            if l == 0:
                dump("xl0", xs, [("xs", t_) for t_ in ALLT])
            if stop_after == "ffn":
                break
        if stop_after in (None, "all"):
            phase_final()
        P.emit(final_reads=[("out", t) for t in range(2, NT)] + bld.dbg_keys)
    return nc, P.stats


def _rope_tables(d):
    n = 2048
    rows_ = n // 64
    r = np.repeat(np.arange(rows_, dtype=np.float32), 64)
    col = np.tile(np.arange(64, dtype=np.float32), rows_)
    quarter = d // 4
    inv = (np.float32(10000.0) ** (-np.arange(quarter, dtype=np.float32) / np.float32(quarter))).astype(np.float32)
    ang = np.concatenate([r[:, None] * inv, col[:, None] * inv], axis=-1).astype(np.float32)
    cos = np.cos(ang).astype(np.float32)
    sin = np.sin(ang).astype(np.float32)
    C = np.ones((d, T), np.float32)
    S = np.zeros((d, T), np.float32)
    C[:, 256:] = np.concatenate([cos, cos], axis=1).T
    S[:, 256:] = np.concatenate([sin, sin], axis=1).T
    return np.stack([C, S])


def _rot_mat(d):
    R = np.zeros((d, d), np.float32)
    h = d // 2
    for m in range(h):
        R[m + h, m] = -1.0
    for m in range(h, d):
        R[m - h, m] = 1.0
    return R


def _host_consts():
    c = np.zeros((128, 128 * 6 + 8), np.float32)
    idx = np.arange(128)
    k = idx[:, None]
    i = idx[None, :]
    c[:, 0:128] = np.eye(128)
    c[:, 128:256] = 1.0
    c[:, 256:384] = (k <= i)
    c[:, 384:512] = (k >= i)
    c[:, 512:640] = np.where(i > k, 0.0, NEGBIG)
    c[:, 640:768] = np.where(i < k, 0.0, NEGBIG)
    c[0, 768] = 1.0
    c[:, 769] = 1024 * EPS; c[:, 770] = 128 * EPS; c[:, 771] = EPS; c[:, 772] = 384 * EPS; c[:, 773] = 256 * EPS
    rm = np.zeros((128, 192), np.float32)
    rm[:, 0:128] = _rot_mat(128)
    rm[0:64, 128:192] = _rot_mat(64)
    hh = np.arange(4, dtype=np.float64)
    lg = np.stack([np.log1p(-(2.0 ** (-(5.0 + hh + 0.5 * d)))) for d in range(2)])
    rc = np.zeros((128, 8 * 128 * 2 + 8), np.float64)
    jj = idx[:, None].astype(np.float64)
    ii = idx[None, :].astype(np.float64)
    cdec = []
    for d in range(2):
        for h in range(4):
            g = lg[d, h]
            q = d * 4 + h
            if d == 0:
                DT = np.where(ii >= jj, np.exp(g * np.maximum(ii - jj, 0)), 0.0)
                GW = np.exp(g * (ii + 1)) * np.ones((128, 1))
                kw = np.exp(g * (127 - idx))
            else:
                DT = np.where(ii <= jj, np.exp(g * np.maximum(jj - ii, 0)), 0.0)
                GW = np.exp(g * (128 - ii)) * np.ones((128, 1))
                kw = np.exp(g * idx)
            rc[:, q * 128:(q + 1) * 128] = DT
            rc[:, 1024 + q * 128: 1024 + (q + 1) * 128] = GW
            rc[:, 2048 + q] = kw
            cdec.append(float(np.exp(g * 128)))
    return c, rm, rc.astype(np.float32), cdec


RET_CDEC = _host_consts()[3]
_NC_CACHE = {}


def _prep_common(inp):
    f = lambda a: np.ascontiguousarray(np.asarray(a, dtype=np.float32))
    pp = lambda v, nch: v.reshape(nch, 128).T
    sm = []
    n1, n2 = f(inp["norm1_w"]), f(inp["norm2_w"])
    sm.append(np.concatenate([pp(n1[l], 8) for l in range(2)], axis=1))
    sm.append(np.concatenate([pp(n2[l], 8) for l in range(2)], axis=1))
    gc = f(inp["gdn_conv_w"])
    sm.append(np.concatenate([gc[l].reshape(3, 12, 128).transpose(2, 1, 0).reshape(128, 36) for l in range(2)], axis=1))
    fc = f(inp["ffn_conv_w"])
    sm.append(np.concatenate([fc[l].reshape(3, 44, 128).transpose(2, 1, 0).reshape(128, 132) for l in range(2)], axis=1))
    fb = f(inp["ffn_conv_b"])
    sm.append(np.concatenate([pp(fb[l], 44) for l in range(2)], axis=1))
    qn_, kvn_ = f(inp["mla_q_norm"]), f(inp["mla_kv_norm"])
    sm.append(np.concatenate([pp(qn_[l], 3) for l in range(2)], axis=1))
    sm.append(np.concatenate([pp(kvn_[l], 2) for l in range(2)], axis=1))
    smallpp = np.ascontiguousarray(np.concatenate(sm, axis=1))
    rep = lambda v: np.broadcast_to(v[None, :], (128, v.shape[0]))
    rw = [rep(f(inp["gdn_norm_w"]).reshape(-1)), rep(f(inp["ret_norm_w"]).reshape(-1)),
          rep(f(inp["gdn_A_log"]).reshape(-1)), rep(f(inp["gdn_dt_bias"]).reshape(-1)), rep(f(inp["final_norm_w"]))]
    rows = np.ascontiguousarray(np.concatenate(rw, axis=1))
    c, rm, rc, _ = _host_consts()
    common = dict(smallpp=smallpp, rows=rows, consts=c, rmats=rm, retc=rc,
                  ropem=_rope_tables(64), roper=_rope_tables(128))
    for k in ("ada_w", "ada_b", "w_in", "mla_w_uq", "mla_w_ukv", "w_br_gdn", "w_br_mla", "w_br_ret", "w_out",
              "ffn_w_up", "ffn_w_down"):
        common[k] = f(inp[k])
    return common


def _prep_core(inp, b):
    f = lambda a: np.ascontiguousarray(np.asarray(a, dtype=np.float32))
    xin = np.concatenate([f(inp["ctx"][b]), f(inp["x"][b])], axis=0)
    def crep_of(v):
        return np.broadcast_to(v.reshape(8, 128).T[:, :, None], (128, 8, 128)).reshape(128, 1024)
    crep = np.stack([crep_of(f(inp["c_ctx"])), crep_of(f(inp["c"][b]))])
    return dict(xin=np.ascontiguousarray(xin), crep=np.ascontiguousarray(crep))


def kernel(**inputs):
    if "nc" not in _NC_CACHE:
        _NC_CACHE["nc"] = build()[0]
    nc = _NC_CACHE["nc"]
    common = _prep_common(inputs)
    in_maps = []
    for b in range(NCORES):
        m = dict(common)
        m.update(_prep_core(inputs, b))
        in_maps.append(m)
    res = run_bass_kernel_spmd(nc, in_maps, core_ids=list(range(NCORES)))
    return np.stack([np.asarray(r["out"], dtype=np.float32) for r in res.results], axis=0)
```

```python
import contextlib
import math
import os
import numpy as np
import concourse.bass as bass
import concourse.mybir as mybir
from concourse.bass_utils import run_bass_kernel_spmd

F32 = mybir.dt.float32
F32R = mybir.dt.float32r
BF16 = mybir.dt.bfloat16
AF = mybir.ActivationFunctionType
ALU = mybir.AluOpType

NCORES = 8
T = 2304
NT = 18
D = 1024
KD = 8
DFF = 2816
NJ = 22
EPS = 1e-6
GQ, GK, GV, GZ, GAB, CQ, CKV, KR, RQ, RK, RV, RG, GATE = 0, 512, 1024, 1536, 2048, 2064, 2448, 2704, 2768, 3280, 3792, 4304, 4816
MLA_SCALE = 192 ** -0.5
NEGBIG = -30000.0
GDN_WARM = 0


class Prog:
    def __init__(self, nc, n_dma_sems=8):
        self.nc = nc
        self.ops = []
        self.last_w = {}
        self.readers = {}
        self.n_dma_sems = n_dma_sems
        self.warm = 0
        self.dummy = None

    def op(self, eng, fn, reads=(), writes=(), dma=False, barrier=False):
        idx = len(self.ops)
        deps = {}
        reads = list(reads)
        writes = list(writes)
        if barrier:
            writes.append("__phase")
        else:
            reads.append("__phase")
        for k in reads:
            w = self.last_w.get(k)
            if w is not None:
                deps[w] = "raw"
        for k in writes:
            w = self.last_w.get(k)
            if w is not None and w not in deps:
                deps[w] = "waw"
            for r in self.readers.get(k, ()):
                if r not in deps:
                    deps[r] = "war"
        for k in reads:
            self.readers.setdefault(k, []).append(idx)
        for k in writes:
            self.last_w[k] = idx
            self.readers[k] = []
        self.ops.append(dict(eng=eng, fn=fn, deps=deps, dma=dma, barrier=barrier, warm=(self.warm if eng == "pe" else 0)))
        return idx

    def barrier(self):
        self.op("dve", lambda e: e.nop(), barrier=True)

    def dma(self, q, out, in_, reads=(), writes=()):
        return self.op(q, lambda e: e.dma_start(out=out, in_=in_), reads, writes, dma=True)

    def emit(self, final_reads=()):
        nc = self.nc
        ops = self.ops
        self.op("sp", lambda e: e.nop(), reads=final_reads)
        n = len(ops)
        pos = [0] * n
        cnt = {}
        for i, o in enumerate(ops):
            c = cnt.get(o["eng"], 0)
            pos[i] = c
            cnt[o["eng"]] = c + 1
        waited_pos = {}
        waited_dma = {}
        need = [[] for _ in range(n)]
        signaling = [False] * n
        for i, o in enumerate(ops):
            E = o["eng"]
            for d in sorted(o["deps"]):
                kind = o["deps"][d]
                od = ops[d]
                F = od["eng"]
                if od["dma"]:
                    s = waited_dma.setdefault(E, set())
                    if d in s:
                        continue
                    s.add(d)
                    need[i].append(d)
                    signaling[d] = True
                else:
                    if F == E and not o["dma"] and not o["barrier"]:
                        if E == "pe":
                            continue
                    if pos[d] <= waited_pos.get((E, F), -1):
                        continue
                    waited_pos[(E, F)] = pos[d]
                    need[i].append(d)
                    signaling[d] = True
        engs = sorted(cnt.keys())
        self.stats = dict(cnt)
        with contextlib.ExitStack() as st:
            esem = {E: st.enter_context(nc.semaphore("s_" + E)) for E in engs}
            dsem = {}
            for E in engs:
                if any(o["dma"] and o["eng"] == E for o in ops):
                    dsem[E] = [st.enter_context(nc.semaphore("d_%s_%d" % (E, j))) for j in range(self.n_dma_sems)]
            ev = [None] * n
            ecount = {E: 0 for E in engs}
            dcount = {E: [0] * self.n_dma_sems for E in dsem}
            dnext = {E: 0 for E in dsem}
            for i, o in enumerate(ops):
                E = o["eng"]
                if o["dma"]:
                    j = dnext[E]
                    dnext[E] = (j + 1) % self.n_dma_sems
                    dcount[E][j] += 1
                    ev[i] = (dsem[E][j], 16 * dcount[E][j])
                    o["dslot"] = j
                    o["dprev"] = 16 * (dcount[E][j] - 1)
                elif signaling[i]:
                    ecount[E] += 1
                    ev[i] = (esem[E], ecount[E])
            for E in engs:
                assert ecount[E] < 60000, (E, ecount[E])
            blk = st.enter_context(nc.Block())
            handles = dict(pe=blk.tensor, act=blk.scalar, dve=blk.vector, pool=blk.gpsimd, sp=blk.sync)
            nw = [0]
            for E in engs:
                my = [i for i in range(n) if ops[i]["eng"] == E]

                def body(e, my=my, E=E):
                    dwaited = [0] * self.n_dma_sems
                    for i in my:
                        o = ops[i]
                        if o["warm"] and need[i]:
                            for _ in range(o["warm"]):
                                self.dummy(e)
                        for d in need[i]:
                            s, v = ev[d]
                            e.wait_ge(s, v)
                            nw[0] += 1
                        if o["dma"]:
                            j = o["dslot"]
                            if o["dprev"] > dwaited[j]:
                                e.wait_ge(dsem[E][j], o["dprev"])
                                dwaited[j] = o["dprev"]
                                nw[0] += 1
                        ins = o["fn"](e)
                        if ev[i] is not None:
                            s, v = ev[i]
                            ins.then_inc(s, 16 if o["dma"] else 1)
                handles[E](body)
            self.stats["waits"] = nw[0]
            self.stats["signals"] = dict(ecount)


class Rot:
    def __init__(self, name, tiles):
        self.name = name
        self.tiles = tiles
        self.i = 0

    def get(self):
        j = self.i % len(self.tiles)
        self.i += 1
        kf = getattr(self, "keyfn", None)
        return self.tiles[j], (kf(j) if kf else (self.name, j))


class B:
    def __init__(self, nc, dbg):
        self.nc = nc
        self.P = Prog(nc)
        self.dbg = dbg
        self.dbg_keys = []

    def sb(self, st, name, shape, dt):
        self.uid = getattr(self, "uid", 0) + 1
        return st.enter_context(self.nc.sbuf_tensor("%s_u%d" % (name, self.uid), list(shape), dt))

    def rot(self, st, name, shape, dt, n):
        return Rot(name, [self.sb(st, "%s%d" % (name, i), shape, dt) for i in range(n)])

    def mm(self, out, pairs, reads, wkey):
        def fn(e):
            m = len(pairs)
            ins = None
            for i, (l, r) in enumerate(pairs):
                ins = e.matmul(out, lhsT=l, rhs=r, start=(i == 0), stop=(i == m - 1))
            return ins
        self.P.op("pe", fn, reads=reads, writes=[wkey])

    def mm_acc(self, out, l, r, start, stop, reads, wkey):
        self.P.op("pe", lambda e: e.matmul(out, lhsT=l, rhs=r, start=start, stop=stop), reads=reads, writes=[wkey])

    def tr(self, out, in_, ident, reads, wkey):
        self.P.op("pe", lambda e: e.transpose(out, in_, ident), reads=reads, writes=[wkey])


def tile_stream(t):
    return 0 if t < 2 else 1


def build(dbg=None, stop_after=None):
    dbg = dbg or ()
    nc = bass.Bass("TRN2", target_bir_lowering=False)
    dram_in = lambda name, shape: nc.dram_tensor(name, list(shape), F32, kind="ExternalInput").ap()
    xin = dram_in("xin", [T, D])
    crep = dram_in("crep", [2, 128, KD * 128])
    ada_w = dram_in("ada_w", [2, D, 6 * D])
    ada_b = dram_in("ada_b", [2, 6 * D])
    w_in = dram_in("w_in", [2, D, 7888])
    w_uq = dram_in("mla_w_uq", [2, 384, 768])
    w_ukv = dram_in("mla_w_ukv", [2, 256, 1024])
    w_br = [dram_in(n, [2, 512, D]) for n in ("w_br_gdn", "w_br_mla", "w_br_ret")]
    w_out = dram_in("w_out", [2, D, D])
    w_up = dram_in("ffn_w_up", [2, D, 2 * DFF])
    w_down = dram_in("ffn_w_down", [2, DFF, D])
    NSM = 2 * 8 * 2 + 2 * 12 * 3 + 2 * 44 * 3 + 2 * 44 + 2 * 3 + 2 * 2
    smallpp = dram_in("smallpp", [128, NSM])
    NROW = 2 * 128 + 2 * 512 + 2 * 8 + 2 * 8 + 1024
    rows = dram_in("rows", [128, NROW])
    NC32 = 128 * 6 + 8
    consts = dram_in("consts", [128, NC32])
    ropem = dram_in("ropem", [2, 64, T])
    roper = dram_in("roper", [2, 128, T])
    rmats = dram_in("rmats", [128, 192])
    retc = dram_in("retc", [128, 8 * 128 * 2 + 8])
    out = nc.dram_tensor("out", [2048, D], F32, kind="ExternalOutput").ap()
    xs = nc.dram_tensor("xs", [T, D], F32).ap()
    oT_s = [nc.dram_tensor("oT%d" % b, [4, 128, T], BF16).ap() for b in range(3)]
    gates_s = nc.dram_tensor("gates_s", [T, 3 * D], BF16).ap()
    aT_s = nc.dram_tensor("aT_s", [NT, 128, NJ, 128], BF16).ap()
    dbg_out = {}
    for name, shape, dt in dbg:
        dbg_out[name] = nc.dram_tensor("dbg_" + name, list(shape), dt, kind="ExternalOutput").ap()

    bld = B(nc, dbg_out)
    P = bld.P
    with contextlib.ExitStack() as top:
        sb = lambda name, shape, dt, st=top: bld.sb(st, name, shape, dt)
        HT = sb("HT", [128, KD, T], BF16)
        c32 = sb("c32", [128, NC32], F32)
        ident32 = c32[:, 0:128]
        ones32 = c32[:, 128:256]
        Umask = [c32[:, 256:384], c32[:, 384:512]]
        NEGS = [c32[:, 512:640], c32[:, 640:768]]
        e0 = c32[:, 768:769]
        epsc = lambda i: c32[:, 769 + i:770 + i]
        identb = sb("identb", [128, 128], BF16)
        onesb = sb("onesb", [128, 128], BF16)
        onesr = sb("onesr", [128, 128], F32R)
        rm32 = sb("rm32", [128, 192], F32)
        rmb = sb("rmb", [128, 192], BF16)
        spp = sb("spp", [128, NSM], F32)
        rws = sb("rws", [128, NROW], F32)
        crs = sb("crs", [128, 2, KD * 128], F32)
        grow = sb("grow", [128, 2, 2, D], F32)
        modpp = sb("modpp", [128, 64], F32)
        AB = sb("ABpp", [128, 2, 2, 2, KD], F32)
        banks = [top.enter_context(nc.psum_tensor("bank%d" % i, [128, 512], F32)) for i in range(8)]
        PSB = Rot("psb", banks[0:3])
        jw = sb("jw", [128, 128], BF16)
        jr = sb("jr", [128, 512], BF16)
        P.op("pool", lambda e: e.memset(jw[:], 0.25), writes=["jw"])
        P.op("pool", lambda e: e.memset(jr[:], 0.5), writes=["jr"])
        P.dummy = lambda e: e.matmul(banks[3][:, 0:512], lhsT=jw[:], rhs=jr[:], start=True, stop=True)
        PSS = Rot("pss", [banks[4 + i % 3][:, ((i // 3) % 4) * 128:((i // 3) % 4 + 1) * 128] for i in range(12)])
        PSS.keyfn = lambda j: ("pssbank", j % 3)
        PSH = Rot("psh", [banks[4 + i % 3][:, ((i // 3) % 2) * 256:((i // 3) % 2 + 1) * 256] for i in range(6)])
        PSH.keyfn = lambda j: ("pssbank", j % 3)
        PSX = banks[7]

        o = 0
        def take(n):
            nonlocal o
            v = (o, o + n)
            o += n
            return v
        r_n1 = take(16); r_n2 = take(16); r_gc = take(72); r_fc = take(264); r_fb = take(88); r_qn = take(6); r_kvn = take(4)
        n1w = lambda l: spp[:, r_n1[0] + l * 8: r_n1[0] + l * 8 + 8]
        n2w = lambda l: spp[:, r_n2[0] + l * 8: r_n2[0] + l * 8 + 8]
        gconv = lambda l, ch, k: spp[:, r_gc[0] + (l * 12 + ch) * 3 + k: r_gc[0] + (l * 12 + ch) * 3 + k + 1]
        fconv = lambda l, ch, k: spp[:, r_fc[0] + (l * 44 + ch) * 3 + k: r_fc[0] + (l * 44 + ch) * 3 + k + 1]
        fconvb = lambda l, ch: spp[:, r_fb[0] + l * 44 + ch: r_fb[0] + l * 44 + ch + 1]
        qn = lambda l, k: spp[:, r_qn[0] + l * 3 + k: r_qn[0] + l * 3 + k + 1]
        kvn = lambda l, k: spp[:, r_kvn[0] + l * 2 + k: r_kvn[0] + l * 2 + k + 1]
        gnw = lambda l: rws[:, l * 128:(l + 1) * 128]
        rnw = lambda l, h: rws[:, 256 + l * 512 + h * 128: 256 + l * 512 + (h + 1) * 128]
        alog = lambda l: rws[:, 1280 + l * 8: 1280 + l * 8 + 8]
        dtb = lambda l: rws[:, 1296 + l * 8: 1296 + l * 8 + 8]
        fnw = rws[:, 1312:1312 + 1024]

        P.dma("sp", c32[:], consts, writes=["c32"])
        P.dma("sp", rm32[:], rmats, writes=["rm32"])
        P.dma("sp", spp[:], smallpp, writes=["spp"])
        P.dma("sp", rws[:], rows, writes=["rws"])
        for s in range(2):
            P.dma("sp", crs[:, s, :], crep[s], writes=[("crs", s)])
        P.op("dve", lambda e: e.tensor_copy(identb[:], ident32), reads=["c32"], writes=["identb"])
        P.op("dve", lambda e: e.tensor_copy(onesb[:], ones32), reads=["c32"], writes=["onesb"])
        P.op("dve", lambda e: e.tensor_copy(onesr[:], ones32), reads=["c32"], writes=["onesr"])
        P.op("dve", lambda e: e.tensor_copy(rmb[:], rm32[:]), reads=["rm32"], writes=["rmb"])
        for s in range(2):
            P.op("act", lambda e, s=s: e.activation(crs[:, s, :], crs[:, s, :], AF.Silu), reads=[("crs", s)], writes=[("crs", s)])
        cst = ["c32", "identb", "onesb", "onesr", "rmb", "spp", "rws"]

        def dump(name, src_ap, reads):
            if name in dbg_out:
                P.dma("sp", dbg_out[name], src_ap, reads=reads, writes=[("dbg", name)])
                bld.dbg_keys.append(("dbg", name))

        def phase_mod(l):
            with contextlib.ExitStack() as st:
                wbuf = bld.rot(st, "adaw", [128, KD, 512], F32, 2)
                bbuf = bld.rot(st, "adab", [1, 512], F32, 2)
                rowt = bld.rot(st, "modrow", [128, 512], F32, 2)
                pp_ps, pp_key = PSX, "psx"
                for nb in range(12):
                    wt, wk = wbuf.get()
                    bt, bk = bbuf.get()
                    P.dma("sp", wt[:], ada_w[l][:, nb * 512:(nb + 1) * 512].rearrange("(k p) c -> p k c", p=128), writes=[wk])
                    P.dma("sp", bt[:], ada_b[l:l + 1, nb * 512:(nb + 1) * 512], writes=[bk])
                    vec = nb // 2
                    half = nb % 2
                    for s in range(2):
                        ps, pk = PSB.get()
                        pairs = [(crs[:, s, k * 128:(k + 1) * 128], wt[:, k, :]) for k in range(KD)]
                        pairs.append((ones32[0:1, :], bt[0:1, :]))
                        bld.mm(ps[:], pairs, [wk, bk, ("crs", s), "c32"], pk)
                        if vec in (2, 5):
                            dst = grow[:, s, 0 if vec == 2 else 1, half * 512:(half + 1) * 512]
                            P.op("act", lambda e, dst=dst, ps=ps: e.copy(dst, ps[:]), reads=[pk], writes=[("grow", s, vec, half)])
                        else:
                            rt, rk = rowt.get()
                            P.op("dve", lambda e, rt=rt, ps=ps: e.tensor_copy(rt[:], ps[:]), reads=[pk], writes=[rk])
                            vi = {0: 0, 1: 1, 3: 2, 4: 3}[vec]
                            for c4 in range(4):
                                col = s * 32 + vi * 8 + half * 4 + c4
                                bld.mm(pp_ps[:, col:col + 1], [(rt[:, c4 * 128:(c4 + 1) * 128], e0)], [rk, "c32"], pp_key)
                P.op("dve", lambda e: e.tensor_copy(modpp[:], pp_ps[:, 0:64]), reads=[pp_key], writes=["modpp"])
                for s in range(2):
                    for nrm in range(2):
                        sh = modpp[:, s * 32 + (2 * nrm) * 8: s * 32 + (2 * nrm) * 8 + 8]
                        sc = modpp[:, s * 32 + (2 * nrm + 1) * 8: s * 32 + (2 * nrm + 1) * 8 + 8]
                        nw = n1w(l) if nrm == 0 else n2w(l)
                        P.op("dve", lambda e, sc=sc, nw=nw, s=s, nrm=nrm: e.scalar_tensor_tensor(out=AB[:, s, nrm, 0, :], in0=sc, scalar=1.0, in1=nw, op0=ALU.add, op1=ALU.mult),
                             reads=["modpp", "spp"], writes=[("AB", s, nrm, 0)])
                        P.op("dve", lambda e, s=s, nrm=nrm: e.tensor_scalar(out=AB[:, s, nrm, 0, :], in0=AB[:, s, nrm, 0, :], scalar1=float(math.sqrt(D)), scalar2=None, op0=ALU.mult),
                             reads=[("AB", s, nrm, 0)], writes=[("AB", s, nrm, 0)])
                        P.op("dve", lambda e, sh=sh, s=s, nrm=nrm: e.tensor_copy(AB[:, s, nrm, 1, :], sh), reads=["modpp"], writes=[("AB", s, nrm, 1)])
                P.barrier()

        def phase_norm(l, nrm, xsrc, tiles):
            with contextlib.ExitStack() as st:
                xb = bld.rot(st, "nx", [128, D], F32, 3)
                junk = bld.sb(st, "njunk", [128, D], BF16)
                xn = bld.rot(st, "nxn", [128, D], BF16, 8)
                ssb = bld.rot(st, "nss", [128, 1], F32, 8)
                groups = []
                cur = []
                for t in tiles:
                    cur.append(t)
                    if len(cur) == 4:
                        groups.append(cur); cur = []
                if cur:
                    groups.append(cur)
                for grp in groups:
                    xns = []
                    for t in grp:
                        xt, xk = xb.get()
                        P.dma("sp", xt[:], xsrc[t * 128:(t + 1) * 128, :], reads=[("xs", t)], writes=[xk])
                        ss, sk = ssb.get()
                        P.op("pool", lambda e, ss=ss: e.memset(ss[:], 0.0), writes=[sk])
                        P.op("dve", lambda e, xt=xt, ss=ss: e.scalar_tensor_tensor(out=junk[:], in0=xt[:], scalar=1.0, in1=xt[:], op0=ALU.mult, op1=ALU.mult, accum_out=ss[:]), reads=[xk, sk], writes=["njunk", sk])
                        P.op("act", lambda e, ss=ss: e.activation(ss[:], ss[:], AF.Sqrt, bias=epsc(0)), reads=[sk, "c32"], writes=[sk])
                        P.op("dve", lambda e, ss=ss: e.reciprocal(ss[:], ss[:]), reads=[sk], writes=[sk])
                        xnt, xnk = xn.get()
                        P.op("act", lambda e, xnt=xnt, xt=xt, ss=ss: e.activation(xnt[:], xt[:], AF.Copy, scale=ss[:]), reads=[xk, sk], writes=[xnk])
                        xns.append((t, xnt, xnk))
                    for k in range(KD):
                        ps, pk = PSB.get()
                        psv = ps[:].bitcast(BF16)
                        for i, (t, xnt, xnk) in enumerate(xns):
                            bld.tr(psv[:, i * 128:(i + 1) * 128], xnt[:, k * 128:(k + 1) * 128], identb[:], [xnk, "identb"], pk)
                        i = 0
                        while i < len(xns):
                            s = tile_stream(xns[i][0])
                            j = i
                            while j < len(xns) and tile_stream(xns[j][0]) == s:
                                j += 1
                            t0 = xns[i][0]
                            dst = HT[:, k, t0 * 128:(t0 + (j - i)) * 128]
                            src = psv[:, i * 128:j * 128]
                            wr = [("HT", tt) for tt in range(t0, t0 + (j - i))]
                            a_ap = AB[:, s, nrm, 0, k:k + 1]
                            b_ap = AB[:, s, nrm, 1, k:k + 1]
                            if k % 2 == 0:
                                P.op("act", lambda e, dst=dst, src=src, a_ap=a_ap, b_ap=b_ap: e.activation(dst, src, AF.Identity, bias=b_ap, scale=a_ap),
                                     reads=[pk, ("AB", s, nrm, 0), ("AB", s, nrm, 1)], writes=wr)
                            else:
                                P.op("dve", lambda e, dst=dst, src=src, a_ap=a_ap, b_ap=b_ap: e.tensor_scalar(out=dst, in0=src, scalar1=a_ap, scalar2=b_ap, op0=ALU.mult, op1=ALU.add),
                                     reads=[pk, ("AB", s, nrm, 0), ("AB", s, nrm, 1)], writes=wr)
                            i = j
                P.barrier()

        HTk = lambda ts: [("HT", t) for t in ts]
        ALLT = list(range(NT))

        def load_w(rotw, src2d, reads=()):
            wt, wk = rotw.get()
            P.dma("pool", wt[:], src2d.rearrange("(k p) c -> p k c", p=128), reads=reads, writes=[wk])
            return wt, wk

        BLKS = [(0, 512), (512, 512), (1024, 512), (1536, 512), (2048, 256)]

        def proj_fm(wt, wk, rhs_of, nk, evac, m=128, blks=BLKS, extra_reads=(), coff=0):
            for bi, (t0, n) in enumerate(blks):
                ps, pk = PSB.get()
                pairs = [(wt[:, k, coff:coff + m], rhs_of(k, t0, n)) for k in range(nk)]
                tl = list(range(t0 // 128, (t0 + n) // 128))
                bld.mm(ps[0:m, 0:n], pairs, [wk] + HTk(tl) + list(extra_reads), pk)
                evac(bi, t0, n, ps, pk)

        hT_rhs = lambda k, t0, n: HT[:, k, t0:t0 + n]

        def head_out_norm(st, l, h, Oacc, zs, nrot, b_idx, tiles, tag):
            oTh = bld.sb(st, tag + "oTh", [128, T], BF16)
            ssq = bld.sb(st, tag + "ssq", [128, NT], F32)
            junk = bld.sb(st, tag + "junk", [128, 128], BF16)
            yb = bld.rot(st, tag + "yb", [128, 128], BF16, 4)
            for t in tiles:
                P.op("act", lambda e, t=t: e.activation(junk[:], Oacc[:, t, :], AF.Square, accum_out=ssq[:, t:t + 1]), reads=[(tag + "O", t)], writes=[tag + "junk", (tag + "ssq", t)])
            t0, t1 = tiles[0], tiles[-1] + 1
            P.op("act", lambda e: e.activation(ssq[:, t0:t1], ssq[:, t0:t1], AF.Sqrt, bias=epsc(2), scale=1.0 / 128.0), reads=[(tag + "ssq", t) for t in tiles] + ["c32"], writes=[tag + "ssqall"])
            P.op("dve", lambda e: e.reciprocal(ssq[:, t0:t1], ssq[:, t0:t1]), reads=[tag + "ssqall"], writes=[tag + "ssqall"])
            grp = [tiles[i:i + 4] for i in range(0, len(tiles), 4)]
            for g in grp:
                ps, pk = PSB.get()
                psv = ps[:].bitcast(BF16)
                for i, t in enumerate(g):
                    y, yk = yb.get()
                    P.op("dve", lambda e, y=y, t=t: e.scalar_tensor_tensor(out=y[:], in0=Oacc[:, t, :], scalar=ssq[:, t:t + 1], in1=zs[:, t, :], op0=ALU.mult, op1=ALU.mult),
                         reads=[(tag + "O", t), tag + "ssqall", (tag + "zs", t)], writes=[yk])
                    bld.tr(psv[:, i * 128:(i + 1) * 128], y[:], identb[:], [yk, "identb"], pk)
                n = len(g) * 128
                P.op("act", lambda e, g=g, n=n, psv=psv: e.copy(oTh[:, g[0] * 128:g[0] * 128 + n], psv[:, 0:n]), reads=[pk], writes=[(tag + "oTh", t) for t in g])
            c0 = tiles[0] * 128
            P.dma("sp", oT_s[b_idx][h][:, c0:T], oTh[:, c0:T], reads=[(tag + "oTh", t) for t in tiles], writes=[("oTs", b_idx, h)])

        def phase_gdn(l, tiles):
            with contextlib.ExitStack() as st:
                wrot = bld.rot(st, "gw", [128, KD, 128], BF16, 3)
                wab = bld.sb(st, "gwab", [128, KD, 16], BF16)
                gbeta = bld.sb(st, "gbeta", [128, NT, 16], F32)
                tmp8 = bld.sb(st, "gtmp8", [128, NT, 8], F32)
                P.dma("pool", wab[:], w_in[l][:, GAB:GAB + 16].rearrange("(k p) c -> p k c", p=128), writes=["gwab"])
                ab_ps, ab_key = PSX, "psx"
                for t in ALLT:
                    bld.mm(ab_ps[:, t * 16:(t + 1) * 16], [(HT[:, k, t * 128:(t + 1) * 128], wab[:, k, :]) for k in range(KD)], ["gwab", ("HT", t)], ab_key)
                abv = ab_ps[:, 0:NT * 16].rearrange("p (t c) -> p t c", c=16)
                for t in ALLT:
                    P.op("dve", lambda e, t=t: e.tensor_tensor(out=tmp8[:, t, :], in0=abv[:, t, 0:8], in1=dtb(l), op=ALU.add), reads=[ab_key, "rws"], writes=[("gtmp8", t)])
                al = bld.sb(st, "galog", [128, 8], F32)
                P.op("act", lambda e: e.activation(al[:], alog(l), AF.Exp), reads=["rws"], writes=["galog"])
                t8all = [("gtmp8", t) for t in ALLT]
                P.op("act", lambda e: e.activation(tmp8[:], tmp8[:], AF.Exp), reads=t8all, writes=["gtmp8all"])
                P.op("act", lambda e: e.activation(tmp8[:], tmp8[:], AF.Ln, bias=1.0), reads=["gtmp8all"], writes=["gtmp8all"])
                for t in ALLT:
                    P.op("dve", lambda e, t=t: e.scalar_tensor_tensor(out=gbeta[:, t, 0:8], in0=tmp8[:, t, :], scalar=-1.0, in1=al[:], op0=ALU.mult, op1=ALU.mult),
                         reads=["gtmp8all", "galog"], writes=[("gb_g", t)])
                P.op("act", lambda e: e.activation(gbeta[:, :, 8:16], abv[:, :, 8:16], AF.Sigmoid), reads=[ab_key], writes=["gb_beta"])
                gbk = [("gb_g", t) for t in ALLT] + ["gb_beta"]
                P.barrier()
                for h in range(4):
                    with contextlib.ExitStack() as sh:
                        gdn_head(sh, l, h, tiles, wrot, gbeta)
                    P.barrier()

        def gdn_head(st, l, h, tiles, wrot, gbeta):
            sbh = lambda name, shape, dt: bld.sb(st, name, shape, dt)
            W = T + 3
            off = lambda t0: t0 + 1 if t0 < 256 else t0 + 2
            raw = bld.rot(st, "graw", [128, W], F32, 2)
            cv = bld.rot(st, "gcv", [128, W], F32, 2)
            qT = sbh("gqT", [128, T], BF16)
            kT = sbh("gkT", [128, T], BF16)
            vT = sbh("gvT", [128, T], BF16)
            ktok = sbh("gktok", [128, NT, 128], BF16)
            vtok = sbh("gvtok", [128, NT, 128], BF16)
            zs = sbh("gzs", [128, NT, 128], F32)
            Oacc = sbh("gO", [128, NT, 128], F32)
            sqr = bld.rot(st, "gsqr", [128, 512], F32R, 2)
            rnb = bld.rot(st, "grnb", [128, 512], F32, 2)
            for fi, (c0, dst) in enumerate(((GQ, qT), (GK, kT), (GV, vT))):
                ch = fi * 4 + h
                wt, wk = load_w(wrot, w_in[l][:, c0 + h * 128: c0 + (h + 1) * 128])
                rw, rk = raw.get()
                P.op("pool", lambda e, rw=rw: e.memset(rw[:], 0.0), writes=[rk])
                def ev(bi, t0, n, ps, pk, rw=rw, rk=rk):
                    o_ = off(t0)
                    if t0 == 0:
                        P.op("act", lambda e: e.copy(rw[:, 1:257], ps[:, 0:256]), reads=[pk], writes=[rk])
                        P.op("dve", lambda e: e.tensor_copy(rw[:, 258:514], ps[:, 256:512]), reads=[pk], writes=[rk])
                    else:
                        eng = "act" if bi % 2 else "dve"
                        if eng == "act":
                            P.op("act", lambda e: e.copy(rw[:, o_:o_ + n], ps[:, 0:n]), reads=[pk], writes=[rk])
                        else:
                            P.op("dve", lambda e: e.tensor_copy(rw[:, o_:o_ + n], ps[:, 0:n]), reads=[pk], writes=[rk])
                proj_fm(wt, wk, hT_rhs, KD, ev)
                c, ck = cv.get()
                P.op("act", lambda e, c=c, rw=rw, ch=ch: e.activation(c[:, 1:W - 1], rw[:, 1:W - 1], AF.Copy, scale=gconv(l, ch, 1)), reads=[rk, "spp"], writes=[ck])
                P.op("dve", lambda e, c=c, rw=rw, ch=ch: e.scalar_tensor_tensor(out=c[:, 1:W - 1], in0=rw[:, 0:W - 2], scalar=gconv(l, ch, 0), in1=c[:, 1:W - 1], op0=ALU.mult, op1=ALU.add), reads=[rk, ck, "spp"], writes=[ck])
                P.op("dve", lambda e, c=c, rw=rw, ch=ch: e.scalar_tensor_tensor(out=c[:, 1:W - 1], in0=rw[:, 2:W], scalar=gconv(l, ch, 2), in1=c[:, 1:W - 1], op0=ALU.mult, op1=ALU.add), reads=[rk, ck, "spp"], writes=[ck])
                P.op("act", lambda e, c=c: e.activation(c[:, 1:W - 1], c[:, 1:W - 1], AF.Silu), reads=[ck], writes=[ck])
                if fi == 2:
                    P.op("dve", lambda e, c=c: e.tensor_copy(vT[:, 0:256], c[:, 1:257]), reads=[ck], writes=[("gT", 2, 0)])
                    P.op("dve", lambda e, c=c: e.tensor_copy(vT[:, 256:T], c[:, 258:W - 1]), reads=[ck], writes=[("gT", 2, 1)])
                else:
                    for bi, (t0, n) in enumerate(BLKS):
                        segs = [(0, 256), (256, 256)] if t0 == 0 else [(t0, n)]
                        sq, sqk = sqr.get()
                        for (s0, sn) in segs:
                            P.op("act", lambda e, c=c, s0=s0, sn=sn, sq=sq, t0=t0: e.activation(sq[:, s0 - t0:s0 - t0 + sn], c[:, off(s0):off(s0) + sn], AF.Square), reads=[ck], writes=[sqk])
                        ps, pk = PSB.get()
                        bld.mm(ps[:, 0:n], [(onesr[:], sq[:, 0:n])], [sqk, "onesr"], pk)
                        rn, rnk = rnb.get()
                        P.op("act", lambda e, rn=rn, ps=ps, n=n: e.activation(rn[:, 0:n], ps[:, 0:n], AF.Sqrt, bias=epsc(2)), reads=[pk, "c32"], writes=[rnk])
                        P.op("dve", lambda e, rn=rn, n=n: e.reciprocal(rn[:, 0:n], rn[:, 0:n]), reads=[rnk], writes=[rnk])
                        scl = float(128 ** -0.5) if fi == 0 else 1.0
                        for (s0, sn) in segs:
                            P.op("dve", lambda e, c=c, s0=s0, sn=sn, rn=rn, t0=t0, dst=dst, scl=scl: e.scalar_tensor_tensor(out=dst[:, s0:s0 + sn], in0=c[:, off(s0):off(s0) + sn], scalar=scl, in1=rn[:, s0 - t0:s0 - t0 + sn], op0=ALU.mult, op1=ALU.mult),
                                 reads=[ck, rnk], writes=[("gT", fi, s0)])
            gTk = lambda fi: [("gT", fi, s0) for s0 in (0, 256, 512, 1024, 1536, 2048)] + [("gT", 2, 0), ("gT", 2, 1)]
            for (src, dstt, fi, nm) in ((kT, ktok, 1, "gktok"), (vT, vtok, 2, "gvtok")):
                for g0 in range(0, NT, 4):
                    g = list(range(g0, min(g0 + 4, NT)))
                    ps, pk = PSB.get()
                    psv = ps[:].bitcast(BF16)
                    for i, t in enumerate(g):
                        bld.tr(psv[:, i * 128:(i + 1) * 128], src[:, t * 128:(t + 1) * 128], identb[:], gTk(fi) + ["identb"], pk)
                    n = len(g) * 128
                    P.op("act" if (g0 // 4) % 2 else "dve",
                         (lambda e, g=g, n=n, psv=psv, dstt=dstt: e.copy(dstt[:, g[0]:g[0] + len(g), :], psv[:, 0:n].rearrange("p (t c) -> p t c", c=128))) if (g0 // 4) % 2 else
                         (lambda e, g=g, n=n, psv=psv, dstt=dstt: e.tensor_copy(dstt[:, g[0]:g[0] + len(g), :], psv[:, 0:n].rearrange("p (t c) -> p t c", c=128))),
                         reads=[pk], writes=[(nm, t) for t in g])
            wt, wk = load_w(wrot, w_in[l][:, GZ + h * 128: GZ + (h + 1) * 128])
            for t in tiles:
                ps, pk = PSS.get()
                bld.mm(ps, [(HT[:, k, t * 128:(t + 1) * 128], wt[:, k, :]) for k in range(KD)], [wk, ("HT", t)], pk)
                P.op("act", lambda e, t=t, ps=ps: e.activation(zs[:, t, :], ps, AF.Silu), reads=[pk], writes=[("gzs", t)])
                P.op("pool", lambda e, t=t: e.tensor_tensor(out=zs[:, t, :], in0=zs[:, t, :], in1=gnw(l), op=ALU.mult), reads=[("gzs", t), "rws"], writes=[("gzs", t)])
            f32t = lambda name, n: bld.rot(st, name, [128, 128], F32, n)
            b16t = lambda name, n: bld.rot(st, name, [128, 128], BF16, n)
            gbr = f32t("g_gb", 3); egr = f32t("g_eg", 5); dsr = f32t("g_ds", 3); dir_ = f32t("g_di", 3)
            colr = bld.rot(st, "g_col", [128, 4], F32, 5)
            Pm = bld.rot(st, "g_P", [128, 256], F32, 4); PTm = bld.rot(st, "g_PT", [128, 256], F32, 4); Rm = bld.rot(st, "g_R", [128, 256], F32, 3)
            ident2 = sbh("g_id2", [128, 256], F32)
            P.op("dve", lambda e: e.tensor_copy(ident2[:, 0:128], ident32), reads=["c32"], writes=["g_id2"])
            P.op("dve", lambda e: e.tensor_copy(ident2[:, 128:256], ident32), reads=["c32"], writes=["g_id2"])
            TTb = bld.rot(st, "g_TT", [128, 256], BF16, 3); atb = b16t("g_at", 5); qdb = b16t("g_qd", 5); kdb = b16t("g_kd", 5)
            rb = b16t("g_r", 3); vnb = b16t("g_vn", 3)
            S32 = [sbh("gS32_%d" % d, [128, 128], F32) for d in range(2)]
            Sb = [bld.rot(st, "gSb%d" % d, [128, 128], BF16, 2) for d in range(2)]
            order = [list(range(NT)), [1, 0] + list(range(NT - 1, 1, -1))]
            cur_Sb = [None, None]
            for d in range(2):
                P.op("pool", lambda e, d=d: e.memset(S32[d][:], 0.0), writes=[("gS32", d)])
                sbt, sbk = Sb[d].get()
                P.op("pool", lambda e, sbt=sbt: e.memset(sbt[:], 0.0), writes=[sbk])
                cur_Sb[d] = (sbt, sbk)
            visited = set()
            kT_k, qT_k = gTk(1), gTk(0)

            def precompute(d, c, P2, P2k):
                q = d * 4 + h
                gcol = gbeta[:, c, q:q + 1]
                bcol = gbeta[:, c, 8 + q:9 + q]
                tcol = slice(c * 128, (c + 1) * 128)
                gb, gbk_ = gbr.get()
                P.op("dve", lambda e: e.tensor_scalar(out=gb[:], in0=Umask[d], scalar1=gcol, scalar2=None, op0=ALU.mult), reads=[("gb_g", c), "c32"], writes=[gbk_])
                psA, kA = PSS.get()
                bld.mm(psA, [(ones32, gb[:])], [gbk_, "c32"], kA)
                psB, kB = PSS.get()
                bld.mm(psB, [(ones32, gb[:]), (ident32, NEGS[d])], [gbk_, "c32"], kB)
                psC, kC = PSS.get()
                bld.mm(psC[:, 0:1], [(Umask[d], gcol)], [("gb_g", c), "c32"], kC)
                yield
                col, colk = colr.get()
                P.op("dve", lambda e: e.tensor_scalar(out=col[:, 1:2], in0=psC[:, 0:1], scalar1=-1.0, scalar2=None, op0=ALU.mult), reads=[kC], writes=[colk])
                P.op("act", lambda e: e.activation(col[:, 0:1], psC[:, 0:1], AF.Exp), reads=[kC], writes=[colk])
                P.op("dve", lambda e: e.tensor_scalar(out=col[:, 0:1], in0=col[:, 0:1], scalar1=-1.0, scalar2=None, op0=ALU.mult), reads=[colk], writes=[colk])
                P.op("dve", lambda e: e.tensor_scalar(out=col[:, 2:3], in0=bcol, scalar1=-1.0, scalar2=None, op0=ALU.mult), reads=["gb_beta"], writes=[colk])
                eg, egk = egr.get()
                P.op("act", lambda e: e.activation(eg[:], psA, AF.Exp), reads=[kA], writes=[egk])
                ds, dsk = dsr.get()
                P.op("act", lambda e: e.activation(ds[:], psB, AF.Exp, bias=col[:, 1:2]), reads=[kB, colk], writes=[dsk])
                yield
                di, dik = dir_.get()
                P.op("dve", lambda e: e.tensor_tensor(out=di[:], in0=ds[:], in1=ident32, op=ALU.add), reads=[dsk, "c32"], writes=[dik])
                psK, kK = PSS.get()
                bld.mm(psK, [(kT[:, tcol], kT[:, tcol])], kT_k, kK)
                psQ, kQ = PSS.get()
                bld.mm(psQ, [(kT[:, tcol], qT[:, tcol])], kT_k + qT_k, kQ)
                yield
                p0 = P2[:, d * 128:(d + 1) * 128]
                p0k = (P2k, d)
                P.op("dve", lambda e: e.scalar_tensor_tensor(out=p0, in0=psK, scalar=col[:, 2:3], in1=ds[:], op0=ALU.mult, op1=ALU.mult), reads=[kK, colk, dsk], writes=[p0k])
                at, atk = atb.get()
                P.op("dve", lambda e: e.tensor_tensor(out=at[:], in0=psQ, in1=di[:], op=ALU.mult), reads=[kQ, dik], writes=[atk])
                last = 127 if d == 0 else 0
                kd, kdk = kdb.get()
                P.op("pool", lambda e: e.tensor_scalar(out=kd[:], in0=ktok[:, c, :], scalar1=di[:, last:last + 1], scalar2=None, op0=ALU.mult), reads=[("gktok", c), dik], writes=[kdk])
                qd, qdk = qdb.get()
                P.op("pool", lambda e: e.tensor_tensor(out=qd[:], in0=qT[:, tcol], in1=eg[:], op=ALU.mult), reads=qT_k + [egk], writes=[qdk])
                pr_out[d] = dict(col=col, colk=colk, eg=eg, egk=egk, at=at, atk=atk, kd=kd, kdk=kdk, qd=qd, qdk=qdk, bcol=bcol, last=last)

            def step(d, c, pre):
                tcol = slice(c * 128, (c + 1) * 128)
                sbt, sbk = cur_Sb[d]
                psk, kk = PSS.get()
                bld.mm(psk, [(kT[:, tcol], sbt[:])], kT_k + [sbk], kk)
                yield
                r, rk_ = rb.get()
                P.op("dve", lambda e: e.scalar_tensor_tensor(out=r[:], in0=psk, scalar=pre["col"][:, 0:1], in1=vtok[:, c, :], op0=ALU.mult, op1=ALU.add), reads=[kk, pre["colk"], ("gvtok", c)], writes=[rk_])
                psv, kv = PSS.get()
                bld.mm(psv, [(pre["tt"], r[:])], [pre["ttk"], rk_], kv)
                yield
                vn, vnk = vnb.get()
                P.op("act", lambda e: e.activation(vn[:], psv, AF.Copy, scale=pre["bcol"]), reads=[kv, "gb_beta"], writes=[vnk])
                pso, ko = PSS.get()
                bld.mm(pso, [(pre["qd"][:], sbt[:]), (pre["at"][:], vn[:])], [pre["qdk"], sbk, pre["atk"], vnk], ko)
                if c in visited:
                    P.op("dve", lambda e: e.tensor_tensor(out=Oacc[:, c, :], in0=pso, in1=Oacc[:, c, :], op=ALU.add), reads=[ko, ("gO", c)], writes=[("gO", c)])
                else:
                    visited.add(c)
                    P.op("act", lambda e: e.copy(Oacc[:, c, :], pso), reads=[ko], writes=[("gO", c)])
                pss_, ks = PSS.get()
                bld.mm(pss_, [(pre["kd"][:], vn[:])], [pre["kdk"], vnk], ks)
                yield
                last = pre["last"]
                P.op("dve", lambda e: e.scalar_tensor_tensor(out=S32[d][:], in0=S32[d][:], scalar=pre["eg"][:, last:last + 1], in1=pss_, op0=ALU.mult, op1=ALU.add), reads=[("gS32", d), pre["egk"], ks], writes=[("gS32", d)])
                nsb, nsbk = Sb[d].get()
                P.op("act", lambda e: e.copy(nsb[:], S32[d][:]), reads=[("gS32", d)], writes=[nsbk])
                cur_Sb[d] = (nsb, nsbk)

            def neumann2(P2, P2k):
                pk_all = [(P2k, 0), (P2k, 1)]
                psT, kT_ = PSH.get()
                for x in range(2):
                    bld.tr(psT[:, x * 128:(x + 1) * 128], P2[:, x * 128:(x + 1) * 128], ident32, pk_all + ["c32"], kT_)
                PT2, PT2k = PTm.get()
                P.op("act", lambda e, PT2=PT2: e.copy(PT2[:], psT), reads=[kT_], writes=[PT2k])
                R2, R2k = Rm.get()
                P.op("dve", lambda e, R2=R2: e.tensor_tensor(out=R2[:], in0=P2[:], in1=ident2[:], op=ALU.add), reads=pk_all + ["g_id2"], writes=[R2k])
                pc, pck, ptc, ptck = P2, pk_all, PT2, [PT2k]
                sl = lambda t_, x: t_[:, x * 128:(x + 1) * 128]
                for lev in range(6):
                    if lev < 5:
                        S1, k1 = PSH.get()
                        for x in range(2):
                            bld.mm(sl(S1, x), [(sl(ptc, x), sl(pc, x))], pck + ptck, k1)
                    S2, k2 = PSH.get()
                    for x in range(2):
                        bld.mm(sl(S2, x), [(sl(pc, x), sl(ptc, x))], pck + ptck, k2)
                    if lev >= 1:
                        S3, k3 = PSH.get()
                        for x in range(2):
                            bld.mm(sl(S3, x), [(sl(ptc, x), sl(R2, x))], ptck + [R2k], k3)
                    Pn, Pnk = Pm.get()
                    if lev < 5:
                        P.op("act", lambda e, Pn=Pn, S1=S1: e.copy(Pn[:], S1), reads=[k1], writes=[Pnk])
                    PTn, PTnk = PTm.get()
                    P.op("dve" if lev < 1 else "act", (lambda e, PTn=PTn, S2=S2: e.tensor_copy(PTn[:], S2)) if lev < 1 else (lambda e, PTn=PTn, S2=S2: e.copy(PTn[:], S2)), reads=[k2], writes=[PTnk])
                    if lev >= 1:
                        Rn, Rnk = Rm.get()
                        P.op("dve", lambda e, Rn=Rn, R2=R2, S3=S3: e.tensor_tensor(out=Rn[:], in0=S3, in1=R2[:], op=ALU.add), reads=[k3, R2k], writes=[Rnk])
                        R2, R2k = Rn, Rnk
                    pc, pck, ptc, ptck = Pn, [Pnk], PTn, [PTnk]
                S3, k3 = PSH.get()
                for x in range(2):
                    bld.mm(sl(S3, x), [(sl(ptc, x), sl(R2, x))], ptck + [R2k], k3)
                tt2, tt2k = TTb.get()
                P.op("dve", lambda e, tt2=tt2, R2=R2, S3=S3: e.tensor_tensor(out=tt2[:], in0=S3, in1=R2[:], op=ALU.add), reads=[k3, R2k], writes=[tt2k])
                return tt2, tt2k

            pres = {}
            pr_out = {}
            P.warm = GDN_WARM
            for s_ in range(NT + 1):
                if s_ < NT:
                    P2, P2k = Pm.get()
                    pr_out.clear()
                    gens = [precompute(d, order[d][s_], P2, P2k) for d in range(2)]
                    while gens:
                        for g_ in list(gens):
                            try:
                                next(g_)
                            except StopIteration:
                                gens.remove(g_)
                    pr = [pr_out[0], pr_out[1]]
                    tt2, tt2k = neumann2(P2, P2k)
                    for d in range(2):
                        pr[d]["tt"] = tt2[:, d * 128:(d + 1) * 128]
                        pr[d]["ttk"] = tt2k
                        pres[(d, order[d][s_])] = pr[d]
                if s_ >= 1:
                    gens = [step(d, order[d][s_ - 1], pres.pop((d, order[d][s_ - 1]))) for d in range(2)]
                    while gens:
                        for g_ in list(gens):
                            try:
                                next(g_)
                            except StopIteration:
                                gens.remove(g_)
            P.warm = 0
            head_out_norm(st, l, h, Oacc, zs, None, 0, tiles, "g")
        def rope_fm(src_bf, src_key, dst, dst_key_of, nrows, rmat, cos_t, sin_t, tabk, tmpA, tmpB):
            for bi, (t0, n) in enumerate(BLKS):
                ps, pk = PSB.get()
                bld.mm(ps[0:nrows, 0:n], [(rmat, src_bf[0:nrows, t0:t0 + n])], [src_key, "rmb"], pk)
                a, ak = tmpA.get()
                b_, bk = tmpB.get()
                P.op("dve", lambda e, a=a, ps=ps, n=n, t0=t0: e.tensor_tensor(out=a[0:nrows, 0:n], in0=ps[0:nrows, 0:n], in1=sin_t[0:nrows, t0:t0 + n], op=ALU.mult), reads=[pk, tabk], writes=[ak])
                P.op("pool", lambda e, b_=b_, n=n, t0=t0: e.tensor_tensor(out=b_[0:nrows, 0:n], in0=src_bf[0:nrows, t0:t0 + n], in1=cos_t[0:nrows, t0:t0 + n], op=ALU.mult), reads=[src_key, tabk], writes=[bk])
                P.op("dve", lambda e, a=a, b_=b_, n=n, t0=t0: e.tensor_tensor(out=dst[0:nrows, t0:t0 + n], in0=a[0:nrows, 0:n], in1=b_[0:nrows, 0:n], op=ALU.add), reads=[ak, bk], writes=[dst_key_of(bi)])

        def phase_mla(l, ctx_out):
            with contextlib.ExitStack() as st:
                sbm = lambda name, shape, dt: bld.sb(st, name, shape, dt)
                wrot = bld.rot(st, "mw", [128, KD, 128], BF16, 3)
                cqn = sbm("cqn", [128, 3, T], BF16)
                ckvn = sbm("ckvn", [128, 2, T], BF16)
                krr = sbm("krr", [64, T], BF16)
                cosm = sbm("cosm", [64, T], F32)
                sinm = sbm("sinm", [64, T], F32)
                wuq = sbm("wuq", [128, 3, 768], BF16)
                wukv = sbm("wukv", [128, 2, 1024], BF16)
                tA = bld.rot(st, "mtA", [128, 512], F32, 2)
                tB = bld.rot(st, "mtB", [128, 512], F32, 2)
                qnsc = sbm("qnsc", [128, 5], F32)
                st1 = contextlib.ExitStack()
                cqraw = bld.sb(st1, "cqraw", [128, 3, T], F32)
                ckvraw = bld.sb(st1, "ckvraw", [128, 2, T], F32)
                krb = bld.sb(st1, "krb", [64, T], BF16)
                sqr = bld.rot(st1, "msqr", [128, 512], F32R, 2)
                rnb = bld.rot(st1, "mrnb", [128, 512], F32, 2)
                P.dma("sp", cosm[:], ropem[0], writes=["ropem"])
                P.dma("sp", sinm[:], ropem[1], writes=["ropem2"])
                P.dma("pool", wuq[:], w_uq[l].rearrange("(k p) c -> p k c", p=128), writes=["wuq"])
                P.dma("pool", wukv[:], w_ukv[l].rearrange("(k p) c -> p k c", p=128), writes=["wukv"])
                for k in range(3):
                    P.op("dve", lambda e, k=k: e.tensor_scalar(out=qnsc[:, k:k + 1], in0=qn(l, k), scalar1=float(math.sqrt(384.0)), scalar2=None, op0=ALU.mult), reads=["spp"], writes=["qnsc"])
                for k in range(2):
                    P.op("dve", lambda e, k=k: e.tensor_scalar(out=qnsc[:, 3 + k:4 + k], in0=kvn(l, k), scalar1=float(math.sqrt(256.0)), scalar2=None, op0=ALU.mult), reads=["spp"], writes=["qnsc"])
                for (c0, nch, rawt, nm) in ((CQ, 3, cqraw, "cqraw"), (CKV, 2, ckvraw, "ckvraw")):
                    for ch in range(nch):
                        wt, wk = load_w(wrot, w_in[l][:, c0 + ch * 128: c0 + (ch + 1) * 128])
                        def ev(bi, t0, n, ps, pk, rawt=rawt, ch=ch, nm=nm):
                            if bi % 2:
                                P.op("act", lambda e: e.copy(rawt[:, ch, t0:t0 + n], ps[:, 0:n]), reads=[pk], writes=[(nm, ch, bi)])
                            else:
                                P.op("dve", lambda e: e.tensor_copy(rawt[:, ch, t0:t0 + n], ps[:, 0:n]), reads=[pk], writes=[(nm, ch, bi)])
                        proj_fm(wt, wk, hT_rhs, KD, ev)
                wt, wk = load_w(wrot, w_in[l][:, KR:KR + 128])
                def evk(bi, t0, n, ps, pk):
                    P.op("act", lambda e: e.copy(krb[:, t0:t0 + n], ps[0:64, 0:n]), reads=[pk], writes=[("krb", bi)])
                proj_fm(wt, wk, hT_rhs, KD, evk, m=64)
                for bi, (t0, n) in enumerate(BLKS):
                    ps, pk = PSB.get()
                    bld.mm(ps[0:64, 0:n], [(rmb[0:64, 128:192], krb[:, t0:t0 + n])], [("krb", bi), "rmb"], pk)
                    a, ak = tA.get()
                    b_, bk = tB.get()
                    P.op("dve", lambda e, a=a, ps=ps, n=n, t0=t0: e.tensor_tensor(out=a[0:64, 0:n], in0=ps[0:64, 0:n], in1=sinm[:, t0:t0 + n], op=ALU.mult), reads=[pk, "ropem2"], writes=[ak])
                    P.op("pool", lambda e, b_=b_, n=n, t0=t0: e.tensor_tensor(out=b_[0:64, 0:n], in0=krb[:, t0:t0 + n], in1=cosm[:, t0:t0 + n], op=ALU.mult), reads=[("krb", bi), "ropem"], writes=[bk])
                    P.op("dve", lambda e, a=a, b_=b_, n=n, t0=t0: e.tensor_tensor(out=krr[:, t0:t0 + n], in0=a[0:64, 0:n], in1=b_[0:64, 0:n], op=ALU.add), reads=[ak, bk], writes=[("krr", bi)])
                for (nch, rawt, nm, dstn, dnm, eps_i, q0) in ((3, cqraw, "cqraw", cqn, "cqn", 3, 0), (2, ckvraw, "ckvraw", ckvn, "ckvn", 4, 3)):
                    for bi, (t0, n) in enumerate(BLKS):
                        ps, pk = PSB.get()
                        for ch in range(nch):
                            sq, sqk = sqr.get()
                            P.op("act", lambda e, sq=sq, ch=ch, t0=t0, n=n, rawt=rawt: e.activation(sq[:, 0:n], rawt[:, ch, t0:t0 + n], AF.Square), reads=[(nm, ch, bi)], writes=[sqk])
                            bld.mm_acc(ps[:, 0:n], onesr[:], sq[:, 0:n], ch == 0, ch == nch - 1, [sqk, "onesr"], pk)
                        rn, rnk = rnb.get()
                        P.op("act", lambda e, rn=rn, ps=ps, n=n, eps_i=eps_i: e.activation(rn[:, 0:n], ps[:, 0:n], AF.Sqrt, bias=epsc(eps_i)), reads=[pk, "c32"], writes=[rnk])
                        P.op("dve", lambda e, rn=rn, n=n: e.reciprocal(rn[:, 0:n], rn[:, 0:n]), reads=[rnk], writes=[rnk])
                        for ch in range(nch):
                            P.op("dve", lambda e, ch=ch, rn=rn, t0=t0, n=n, rawt=rawt, dstn=dstn, q0=q0: e.scalar_tensor_tensor(out=dstn[:, ch, t0:t0 + n], in0=rawt[:, ch, t0:t0 + n], scalar=qnsc[:, q0 + ch:q0 + ch + 1], in1=rn[:, 0:n], op0=ALU.mult, op1=ALU.mult),
                                 reads=[(nm, ch, bi), rnk, "qnsc"], writes=[(dnm, bi)])
                P.barrier()
                st1.close()
                qnope = sbm("qnope", [128, T], BF16)
                qrb = sbm("qrb", [64, T], BF16)
                qrr = sbm("qrr", [64, T], BF16)
                knope = sbm("knope", [128, T], BF16)
                vtok = sbm("mvtok", [128, NT, 128], BF16)
                oTh = sbm("moTh", [128, T], BF16)
                pT = bld.rot(st, "mpT", [128, 512], BF16, 3)
                rden = bld.rot(st, "mrden", [128, 512], F32, 2)
                cqk = lambda: [("cqn", bi) for bi in range(5)]
                ckk = lambda: [("ckvn", bi) for bi in range(5)]
                for h in range(4):
                    for bi, (t0, n) in enumerate(BLKS):
                        ps, pk = PSB.get()
                        bld.mm(ps[:, 0:n], [(wuq[:, k, h * 192:h * 192 + 128], cqn[:, k, t0:t0 + n]) for k in range(3)], ["wuq", ("cqn", bi)], pk)
                        P.op("act", lambda e, ps=ps, t0=t0, n=n: e.activation(qnope[:, t0:t0 + n], ps[:, 0:n], AF.Copy, scale=float(MLA_SCALE)), reads=[pk], writes=[("qnope", bi)])
                        ps, pk = PSB.get()
                        bld.mm(ps[0:64, 0:n], [(wuq[:, k, h * 192 + 128:h * 192 + 192], cqn[:, k, t0:t0 + n]) for k in range(3)], ["wuq", ("cqn", bi)], pk)
                        P.op("act", lambda e, ps=ps, t0=t0, n=n: e.activation(qrb[:, t0:t0 + n], ps[0:64, 0:n], AF.Copy, scale=float(MLA_SCALE)), reads=[pk], writes=[("qrb", bi)])
                        ps, pk = PSB.get()
                        bld.mm(ps[:, 0:n], [(wukv[:, k, h * 256:h * 256 + 128], ckvn[:, k, t0:t0 + n]) for k in range(2)], ["wukv", ("ckvn", bi)], pk)
                        P.op("dve", lambda e, ps=ps, t0=t0, n=n: e.tensor_copy(knope[:, t0:t0 + n], ps[:, 0:n]), reads=[pk], writes=[("knope", bi)])
                        ps, pk = PSB.get()
                        bld.mm(ps[0:64, 0:n], [(rmb[0:64, 128:192], qrb[:, t0:t0 + n])], [("qrb", bi), "rmb"], pk)
                        a, ak = tA.get()
                        b_, bk = tB.get()
                        P.op("dve", lambda e, a=a, ps=ps, n=n, t0=t0: e.tensor_tensor(out=a[0:64, 0:n], in0=ps[0:64, 0:n], in1=sinm[:, t0:t0 + n], op=ALU.mult), reads=[pk, "ropem2"], writes=[ak])
                        P.op("pool", lambda e, b_=b_, n=n, t0=t0: e.tensor_tensor(out=b_[0:64, 0:n], in0=qrb[:, t0:t0 + n], in1=cosm[:, t0:t0 + n], op=ALU.mult), reads=[("qrb", bi), "ropem"], writes=[bk])
                        P.op("dve", lambda e, a=a, b_=b_, n=n, t0=t0: e.tensor_tensor(out=qrr[:, t0:t0 + n], in0=a[0:64, 0:n], in1=b_[0:64, 0:n], op=ALU.add), reads=[ak, bk], writes=[("qrr", bi)])
                    for t in ALLT:
                        ps, pk = PSB.get()
                        bld.mm(ps[:, 0:128], [(ckvn[:, k, t * 128:(t + 1) * 128], wukv[:, k, h * 256 + 128:h * 256 + 256]) for k in range(2)], ["wukv", ("ckvn", min(t // 4, 4))], pk)
                        P.op("act", lambda e, t=t, ps=ps: e.copy(vtok[:, t, :], ps[:, 0:128]), reads=[pk], writes=[("mvtok", t)])
                    qgroups = [(256 + g * 512, 512, ALLT) for g in range(4)]
                    if ctx_out:
                        qgroups = [(0, 256, [0, 1])] + qgroups
                    for (q0, nq, ktiles) in qgroups:
                        qb = min(q0 // 512, 4)
                        qbs = sorted(set([min(q0 // 512, 4), min((q0 + nq - 1) // 512, 4)]))
                        o_ps, o_k = banks[6], "bank6"
                        d_ps, d_k = banks[7], "psx"
                        def s_mm(kt):
                            kb = min(kt // 4, 4)
                            ps, pk = PSB.get()
                            bld.mm(ps[:, 0:nq], [(knope[:, kt * 128:(kt + 1) * 128], qnope[:, q0:q0 + nq]), (krr[:, kt * 128:(kt + 1) * 128], qrr[:, q0:q0 + nq])],
                                   [("knope", kb), ("krr", kb)] + [("qnope", b) for b in qbs] + [("qrr", b) for b in qbs], pk)
                            return ps, pk
                        pend = [s_mm(kt) for kt in ktiles[:2]]
                        for i, kt in enumerate(ktiles):
                            ps, pk = pend.pop(0)
                            p_, pkk = pT.get()
                            P.op("act", lambda e, p_=p_, ps=ps, nq=nq: e.activation(p_[:, 0:nq], ps[:, 0:nq], AF.Exp), reads=[pk], writes=[pkk])
                            if i + 2 < len(ktiles):
                                pend.append(s_mm(ktiles[i + 2]))
                            bld.mm_acc(o_ps[:, 0:nq], vtok[:, kt, :], p_[:, 0:nq], i == 0, i == len(ktiles) - 1, [("mvtok", kt), pkk], o_k)
                            bld.mm_acc(d_ps[:, 0:nq], onesb[:], p_[:, 0:nq], i == 0, i == len(ktiles) - 1, ["onesb", pkk], d_k)
                        rd, rdk = rden.get()
                        P.op("dve", lambda e, rd=rd, nq=nq: e.reciprocal(rd[:, 0:nq], d_ps[:, 0:nq]), reads=[d_k], writes=[rdk])
                        P.op("dve", lambda e, rd=rd, nq=nq, q0=q0: e.tensor_tensor(out=oTh[:, q0:q0 + nq], in0=o_ps[:, 0:nq], in1=rd[:, 0:nq], op=ALU.mult), reads=[o_k, rdk], writes=[("moTh", q0)])
                    c0 = 0 if ctx_out else 256
                    P.dma("sp", oT_s[1][h][:, c0:T], oTh[:, c0:T], reads=[("moTh", q) for q in ([0] if ctx_out else []) + [256 + g * 512 for g in range(4)]], writes=[("oTs", 1, h)])
                P.barrier()
        def phase_ret(l, tiles):
            with contextlib.ExitStack() as st:
                sbm = lambda name, shape, dt: bld.sb(st, name, shape, dt)
                wrot = bld.rot(st, "rw", [128, KD, 128], BF16, 3)
                cosr = sbm("cosr", [128, T], F32)
                sinr = sbm("sinr", [128, T], F32)
                rcs = sbm("rcs", [128, 8 * 128 * 2 + 8], F32)
                P.dma("sp", cosr[:], roper[0], writes=["roper"])
                P.dma("sp", sinr[:], roper[1], writes=["roper2"])
                P.dma("sp", rcs[:], retc, writes=["rcs"])
                DTm = lambda q: rcs[:, q * 128:(q + 1) * 128]
                GWm = lambda q: rcs[:, 1024 + q * 128:1024 + (q + 1) * 128]
                kwc = lambda q: rcs[:, 2048 + q:2049 + q]
                rawb = bld.rot(st, "rrawb", [128, T], BF16, 2)
                qT = sbm("rqT", [128, T], BF16)
                kT = sbm("rkT", [128, T], BF16)
                ktok = sbm("rktok", [128, NT, 128], BF16)
                vtok = sbm("rvtok", [128, NT, 128], BF16)
                zs = sbm("rzs", [128, NT, 128], F32)
                tA = bld.rot(st, "rtA", [128, 512], F32, 2)
                tB = bld.rot(st, "rtB", [128, 512], F32, 2)
                atb = bld.rot(st, "r_at", [128, 128], BF16, 5)
                qwb = bld.rot(st, "r_qw", [128, 128], BF16, 5)
                kwb = bld.rot(st, "r_kw", [128, 128], BF16, 5)
                for h in range(4):
                    with contextlib.ExitStack() as sh:
                        Oacc = bld.sb(sh, "rO", [128, NT, 128], F32)
                        for fi, (c0, dst, scl) in enumerate(((RQ, qT, float(128 ** -0.5)), (RK, kT, 1.0))):
                            wt, wk = load_w(wrot, w_in[l][:, c0 + h * 128:c0 + (h + 1) * 128])
                            rb_, rbk = rawb.get()
                            def ev(bi, t0, n, ps, pk, rb_=rb_, rbk=rbk, scl=scl):
                                P.op("act", lambda e: e.activation(rb_[:, t0:t0 + n], ps[:, 0:n], AF.Copy, scale=scl), reads=[pk], writes=[(rbk, bi)])
                            proj_fm(wt, wk, hT_rhs, KD, ev)
                            for bi, (t0, n) in enumerate(BLKS):
                                ps, pk = PSB.get()
                                bld.mm(ps[:, 0:n], [(rmb[:, 0:128], rb_[:, t0:t0 + n])], [(rbk, bi), "rmb"], pk)
                                a, ak = tA.get()
                                b_, bk = tB.get()
                                P.op("dve", lambda e, a=a, ps=ps, n=n, t0=t0: e.tensor_tensor(out=a[:, 0:n], in0=ps[:, 0:n], in1=sinr[:, t0:t0 + n], op=ALU.mult), reads=[pk, "roper2"], writes=[ak])
                                P.op("pool", lambda e, b_=b_, n=n, t0=t0, rb_=rb_: e.tensor_tensor(out=b_[:, 0:n], in0=rb_[:, t0:t0 + n], in1=cosr[:, t0:t0 + n], op=ALU.mult), reads=[(rbk, bi), "roper"], writes=[bk])
                                P.op("dve", lambda e, a=a, b_=b_, n=n, t0=t0, dst=dst: e.tensor_tensor(out=dst[:, t0:t0 + n], in0=a[:, 0:n], in1=b_[:, 0:n], op=ALU.add), reads=[ak, bk], writes=[("rT", fi, bi)])
                        rTk = lambda fi: [("rT", fi, bi) for bi in range(5)]
                        for g0 in range(0, NT, 4):
                            g = list(range(g0, min(g0 + 4, NT)))
                            ps, pk = PSB.get()
                            psv = ps[:].bitcast(BF16)
                            for i, t in enumerate(g):
                                bld.tr(psv[:, i * 128:(i + 1) * 128], kT[:, t * 128:(t + 1) * 128], identb[:], [("rT", 1, min(t // 4, 4)), "identb"], pk)
                            n = len(g) * 128
                            P.op("dve", lambda e, g=g, n=n, psv=psv: e.tensor_copy(ktok[:, g[0]:g[0] + len(g), :], psv[:, 0:n].rearrange("p (t c) -> p t c", c=128)), reads=[pk], writes=[("rktok", t) for t in g])
                        wt, wk = load_w(wrot, w_in[l][:, RV + h * 128:RV + (h + 1) * 128])
                        for t in ALLT:
                            ps, pk = PSS.get()
                            bld.mm(ps, [(HT[:, k, t * 128:(t + 1) * 128], wt[:, k, :]) for k in range(KD)], [wk, ("HT", t)], pk)
                            P.op("act", lambda e, t=t, ps=ps: e.copy(vtok[:, t, :], ps), reads=[pk], writes=[("rvtok", t)])
                        wt, wk = load_w(wrot, w_in[l][:, RG + h * 128:RG + (h + 1) * 128])
                        for t in tiles:
                            ps, pk = PSS.get()
                            bld.mm(ps, [(HT[:, k, t * 128:(t + 1) * 128], wt[:, k, :]) for k in range(KD)], [wk, ("HT", t)], pk)
                            P.op("act", lambda e, t=t, ps=ps: e.activation(zs[:, t, :], ps, AF.Silu), reads=[pk], writes=[("rzs", t)])
                            P.op("pool", lambda e, t=t, h=h: e.tensor_tensor(out=zs[:, t, :], in0=zs[:, t, :], in1=rnw(l, h), op=ALU.mult), reads=[("rzs", t), "rws"], writes=[("rzs", t)])
                        S32 = [bld.sb(sh, "rS32_%d" % d, [128, 128], F32) for d in range(2)]
                        Sb = [bld.rot(sh, "rSb%d" % d, [128, 128], BF16, 2) for d in range(2)]
                        order = [list(range(NT)), [1, 0] + list(range(NT - 1, 1, -1))]
                        cur = [None, None]
                        for d in range(2):
                            P.op("pool", lambda e, d=d, S32=S32: e.memset(S32[d][:], 0.0), writes=[("rS32", d)])
                            sbt, sbk = Sb[d].get()
                            P.op("pool", lambda e, sbt=sbt: e.memset(sbt[:], 0.0), writes=[sbk])
                            cur[d] = (sbt, sbk)
                        visited = set()
                        atA = bld.sb(sh, "r_atA", [128, 2 * NT, 128], BF16)
                        qwA = bld.sb(sh, "r_qwA", [128, 2 * NT, 128], BF16)
                        kwA = bld.sb(sh, "r_kwA", [128, 2 * NT, 128], BF16)
                        for c in ALLT:
                            tcol = slice(c * 128, (c + 1) * 128)
                            cb = min(c // 4, 4)
                            psQ, kQ = PSS.get()
                            bld.mm(psQ, [(kT[:, tcol], qT[:, tcol])], [("rT", 0, cb), ("rT", 1, cb)], kQ)
                            for d in range(2):
                                q = d * 4 + h
                                ix = d * NT + c
                                P.op("dve", lambda e, psQ=psQ, q=q, ix=ix, atA=atA: e.tensor_tensor(out=atA[:, ix, :], in0=psQ, in1=DTm(q), op=ALU.mult), reads=[kQ, "rcs"], writes=[("r_at", ix)])
                                P.op("dve" if d == 0 else "pool", lambda e, tcol=tcol, q=q, ix=ix, qwA=qwA: e.tensor_tensor(out=qwA[:, ix, :], in0=qT[:, tcol], in1=GWm(q), op=ALU.mult), reads=[("rT", 0, cb), "rcs"], writes=[("r_qw", ix)])
                                P.op("act", lambda e, c=c, q=q, ix=ix, kwA=kwA: e.activation(kwA[:, ix, :], ktok[:, c, :], AF.Copy, scale=kwc(q)), reads=[("rktok", c), "rcs"], writes=[("r_kw", ix)])
                        for s_ in range(NT):
                            for d in range(2):
                                c = order[d][s_]
                                q = d * 4 + h
                                ix = d * NT + c
                                sbt, sbk = cur[d]
                                pso, ko = PSS.get()
                                bld.mm(pso, [(qwA[:, ix, :], sbt[:]), (atA[:, ix, :], vtok[:, c, :])], [("r_qw", ix), sbk, ("r_at", ix), ("rvtok", c)], ko)
                                pss_, ks = PSS.get()
                                bld.mm(pss_, [(kwA[:, ix, :], vtok[:, c, :])], [("r_kw", ix), ("rvtok", c)], ks)
                                P.op("dve", lambda e, d=d, q=q, pss_=pss_, S32=S32: e.scalar_tensor_tensor(out=S32[d][:], in0=S32[d][:], scalar=float(RET_CDEC[q]), in1=pss_, op0=ALU.mult, op1=ALU.add), reads=[("rS32", d), ks], writes=[("rS32", d)])
                                nsb, nsbk = Sb[d].get()
                                P.op("act", lambda e, nsb=nsb, d=d, S32=S32: e.copy(nsb[:], S32[d][:]), reads=[("rS32", d)], writes=[nsbk])
                                cur[d] = (nsb, nsbk)
                                if c in visited:
                                    P.op("dve", lambda e, c=c, pso=pso, Oacc=Oacc: e.tensor_tensor(out=Oacc[:, c, :], in0=pso, in1=Oacc[:, c, :], op=ALU.add), reads=[ko, ("rO", c)], writes=[("rO", c)])
                                else:
                                    visited.add(c)
                                    P.op("act", lambda e, c=c, pso=pso, Oacc=Oacc: e.copy(Oacc[:, c, :], pso), reads=[ko], writes=[("rO", c)])
                        head_out_norm(sh, l, h, Oacc, zs, None, 2, tiles, "r")
                    P.barrier()
        def phase_gates(l, tiles):
            with contextlib.ExitStack() as st:
                wrot = bld.rot(st, "gtw", [128, KD, 512], BF16, 2)
                gb = bld.rot(st, "gtb", [128, 512], BF16, 4)
                for cb in range(6):
                    wt, wk = load_w(wrot, w_in[l][:, GATE + cb * 512:GATE + (cb + 1) * 512])
                    for t in tiles:
                        ps, pk = PSB.get()
                        bld.mm(ps[:], [(HT[:, k, t * 128:(t + 1) * 128], wt[:, k, :]) for k in range(KD)], [wk, ("HT", t)], pk)
                        g, gk = gb.get()
                        P.op("act", lambda e, g=g, ps=ps: e.activation(g[:], ps[:], AF.Sigmoid), reads=[pk], writes=[gk])
                        P.dma("sp", gates_s[t * 128:(t + 1) * 128, cb * 512:(cb + 1) * 512], g[:], reads=[gk], writes=[("gates", t, cb)])
                P.barrier()

        def phase_merge(l, tiles, xsrc):
            stw = contextlib.ExitStack()
            wo = bld.sb(stw, "wo", [128, KD, D], BF16)
            P.dma("pool", wo[:], w_out[l].rearrange("(k p) c -> p k c", p=128), writes=["wo"])
            with contextlib.ExitStack() as st:
                wbr = [bld.sb(st, "wbr%d" % b, [128, 4, D], BF16) for b in range(3)]
                for b in range(3):
                    P.dma("pool", wbr[b][:], w_br[b][l].rearrange("(k p) c -> p k c", p=128), writes=[("wbr", b)])
                gt = bld.rot(st, "mgt", [128, 3 * D], BF16, 2)
                ot = bld.rot(st, "mot", [128, 3, 4, 128], BF16, 2)
                t32 = bld.rot(st, "mt32", [128, 512], F32, 4)
                mb = bld.rot(st, "mmb", [128, D], BF16, 2)
                for t in tiles:
                    g, gk = gt.get()
                    P.dma("sp", g[:], gates_s[t * 128:(t + 1) * 128, :], reads=[("gates", t, cb) for cb in range(6)], writes=[gk])
                    o_, ok_ = ot.get()
                    for b in range(3):
                        P.dma("sp", o_[:, b, :, :], oT_s[b][:, :, t * 128:(t + 1) * 128].rearrange("h p c -> p h c"), reads=[("oTs", b, h) for h in range(4)], writes=[(ok_, b)])
                    m, mk = mb.get()
                    for half in range(2):
                        acc, acck = t32.get()
                        for b in range(3):
                            ps, pk = PSB.get()
                            bld.mm(ps[:], [(o_[:, b, k, :], wbr[b][:, k, half * 512:(half + 1) * 512]) for k in range(4)], [(ok_, b), ("wbr", b)], pk)
                            gsl = g[:, b * D + half * 512: b * D + (half + 1) * 512]
                            if b == 0:
                                P.op("dve", lambda e, acc=acc, ps=ps, gsl=gsl: e.tensor_tensor(out=acc[:], in0=ps[:], in1=gsl, op=ALU.mult), reads=[pk, gk], writes=[acck])
                            else:
                                tmp, tk = t32.get()
                                P.op("dve", lambda e, tmp=tmp, ps=ps, gsl=gsl: e.tensor_tensor(out=tmp[:], in0=ps[:], in1=gsl, op=ALU.mult), reads=[pk, gk], writes=[tk])
                                if b == 1:
                                    P.op("pool", lambda e, acc=acc, tmp=tmp: e.tensor_tensor(out=acc[:], in0=acc[:], in1=tmp[:], op=ALU.add), reads=[acck, tk], writes=[acck])
                                else:
                                    P.op("pool", lambda e, acc=acc, tmp=tmp, m=m, half=half: e.tensor_tensor(out=m[:, half * 512:(half + 1) * 512], in0=acc[:], in1=tmp[:], op=ALU.add), reads=[acck, tk], writes=[(mk, half)])
                    ps, pk = PSB.get()
                    psv = ps[:].bitcast(BF16)
                    for k in range(KD):
                        bld.tr(psv[:, k * 128:(k + 1) * 128], m[:, k * 128:(k + 1) * 128], identb[:], [(mk, 0), (mk, 1), "identb"], pk)
                    P.op("act", lambda e, t=t, psv=psv: e.copy(HT[:, :, t * 128:(t + 1) * 128], psv[:, 0:1024].rearrange("p (k c) -> p k c", c=128)), reads=[pk], writes=[("HT", t)])
                P.barrier()
            with contextlib.ExitStack() as st:
                residual_phase(st, l, tiles, xsrc, 0, lambda t, half: [(HT[:, k, t * 128:(t + 1) * 128], wo[:, k, half * 512:(half + 1) * 512]) for k in range(KD)],
                               lambda t: [("HT", t), "wo"])
                P.barrier()
            stw.close()

        def residual_phase(st, l, tiles, xsrc, which, pairs_of, reads_of):
            xb = bld.rot(st, "rx", [128, D], F32, 3)
            yb = bld.rot(st, "ry", [128, D], F32, 2)
            for t in tiles:
                s = tile_stream(t)
                xt, xk = xb.get()
                P.dma("sp", xt[:], xsrc[t * 128:(t + 1) * 128, :], reads=[("xs", t)], writes=[xk])
                y, yk = yb.get()
                for half in range(2):
                    ps, pk = PSB.get()
                    bld.mm(ps[:], pairs_of(t, half), reads_of(t), pk)
                    sl = slice(half * 512, (half + 1) * 512)
                    P.op("dve", lambda e, y=y, ps=ps, sl=sl, s=s: e.tensor_tensor(out=y[:, sl], in0=ps[:], in1=grow[:, s, which, sl], op=ALU.mult),
                         reads=[pk] + [("grow", s, 2 if which == 0 else 5, h_) for h_ in range(2)], writes=[(yk, half)])
                    P.op("pool", lambda e, y=y, xt=xt, sl=sl: e.tensor_tensor(out=y[:, sl], in0=y[:, sl], in1=xt[:, sl], op=ALU.add), reads=[(yk, half), xk], writes=[(yk, half)])
                P.dma("pool", xs[t * 128:(t + 1) * 128, :], y[:], reads=[(yk, 0), (yk, 1)], writes=[("xs", t)])

        def phase_ffn(l, tiles):
            t_lo = tiles[0] * 128
            stw = contextlib.ExitStack()
            wd = bld.sb(stw, "wd", [128, NJ, D], BF16)
            with contextlib.ExitStack() as st:
                wrot = bld.rot(st, "fw", [128, KD, 512], BF16, 3)
                wcur = {}
                W = T + 3
                off = lambda t0: t0 + 1 if t0 < 256 else t0 + 2
                raw = bld.rot(st, "fraw", [128, W], F32, 3)
                cv = bld.rot(st, "fcv", [128, W], F32, 2)
                aT = bld.rot(st, "faT", [128, T], BF16, 2)
                blks = BLKS if t_lo == 0 else [(256 + i * 512, 512) for i in range(4)]
                for rw_ in raw.tiles:
                    P.op("pool", lambda e, rw_=rw_: e.memset(rw_[:], 0.0), writes=[("fraw", raw.tiles.index(rw_))])
                for j in range(NJ):
                    cvs = []
                    for gi, c0 in enumerate((j * 128, DFF + j * 128)):
                        ch = c0 // 128
                        if j % 4 == 0:
                            ng = min(4, NJ - j)
                            wtf, wk = wrot.get()
                            P.dma("pool", wtf[:, :, 0:ng * 128], w_up[l][:, c0:c0 + ng * 128].rearrange("(k p) c -> p k c", p=128), writes=[wk])
                            wcur[gi] = (wtf, wk)
                        wt, wk = wcur[gi]
                        rw, rk = raw.get()
                        def ev(bi, t0, n, ps, pk, rw=rw, rk=rk):
                            if t0 == 0:
                                P.op("act", lambda e: e.copy(rw[:, 1:257], ps[:, 0:256]), reads=[pk], writes=[rk])
                                P.op("dve", lambda e: e.tensor_copy(rw[:, 258:514], ps[:, 256:512]), reads=[pk], writes=[rk])
                            elif bi % 2:
                                P.op("act", lambda e: e.copy(rw[:, off(t0):off(t0) + n], ps[:, 0:n]), reads=[pk], writes=[rk])
                            else:
                                P.op("dve", lambda e: e.tensor_copy(rw[:, off(t0):off(t0) + n], ps[:, 0:n]), reads=[pk], writes=[rk])
                        proj_fm(wt, wk, hT_rhs, KD, ev, blks=blks, coff=(j % 4) * 128)
                        c, ck = cv.get()
                        lo = off(t_lo)
                        P.op("act", lambda e, c=c, rw=rw, ch=ch: e.activation(c[:, lo:W - 1], rw[:, lo:W - 1], AF.Identity, bias=fconvb(l, ch), scale=fconv(l, ch, 1)), reads=[rk, "spp"], writes=[ck])
                        P.op("dve", lambda e, c=c, rw=rw, ch=ch: e.scalar_tensor_tensor(out=c[:, lo:W - 1], in0=rw[:, lo - 1:W - 2], scalar=fconv(l, ch, 0), in1=c[:, lo:W - 1], op0=ALU.mult, op1=ALU.add), reads=[rk, ck, "spp"], writes=[ck])
                        P.op("dve", lambda e, c=c, rw=rw, ch=ch: e.scalar_tensor_tensor(out=c[:, lo:W - 1], in0=rw[:, lo + 1:W], scalar=fconv(l, ch, 2), in1=c[:, lo:W - 1], op0=ALU.mult, op1=ALU.add), reads=[rk, ck, "spp"], writes=[ck])
                        cvs.append((c, ck))
                    (cg, cgk), (cval, cvk) = cvs
                    P.op("act", lambda e, cg=cg: e.activation(cg[:, lo:W - 1], cg[:, lo:W - 1], AF.Silu), reads=[cgk], writes=[cgk])
                    a, ak = aT.get()
                    if t_lo == 0:
                        P.op("dve", lambda e, a=a, cg=cg, cval=cval: e.tensor_tensor(out=a[:, 0:256], in0=cg[:, 1:257], in1=cval[:, 1:257], op=ALU.mult), reads=[cgk, cvk], writes=[(ak, 0)])
                    P.op("dve", lambda e, a=a, cg=cg, cval=cval: e.tensor_tensor(out=a[:, 256:T], in0=cg[:, 258:W - 1], in1=cval[:, 258:W - 1], op=ALU.mult), reads=[cgk, cvk], writes=[(ak, 1)])
                    P.dma("sp", aT_s[tiles[0]:NT, :, j, :].rearrange("t p c -> p t c"), a[:, t_lo:T].rearrange("p (t c) -> p t c", c=128), reads=[(ak, 0), (ak, 1)], writes=[("aTs", j)])
                    P.dma("pool", wd[:, j, :], w_down[l][j * 128:(j + 1) * 128, :], writes=[("wd", j)])
                P.barrier()
            with contextlib.ExitStack() as st:
                ab_ = bld.rot(st, "fab", [128, NJ, 128], BF16, 2)
                cur = {}
                def pairs_of(t, half):
                    if half == 0:
                        a, ak = ab_.get()
                        P.dma("sp", a[:], aT_s[t], reads=[("aTs", j) for j in range(NJ)], writes=[ak])
                        cur[t] = (a, ak)
                    a, ak = cur[t]
                    return [(a[:, j, :], wd[:, j, half * 512:(half + 1) * 512]) for j in range(NJ)]
                residual_phase(st, l, tiles, xs, 1, pairs_of, lambda t: [cur[t][1]] + [("wd", j) for j in range(NJ)])
                P.barrier()
            stw.close()

        def phase_final():
            with contextlib.ExitStack() as st:
                xb = bld.rot(st, "fx", [128, D], F32, 3)
                junk = bld.sb(st, "fjunk", [128, D], BF16)
                ssb = bld.rot(st, "fss", [128, 1], F32, 4)
                ob = bld.rot(st, "fo", [128, D], F32, 3)
                for t in range(2, NT):
                    xt, xk = xb.get()
                    P.dma("sp", xt[:], xs[t * 128:(t + 1) * 128, :], reads=[("xs", t)], writes=[xk])
                    ss, sk = ssb.get()
                    P.op("act", lambda e, xt=xt, ss=ss: e.activation(junk[:], xt[:], AF.Square, accum_out=ss[:]), reads=[xk], writes=["fjunk", sk])
                    P.op("act", lambda e, ss=ss: e.activation(ss[:], ss[:], AF.Sqrt, bias=epsc(0)), reads=[sk, "c32"], writes=[sk])
                    P.op("dve", lambda e, ss=ss: e.reciprocal(ss[:], ss[:]), reads=[sk], writes=[sk])
                    P.op("dve", lambda e, ss=ss: e.tensor_scalar(out=ss[:], in0=ss[:], scalar1=float(math.sqrt(D)), scalar2=None, op0=ALU.mult), reads=[sk], writes=[sk])
                    o_, ok_ = ob.get()
                    P.op("dve", lambda e, o_=o_, xt=xt, ss=ss: e.scalar_tensor_tensor(out=o_[:], in0=xt[:], scalar=ss[:], in1=fnw, op0=ALU.mult, op1=ALU.mult), reads=[xk, sk, "rws"], writes=[ok_])
                    P.dma("pool", out[(t - 2) * 128:(t - 1) * 128, :], o_[:], reads=[ok_], writes=[("out", t)])
        P.barrier()
        for l in range(2):
            ctx_out = (l == 0)
            tiles = ALLT if ctx_out else list(range(2, NT))
            xsrc = xin if l == 0 else xs
            phase_mod(l)
            if stop_after == "mod":
                dump("modpp", modpp[:], ["modpp"])
                break
            phase_norm(l, 0, xsrc, ALLT)
            if l == 0:
                dump("HT", HT[:], HTk(ALLT))
                dump("modpp", modpp[:], ["modpp"])
                dump("grow", grow[:], [("grow", s_, v_, h_) for s_ in range(2) for v_ in (2, 5) for h_ in range(2)])
            if stop_after == "norm":
                break
            if stop_after in (None, "all", "gdn", "merge", "ffn"):
                phase_gdn(l, tiles)
                if l == 0:
                    dump("oTa", oT_s[0], [("oTs", 0, h_) for h_ in range(4)])
                if stop_after == "gdn":
                    break
            if stop_after in (None, "all", "mla", "merge", "ffn"):
                phase_mla(l, ctx_out)
                if l == 0:
                    dump("oTb", oT_s[1], [("oTs", 1, h_) for h_ in range(4)])
                if stop_after == "mla":
                    break
            if stop_after in (None, "all", "ret", "merge", "ffn"):
                phase_ret(l, tiles)
                if l == 0:
                    dump("oTc", oT_s[2], [("oTs", 2, h_) for h_ in range(4)])
                if stop_after == "ret":
                    break
            phase_gates(l, tiles)
            phase_merge(l, tiles, xsrc)
            if l == 0:
                dump("xmid", xs, [("xs", t_) for t_ in ALLT])
            if stop_after == "merge":
                break
            phase_norm(l, 1, xs, tiles)
            phase_ffn(l, tiles)
            if l == 0:
                dump("xl0", xs, [("xs", t_) for t_ in ALLT])
            if stop_after == "ffn":
                break
        if stop_after in (None, "all"):
            phase_final()
        P.emit(final_reads=[("out", t) for t in range(2, NT)] + bld.dbg_keys)
    return nc, P.stats


def _rope_tables(d):
    n = 2048
    rows_ = n // 64
    r = np.repeat(np.arange(rows_, dtype=np.float32), 64)
    col = np.tile(np.arange(64, dtype=np.float32), rows_)
    quarter = d // 4
    inv = (np.float32(10000.0) ** (-np.arange(quarter, dtype=np.float32) / np.float32(quarter))).astype(np.float32)
    ang = np.concatenate([r[:, None] * inv, col[:, None] * inv], axis=-1).astype(np.float32)
    cos = np.cos(ang).astype(np.float32)
    sin = np.sin(ang).astype(np.float32)
    C = np.ones((d, T), np.float32)
    S = np.zeros((d, T), np.float32)
    C[:, 256:] = np.concatenate([cos, cos], axis=1).T
    S[:, 256:] = np.concatenate([sin, sin], axis=1).T
    return np.stack([C, S])


def _rot_mat(d):
    R = np.zeros((d, d), np.float32)
    h = d // 2
    for m in range(h):
        R[m + h, m] = -1.0
    for m in range(h, d):
        R[m - h, m] = 1.0
    return R


def _host_consts():
    c = np.zeros((128, 128 * 6 + 8), np.float32)
    idx = np.arange(128)
    k = idx[:, None]
    i = idx[None, :]
    c[:, 0:128] = np.eye(128)
    c[:, 128:256] = 1.0
    c[:, 256:384] = (k <= i)
    c[:, 384:512] = (k >= i)
    c[:, 512:640] = np.where(i > k, 0.0, NEGBIG)
    c[:, 640:768] = np.where(i < k, 0.0, NEGBIG)
    c[0, 768] = 1.0
    c[:, 769] = 1024 * EPS; c[:, 770] = 128 * EPS; c[:, 771] = EPS; c[:, 772] = 384 * EPS; c[:, 773] = 256 * EPS
    rm = np.zeros((128, 192), np.float32)
    rm[:, 0:128] = _rot_mat(128)
    rm[0:64, 128:192] = _rot_mat(64)
    hh = np.arange(4, dtype=np.float64)
    lg = np.stack([np.log1p(-(2.0 ** (-(5.0 + hh + 0.5 * d)))) for d in range(2)])
    rc = np.zeros((128, 8 * 128 * 2 + 8), np.float64)
    jj = idx[:, None].astype(np.float64)
    ii = idx[None, :].astype(np.float64)
    cdec = []
    for d in range(2):
        for h in range(4):
            g = lg[d, h]
            q = d * 4 + h
            if d == 0:
                DT = np.where(ii >= jj, np.exp(g * np.maximum(ii - jj, 0)), 0.0)
                GW = np.exp(g * (ii + 1)) * np.ones((128, 1))
                kw = np.exp(g * (127 - idx))
            else:
                DT = np.where(ii <= jj, np.exp(g * np.maximum(jj - ii, 0)), 0.0)
                GW = np.exp(g * (128 - ii)) * np.ones((128, 1))
                kw = np.exp(g * idx)
            rc[:, q * 128:(q + 1) * 128] = DT
            rc[:, 1024 + q * 128: 1024 + (q + 1) * 128] = GW
            rc[:, 2048 + q] = kw
            cdec.append(float(np.exp(g * 128)))
    return c, rm, rc.astype(np.float32), cdec


RET_CDEC = _host_consts()[3]
_NC_CACHE = {}


def _prep_common(inp):
    f = lambda a: np.ascontiguousarray(np.asarray(a, dtype=np.float32))
    pp = lambda v, nch: v.reshape(nch, 128).T
    sm = []
    n1, n2 = f(inp["norm1_w"]), f(inp["norm2_w"])
    sm.append(np.concatenate([pp(n1[l], 8) for l in range(2)], axis=1))
    sm.append(np.concatenate([pp(n2[l], 8) for l in range(2)], axis=1))
    gc = f(inp["gdn_conv_w"])
    sm.append(np.concatenate([gc[l].reshape(3, 12, 128).transpose(2, 1, 0).reshape(128, 36) for l in range(2)], axis=1))
    fc = f(inp["ffn_conv_w"])
    sm.append(np.concatenate([fc[l].reshape(3, 44, 128).transpose(2, 1, 0).reshape(128, 132) for l in range(2)], axis=1))
    fb = f(inp["ffn_conv_b"])
    sm.append(np.concatenate([pp(fb[l], 44) for l in range(2)], axis=1))
    qn_, kvn_ = f(inp["mla_q_norm"]), f(inp["mla_kv_norm"])
    sm.append(np.concatenate([pp(qn_[l], 3) for l in range(2)], axis=1))
    sm.append(np.concatenate([pp(kvn_[l], 2) for l in range(2)], axis=1))
    smallpp = np.ascontiguousarray(np.concatenate(sm, axis=1))
    rep = lambda v: np.broadcast_to(v[None, :], (128, v.shape[0]))
    rw = [rep(f(inp["gdn_norm_w"]).reshape(-1)), rep(f(inp["ret_norm_w"]).reshape(-1)),
          rep(f(inp["gdn_A_log"]).reshape(-1)), rep(f(inp["gdn_dt_bias"]).reshape(-1)), rep(f(inp["final_norm_w"]))]
    rows = np.ascontiguousarray(np.concatenate(rw, axis=1))
    c, rm, rc, _ = _host_consts()
    common = dict(smallpp=smallpp, rows=rows, consts=c, rmats=rm, retc=rc,
                  ropem=_rope_tables(64), roper=_rope_tables(128))
    for k in ("ada_w", "ada_b", "w_in", "mla_w_uq", "mla_w_ukv", "w_br_gdn", "w_br_mla", "w_br_ret", "w_out",
              "ffn_w_up", "ffn_w_down"):
        common[k] = f(inp[k])
    return common


def _prep_core(inp, b):
    f = lambda a: np.ascontiguousarray(np.asarray(a, dtype=np.float32))
    xin = np.concatenate([f(inp["ctx"][b]), f(inp["x"][b])], axis=0)
    def crep_of(v):
        return np.broadcast_to(v.reshape(8, 128).T[:, :, None], (128, 8, 128)).reshape(128, 1024)
    crep = np.stack([crep_of(f(inp["c_ctx"])), crep_of(f(inp["c"][b]))])
    return dict(xin=np.ascontiguousarray(xin), crep=np.ascontiguousarray(crep))


def kernel(**inputs):
    if "nc" not in _NC_CACHE:
        _NC_CACHE["nc"] = build()[0]
    nc = _NC_CACHE["nc"]
    common = _prep_common(inputs)
    in_maps = []
    for b in range(NCORES):
        m = dict(common)
        m.update(_prep_core(inputs, b))
        in_maps.append(m)
    res = run_bass_kernel_spmd(nc, in_maps, core_ids=list(range(NCORES)))
    return np.stack([np.asarray(r["out"], dtype=np.float32) for r in res.results], axis=0)
```

```python
import contextlib
import math
import os
import numpy as np
import concourse.bass as bass
import concourse.mybir as mybir
from concourse.bass_utils import run_bass_kernel_spmd

F32 = mybir.dt.float32
F32R = mybir.dt.float32r
BF16 = mybir.dt.bfloat16
AF = mybir.ActivationFunctionType
ALU = mybir.AluOpType

NCORES = 8
T = 2304
NT = 18
D = 1024
KD = 8
DFF = 2816
NJ = 22
EPS = 1e-6
GQ, GK, GV, GZ, GAB, CQ, CKV, KR, RQ, RK, RV, RG, GATE = 0, 512, 1024, 1536, 2048, 2064, 2448, 2704, 2768, 3280, 3792, 4304, 4816
MLA_SCALE = 192 ** -0.5
NEGBIG = -30000.0
GDN_WARM = 1


class Prog:
    def __init__(self, nc, n_dma_sems=16):
        self.nc = nc
        self.ops = []
        self.last_w = {}
        self.readers = {}
        self.n_dma_sems = n_dma_sems
        self.warm = 0
        self.dummy = None

    def op(self, eng, fn, reads=(), writes=(), dma=False, barrier=False):
        idx = len(self.ops)
        deps = {}
        reads = list(reads)
        writes = list(writes)
        if barrier:
            writes.append("__phase")
        else:
            reads.append("__phase")
        for k in reads:
            w = self.last_w.get(k)
            if w is not None:
                deps[w] = "raw"
        for k in writes:
            w = self.last_w.get(k)
            if w is not None and w not in deps:
                deps[w] = "waw"
            for r in self.readers.get(k, ()):
                if r not in deps:
                    deps[r] = "war"
        for k in reads:
            self.readers.setdefault(k, []).append(idx)
        for k in writes:
            self.last_w[k] = idx
            self.readers[k] = []
        self.ops.append(dict(eng=eng, fn=fn, deps=deps, dma=dma, barrier=barrier, warm=(self.warm if eng == "pe" else 0)))
        return idx

    def barrier(self):
        self.op("dve", lambda e: e.nop(), barrier=True)

    def dma(self, q, out, in_, reads=(), writes=()):
        return self.op(q, lambda e: e.dma_start(out=out, in_=in_), reads, writes, dma=True)

    def emit(self, final_reads=()):
        nc = self.nc
        ops = self.ops
        self.op("sp", lambda e: e.nop(), reads=final_reads)
        n = len(ops)
        pos = [0] * n
        cnt = {}
        for i, o in enumerate(ops):
            c = cnt.get(o["eng"], 0)
            pos[i] = c
            cnt[o["eng"]] = c + 1
        waited_pos = {}
        waited_dma = {}
        need = [[] for _ in range(n)]
        signaling = [False] * n
        for i, o in enumerate(ops):
            E = o["eng"]
            for d in sorted(o["deps"]):
                kind = o["deps"][d]
                od = ops[d]
                F = od["eng"]
                if od["dma"]:
                    s = waited_dma.setdefault(E, set())
                    if d in s:
                        continue
                    s.add(d)
                    need[i].append(d)
                    signaling[d] = True
                else:
                    if F == E and not o["dma"] and not o["barrier"]:
                        if E == "pe":
                            continue
                    if pos[d] <= waited_pos.get((E, F), -1):
                        continue
                    waited_pos[(E, F)] = pos[d]
                    need[i].append(d)
                    signaling[d] = True
        engs = sorted(cnt.keys())
        self.stats = dict(cnt)
        with contextlib.ExitStack() as st:
            esem = {E: st.enter_context(nc.semaphore("s_" + E)) for E in engs}
            dsem = {}
            for E in engs:
                if any(o["dma"] and o["eng"] == E for o in ops):
                    dsem[E] = [st.enter_context(nc.semaphore("d_%s_%d" % (E, j))) for j in range(self.n_dma_sems)]
            ev = [None] * n
            ecount = {E: 0 for E in engs}
            dcount = {E: [0] * self.n_dma_sems for E in dsem}
            dnext = {E: 0 for E in dsem}
            for i, o in enumerate(ops):
                E = o["eng"]
                if o["dma"]:
                    j = dnext[E]
                    dnext[E] = (j + 1) % self.n_dma_sems
                    dcount[E][j] += 1
                    ev[i] = (dsem[E][j], 16 * dcount[E][j])
                    o["dslot"] = j
                    o["dprev"] = 16 * (dcount[E][j] - 1)
                elif signaling[i]:
                    ecount[E] += 1
                    ev[i] = (esem[E], ecount[E])
            for E in engs:
                assert ecount[E] < 60000, (E, ecount[E])
            blk = st.enter_context(nc.Block())
            handles = dict(pe=blk.tensor, act=blk.scalar, dve=blk.vector, pool=blk.gpsimd, sp=blk.sync)
            nw = [0]
            for E in engs:
                my = [i for i in range(n) if ops[i]["eng"] == E]

                def body(e, my=my, E=E):
                    dwaited = [0] * self.n_dma_sems
                    for i in my:
                        o = ops[i]
                        if o["warm"] and need[i]:
                            for _ in range(o["warm"]):
                                self.dummy(e)
                        for d in need[i]:
                            s, v = ev[d]
                            e.wait_ge(s, v)
                            nw[0] += 1
                        if o["dma"]:
                            j = o["dslot"]
                            if o["dprev"] > dwaited[j]:
                                e.wait_ge(dsem[E][j], o["dprev"])
                                dwaited[j] = o["dprev"]
                                nw[0] += 1
                        ins = o["fn"](e)
                        if ev[i] is not None:
                            s, v = ev[i]
                            ins.then_inc(s, 16 if o["dma"] else 1)
                handles[E](body)
            self.stats["waits"] = nw[0]
            self.stats["signals"] = dict(ecount)


class Rot:
    def __init__(self, name, tiles):
        self.name = name
        self.tiles = tiles
        self.i = 0

    def get(self):
        j = self.i % len(self.tiles)
        self.i += 1
        kf = getattr(self, "keyfn", None)
        return self.tiles[j], (kf(j) if kf else (self.name, j))


class B:
    def __init__(self, nc, dbg):
        self.nc = nc
        self.P = Prog(nc)
        self.dbg = dbg
        self.dbg_keys = []

    def sb(self, st, name, shape, dt):
        self.uid = getattr(self, "uid", 0) + 1
        return st.enter_context(self.nc.sbuf_tensor("%s_u%d" % (name, self.uid), list(shape), dt))

    def rot(self, st, name, shape, dt, n):
        return Rot(name, [self.sb(st, "%s%d" % (name, i), shape, dt) for i in range(n)])

    def mm(self, out, pairs, reads, wkey):
        def fn(e):
            m = len(pairs)
            ins = None
            for i, (l, r) in enumerate(pairs):
                ins = e.matmul(out, lhsT=l, rhs=r, start=(i == 0), stop=(i == m - 1))
            return ins
        self.P.op("pe", fn, reads=reads, writes=[wkey])

    def mm_acc(self, out, l, r, start, stop, reads, wkey):
        self.P.op("pe", lambda e: e.matmul(out, lhsT=l, rhs=r, start=start, stop=stop), reads=reads, writes=[wkey])

    def tr(self, out, in_, ident, reads, wkey):
        self.P.op("pe", lambda e: e.transpose(out, in_, ident), reads=reads, writes=[wkey])


def tile_stream(t):
    return 0 if t < 2 else 1


def build(dbg=None, stop_after=None):
    dbg = dbg or ()
    nc = bass.Bass("TRN2", target_bir_lowering=False)
    dram_in = lambda name, shape: nc.dram_tensor(name, list(shape), F32, kind="ExternalInput").ap()
    xin = dram_in("xin", [T, D])
    crep = dram_in("crep", [2, 128, KD * 128])
    ada_w = dram_in("ada_w", [2, D, 6 * D])
    ada_b = dram_in("ada_b", [2, 6 * D])
    w_in = dram_in("w_in", [2, D, 7888])
    w_uq = dram_in("mla_w_uq", [2, 384, 768])
    w_ukv = dram_in("mla_w_ukv", [2, 256, 1024])
    w_br = [dram_in(n, [2, 512, D]) for n in ("w_br_gdn", "w_br_mla", "w_br_ret")]
    w_out = dram_in("w_out", [2, D, D])
    w_up = dram_in("ffn_w_up", [2, D, 2 * DFF])
    w_down = dram_in("ffn_w_down", [2, DFF, D])
    NSM = 2 * 8 * 2 + 2 * 12 * 3 + 2 * 44 * 3 + 2 * 44 + 2 * 3 + 2 * 2
    smallpp = dram_in("smallpp", [128, NSM])
    NROW = 2 * 128 + 2 * 512 + 2 * 8 + 2 * 8 + 1024
    rows = dram_in("rows", [128, NROW])
    NC32 = 128 * 6 + 8
    consts = dram_in("consts", [128, NC32])
    ropem = dram_in("ropem", [2, 64, T])
    roper = dram_in("roper", [2, 128, T])
    rmats = dram_in("rmats", [128, 192])
    retc = dram_in("retc", [128, 8 * 128 * 2 + 8])
    out = nc.dram_tensor("out", [2048, D], F32, kind="ExternalOutput").ap()
    xs = nc.dram_tensor("xs", [T, D], F32).ap()
    oT_s = [nc.dram_tensor("oT%d" % b, [4, 128, T], BF16).ap() for b in range(3)]
    gates_s = nc.dram_tensor("gates_s", [T, 3 * D], BF16).ap()
    aT_s = nc.dram_tensor("aT_s", [NT, 128, NJ, 128], BF16).ap()
    dbg_out = {}
    for name, shape, dt in dbg:
        dbg_out[name] = nc.dram_tensor("dbg_" + name, list(shape), dt, kind="ExternalOutput").ap()

    bld = B(nc, dbg_out)
    P = bld.P
    with contextlib.ExitStack() as top:
        sb = lambda name, shape, dt, st=top: bld.sb(st, name, shape, dt)
        HT = sb("HT", [128, KD, T], BF16)
        c32 = sb("c32", [128, NC32], F32)
        ident32 = c32[:, 0:128]
        ones32 = c32[:, 128:256]
        Umask = [c32[:, 256:384], c32[:, 384:512]]
        NEGS = [c32[:, 512:640], c32[:, 640:768]]
        e0 = c32[:, 768:769]
        epsc = lambda i: c32[:, 769 + i:770 + i]
        identb = sb("identb", [128, 128], BF16)
        onesb = sb("onesb", [128, 128], BF16)
        onesr = sb("onesr", [128, 128], F32R)
        rm32 = sb("rm32", [128, 192], F32)
        rmb = sb("rmb", [128, 192], BF16)
        spp = sb("spp", [128, NSM], F32)
        rws = sb("rws", [128, NROW], F32)
        crs = sb("crs", [128, 2, KD * 128], F32)
        grow = sb("grow", [128, 2, 2, D], F32)
        modpp = sb("modpp", [128, 64], F32)
        AB = sb("ABpp", [128, 2, 2, 2, KD], F32)
        banks = [top.enter_context(nc.psum_tensor("bank%d" % i, [128, 512], F32)) for i in range(8)]
        PSB = Rot("psb", banks[0:3])
        jw = sb("jw", [128, 128], BF16)
        jr = sb("jr", [128, 512], BF16)
        P.op("pool", lambda e: e.memset(jw[:], 0.25), writes=["jw"])
        P.op("pool", lambda e: e.memset(jr[:], 0.5), writes=["jr"])
        P.dummy = lambda e: e.matmul(banks[3][:, 0:512], lhsT=jw[:], rhs=jr[:], start=True, stop=True)
        PSS = Rot("pss", [banks[4 + i % 3][:, ((i // 3) % 4) * 128:((i // 3) % 4 + 1) * 128] for i in range(12)])
        PSS.keyfn = lambda j: ("pssbank", j % 3)
        PSH = Rot("psh", [banks[4 + i % 3][:, ((i // 3) % 2) * 256:((i // 3) % 2 + 1) * 256] for i in range(6)])
        PSH.keyfn = lambda j: ("pssbank", j % 3)
        PSX = banks[7]

        o = 0
        def take(n):
            nonlocal o
            v = (o, o + n)
            o += n
            return v
        r_n1 = take(16); r_n2 = take(16); r_gc = take(72); r_fc = take(264); r_fb = take(88); r_qn = take(6); r_kvn = take(4)
        n1w = lambda l: spp[:, r_n1[0] + l * 8: r_n1[0] + l * 8 + 8]
        n2w = lambda l: spp[:, r_n2[0] + l * 8: r_n2[0] + l * 8 + 8]
        gconv = lambda l, ch, k: spp[:, r_gc[0] + (l * 12 + ch) * 3 + k: r_gc[0] + (l * 12 + ch) * 3 + k + 1]
        fconv = lambda l, ch, k: spp[:, r_fc[0] + (l * 44 + ch) * 3 + k: r_fc[0] + (l * 44 + ch) * 3 + k + 1]
        fconvb = lambda l, ch: spp[:, r_fb[0] + l * 44 + ch: r_fb[0] + l * 44 + ch + 1]
        qn = lambda l, k: spp[:, r_qn[0] + l * 3 + k: r_qn[0] + l * 3 + k + 1]
        kvn = lambda l, k: spp[:, r_kvn[0] + l * 2 + k: r_kvn[0] + l * 2 + k + 1]
        gnw = lambda l: rws[:, l * 128:(l + 1) * 128]
        rnw = lambda l, h: rws[:, 256 + l * 512 + h * 128: 256 + l * 512 + (h + 1) * 128]
        alog = lambda l: rws[:, 1280 + l * 8: 1280 + l * 8 + 8]
        dtb = lambda l: rws[:, 1296 + l * 8: 1296 + l * 8 + 8]
        fnw = rws[:, 1312:1312 + 1024]

        P.dma("sp", c32[:], consts, writes=["c32"])
        P.dma("sp", rm32[:], rmats, writes=["rm32"])
        P.dma("sp", spp[:], smallpp, writes=["spp"])
        P.dma("sp", rws[:], rows, writes=["rws"])
        for s in range(2):
            P.dma("sp", crs[:, s, :], crep[s], writes=[("crs", s)])
        P.op("dve", lambda e: e.tensor_copy(identb[:], ident32), reads=["c32"], writes=["identb"])
        P.op("dve", lambda e: e.tensor_copy(onesb[:], ones32), reads=["c32"], writes=["onesb"])
        P.op("dve", lambda e: e.tensor_copy(onesr[:], ones32), reads=["c32"], writes=["onesr"])
        P.op("dve", lambda e: e.tensor_copy(rmb[:], rm32[:]), reads=["rm32"], writes=["rmb"])
        for s in range(2):
            P.op("act", lambda e, s=s: e.activation(crs[:, s, :], crs[:, s, :], AF.Silu), reads=[("crs", s)], writes=[("crs", s)])
        cst = ["c32", "identb", "onesb", "onesr", "rmb", "spp", "rws"]

        def dump(name, src_ap, reads):
            if name in dbg_out:
                P.dma("sp", dbg_out[name], src_ap, reads=reads, writes=[("dbg", name)])
                bld.dbg_keys.append(("dbg", name))

        def phase_mod(l):
            with contextlib.ExitStack() as st:
                wbuf = bld.rot(st, "adaw", [128, KD, 512], F32, 2)
                bbuf = bld.rot(st, "adab", [1, 512], F32, 2)
                rowt = bld.rot(st, "modrow", [128, 512], F32, 2)
                pp_ps, pp_key = PSX, "psx"
                for nb in range(12):
                    wt, wk = wbuf.get()
                    bt, bk = bbuf.get()
                    P.dma("sp", wt[:], ada_w[l][:, nb * 512:(nb + 1) * 512].rearrange("(k p) c -> p k c", p=128), writes=[wk])
                    P.dma("sp", bt[:], ada_b[l:l + 1, nb * 512:(nb + 1) * 512], writes=[bk])
                    vec = nb // 2
                    half = nb % 2
                    for s in range(2):
                        ps, pk = PSB.get()
                        pairs = [(crs[:, s, k * 128:(k + 1) * 128], wt[:, k, :]) for k in range(KD)]
                        pairs.append((ones32[0:1, :], bt[0:1, :]))
                        bld.mm(ps[:], pairs, [wk, bk, ("crs", s), "c32"], pk)
                        if vec in (2, 5):
                            dst = grow[:, s, 0 if vec == 2 else 1, half * 512:(half + 1) * 512]
                            P.op("act", lambda e, dst=dst, ps=ps: e.copy(dst, ps[:]), reads=[pk], writes=[("grow", s, vec, half)])
                        else:
                            rt, rk = rowt.get()
                            P.op("dve", lambda e, rt=rt, ps=ps: e.tensor_copy(rt[:], ps[:]), reads=[pk], writes=[rk])
                            vi = {0: 0, 1: 1, 3: 2, 4: 3}[vec]
                            for c4 in range(4):
                                col = s * 32 + vi * 8 + half * 4 + c4
                                bld.mm(pp_ps[:, col:col + 1], [(rt[:, c4 * 128:(c4 + 1) * 128], e0)], [rk, "c32"], pp_key)
                P.op("dve", lambda e: e.tensor_copy(modpp[:], pp_ps[:, 0:64]), reads=[pp_key], writes=["modpp"])
                for s in range(2):
                    for nrm in range(2):
                        sh = modpp[:, s * 32 + (2 * nrm) * 8: s * 32 + (2 * nrm) * 8 + 8]
                        sc = modpp[:, s * 32 + (2 * nrm + 1) * 8: s * 32 + (2 * nrm + 1) * 8 + 8]
                        nw = n1w(l) if nrm == 0 else n2w(l)
                        P.op("dve", lambda e, sc=sc, nw=nw, s=s, nrm=nrm: e.scalar_tensor_tensor(out=AB[:, s, nrm, 0, :], in0=sc, scalar=1.0, in1=nw, op0=ALU.add, op1=ALU.mult),
                             reads=["modpp", "spp"], writes=[("AB", s, nrm, 0)])
                        P.op("dve", lambda e, s=s, nrm=nrm: e.tensor_scalar(out=AB[:, s, nrm, 0, :], in0=AB[:, s, nrm, 0, :], scalar1=float(math.sqrt(D)), scalar2=None, op0=ALU.mult),
                             reads=[("AB", s, nrm, 0)], writes=[("AB", s, nrm, 0)])
                        P.op("dve", lambda e, sh=sh, s=s, nrm=nrm: e.tensor_copy(AB[:, s, nrm, 1, :], sh), reads=["modpp"], writes=[("AB", s, nrm, 1)])
                P.barrier()

        def phase_norm(l, nrm, xsrc, tiles):
            with contextlib.ExitStack() as st:
                xb = bld.rot(st, "nx", [128, D], F32, 3)
                junk = bld.sb(st, "njunk", [128, D], BF16)
                xn = bld.rot(st, "nxn", [128, D], BF16, 8)
                ssb = bld.rot(st, "nss", [128, 1], F32, 8)
                groups = []
                cur = []
                for t in tiles:
                    cur.append(t)
                    if len(cur) == 4:
                        groups.append(cur); cur = []
                if cur:
                    groups.append(cur)
                for grp in groups:
                    xns = []
                    for t in grp:
                        xt, xk = xb.get()
                        P.dma("sp", xt[:], xsrc[t * 128:(t + 1) * 128, :], reads=[("xs", t)], writes=[xk])
                        ss, sk = ssb.get()
                        P.op("pool", lambda e, ss=ss: e.memset(ss[:], 0.0), writes=[sk])
                        P.op("dve", lambda e, xt=xt, ss=ss: e.scalar_tensor_tensor(out=junk[:], in0=xt[:], scalar=1.0, in1=xt[:], op0=ALU.mult, op1=ALU.mult, accum_out=ss[:]), reads=[xk, sk], writes=["njunk", sk])
                        P.op("act", lambda e, ss=ss: e.activation(ss[:], ss[:], AF.Sqrt, bias=epsc(0)), reads=[sk, "c32"], writes=[sk])
                        P.op("dve", lambda e, ss=ss: e.reciprocal(ss[:], ss[:]), reads=[sk], writes=[sk])
                        xnt, xnk = xn.get()
                        P.op("act", lambda e, xnt=xnt, xt=xt, ss=ss: e.activation(xnt[:], xt[:], AF.Copy, scale=ss[:]), reads=[xk, sk], writes=[xnk])
                        xns.append((t, xnt, xnk))
                    for k in range(KD):
                        ps, pk = PSB.get()
                        psv = ps[:].bitcast(BF16)
                        for i, (t, xnt, xnk) in enumerate(xns):
                            bld.tr(psv[:, i * 128:(i + 1) * 128], xnt[:, k * 128:(k + 1) * 128], identb[:], [xnk, "identb"], pk)
                        i = 0
                        while i < len(xns):
                            s = tile_stream(xns[i][0])
                            j = i
                            while j < len(xns) and tile_stream(xns[j][0]) == s:
                                j += 1
                            t0 = xns[i][0]
                            dst = HT[:, k, t0 * 128:(t0 + (j - i)) * 128]
                            src = psv[:, i * 128:j * 128]
                            wr = [("HT", tt) for tt in range(t0, t0 + (j - i))]
                            a_ap = AB[:, s, nrm, 0, k:k + 1]
                            b_ap = AB[:, s, nrm, 1, k:k + 1]
                            if k % 2 == 0:
                                P.op("act", lambda e, dst=dst, src=src, a_ap=a_ap, b_ap=b_ap: e.activation(dst, src, AF.Identity, bias=b_ap, scale=a_ap),
                                     reads=[pk, ("AB", s, nrm, 0), ("AB", s, nrm, 1)], writes=wr)
                            else:
                                P.op("dve", lambda e, dst=dst, src=src, a_ap=a_ap, b_ap=b_ap: e.tensor_scalar(out=dst, in0=src, scalar1=a_ap, scalar2=b_ap, op0=ALU.mult, op1=ALU.add),
                                     reads=[pk, ("AB", s, nrm, 0), ("AB", s, nrm, 1)], writes=wr)
                            i = j
                P.barrier()

        HTk = lambda ts: [("HT", t) for t in ts]
        ALLT = list(range(NT))

        def load_w(rotw, src2d, reads=()):
            wt, wk = rotw.get()
            P.dma("pool", wt[:], src2d.rearrange("(k p) c -> p k c", p=128), reads=reads, writes=[wk])
            return wt, wk

        BLKS = [(0, 512), (512, 512), (1024, 512), (1536, 512), (2048, 256)]

        def proj_fm(wt, wk, rhs_of, nk, evac, m=128, blks=BLKS, extra_reads=(), coff=0):
            for bi, (t0, n) in enumerate(blks):
                ps, pk = PSB.get()
                pairs = [(wt[:, k, coff:coff + m], rhs_of(k, t0, n)) for k in range(nk)]
                tl = list(range(t0 // 128, (t0 + n) // 128))
                bld.mm(ps[0:m, 0:n], pairs, [wk] + HTk(tl) + list(extra_reads), pk)
                evac(bi, t0, n, ps, pk)

        hT_rhs = lambda k, t0, n: HT[:, k, t0:t0 + n]

        def head_out_norm(st, l, h, Oacc, zs, nrot, b_idx, tiles, tag):
            oTh = bld.sb(st, tag + "oTh", [128, T], BF16)
            ssq = bld.sb(st, tag + "ssq", [128, NT], F32)
            junk = bld.sb(st, tag + "junk", [128, 128], BF16)
            yb = bld.rot(st, tag + "yb", [128, 128], BF16, 4)
            for t in tiles:
                P.op("act", lambda e, t=t: e.activation(junk[:], Oacc[:, t, :], AF.Square, accum_out=ssq[:, t:t + 1]), reads=[(tag + "O", t)], writes=[tag + "junk", (tag + "ssq", t)])
            t0, t1 = tiles[0], tiles[-1] + 1
            P.op("act", lambda e: e.activation(ssq[:, t0:t1], ssq[:, t0:t1], AF.Sqrt, bias=epsc(2), scale=1.0 / 128.0), reads=[(tag + "ssq", t) for t in tiles] + ["c32"], writes=[tag + "ssqall"])
            P.op("dve", lambda e: e.reciprocal(ssq[:, t0:t1], ssq[:, t0:t1]), reads=[tag + "ssqall"], writes=[tag + "ssqall"])
            grp = [tiles[i:i + 4] for i in range(0, len(tiles), 4)]
            for g in grp:
                ps, pk = PSB.get()
                psv = ps[:].bitcast(BF16)
                for i, t in enumerate(g):
                    y, yk = yb.get()
                    P.op("dve", lambda e, y=y, t=t: e.scalar_tensor_tensor(out=y[:], in0=Oacc[:, t, :], scalar=ssq[:, t:t + 1], in1=zs[:, t, :], op0=ALU.mult, op1=ALU.mult),
                         reads=[(tag + "O", t), tag + "ssqall", (tag + "zs", t)], writes=[yk])
                    bld.tr(psv[:, i * 128:(i + 1) * 128], y[:], identb[:], [yk, "identb"], pk)
                n = len(g) * 128
                P.op("act", lambda e, g=g, n=n, psv=psv: e.copy(oTh[:, g[0] * 128:g[0] * 128 + n], psv[:, 0:n]), reads=[pk], writes=[(tag + "oTh", t) for t in g])
            c0 = tiles[0] * 128
            P.dma("sp", oT_s[b_idx][h][:, c0:T], oTh[:, c0:T], reads=[(tag + "oTh", t) for t in tiles], writes=[("oTs", b_idx, h)])

        def phase_gdn(l, tiles):
            with contextlib.ExitStack() as st:
                wrot = bld.rot(st, "gw", [128, KD, 128], BF16, 3)
                wab = bld.sb(st, "gwab", [128, KD, 16], BF16)
                gbeta = bld.sb(st, "gbeta", [128, NT, 16], F32)
                tmp8 = bld.sb(st, "gtmp8", [128, NT, 8], F32)
                P.dma("pool", wab[:], w_in[l][:, GAB:GAB + 16].rearrange("(k p) c -> p k c", p=128), writes=["gwab"])
                ab_ps, ab_key = PSX, "psx"
                for t in ALLT:
                    bld.mm(ab_ps[:, t * 16:(t + 1) * 16], [(HT[:, k, t * 128:(t + 1) * 128], wab[:, k, :]) for k in range(KD)], ["gwab", ("HT", t)], ab_key)
                abv = ab_ps[:, 0:NT * 16].rearrange("p (t c) -> p t c", c=16)
                for t in ALLT:
                    P.op("dve", lambda e, t=t: e.tensor_tensor(out=tmp8[:, t, :], in0=abv[:, t, 0:8], in1=dtb(l), op=ALU.add), reads=[ab_key, "rws"], writes=[("gtmp8", t)])
                al = bld.sb(st, "galog", [128, 8], F32)
                P.op("act", lambda e: e.activation(al[:], alog(l), AF.Exp), reads=["rws"], writes=["galog"])
                t8all = [("gtmp8", t) for t in ALLT]
                P.op("act", lambda e: e.activation(tmp8[:], tmp8[:], AF.Exp), reads=t8all, writes=["gtmp8all"])
                P.op("act", lambda e: e.activation(tmp8[:], tmp8[:], AF.Ln, bias=1.0), reads=["gtmp8all"], writes=["gtmp8all"])
                for t in ALLT:
                    P.op("dve", lambda e, t=t: e.scalar_tensor_tensor(out=gbeta[:, t, 0:8], in0=tmp8[:, t, :], scalar=-1.0, in1=al[:], op0=ALU.mult, op1=ALU.mult),
                         reads=["gtmp8all", "galog"], writes=[("gb_g", t)])
                P.op("act", lambda e: e.activation(gbeta[:, :, 8:16], abv[:, :, 8:16], AF.Sigmoid), reads=[ab_key], writes=["gb_beta"])
                gbk = [("gb_g", t) for t in ALLT] + ["gb_beta"]
                P.barrier()
                for h in range(4):
                    with contextlib.ExitStack() as sh:
                        gdn_head(sh, l, h, tiles, wrot, gbeta)
                    P.barrier()

        def gdn_head(st, l, h, tiles, wrot, gbeta):
            sbh = lambda name, shape, dt: bld.sb(st, name, shape, dt)
            W = T + 3
            off = lambda t0: t0 + 1 if t0 < 256 else t0 + 2
            raw = bld.rot(st, "graw", [128, W], F32, 2)
            cv = bld.rot(st, "gcv", [128, W], F32, 2)
            qT = sbh("gqT", [128, T], BF16)
            kT = sbh("gkT", [128, T], BF16)
            vT = sbh("gvT", [128, T], BF16)
            ktok = sbh("gktok", [128, NT, 128], BF16)
            vtok = sbh("gvtok", [128, NT, 128], BF16)
            zs = sbh("gzs", [128, NT, 128], F32)
            Oacc = sbh("gO", [128, NT, 128], F32)
            sqr = bld.rot(st, "gsqr", [128, 512], F32R, 2)
            rnb = bld.rot(st, "grnb", [128, 512], F32, 2)
            for fi, (c0, dst) in enumerate(((GQ, qT), (GK, kT), (GV, vT))):
                ch = fi * 4 + h
                wt, wk = load_w(wrot, w_in[l][:, c0 + h * 128: c0 + (h + 1) * 128])
                rw, rk = raw.get()
                P.op("pool", lambda e, rw=rw: e.memset(rw[:], 0.0), writes=[rk])
                def ev(bi, t0, n, ps, pk, rw=rw, rk=rk):
                    o_ = off(t0)
                    if t0 == 0:
                        P.op("act", lambda e: e.copy(rw[:, 1:257], ps[:, 0:256]), reads=[pk], writes=[rk])
                        P.op("dve", lambda e: e.tensor_copy(rw[:, 258:514], ps[:, 256:512]), reads=[pk], writes=[rk])
                    else:
                        eng = "act" if bi % 2 else "dve"
                        if eng == "act":
                            P.op("act", lambda e: e.copy(rw[:, o_:o_ + n], ps[:, 0:n]), reads=[pk], writes=[rk])
                        else:
                            P.op("dve", lambda e: e.tensor_copy(rw[:, o_:o_ + n], ps[:, 0:n]), reads=[pk], writes=[rk])
                proj_fm(wt, wk, hT_rhs, KD, ev)
                c, ck = cv.get()
                P.op("act", lambda e, c=c, rw=rw, ch=ch: e.activation(c[:, 1:W - 1], rw[:, 1:W - 1], AF.Copy, scale=gconv(l, ch, 1)), reads=[rk, "spp"], writes=[ck])
                P.op("dve", lambda e, c=c, rw=rw, ch=ch: e.scalar_tensor_tensor(out=c[:, 1:W - 1], in0=rw[:, 0:W - 2], scalar=gconv(l, ch, 0), in1=c[:, 1:W - 1], op0=ALU.mult, op1=ALU.add), reads=[rk, ck, "spp"], writes=[ck])
                P.op("dve", lambda e, c=c, rw=rw, ch=ch: e.scalar_tensor_tensor(out=c[:, 1:W - 1], in0=rw[:, 2:W], scalar=gconv(l, ch, 2), in1=c[:, 1:W - 1], op0=ALU.mult, op1=ALU.add), reads=[rk, ck, "spp"], writes=[ck])
                P.op("act", lambda e, c=c: e.activation(c[:, 1:W - 1], c[:, 1:W - 1], AF.Silu), reads=[ck], writes=[ck])
                if fi == 2:
                    P.op("dve", lambda e, c=c: e.tensor_copy(vT[:, 0:256], c[:, 1:257]), reads=[ck], writes=[("gT", 2, 0)])
                    P.op("dve", lambda e, c=c: e.tensor_copy(vT[:, 256:T], c[:, 258:W - 1]), reads=[ck], writes=[("gT", 2, 1)])
                else:
                    for bi, (t0, n) in enumerate(BLKS):
                        segs = [(0, 256), (256, 256)] if t0 == 0 else [(t0, n)]
                        sq, sqk = sqr.get()
                        for (s0, sn) in segs:
                            P.op("act", lambda e, c=c, s0=s0, sn=sn, sq=sq, t0=t0: e.activation(sq[:, s0 - t0:s0 - t0 + sn], c[:, off(s0):off(s0) + sn], AF.Square), reads=[ck], writes=[sqk])
                        ps, pk = PSB.get()
                        bld.mm(ps[:, 0:n], [(onesr[:], sq[:, 0:n])], [sqk, "onesr"], pk)
                        rn, rnk = rnb.get()
                        P.op("act", lambda e, rn=rn, ps=ps, n=n: e.activation(rn[:, 0:n], ps[:, 0:n], AF.Sqrt, bias=epsc(2)), reads=[pk, "c32"], writes=[rnk])
                        P.op("dve", lambda e, rn=rn, n=n: e.reciprocal(rn[:, 0:n], rn[:, 0:n]), reads=[rnk], writes=[rnk])
                        scl = float(128 ** -0.5) if fi == 0 else 1.0
                        for (s0, sn) in segs:
                            P.op("dve", lambda e, c=c, s0=s0, sn=sn, rn=rn, t0=t0, dst=dst, scl=scl: e.scalar_tensor_tensor(out=dst[:, s0:s0 + sn], in0=c[:, off(s0):off(s0) + sn], scalar=scl, in1=rn[:, s0 - t0:s0 - t0 + sn], op0=ALU.mult, op1=ALU.mult),
                                 reads=[ck, rnk], writes=[("gT", fi, s0)])
            gTk = lambda fi: [("gT", fi, s0) for s0 in (0, 256, 512, 1024, 1536, 2048)] + [("gT", 2, 0), ("gT", 2, 1)]
            for (src, dstt, fi, nm) in ((kT, ktok, 1, "gktok"), (vT, vtok, 2, "gvtok")):
                for g0 in range(0, NT, 4):
                    g = list(range(g0, min(g0 + 4, NT)))
                    ps, pk = PSB.get()
                    psv = ps[:].bitcast(BF16)
                    for i, t in enumerate(g):
                        bld.tr(psv[:, i * 128:(i + 1) * 128], src[:, t * 128:(t + 1) * 128], identb[:], gTk(fi) + ["identb"], pk)
                    n = len(g) * 128
                    P.op("act" if (g0 // 4) % 2 else "dve",
                         (lambda e, g=g, n=n, psv=psv, dstt=dstt: e.copy(dstt[:, g[0]:g[0] + len(g), :], psv[:, 0:n].rearrange("p (t c) -> p t c", c=128))) if (g0 // 4) % 2 else
                         (lambda e, g=g, n=n, psv=psv, dstt=dstt: e.tensor_copy(dstt[:, g[0]:g[0] + len(g), :], psv[:, 0:n].rearrange("p (t c) -> p t c", c=128))),
                         reads=[pk], writes=[(nm, t) for t in g])
            wt, wk = load_w(wrot, w_in[l][:, GZ + h * 128: GZ + (h + 1) * 128])
            for t in tiles:
                ps, pk = PSS.get()
                bld.mm(ps, [(HT[:, k, t * 128:(t + 1) * 128], wt[:, k, :]) for k in range(KD)], [wk, ("HT", t)], pk)
                P.op("act", lambda e, t=t, ps=ps: e.activation(zs[:, t, :], ps, AF.Silu), reads=[pk], writes=[("gzs", t)])
                P.op("pool", lambda e, t=t: e.tensor_tensor(out=zs[:, t, :], in0=zs[:, t, :], in1=gnw(l), op=ALU.mult), reads=[("gzs", t), "rws"], writes=[("gzs", t)])
            f32t = lambda name, n: bld.rot(st, name, [128, 128], F32, n)
            b16t = lambda name, n: bld.rot(st, name, [128, 128], BF16, n)
            gbr = f32t("g_gb", 3); egr = f32t("g_eg", 5); dsr = f32t("g_ds", 3); dir_ = f32t("g_di", 3)
            colr = bld.rot(st, "g_col", [128, 4], F32, 5)
            Pm = bld.rot(st, "g_P", [128, 256], F32, 4); PTm = bld.rot(st, "g_PT", [128, 256], F32, 4); Rm = bld.rot(st, "g_R", [128, 256], F32, 3)
            ident2 = sbh("g_id2", [128, 256], F32)
            P.op("dve", lambda e: e.tensor_copy(ident2[:, 0:128], ident32), reads=["c32"], writes=["g_id2"])
            P.op("dve", lambda e: e.tensor_copy(ident2[:, 128:256], ident32), reads=["c32"], writes=["g_id2"])
            TTb = bld.rot(st, "g_TT", [128, 256], BF16, 3); atb = b16t("g_at", 5); qdb = b16t("g_qd", 5); kdb = b16t("g_kd", 5)
            rb = b16t("g_r", 3); vnb = b16t("g_vn", 3)
            S32 = [sbh("gS32_%d" % d, [128, 128], F32) for d in range(2)]
            Sb = [bld.rot(st, "gSb%d" % d, [128, 128], BF16, 2) for d in range(2)]
            order = [list(range(NT)), [1, 0] + list(range(NT - 1, 1, -1))]
            cur_Sb = [None, None]
            for d in range(2):
                P.op("pool", lambda e, d=d: e.memset(S32[d][:], 0.0), writes=[("gS32", d)])
                sbt, sbk = Sb[d].get()
                P.op("pool", lambda e, sbt=sbt: e.memset(sbt[:], 0.0), writes=[sbk])
                cur_Sb[d] = (sbt, sbk)
            visited = set()
            kT_k, qT_k = gTk(1), gTk(0)

            def precompute(d, c, P2, P2k):
                q = d * 4 + h
                gcol = gbeta[:, c, q:q + 1]
                bcol = gbeta[:, c, 8 + q:9 + q]
                tcol = slice(c * 128, (c + 1) * 128)
                gb, gbk_ = gbr.get()
                P.op("dve", lambda e: e.tensor_scalar(out=gb[:], in0=Umask[d], scalar1=gcol, scalar2=None, op0=ALU.mult), reads=[("gb_g", c), "c32"], writes=[gbk_])
                psA, kA = PSS.get()
                bld.mm(psA, [(ones32, gb[:])], [gbk_, "c32"], kA)
                psB, kB = PSS.get()
                bld.mm(psB, [(ones32, gb[:]), (ident32, NEGS[d])], [gbk_, "c32"], kB)
                psC, kC = PSS.get()
                bld.mm(psC[:, 0:1], [(Umask[d], gcol)], [("gb_g", c), "c32"], kC)
                yield
                col, colk = colr.get()
                P.op("dve", lambda e: e.tensor_scalar(out=col[:, 1:2], in0=psC[:, 0:1], scalar1=-1.0, scalar2=None, op0=ALU.mult), reads=[kC], writes=[colk])
                P.op("act", lambda e: e.activation(col[:, 0:1], psC[:, 0:1], AF.Exp), reads=[kC], writes=[colk])
                P.op("dve", lambda e: e.tensor_scalar(out=col[:, 0:1], in0=col[:, 0:1], scalar1=-1.0, scalar2=None, op0=ALU.mult), reads=[colk], writes=[colk])
                P.op("dve", lambda e: e.tensor_scalar(out=col[:, 2:3], in0=bcol, scalar1=-1.0, scalar2=None, op0=ALU.mult), reads=["gb_beta"], writes=[colk])
                eg, egk = egr.get()
                P.op("act", lambda e: e.activation(eg[:], psA, AF.Exp), reads=[kA], writes=[egk])
                ds, dsk = dsr.get()
                P.op("act", lambda e: e.activation(ds[:], psB, AF.Exp, bias=col[:, 1:2]), reads=[kB, colk], writes=[dsk])
                yield
                di, dik = dir_.get()
                P.op("dve", lambda e: e.tensor_tensor(out=di[:], in0=ds[:], in1=ident32, op=ALU.add), reads=[dsk, "c32"], writes=[dik])
                psK, kK = PSS.get()
                bld.mm(psK, [(kT[:, tcol], kT[:, tcol])], kT_k, kK)
                psQ, kQ = PSS.get()
                bld.mm(psQ, [(kT[:, tcol], qT[:, tcol])], kT_k + qT_k, kQ)
                yield
                p0 = P2[:, d * 128:(d + 1) * 128]
                p0k = (P2k, d)
                P.op("dve", lambda e: e.scalar_tensor_tensor(out=p0, in0=psK, scalar=col[:, 2:3], in1=ds[:], op0=ALU.mult, op1=ALU.mult), reads=[kK, colk, dsk], writes=[p0k])
                at, atk = atb.get()
                P.op("dve", lambda e: e.tensor_tensor(out=at[:], in0=psQ, in1=di[:], op=ALU.mult), reads=[kQ, dik], writes=[atk])
                last = 127 if d == 0 else 0
                kd, kdk = kdb.get()
                P.op("pool", lambda e: e.tensor_scalar(out=kd[:], in0=ktok[:, c, :], scalar1=di[:, last:last + 1], scalar2=None, op0=ALU.mult), reads=[("gktok", c), dik], writes=[kdk])
                qd, qdk = qdb.get()
                P.op("pool", lambda e: e.tensor_tensor(out=qd[:], in0=qT[:, tcol], in1=eg[:], op=ALU.mult), reads=qT_k + [egk], writes=[qdk])
                pr_out[d] = dict(col=col, colk=colk, eg=eg, egk=egk, at=at, atk=atk, kd=kd, kdk=kdk, qd=qd, qdk=qdk, bcol=bcol, last=last)

            def step(d, c, pre):
                tcol = slice(c * 128, (c + 1) * 128)
                sbt, sbk = cur_Sb[d]
                psk, kk = PSS.get()
                bld.mm(psk, [(kT[:, tcol], sbt[:])], kT_k + [sbk], kk)
                yield
                r, rk_ = rb.get()
                P.op("dve", lambda e: e.scalar_tensor_tensor(out=r[:], in0=psk, scalar=pre["col"][:, 0:1], in1=vtok[:, c, :], op0=ALU.mult, op1=ALU.add), reads=[kk, pre["colk"], ("gvtok", c)], writes=[rk_])
                psv, kv = PSS.get()
                bld.mm(psv, [(pre["tt"], r[:])], [pre["ttk"], rk_], kv)
                yield
                vn, vnk = vnb.get()
                P.op("act", lambda e: e.activation(vn[:], psv, AF.Copy, scale=pre["bcol"]), reads=[kv, "gb_beta"], writes=[vnk])
                pso, ko = PSS.get()
                bld.mm(pso, [(pre["qd"][:], sbt[:]), (pre["at"][:], vn[:])], [pre["qdk"], sbk, pre["atk"], vnk], ko)
                if c in visited:
                    P.op("dve", lambda e: e.tensor_tensor(out=Oacc[:, c, :], in0=pso, in1=Oacc[:, c, :], op=ALU.add), reads=[ko, ("gO", c)], writes=[("gO", c)])
                else:
                    visited.add(c)
                    P.op("act", lambda e: e.copy(Oacc[:, c, :], pso), reads=[ko], writes=[("gO", c)])
                pss_, ks = PSS.get()
                bld.mm(pss_, [(pre["kd"][:], vn[:])], [pre["kdk"], vnk], ks)
                yield
                last = pre["last"]
                P.op("dve", lambda e: e.scalar_tensor_tensor(out=S32[d][:], in0=S32[d][:], scalar=pre["eg"][:, last:last + 1], in1=pss_, op0=ALU.mult, op1=ALU.add), reads=[("gS32", d), pre["egk"], ks], writes=[("gS32", d)])
                nsb, nsbk = Sb[d].get()
                P.op("act", lambda e: e.copy(nsb[:], S32[d][:]), reads=[("gS32", d)], writes=[nsbk])
                cur_Sb[d] = (nsb, nsbk)

            def neumann2(P2, P2k):
                pk_all = [(P2k, 0), (P2k, 1)]
                psT, kT_ = PSH.get()
                for x in range(2):
                    bld.tr(psT[:, x * 128:(x + 1) * 128], P2[:, x * 128:(x + 1) * 128], ident32, pk_all + ["c32"], kT_)
                PT2, PT2k = PTm.get()
                P.op("act", lambda e, PT2=PT2: e.copy(PT2[:], psT), reads=[kT_], writes=[PT2k])
                R2, R2k = Rm.get()
                P.op("dve", lambda e, R2=R2: e.tensor_tensor(out=R2[:], in0=P2[:], in1=ident2[:], op=ALU.add), reads=pk_all + ["g_id2"], writes=[R2k])
                pc, pck, ptc, ptck = P2, pk_all, PT2, [PT2k]
                sl = lambda t_, x: t_[:, x * 128:(x + 1) * 128]
                for lev in range(6):
                    if lev < 5:
                        S1, k1 = PSH.get()
                        for x in range(2):
                            bld.mm(sl(S1, x), [(sl(ptc, x), sl(pc, x))], pck + ptck, k1)
                    S2, k2 = PSH.get()
                    for x in range(2):
                        bld.mm(sl(S2, x), [(sl(pc, x), sl(ptc, x))], pck + ptck, k2)
                    if lev >= 1:
                        S3, k3 = PSH.get()
                        for x in range(2):
                            bld.mm(sl(S3, x), [(sl(ptc, x), sl(R2, x))], ptck + [R2k], k3)
                    Pn, Pnk = Pm.get()
                    if lev < 5:
                        P.op("act", lambda e, Pn=Pn, S1=S1: e.copy(Pn[:], S1), reads=[k1], writes=[Pnk])
                    PTn, PTnk = PTm.get()
                    P.op("dve" if lev < 1 else "act", (lambda e, PTn=PTn, S2=S2: e.tensor_copy(PTn[:], S2)) if lev < 1 else (lambda e, PTn=PTn, S2=S2: e.copy(PTn[:], S2)), reads=[k2], writes=[PTnk])
                    if lev >= 1:
                        Rn, Rnk = Rm.get()
                        P.op("dve", lambda e, Rn=Rn, R2=R2, S3=S3: e.tensor_tensor(out=Rn[:], in0=S3, in1=R2[:], op=ALU.add), reads=[k3, R2k], writes=[Rnk])
                        R2, R2k = Rn, Rnk
                    pc, pck, ptc, ptck = Pn, [Pnk], PTn, [PTnk]
                S3, k3 = PSH.get()
                for x in range(2):
                    bld.mm(sl(S3, x), [(sl(ptc, x), sl(R2, x))], ptck + [R2k], k3)
                tt2, tt2k = TTb.get()
                P.op("dve", lambda e, tt2=tt2, R2=R2, S3=S3: e.tensor_tensor(out=tt2[:], in0=S3, in1=R2[:], op=ALU.add), reads=[k3, R2k], writes=[tt2k])
                return tt2, tt2k

            pres = {}
            pr_out = {}
            P.warm = GDN_WARM
            for s_ in range(NT + 1):
                if s_ < NT:
                    P2, P2k = Pm.get()
                    pr_out.clear()
                    gens = [precompute(d, order[d][s_], P2, P2k) for d in range(2)]
                    while gens:
                        for g_ in list(gens):
                            try:
                                next(g_)
                            except StopIteration:
                                gens.remove(g_)
                    pr = [pr_out[0], pr_out[1]]
                    tt2, tt2k = neumann2(P2, P2k)
                    for d in range(2):
                        pr[d]["tt"] = tt2[:, d * 128:(d + 1) * 128]
                        pr[d]["ttk"] = tt2k
                        pres[(d, order[d][s_])] = pr[d]
                if s_ >= 1:
                    gens = [step(d, order[d][s_ - 1], pres.pop((d, order[d][s_ - 1]))) for d in range(2)]
                    while gens:
                        for g_ in list(gens):
                            try:
                                next(g_)
                            except StopIteration:
                                gens.remove(g_)
            P.warm = 0
            head_out_norm(st, l, h, Oacc, zs, None, 0, tiles, "g")
        def rope_fm(src_bf, src_key, dst, dst_key_of, nrows, rmat, cos_t, sin_t, tabk, tmpA, tmpB):
            for bi, (t0, n) in enumerate(BLKS):
                ps, pk = PSB.get()
                bld.mm(ps[0:nrows, 0:n], [(rmat, src_bf[0:nrows, t0:t0 + n])], [src_key, "rmb"], pk)
                a, ak = tmpA.get()
                b_, bk = tmpB.get()
                P.op("dve", lambda e, a=a, ps=ps, n=n, t0=t0: e.tensor_tensor(out=a[0:nrows, 0:n], in0=ps[0:nrows, 0:n], in1=sin_t[0:nrows, t0:t0 + n], op=ALU.mult), reads=[pk, tabk], writes=[ak])
                P.op("pool", lambda e, b_=b_, n=n, t0=t0: e.tensor_tensor(out=b_[0:nrows, 0:n], in0=src_bf[0:nrows, t0:t0 + n], in1=cos_t[0:nrows, t0:t0 + n], op=ALU.mult), reads=[src_key, tabk], writes=[bk])
                P.op("dve", lambda e, a=a, b_=b_, n=n, t0=t0: e.tensor_tensor(out=dst[0:nrows, t0:t0 + n], in0=a[0:nrows, 0:n], in1=b_[0:nrows, 0:n], op=ALU.add), reads=[ak, bk], writes=[dst_key_of(bi)])

        def phase_mla(l, ctx_out):
            with contextlib.ExitStack() as st:
                sbm = lambda name, shape, dt: bld.sb(st, name, shape, dt)
                wrot = bld.rot(st, "mw", [128, KD, 128], BF16, 3)
                cqn = sbm("cqn", [128, 3, T], BF16)
                ckvn = sbm("ckvn", [128, 2, T], BF16)
                krr = sbm("krr", [64, T], BF16)
                cosm = sbm("cosm", [64, T], F32)
                sinm = sbm("sinm", [64, T], F32)
                wuq = sbm("wuq", [128, 3, 768], BF16)
                wukv = sbm("wukv", [128, 2, 1024], BF16)
                tA = bld.rot(st, "mtA", [128, 512], F32, 2)
                tB = bld.rot(st, "mtB", [128, 512], F32, 2)
                qnsc = sbm("qnsc", [128, 5], F32)
                st1 = contextlib.ExitStack()
                cqraw = bld.sb(st1, "cqraw", [128, 3, T], F32)
                ckvraw = bld.sb(st1, "ckvraw", [128, 2, T], F32)
                krb = bld.sb(st1, "krb", [64, T], BF16)
                sqr = bld.rot(st1, "msqr", [128, 512], F32R, 2)
                rnb = bld.rot(st1, "mrnb", [128, 512], F32, 2)
                P.dma("sp", cosm[:], ropem[0], writes=["ropem"])
                P.dma("sp", sinm[:], ropem[1], writes=["ropem2"])
                P.dma("pool", wuq[:], w_uq[l].rearrange("(k p) c -> p k c", p=128), writes=["wuq"])
                P.dma("pool", wukv[:], w_ukv[l].rearrange("(k p) c -> p k c", p=128), writes=["wukv"])
                for k in range(3):
                    P.op("dve", lambda e, k=k: e.tensor_scalar(out=qnsc[:, k:k + 1], in0=qn(l, k), scalar1=float(math.sqrt(384.0)), scalar2=None, op0=ALU.mult), reads=["spp"], writes=["qnsc"])
                for k in range(2):
                    P.op("dve", lambda e, k=k: e.tensor_scalar(out=qnsc[:, 3 + k:4 + k], in0=kvn(l, k), scalar1=float(math.sqrt(256.0)), scalar2=None, op0=ALU.mult), reads=["spp"], writes=["qnsc"])
                for (c0, nch, rawt, nm) in ((CQ, 3, cqraw, "cqraw"), (CKV, 2, ckvraw, "ckvraw")):
                    for ch in range(nch):
                        wt, wk = load_w(wrot, w_in[l][:, c0 + ch * 128: c0 + (ch + 1) * 128])
                        def ev(bi, t0, n, ps, pk, rawt=rawt, ch=ch, nm=nm):
                            if bi % 2:
                                P.op("act", lambda e: e.copy(rawt[:, ch, t0:t0 + n], ps[:, 0:n]), reads=[pk], writes=[(nm, ch, bi)])
                            else:
                                P.op("dve", lambda e: e.tensor_copy(rawt[:, ch, t0:t0 + n], ps[:, 0:n]), reads=[pk], writes=[(nm, ch, bi)])
                        proj_fm(wt, wk, hT_rhs, KD, ev)
                wt, wk = load_w(wrot, w_in[l][:, KR:KR + 128])
                def evk(bi, t0, n, ps, pk):
                    P.op("act", lambda e: e.copy(krb[:, t0:t0 + n], ps[0:64, 0:n]), reads=[pk], writes=[("krb", bi)])
                proj_fm(wt, wk, hT_rhs, KD, evk, m=64)
                for bi, (t0, n) in enumerate(BLKS):
                    ps, pk = PSB.get()
                    bld.mm(ps[0:64, 0:n], [(rmb[0:64, 128:192], krb[:, t0:t0 + n])], [("krb", bi), "rmb"], pk)
                    a, ak = tA.get()
                    b_, bk = tB.get()
                    P.op("dve", lambda e, a=a, ps=ps, n=n, t0=t0: e.tensor_tensor(out=a[0:64, 0:n], in0=ps[0:64, 0:n], in1=sinm[:, t0:t0 + n], op=ALU.mult), reads=[pk, "ropem2"], writes=[ak])
                    P.op("pool", lambda e, b_=b_, n=n, t0=t0: e.tensor_tensor(out=b_[0:64, 0:n], in0=krb[:, t0:t0 + n], in1=cosm[:, t0:t0 + n], op=ALU.mult), reads=[("krb", bi), "ropem"], writes=[bk])
                    P.op("dve", lambda e, a=a, b_=b_, n=n, t0=t0: e.tensor_tensor(out=krr[:, t0:t0 + n], in0=a[0:64, 0:n], in1=b_[0:64, 0:n], op=ALU.add), reads=[ak, bk], writes=[("krr", bi)])
                for (nch, rawt, nm, dstn, dnm, eps_i, q0) in ((3, cqraw, "cqraw", cqn, "cqn", 3, 0), (2, ckvraw, "ckvraw", ckvn, "ckvn", 4, 3)):
                    for bi, (t0, n) in enumerate(BLKS):
                        ps, pk = PSB.get()
                        for ch in range(nch):
                            sq, sqk = sqr.get()
                            P.op("act", lambda e, sq=sq, ch=ch, t0=t0, n=n, rawt=rawt: e.activation(sq[:, 0:n], rawt[:, ch, t0:t0 + n], AF.Square), reads=[(nm, ch, bi)], writes=[sqk])
                            bld.mm_acc(ps[:, 0:n], onesr[:], sq[:, 0:n], ch == 0, ch == nch - 1, [sqk, "onesr"], pk)
                        rn, rnk = rnb.get()
                        P.op("act", lambda e, rn=rn, ps=ps, n=n, eps_i=eps_i: e.activation(rn[:, 0:n], ps[:, 0:n], AF.Sqrt, bias=epsc(eps_i)), reads=[pk, "c32"], writes=[rnk])
                        P.op("dve", lambda e, rn=rn, n=n: e.reciprocal(rn[:, 0:n], rn[:, 0:n]), reads=[rnk], writes=[rnk])
                        for ch in range(nch):
                            P.op("dve", lambda e, ch=ch, rn=rn, t0=t0, n=n, rawt=rawt, dstn=dstn, q0=q0: e.scalar_tensor_tensor(out=dstn[:, ch, t0:t0 + n], in0=rawt[:, ch, t0:t0 + n], scalar=qnsc[:, q0 + ch:q0 + ch + 1], in1=rn[:, 0:n], op0=ALU.mult, op1=ALU.mult),
                                 reads=[(nm, ch, bi), rnk, "qnsc"], writes=[(dnm, bi)])
                P.barrier()
                st1.close()
                qnope = sbm("qnope", [128, T], BF16)
                qrb = sbm("qrb", [64, T], BF16)
                qrr = sbm("qrr", [64, T], BF16)
                knope = sbm("knope", [128, T], BF16)
                vtok = sbm("mvtok", [128, NT, 128], BF16)
                oTh = sbm("moTh", [128, T], BF16)
                pT = bld.rot(st, "mpT", [128, 512], BF16, 3)
                rden = bld.rot(st, "mrden", [128, 512], F32, 2)
                cqk = lambda: [("cqn", bi) for bi in range(5)]
                ckk = lambda: [("ckvn", bi) for bi in range(5)]
                for h in range(4):
                    for bi, (t0, n) in enumerate(BLKS):
                        ps, pk = PSB.get()
                        bld.mm(ps[:, 0:n], [(wuq[:, k, h * 192:h * 192 + 128], cqn[:, k, t0:t0 + n]) for k in range(3)], ["wuq", ("cqn", bi)], pk)
                        P.op("act", lambda e, ps=ps, t0=t0, n=n: e.activation(qnope[:, t0:t0 + n], ps[:, 0:n], AF.Copy, scale=float(MLA_SCALE)), reads=[pk], writes=[("qnope", bi)])
                        ps, pk = PSB.get()
                        bld.mm(ps[0:64, 0:n], [(wuq[:, k, h * 192 + 128:h * 192 + 192], cqn[:, k, t0:t0 + n]) for k in range(3)], ["wuq", ("cqn", bi)], pk)
                        P.op("act", lambda e, ps=ps, t0=t0, n=n: e.activation(qrb[:, t0:t0 + n], ps[0:64, 0:n], AF.Copy, scale=float(MLA_SCALE)), reads=[pk], writes=[("qrb", bi)])
                        ps, pk = PSB.get()
                        bld.mm(ps[:, 0:n], [(wukv[:, k, h * 256:h * 256 + 128], ckvn[:, k, t0:t0 + n]) for k in range(2)], ["wukv", ("ckvn", bi)], pk)
                        P.op("dve", lambda e, ps=ps, t0=t0, n=n: e.tensor_copy(knope[:, t0:t0 + n], ps[:, 0:n]), reads=[pk], writes=[("knope", bi)])
                        ps, pk = PSB.get()
                        bld.mm(ps[0:64, 0:n], [(rmb[0:64, 128:192], qrb[:, t0:t0 + n])], [("qrb", bi), "rmb"], pk)
                        a, ak = tA.get()
                        b_, bk = tB.get()
                        P.op("dve", lambda e, a=a, ps=ps, n=n, t0=t0: e.tensor_tensor(out=a[0:64, 0:n], in0=ps[0:64, 0:n], in1=sinm[:, t0:t0 + n], op=ALU.mult), reads=[pk, "ropem2"], writes=[ak])
                        P.op("pool", lambda e, b_=b_, n=n, t0=t0: e.tensor_tensor(out=b_[0:64, 0:n], in0=qrb[:, t0:t0 + n], in1=cosm[:, t0:t0 + n], op=ALU.mult), reads=[("qrb", bi), "ropem"], writes=[bk])
                        P.op("dve", lambda e, a=a, b_=b_, n=n, t0=t0: e.tensor_tensor(out=qrr[:, t0:t0 + n], in0=a[0:64, 0:n], in1=b_[0:64, 0:n], op=ALU.add), reads=[ak, bk], writes=[("qrr", bi)])
                    for t in ALLT:
                        ps, pk = PSB.get()
                        bld.mm(ps[:, 0:128], [(ckvn[:, k, t * 128:(t + 1) * 128], wukv[:, k, h * 256 + 128:h * 256 + 256]) for k in range(2)], ["wukv", ("ckvn", min(t // 4, 4))], pk)
                        P.op("act", lambda e, t=t, ps=ps: e.copy(vtok[:, t, :], ps[:, 0:128]), reads=[pk], writes=[("mvtok", t)])
                    qgroups = [(256 + g * 512, 512, ALLT) for g in range(4)]
                    if ctx_out:
                        qgroups = [(0, 256, [0, 1])] + qgroups
                    for (q0, nq, ktiles) in qgroups:
                        qb = min(q0 // 512, 4)
                        qbs = sorted(set([min(q0 // 512, 4), min((q0 + nq - 1) // 512, 4)]))
                        o_ps, o_k = banks[6], "bank6"
                        d_ps, d_k = banks[7], "psx"
                        def s_mm(kt):
                            kb = min(kt // 4, 4)
                            ps, pk = PSB.get()
                            bld.mm(ps[:, 0:nq], [(knope[:, kt * 128:(kt + 1) * 128], qnope[:, q0:q0 + nq]), (krr[:, kt * 128:(kt + 1) * 128], qrr[:, q0:q0 + nq])],
                                   [("knope", kb), ("krr", kb)] + [("qnope", b) for b in qbs] + [("qrr", b) for b in qbs], pk)
                            return ps, pk
                        pend = [s_mm(kt) for kt in ktiles[:2]]
                        for i, kt in enumerate(ktiles):
                            ps, pk = pend.pop(0)
                            p_, pkk = pT.get()
                            P.op("act", lambda e, p_=p_, ps=ps, nq=nq: e.activation(p_[:, 0:nq], ps[:, 0:nq], AF.Exp), reads=[pk], writes=[pkk])
                            if i + 2 < len(ktiles):
                                pend.append(s_mm(ktiles[i + 2]))
                            bld.mm_acc(o_ps[:, 0:nq], vtok[:, kt, :], p_[:, 0:nq], i == 0, i == len(ktiles) - 1, [("mvtok", kt), pkk], o_k)
                            bld.mm_acc(d_ps[:, 0:nq], onesb[:], p_[:, 0:nq], i == 0, i == len(ktiles) - 1, ["onesb", pkk], d_k)
                        rd, rdk = rden.get()
                        P.op("dve", lambda e, rd=rd, nq=nq: e.reciprocal(rd[:, 0:nq], d_ps[:, 0:nq]), reads=[d_k], writes=[rdk])
                        P.op("dve", lambda e, rd=rd, nq=nq, q0=q0: e.tensor_tensor(out=oTh[:, q0:q0 + nq], in0=o_ps[:, 0:nq], in1=rd[:, 0:nq], op=ALU.mult), reads=[o_k, rdk], writes=[("moTh", q0)])
                    c0 = 0 if ctx_out else 256
                    P.dma("sp", oT_s[1][h][:, c0:T], oTh[:, c0:T], reads=[("moTh", q) for q in ([0] if ctx_out else []) + [256 + g * 512 for g in range(4)]], writes=[("oTs", 1, h)])
                P.barrier()
        def phase_ret(l, tiles):
            with contextlib.ExitStack() as st:
                sbm = lambda name, shape, dt: bld.sb(st, name, shape, dt)
                wrot = bld.rot(st, "rw", [128, KD, 128], BF16, 3)
                cosr = sbm("cosr", [128, T], F32)
                sinr = sbm("sinr", [128, T], F32)
                rcs = sbm("rcs", [128, 8 * 128 * 2 + 8], F32)
                P.dma("sp", cosr[:], roper[0], writes=["roper"])
                P.dma("sp", sinr[:], roper[1], writes=["roper2"])
                P.dma("sp", rcs[:], retc, writes=["rcs"])
                DTm = lambda q: rcs[:, q * 128:(q + 1) * 128]
                GWm = lambda q: rcs[:, 1024 + q * 128:1024 + (q + 1) * 128]
                kwc = lambda q: rcs[:, 2048 + q:2049 + q]
                rawb = bld.rot(st, "rrawb", [128, T], BF16, 2)
                qT = sbm("rqT", [128, T], BF16)
                kT = sbm("rkT", [128, T], BF16)
                ktok = sbm("rktok", [128, NT, 128], BF16)
                vtok = sbm("rvtok", [128, NT, 128], BF16)
                zs = sbm("rzs", [128, NT, 128], F32)
                tA = bld.rot(st, "rtA", [128, 512], F32, 2)
                tB = bld.rot(st, "rtB", [128, 512], F32, 2)
                atb = bld.rot(st, "r_at", [128, 128], BF16, 5)
                qwb = bld.rot(st, "r_qw", [128, 128], BF16, 5)
                kwb = bld.rot(st, "r_kw", [128, 128], BF16, 5)
                for h in range(4):
                    with contextlib.ExitStack() as sh:
                        Oacc = bld.sb(sh, "rO", [128, NT, 128], F32)
                        for fi, (c0, dst, scl) in enumerate(((RQ, qT, float(128 ** -0.5)), (RK, kT, 1.0))):
                            wt, wk = load_w(wrot, w_in[l][:, c0 + h * 128:c0 + (h + 1) * 128])
                            rb_, rbk = rawb.get()
                            def ev(bi, t0, n, ps, pk, rb_=rb_, rbk=rbk, scl=scl):
                                P.op("act", lambda e: e.activation(rb_[:, t0:t0 + n], ps[:, 0:n], AF.Copy, scale=scl), reads=[pk], writes=[(rbk, bi)])
                            proj_fm(wt, wk, hT_rhs, KD, ev)
                            for bi, (t0, n) in enumerate(BLKS):
                                ps, pk = PSB.get()
                                bld.mm(ps[:, 0:n], [(rmb[:, 0:128], rb_[:, t0:t0 + n])], [(rbk, bi), "rmb"], pk)
                                a, ak = tA.get()
                                b_, bk = tB.get()
                                P.op("dve", lambda e, a=a, ps=ps, n=n, t0=t0: e.tensor_tensor(out=a[:, 0:n], in0=ps[:, 0:n], in1=sinr[:, t0:t0 + n], op=ALU.mult), reads=[pk, "roper2"], writes=[ak])
                                P.op("pool", lambda e, b_=b_, n=n, t0=t0, rb_=rb_: e.tensor_tensor(out=b_[:, 0:n], in0=rb_[:, t0:t0 + n], in1=cosr[:, t0:t0 + n], op=ALU.mult), reads=[(rbk, bi), "roper"], writes=[bk])
                                P.op("dve", lambda e, a=a, b_=b_, n=n, t0=t0, dst=dst: e.tensor_tensor(out=dst[:, t0:t0 + n], in0=a[:, 0:n], in1=b_[:, 0:n], op=ALU.add), reads=[ak, bk], writes=[("rT", fi, bi)])
                        rTk = lambda fi: [("rT", fi, bi) for bi in range(5)]
                        for g0 in range(0, NT, 4):
                            g = list(range(g0, min(g0 + 4, NT)))
                            ps, pk = PSB.get()
                            psv = ps[:].bitcast(BF16)
                            for i, t in enumerate(g):
                                bld.tr(psv[:, i * 128:(i + 1) * 128], kT[:, t * 128:(t + 1) * 128], identb[:], [("rT", 1, min(t // 4, 4)), "identb"], pk)
                            n = len(g) * 128
                            P.op("dve", lambda e, g=g, n=n, psv=psv: e.tensor_copy(ktok[:, g[0]:g[0] + len(g), :], psv[:, 0:n].rearrange("p (t c) -> p t c", c=128)), reads=[pk], writes=[("rktok", t) for t in g])
                        wt, wk = load_w(wrot, w_in[l][:, RV + h * 128:RV + (h + 1) * 128])
                        for t in ALLT:
                            ps, pk = PSS.get()
                            bld.mm(ps, [(HT[:, k, t * 128:(t + 1) * 128], wt[:, k, :]) for k in range(KD)], [wk, ("HT", t)], pk)
                            P.op("act", lambda e, t=t, ps=ps: e.copy(vtok[:, t, :], ps), reads=[pk], writes=[("rvtok", t)])
                        wt, wk = load_w(wrot, w_in[l][:, RG + h * 128:RG + (h + 1) * 128])
                        for t in tiles:
                            ps, pk = PSS.get()
                            bld.mm(ps, [(HT[:, k, t * 128:(t + 1) * 128], wt[:, k, :]) for k in range(KD)], [wk, ("HT", t)], pk)
                            P.op("act", lambda e, t=t, ps=ps: e.activation(zs[:, t, :], ps, AF.Silu), reads=[pk], writes=[("rzs", t)])
                            P.op("pool", lambda e, t=t, h=h: e.tensor_tensor(out=zs[:, t, :], in0=zs[:, t, :], in1=rnw(l, h), op=ALU.mult), reads=[("rzs", t), "rws"], writes=[("rzs", t)])
                        S32 = [bld.sb(sh, "rS32_%d" % d, [128, 128], F32) for d in range(2)]
                        Sb = [bld.rot(sh, "rSb%d" % d, [128, 128], BF16, 2) for d in range(2)]
                        order = [list(range(NT)), [1, 0] + list(range(NT - 1, 1, -1))]
                        cur = [None, None]
                        for d in range(2):
                            P.op("pool", lambda e, d=d, S32=S32: e.memset(S32[d][:], 0.0), writes=[("rS32", d)])
                            sbt, sbk = Sb[d].get()
                            P.op("pool", lambda e, sbt=sbt: e.memset(sbt[:], 0.0), writes=[sbk])
                            cur[d] = (sbt, sbk)
                        visited = set()
                        atA = bld.sb(sh, "r_atA", [128, 2 * NT, 128], BF16)
                        qwA = bld.sb(sh, "r_qwA", [128, 2 * NT, 128], BF16)
                        kwA = bld.sb(sh, "r_kwA", [128, 2 * NT, 128], BF16)
                        for c in ALLT:
                            tcol = slice(c * 128, (c + 1) * 128)
                            cb = min(c // 4, 4)
                            psQ, kQ = PSS.get()
                            bld.mm(psQ, [(kT[:, tcol], qT[:, tcol])], [("rT", 0, cb), ("rT", 1, cb)], kQ)
                            for d in range(2):
                                q = d * 4 + h
                                ix = d * NT + c
                                P.op("dve", lambda e, psQ=psQ, q=q, ix=ix, atA=atA: e.tensor_tensor(out=atA[:, ix, :], in0=psQ, in1=DTm(q), op=ALU.mult), reads=[kQ, "rcs"], writes=[("r_at", ix)])
                                P.op("dve" if d == 0 else "pool", lambda e, tcol=tcol, q=q, ix=ix, qwA=qwA: e.tensor_tensor(out=qwA[:, ix, :], in0=qT[:, tcol], in1=GWm(q), op=ALU.mult), reads=[("rT", 0, cb), "rcs"], writes=[("r_qw", ix)])
                                P.op("act", lambda e, c=c, q=q, ix=ix, kwA=kwA: e.activation(kwA[:, ix, :], ktok[:, c, :], AF.Copy, scale=kwc(q)), reads=[("rktok", c), "rcs"], writes=[("r_kw", ix)])
                        for s_ in range(NT):
                            for d in range(2):
                                c = order[d][s_]
                                q = d * 4 + h
                                ix = d * NT + c
                                sbt, sbk = cur[d]
                                pso, ko = PSS.get()
                                bld.mm(pso, [(qwA[:, ix, :], sbt[:]), (atA[:, ix, :], vtok[:, c, :])], [("r_qw", ix), sbk, ("r_at", ix), ("rvtok", c)], ko)
                                pss_, ks = PSS.get()
                                bld.mm(pss_, [(kwA[:, ix, :], vtok[:, c, :])], [("r_kw", ix), ("rvtok", c)], ks)
                                P.op("dve", lambda e, d=d, q=q, pss_=pss_, S32=S32: e.scalar_tensor_tensor(out=S32[d][:], in0=S32[d][:], scalar=float(RET_CDEC[q]), in1=pss_, op0=ALU.mult, op1=ALU.add), reads=[("rS32", d), ks], writes=[("rS32", d)])
                                nsb, nsbk = Sb[d].get()
                                P.op("act", lambda e, nsb=nsb, d=d, S32=S32: e.copy(nsb[:], S32[d][:]), reads=[("rS32", d)], writes=[nsbk])
                                cur[d] = (nsb, nsbk)
                                if c in visited:
                                    P.op("dve", lambda e, c=c, pso=pso, Oacc=Oacc: e.tensor_tensor(out=Oacc[:, c, :], in0=pso, in1=Oacc[:, c, :], op=ALU.add), reads=[ko, ("rO", c)], writes=[("rO", c)])
                                else:
                                    visited.add(c)
                                    P.op("act", lambda e, c=c, pso=pso, Oacc=Oacc: e.copy(Oacc[:, c, :], pso), reads=[ko], writes=[("rO", c)])
                        head_out_norm(sh, l, h, Oacc, zs, None, 2, tiles, "r")
                    P.barrier()
        def phase_gates(l, tiles):
            with contextlib.ExitStack() as st:
                wrot = bld.rot(st, "gtw", [128, KD, 512], BF16, 2)
                gb = bld.rot(st, "gtb", [128, 512], BF16, 4)
                for cb in range(6):
                    wt, wk = load_w(wrot, w_in[l][:, GATE + cb * 512:GATE + (cb + 1) * 512])
                    for t in tiles:
                        ps, pk = PSB.get()
                        bld.mm(ps[:], [(HT[:, k, t * 128:(t + 1) * 128], wt[:, k, :]) for k in range(KD)], [wk, ("HT", t)], pk)
                        g, gk = gb.get()
                        P.op("act", lambda e, g=g, ps=ps: e.activation(g[:], ps[:], AF.Sigmoid), reads=[pk], writes=[gk])
                        P.dma("sp", gates_s[t * 128:(t + 1) * 128, cb * 512:(cb + 1) * 512], g[:], reads=[gk], writes=[("gates", t, cb)])
                P.barrier()

        def phase_merge(l, tiles, xsrc):
            stw = contextlib.ExitStack()
            wo = bld.sb(stw, "wo", [128, KD, D], BF16)
            P.dma("pool", wo[:], w_out[l].rearrange("(k p) c -> p k c", p=128), writes=["wo"])
            with contextlib.ExitStack() as st:
                wbr = [bld.sb(st, "wbr%d" % b, [128, 4, D], BF16) for b in range(3)]
                for b in range(3):
                    P.dma("pool", wbr[b][:], w_br[b][l].rearrange("(k p) c -> p k c", p=128), writes=[("wbr", b)])
                gt = bld.rot(st, "mgt", [128, 3 * D], BF16, 2)
                ot = bld.rot(st, "mot", [128, 3, 4, 128], BF16, 2)
                t32 = bld.rot(st, "mt32", [128, 512], F32, 4)
                mb = bld.rot(st, "mmb", [128, D], BF16, 2)
                for t in tiles:
                    g, gk = gt.get()
                    P.dma("sp", g[:], gates_s[t * 128:(t + 1) * 128, :], reads=[("gates", t, cb) for cb in range(6)], writes=[gk])
                    o_, ok_ = ot.get()
                    for b in range(3):
                        P.dma("sp", o_[:, b, :, :], oT_s[b][:, :, t * 128:(t + 1) * 128].rearrange("h p c -> p h c"), reads=[("oTs", b, h) for h in range(4)], writes=[(ok_, b)])
                    m, mk = mb.get()
                    for half in range(2):
                        acc, acck = t32.get()
                        for b in range(3):
                            ps, pk = PSB.get()
                            bld.mm(ps[:], [(o_[:, b, k, :], wbr[b][:, k, half * 512:(half + 1) * 512]) for k in range(4)], [(ok_, b), ("wbr", b)], pk)
                            gsl = g[:, b * D + half * 512: b * D + (half + 1) * 512]
                            if b == 0:
                                P.op("dve", lambda e, acc=acc, ps=ps, gsl=gsl: e.tensor_tensor(out=acc[:], in0=ps[:], in1=gsl, op=ALU.mult), reads=[pk, gk], writes=[acck])
                            else:
                                tmp, tk = t32.get()
                                P.op("dve", lambda e, tmp=tmp, ps=ps, gsl=gsl: e.tensor_tensor(out=tmp[:], in0=ps[:], in1=gsl, op=ALU.mult), reads=[pk, gk], writes=[tk])
                                if b == 1:
                                    P.op("pool", lambda e, acc=acc, tmp=tmp: e.tensor_tensor(out=acc[:], in0=acc[:], in1=tmp[:], op=ALU.add), reads=[acck, tk], writes=[acck])
                                else:
                                    P.op("pool", lambda e, acc=acc, tmp=tmp, m=m, half=half: e.tensor_tensor(out=m[:, half * 512:(half + 1) * 512], in0=acc[:], in1=tmp[:], op=ALU.add), reads=[acck, tk], writes=[(mk, half)])
                    ps, pk = PSB.get()
                    psv = ps[:].bitcast(BF16)
                    for k in range(KD):
                        bld.tr(psv[:, k * 128:(k + 1) * 128], m[:, k * 128:(k + 1) * 128], identb[:], [(mk, 0), (mk, 1), "identb"], pk)
                    P.op("act", lambda e, t=t, psv=psv: e.copy(HT[:, :, t * 128:(t + 1) * 128], psv[:, 0:1024].rearrange("p (k c) -> p k c", c=128)), reads=[pk], writes=[("HT", t)])
                P.barrier()
            with contextlib.ExitStack() as st:
                residual_phase(st, l, tiles, xsrc, 0, lambda t, half: [(HT[:, k, t * 128:(t + 1) * 128], wo[:, k, half * 512:(half + 1) * 512]) for k in range(KD)],
                               lambda t: [("HT", t), "wo"])
                P.barrier()
            stw.close()

        def residual_phase(st, l, tiles, xsrc, which, pairs_of, reads_of):
            xb = bld.rot(st, "rx", [128, D], F32, 3)
            yb = bld.rot(st, "ry", [128, D], F32, 2)
            for t in tiles:
                s = tile_stream(t)
                xt, xk = xb.get()
                P.dma("sp", xt[:], xsrc[t * 128:(t + 1) * 128, :], reads=[("xs", t)], writes=[xk])
                y, yk = yb.get()
                for half in range(2):
                    ps, pk = PSB.get()
                    bld.mm(ps[:], pairs_of(t, half), reads_of(t), pk)
                    sl = slice(half * 512, (half + 1) * 512)
                    P.op("dve", lambda e, y=y, ps=ps, sl=sl, s=s: e.tensor_tensor(out=y[:, sl], in0=ps[:], in1=grow[:, s, which, sl], op=ALU.mult),
                         reads=[pk] + [("grow", s, 2 if which == 0 else 5, h_) for h_ in range(2)], writes=[(yk, half)])
                    P.op("pool", lambda e, y=y, xt=xt, sl=sl: e.tensor_tensor(out=y[:, sl], in0=y[:, sl], in1=xt[:, sl], op=ALU.add), reads=[(yk, half), xk], writes=[(yk, half)])
                P.dma("pool", xs[t * 128:(t + 1) * 128, :], y[:], reads=[(yk, 0), (yk, 1)], writes=[("xs", t)])

        def phase_ffn(l, tiles):
            t_lo = tiles[0] * 128
            stw = contextlib.ExitStack()
            wd = bld.sb(stw, "wd", [128, NJ, D], BF16)
            with contextlib.ExitStack() as st:
                wrot = bld.rot(st, "fw", [128, KD, 512], BF16, 3)
                wcur = {}
                W = T + 3
                off = lambda t0: t0 + 1 if t0 < 256 else t0 + 2
                raw = bld.rot(st, "fraw", [128, W], F32, 3)
                cv = bld.rot(st, "fcv", [128, W], F32, 2)
                aT = bld.rot(st, "faT", [128, T], BF16, 2)
                blks = BLKS if t_lo == 0 else [(256 + i * 512, 512) for i in range(4)]
                for rw_ in raw.tiles:
                    P.op("pool", lambda e, rw_=rw_: e.memset(rw_[:], 0.0), writes=[("fraw", raw.tiles.index(rw_))])
                for j in range(NJ):
                    cvs = []
                    for gi, c0 in enumerate((j * 128, DFF + j * 128)):
                        ch = c0 // 128
                        if j % 4 == 0:
                            ng = min(4, NJ - j)
                            wtf, wk = wrot.get()
                            P.dma("pool", wtf[:, :, 0:ng * 128], w_up[l][:, c0:c0 + ng * 128].rearrange("(k p) c -> p k c", p=128), writes=[wk])
                            wcur[gi] = (wtf, wk)
                        wt, wk = wcur[gi]
                        rw, rk = raw.get()
                        def ev(bi, t0, n, ps, pk, rw=rw, rk=rk):
                            if t0 == 0:
                                P.op("act", lambda e: e.copy(rw[:, 1:257], ps[:, 0:256]), reads=[pk], writes=[rk])
                                P.op("dve", lambda e: e.tensor_copy(rw[:, 258:514], ps[:, 256:512]), reads=[pk], writes=[rk])
                            elif bi % 2:
                                P.op("act", lambda e: e.copy(rw[:, off(t0):off(t0) + n], ps[:, 0:n]), reads=[pk], writes=[rk])
                            else:
                                P.op("dve", lambda e: e.tensor_copy(rw[:, off(t0):off(t0) + n], ps[:, 0:n]), reads=[pk], writes=[rk])
                        proj_fm(wt, wk, hT_rhs, KD, ev, blks=blks, coff=(j % 4) * 128)
                        c, ck = cv.get()
                        lo = off(t_lo)
                        P.op("act", lambda e, c=c, rw=rw, ch=ch: e.activation(c[:, lo:W - 1], rw[:, lo:W - 1], AF.Identity, bias=fconvb(l, ch), scale=fconv(l, ch, 1)), reads=[rk, "spp"], writes=[ck])
                        P.op("dve", lambda e, c=c, rw=rw, ch=ch: e.scalar_tensor_tensor(out=c[:, lo:W - 1], in0=rw[:, lo - 1:W - 2], scalar=fconv(l, ch, 0), in1=c[:, lo:W - 1], op0=ALU.mult, op1=ALU.add), reads=[rk, ck, "spp"], writes=[ck])
                        P.op("dve", lambda e, c=c, rw=rw, ch=ch: e.scalar_tensor_tensor(out=c[:, lo:W - 1], in0=rw[:, lo + 1:W], scalar=fconv(l, ch, 2), in1=c[:, lo:W - 1], op0=ALU.mult, op1=ALU.add), reads=[rk, ck, "spp"], writes=[ck])
                        cvs.append((c, ck))
                    (cg, cgk), (cval, cvk) = cvs
                    P.op("act", lambda e, cg=cg: e.activation(cg[:, lo:W - 1], cg[:, lo:W - 1], AF.Silu), reads=[cgk], writes=[cgk])
                    a, ak = aT.get()
                    if t_lo == 0:
                        P.op("dve", lambda e, a=a, cg=cg, cval=cval: e.tensor_tensor(out=a[:, 0:256], in0=cg[:, 1:257], in1=cval[:, 1:257], op=ALU.mult), reads=[cgk, cvk], writes=[(ak, 0)])
                    P.op("dve", lambda e, a=a, cg=cg, cval=cval: e.tensor_tensor(out=a[:, 256:T], in0=cg[:, 258:W - 1], in1=cval[:, 258:W - 1], op=ALU.mult), reads=[cgk, cvk], writes=[(ak, 1)])
                    P.dma("sp", aT_s[tiles[0]:NT, :, j, :].rearrange("t p c -> p t c"), a[:, t_lo:T].rearrange("p (t c) -> p t c", c=128), reads=[(ak, 0), (ak, 1)], writes=[("aTs", j)])
                    P.dma("pool", wd[:, j, :], w_down[l][j * 128:(j + 1) * 128, :], writes=[("wd", j)])
                P.barrier()
            with contextlib.ExitStack() as st:
                ab_ = bld.rot(st, "fab", [128, NJ, 128], BF16, 2)
                cur = {}
                def pairs_of(t, half):
                    if half == 0:
                        a, ak = ab_.get()
                        P.dma("sp", a[:], aT_s[t], reads=[("aTs", j) for j in range(NJ)], writes=[ak])
                        cur[t] = (a, ak)
                    a, ak = cur[t]
                    return [(a[:, j, :], wd[:, j, half * 512:(half + 1) * 512]) for j in range(NJ)]
                residual_phase(st, l, tiles, xs, 1, pairs_of, lambda t: [cur[t][1]] + [("wd", j) for j in range(NJ)])
                P.barrier()
            stw.close()

        def phase_final():
            with contextlib.ExitStack() as st:
                xb = bld.rot(st, "fx", [128, D], F32, 3)
                junk = bld.sb(st, "fjunk", [128, D], BF16)
                ssb = bld.rot(st, "fss", [128, 1], F32, 4)
                ob = bld.rot(st, "fo", [128, D], F32, 3)
                for t in range(2, NT):
                    xt, xk = xb.get()
                    P.dma("sp", xt[:], xs[t * 128:(t + 1) * 128, :], reads=[("xs", t)], writes=[xk])
                    ss, sk = ssb.get()
                    P.op("act", lambda e, xt=xt, ss=ss: e.activation(junk[:], xt[:], AF.Square, accum_out=ss[:]), reads=[xk], writes=["fjunk", sk])
                    P.op("act", lambda e, ss=ss: e.activation(ss[:], ss[:], AF.Sqrt, bias=epsc(0)), reads=[sk, "c32"], writes=[sk])
                    P.op("dve", lambda e, ss=ss: e.reciprocal(ss[:], ss[:]), reads=[sk], writes=[sk])
                    P.op("dve", lambda e, ss=ss: e.tensor_scalar(out=ss[:], in0=ss[:], scalar1=float(math.sqrt(D)), scalar2=None, op0=ALU.mult), reads=[sk], writes=[sk])
                    o_, ok_ = ob.get()
                    P.op("dve", lambda e, o_=o_, xt=xt, ss=ss: e.scalar_tensor_tensor(out=o_[:], in0=xt[:], scalar=ss[:], in1=fnw, op0=ALU.mult, op1=ALU.mult), reads=[xk, sk, "rws"], writes=[ok_])
                    P.dma("pool", out[(t - 2) * 128:(t - 1) * 128, :], o_[:], reads=[ok_], writes=[("out", t)])
        P.barrier()
        for l in range(2):
            ctx_out = (l == 0)
            tiles = ALLT if ctx_out else list(range(2, NT))
            xsrc = xin if l == 0 else xs
            phase_mod(l)
            if stop_after == "mod":
                dump("modpp", modpp[:], ["modpp"])
                break
            phase_norm(l, 0, xsrc, ALLT)
            if l == 0:
                dump("HT", HT[:], HTk(ALLT))
                dump("modpp", modpp[:], ["modpp"])
                dump("grow", grow[:], [("grow", s_, v_, h_) for s_ in range(2) for v_ in (2, 5) for h_ in range(2)])
            if stop_after == "norm":
                break
            if stop_after in (None, "all", "gdn", "merge", "ffn"):
                phase_gdn(l, tiles)
                if l == 0:
                    dump("oTa", oT_s[0], [("oTs", 0, h_) for h_ in range(4)])
                if stop_after == "gdn":
                    break
            if stop_after in (None, "all", "mla", "merge", "ffn"):
                phase_mla(l, ctx_out)
                if l == 0:
                    dump("oTb", oT_s[1], [("oTs", 1, h_) for h_ in range(4)])
                if stop_after == "mla":
                    break
            if stop_after in (None, "all", "ret", "merge", "ffn"):
                phase_ret(l, tiles)
                if l == 0:
                    dump("oTc", oT_s[2], [("oTs", 2, h_) for h_ in range(4)])
                if stop_after == "ret":
                    break
            phase_gates(l, tiles)
            phase_merge(l, tiles, xsrc)
            if l == 0:
                dump("xmid", xs, [("xs", t_) for t_ in ALLT])
            if stop_after == "merge":
                break
            phase_norm(l, 1, xs, tiles)
            phase_ffn(l, tiles)
            if l == 0:
                dump("xl0", xs, [("xs", t_) for t_ in ALLT])
            if stop_after == "ffn":
                break
        if stop_after in (None, "all"):
            phase_final()
        P.emit(final_reads=[("out", t) for t in range(2, NT)] + bld.dbg_keys)
    return nc, P.stats


def _rope_tables(d):
    n = 2048
    rows_ = n // 64
    r = np.repeat(np.arange(rows_, dtype=np.float32), 64)
    col = np.tile(np.arange(64, dtype=np.float32), rows_)
    quarter = d // 4
    inv = (np.float32(10000.0) ** (-np.arange(quarter, dtype=np.float32) / np.float32(quarter))).astype(np.float32)
    ang = np.concatenate([r[:, None] * inv, col[:, None] * inv], axis=-1).astype(np.float32)
    cos = np.cos(ang).astype(np.float32)
    sin = np.sin(ang).astype(np.float32)
    C = np.ones((d, T), np.float32)
    S = np.zeros((d, T), np.float32)
    C[:, 256:] = np.concatenate([cos, cos], axis=1).T
    S[:, 256:] = np.concatenate([sin, sin], axis=1).T
    return np.stack([C, S])


def _rot_mat(d):
    R = np.zeros((d, d), np.float32)
    h = d // 2
    for m in range(h):
        R[m + h, m] = -1.0
    for m in range(h, d):
        R[m - h, m] = 1.0
    return R


def _host_consts():
    c = np.zeros((128, 128 * 6 + 8), np.float32)
    idx = np.arange(128)
    k = idx[:, None]
    i = idx[None, :]
    c[:, 0:128] = np.eye(128)
    c[:, 128:256] = 1.0
    c[:, 256:384] = (k <= i)
    c[:, 384:512] = (k >= i)
    c[:, 512:640] = np.where(i > k, 0.0, NEGBIG)
    c[:, 640:768] = np.where(i < k, 0.0, NEGBIG)
    c[0, 768] = 1.0
    c[:, 769] = 1024 * EPS; c[:, 770] = 128 * EPS; c[:, 771] = EPS; c[:, 772] = 384 * EPS; c[:, 773] = 256 * EPS
    rm = np.zeros((128, 192), np.float32)
    rm[:, 0:128] = _rot_mat(128)
    rm[0:64, 128:192] = _rot_mat(64)
    hh = np.arange(4, dtype=np.float64)
    lg = np.stack([np.log1p(-(2.0 ** (-(5.0 + hh + 0.5 * d)))) for d in range(2)])
    rc = np.zeros((128, 8 * 128 * 2 + 8), np.float64)
    jj = idx[:, None].astype(np.float64)
    ii = idx[None, :].astype(np.float64)
    cdec = []
    for d in range(2):
        for h in range(4):
            g = lg[d, h]
            q = d * 4 + h
            if d == 0:
                DT = np.where(ii >= jj, np.exp(g * np.maximum(ii - jj, 0)), 0.0)
                GW = np.exp(g * (ii + 1)) * np.ones((128, 1))
                kw = np.exp(g * (127 - idx))
            else:
                DT = np.where(ii <= jj, np.exp(g * np.maximum(jj - ii, 0)), 0.0)
                GW = np.exp(g * (128 - ii)) * np.ones((128, 1))
                kw = np.exp(g * idx)
            rc[:, q * 128:(q + 1) * 128] = DT
            rc[:, 1024 + q * 128: 1024 + (q + 1) * 128] = GW
            rc[:, 2048 + q] = kw
            cdec.append(float(np.exp(g * 128)))
    return c, rm, rc.astype(np.float32), cdec


RET_CDEC = _host_consts()[3]
_NC_CACHE = {}


def _prep_common(inp):
    f = lambda a: np.ascontiguousarray(np.asarray(a, dtype=np.float32))
    pp = lambda v, nch: v.reshape(nch, 128).T
    sm = []
    n1, n2 = f(inp["norm1_w"]), f(inp["norm2_w"])
    sm.append(np.concatenate([pp(n1[l], 8) for l in range(2)], axis=1))
    sm.append(np.concatenate([pp(n2[l], 8) for l in range(2)], axis=1))
    gc = f(inp["gdn_conv_w"])
    sm.append(np.concatenate([gc[l].reshape(3, 12, 128).transpose(2, 1, 0).reshape(128, 36) for l in range(2)], axis=1))
    fc = f(inp["ffn_conv_w"])
    sm.append(np.concatenate([fc[l].reshape(3, 44, 128).transpose(2, 1, 0).reshape(128, 132) for l in range(2)], axis=1))
    fb = f(inp["ffn_conv_b"])
    sm.append(np.concatenate([pp(fb[l], 44) for l in range(2)], axis=1))
    qn_, kvn_ = f(inp["mla_q_norm"]), f(inp["mla_kv_norm"])
    sm.append(np.concatenate([pp(qn_[l], 3) for l in range(2)], axis=1))
    sm.append(np.concatenate([pp(kvn_[l], 2) for l in range(2)], axis=1))
    smallpp = np.ascontiguousarray(np.concatenate(sm, axis=1))
    rep = lambda v: np.broadcast_to(v[None, :], (128, v.shape[0]))
    rw = [rep(f(inp["gdn_norm_w"]).reshape(-1)), rep(f(inp["ret_norm_w"]).reshape(-1)),
          rep(f(inp["gdn_A_log"]).reshape(-1)), rep(f(inp["gdn_dt_bias"]).reshape(-1)), rep(f(inp["final_norm_w"]))]
    rows = np.ascontiguousarray(np.concatenate(rw, axis=1))
    c, rm, rc, _ = _host_consts()
    common = dict(smallpp=smallpp, rows=rows, consts=c, rmats=rm, retc=rc,
                  ropem=_rope_tables(64), roper=_rope_tables(128))
    for k in ("ada_w", "ada_b", "w_in", "mla_w_uq", "mla_w_ukv", "w_br_gdn", "w_br_mla", "w_br_ret", "w_out",
              "ffn_w_up", "ffn_w_down"):
        common[k] = f(inp[k])
    return common


def _prep_core(inp, b):
    f = lambda a: np.ascontiguousarray(np.asarray(a, dtype=np.float32))
    xin = np.concatenate([f(inp["ctx"][b]), f(inp["x"][b])], axis=0)
    def crep_of(v):
        return np.broadcast_to(v.reshape(8, 128).T[:, :, None], (128, 8, 128)).reshape(128, 1024)
    crep = np.stack([crep_of(f(inp["c_ctx"])), crep_of(f(inp["c"][b]))])
    return dict(xin=np.ascontiguousarray(xin), crep=np.ascontiguousarray(crep))


def kernel(**inputs):
    if "nc" not in _NC_CACHE:
        _NC_CACHE["nc"] = build()[0]
    nc = _NC_CACHE["nc"]
    common = _prep_common(inputs)
    in_maps = []
    for b in range(NCORES):
        m = dict(common)
        m.update(_prep_core(inputs, b))
        in_maps.append(m)
    res = run_bass_kernel_spmd(nc, in_maps, core_ids=list(range(NCORES)))
    return np.stack([np.asarray(r["out"], dtype=np.float32) for r in res.results], axis=0)
```

```python
import contextlib
import math
import os
import numpy as np
import concourse.bass as bass
import concourse.mybir as mybir
from concourse.bass_utils import run_bass_kernel_spmd

F32 = mybir.dt.float32
F32R = mybir.dt.float32r
BF16 = mybir.dt.bfloat16
AF = mybir.ActivationFunctionType
ALU = mybir.AluOpType

NCORES = 8
T = 2304
NT = 18
D = 1024
KD = 8
DFF = 2816
NJ = 22
EPS = 1e-6
GQ, GK, GV, GZ, GAB, CQ, CKV, KR, RQ, RK, RV, RG, GATE = 0, 512, 1024, 1536, 2048, 2064, 2448, 2704, 2768, 3280, 3792, 4304, 4816
MLA_SCALE = 192 ** -0.5
NEGBIG = -30000.0
GDN_WARM = 2


class Prog:
    def __init__(self, nc, n_dma_sems=8):
        self.nc = nc
        self.ops = []
        self.last_w = {}
        self.readers = {}
        self.n_dma_sems = n_dma_sems
        self.warm = 0
        self.dummy = None

    def op(self, eng, fn, reads=(), writes=(), dma=False, barrier=False):
        idx = len(self.ops)
        deps = {}
        reads = list(reads)
        writes = list(writes)
        if barrier:
            writes.append("__phase")
        else:
            reads.append("__phase")
        for k in reads:
            w = self.last_w.get(k)
            if w is not None:
                deps[w] = "raw"
        for k in writes:
            w = self.last_w.get(k)
            if w is not None and w not in deps:
                deps[w] = "waw"
            for r in self.readers.get(k, ()):
                if r not in deps:
                    deps[r] = "war"
        for k in reads:
            self.readers.setdefault(k, []).append(idx)
        for k in writes:
            self.last_w[k] = idx
            self.readers[k] = []
        self.ops.append(dict(eng=eng, fn=fn, deps=deps, dma=dma, barrier=barrier, warm=(self.warm if eng == "pe" else 0)))
        return idx

    def barrier(self):
        self.op("dve", lambda e: e.nop(), barrier=True)

    def dma(self, q, out, in_, reads=(), writes=()):
        return self.op(q, lambda e: e.dma_start(out=out, in_=in_), reads, writes, dma=True)

    def emit(self, final_reads=()):
        nc = self.nc
        ops = self.ops
        self.op("sp", lambda e: e.nop(), reads=final_reads)
        n = len(ops)
        pos = [0] * n
        cnt = {}
        for i, o in enumerate(ops):
            c = cnt.get(o["eng"], 0)
            pos[i] = c
            cnt[o["eng"]] = c + 1
        waited_pos = {}
        waited_dma = {}
        need = [[] for _ in range(n)]
        signaling = [False] * n
        for i, o in enumerate(ops):
            E = o["eng"]
            for d in sorted(o["deps"]):
                kind = o["deps"][d]
                od = ops[d]
                F = od["eng"]
                if od["dma"]:
                    s = waited_dma.setdefault(E, set())
                    if d in s:
                        continue
                    s.add(d)
                    need[i].append(d)
                    signaling[d] = True
                else:
                    if F == E and not o["dma"] and not o["barrier"]:
                        if E == "pe":
                            continue
                    if pos[d] <= waited_pos.get((E, F), -1):
                        continue
                    waited_pos[(E, F)] = pos[d]
                    need[i].append(d)
                    signaling[d] = True
        engs = sorted(cnt.keys())
        self.stats = dict(cnt)
        with contextlib.ExitStack() as st:
            esem = {E: st.enter_context(nc.semaphore("s_" + E)) for E in engs}
            dsem = {}
            for E in engs:
                if any(o["dma"] and o["eng"] == E for o in ops):
                    dsem[E] = [st.enter_context(nc.semaphore("d_%s_%d" % (E, j))) for j in range(self.n_dma_sems)]
            ev = [None] * n
            ecount = {E: 0 for E in engs}
            dcount = {E: [0] * self.n_dma_sems for E in dsem}
            dnext = {E: 0 for E in dsem}
            for i, o in enumerate(ops):
                E = o["eng"]
                if o["dma"]:
                    j = dnext[E]
                    dnext[E] = (j + 1) % self.n_dma_sems
                    dcount[E][j] += 1
                    ev[i] = (dsem[E][j], 16 * dcount[E][j])
                    o["dslot"] = j
                    o["dprev"] = 16 * (dcount[E][j] - 1)
                elif signaling[i]:
                    ecount[E] += 1
                    ev[i] = (esem[E], ecount[E])
            for E in engs:
                assert ecount[E] < 60000, (E, ecount[E])
            blk = st.enter_context(nc.Block())
            handles = dict(pe=blk.tensor, act=blk.scalar, dve=blk.vector, pool=blk.gpsimd, sp=blk.sync)
            nw = [0]
            for E in engs:
                my = [i for i in range(n) if ops[i]["eng"] == E]

                def body(e, my=my, E=E):
                    dwaited = [0] * self.n_dma_sems
                    for i in my:
                        o = ops[i]
                        if o["warm"] and need[i]:
                            for _ in range(o["warm"]):
                                self.dummy(e)
                        for d in need[i]:
                            s, v = ev[d]
                            e.wait_ge(s, v)
                            nw[0] += 1
                        if o["dma"]:
                            j = o["dslot"]
                            if o["dprev"] > dwaited[j]:
                                e.wait_ge(dsem[E][j], o["dprev"])
                                dwaited[j] = o["dprev"]
                                nw[0] += 1
                        ins = o["fn"](e)
                        if ev[i] is not None:
                            s, v = ev[i]
                            ins.then_inc(s, 16 if o["dma"] else 1)
                handles[E](body)
            self.stats["waits"] = nw[0]
            self.stats["signals"] = dict(ecount)


class Rot:
    def __init__(self, name, tiles):
        self.name = name
        self.tiles = tiles
        self.i = 0

    def get(self):
        j = self.i % len(self.tiles)
        self.i += 1
        kf = getattr(self, "keyfn", None)
        return self.tiles[j], (kf(j) if kf else (self.name, j))


class B:
    def __init__(self, nc, dbg):
        self.nc = nc
        self.P = Prog(nc)
        self.dbg = dbg
        self.dbg_keys = []

    def sb(self, st, name, shape, dt):
        self.uid = getattr(self, "uid", 0) + 1
        return st.enter_context(self.nc.sbuf_tensor("%s_u%d" % (name, self.uid), list(shape), dt))

    def rot(self, st, name, shape, dt, n):
        return Rot(name, [self.sb(st, "%s%d" % (name, i), shape, dt) for i in range(n)])

    def mm(self, out, pairs, reads, wkey):
        def fn(e):
            m = len(pairs)
            ins = None
            for i, (l, r) in enumerate(pairs):
                ins = e.matmul(out, lhsT=l, rhs=r, start=(i == 0), stop=(i == m - 1))
            return ins
        self.P.op("pe", fn, reads=reads, writes=[wkey])

    def mm_acc(self, out, l, r, start, stop, reads, wkey):
        self.P.op("pe", lambda e: e.matmul(out, lhsT=l, rhs=r, start=start, stop=stop), reads=reads, writes=[wkey])

    def tr(self, out, in_, ident, reads, wkey):
        self.P.op("pe", lambda e: e.transpose(out, in_, ident), reads=reads, writes=[wkey])


def tile_stream(t):
    return 0 if t < 2 else 1


def build(dbg=None, stop_after=None):
    dbg = dbg or ()
    nc = bass.Bass("TRN2", target_bir_lowering=False)
    dram_in = lambda name, shape: nc.dram_tensor(name, list(shape), F32, kind="ExternalInput").ap()
    xin = dram_in("xin", [T, D])
    crep = dram_in("crep", [2, 128, KD * 128])
    ada_w = dram_in("ada_w", [2, D, 6 * D])
    ada_b = dram_in("ada_b", [2, 6 * D])
    w_in = dram_in("w_in", [2, D, 7888])
    w_uq = dram_in("mla_w_uq", [2, 384, 768])
    w_ukv = dram_in("mla_w_ukv", [2, 256, 1024])
    w_br = [dram_in(n, [2, 512, D]) for n in ("w_br_gdn", "w_br_mla", "w_br_ret")]
    w_out = dram_in("w_out", [2, D, D])
    w_up = dram_in("ffn_w_up", [2, D, 2 * DFF])
    w_down = dram_in("ffn_w_down", [2, DFF, D])
    NSM = 2 * 8 * 2 + 2 * 12 * 3 + 2 * 44 * 3 + 2 * 44 + 2 * 3 + 2 * 2
    smallpp = dram_in("smallpp", [128, NSM])
    NROW = 2 * 128 + 2 * 512 + 2 * 8 + 2 * 8 + 1024
    rows = dram_in("rows", [128, NROW])
    NC32 = 128 * 6 + 8
    consts = dram_in("consts", [128, NC32])
    ropem = dram_in("ropem", [2, 64, T])
    roper = dram_in("roper", [2, 128, T])
    rmats = dram_in("rmats", [128, 192])
    retc = dram_in("retc", [128, 8 * 128 * 2 + 8])
    out = nc.dram_tensor("out", [2048, D], F32, kind="ExternalOutput").ap()
    xs = nc.dram_tensor("xs", [T, D], F32).ap()
    oT_s = [nc.dram_tensor("oT%d" % b, [4, 128, T], BF16).ap() for b in range(3)]
    gates_s = nc.dram_tensor("gates_s", [T, 3 * D], BF16).ap()
    aT_s = nc.dram_tensor("aT_s", [NT, 128, NJ, 128], BF16).ap()
    dbg_out = {}
    for name, shape, dt in dbg:
        dbg_out[name] = nc.dram_tensor("dbg_" + name, list(shape), dt, kind="ExternalOutput").ap()

    bld = B(nc, dbg_out)
    P = bld.P
    with contextlib.ExitStack() as top:
        sb = lambda name, shape, dt, st=top: bld.sb(st, name, shape, dt)
        HT = sb("HT", [128, KD, T], BF16)
        c32 = sb("c32", [128, NC32], F32)
        ident32 = c32[:, 0:128]
        ones32 = c32[:, 128:256]
        Umask = [c32[:, 256:384], c32[:, 384:512]]
        NEGS = [c32[:, 512:640], c32[:, 640:768]]
        e0 = c32[:, 768:769]
        epsc = lambda i: c32[:, 769 + i:770 + i]
        identb = sb("identb", [128, 128], BF16)
        onesb = sb("onesb", [128, 128], BF16)
        onesr = sb("onesr", [128, 128], F32R)
        rm32 = sb("rm32", [128, 192], F32)
        rmb = sb("rmb", [128, 192], BF16)
        spp = sb("spp", [128, NSM], F32)
        rws = sb("rws", [128, NROW], F32)
        crs = sb("crs", [128, 2, KD * 128], F32)
        grow = sb("grow", [128, 2, 2, D], F32)
        modpp = sb("modpp", [128, 64], F32)
        AB = sb("ABpp", [128, 2, 2, 2, KD], F32)
        banks = [top.enter_context(nc.psum_tensor("bank%d" % i, [128, 512], F32)) for i in range(8)]
        PSB = Rot("psb", banks[0:3])
        jw = sb("jw", [128, 128], BF16)
        jr = sb("jr", [128, 512], BF16)
        P.op("pool", lambda e: e.memset(jw[:], 0.25), writes=["jw"])
        P.op("pool", lambda e: e.memset(jr[:], 0.5), writes=["jr"])
        P.dummy = lambda e: e.matmul(banks[3][:, 0:512], lhsT=jw[:], rhs=jr[:], start=True, stop=True)
        PSS = Rot("pss", [banks[4 + i % 3][:, ((i // 3) % 4) * 128:((i // 3) % 4 + 1) * 128] for i in range(12)])
        PSS.keyfn = lambda j: ("pssbank", j % 3)
        PSH = Rot("psh", [banks[4 + i % 3][:, ((i // 3) % 2) * 256:((i // 3) % 2 + 1) * 256] for i in range(6)])
        PSH.keyfn = lambda j: ("pssbank", j % 3)
        PSX = banks[7]

        o = 0
        def take(n):
            nonlocal o
            v = (o, o + n)
            o += n
            return v
        r_n1 = take(16); r_n2 = take(16); r_gc = take(72); r_fc = take(264); r_fb = take(88); r_qn = take(6); r_kvn = take(4)
        n1w = lambda l: spp[:, r_n1[0] + l * 8: r_n1[0] + l * 8 + 8]
        n2w = lambda l: spp[:, r_n2[0] + l * 8: r_n2[0] + l * 8 + 8]
        gconv = lambda l, ch, k: spp[:, r_gc[0] + (l * 12 + ch) * 3 + k: r_gc[0] + (l * 12 + ch) * 3 + k + 1]
        fconv = lambda l, ch, k: spp[:, r_fc[0] + (l * 44 + ch) * 3 + k: r_fc[0] + (l * 44 + ch) * 3 + k + 1]
        fconvb = lambda l, ch: spp[:, r_fb[0] + l * 44 + ch: r_fb[0] + l * 44 + ch + 1]
        qn = lambda l, k: spp[:, r_qn[0] + l * 3 + k: r_qn[0] + l * 3 + k + 1]
        kvn = lambda l, k: spp[:, r_kvn[0] + l * 2 + k: r_kvn[0] + l * 2 + k + 1]
        gnw = lambda l: rws[:, l * 128:(l + 1) * 128]
        rnw = lambda l, h: rws[:, 256 + l * 512 + h * 128: 256 + l * 512 + (h + 1) * 128]
        alog = lambda l: rws[:, 1280 + l * 8: 1280 + l * 8 + 8]
        dtb = lambda l: rws[:, 1296 + l * 8: 1296 + l * 8 + 8]
        fnw = rws[:, 1312:1312 + 1024]

        P.dma("sp", c32[:], consts, writes=["c32"])
        P.dma("sp", rm32[:], rmats, writes=["rm32"])
        P.dma("sp", spp[:], smallpp, writes=["spp"])
        P.dma("sp", rws[:], rows, writes=["rws"])
        for s in range(2):
            P.dma("sp", crs[:, s, :], crep[s], writes=[("crs", s)])
        P.op("dve", lambda e: e.tensor_copy(identb[:], ident32), reads=["c32"], writes=["identb"])
        P.op("dve", lambda e: e.tensor_copy(onesb[:], ones32), reads=["c32"], writes=["onesb"])
        P.op("dve", lambda e: e.tensor_copy(onesr[:], ones32), reads=["c32"], writes=["onesr"])
        P.op("dve", lambda e: e.tensor_copy(rmb[:], rm32[:]), reads=["rm32"], writes=["rmb"])
        for s in range(2):
            P.op("act", lambda e, s=s: e.activation(crs[:, s, :], crs[:, s, :], AF.Silu), reads=[("crs", s)], writes=[("crs", s)])
        cst = ["c32", "identb", "onesb", "onesr", "rmb", "spp", "rws"]

        def dump(name, src_ap, reads):
            if name in dbg_out:
                P.dma("sp", dbg_out[name], src_ap, reads=reads, writes=[("dbg", name)])
                bld.dbg_keys.append(("dbg", name))

        def phase_mod(l):
            with contextlib.ExitStack() as st:
                wbuf = bld.rot(st, "adaw", [128, KD, 512], F32, 2)
                bbuf = bld.rot(st, "adab", [1, 512], F32, 2)
                rowt = bld.rot(st, "modrow", [128, 512], F32, 2)
                pp_ps, pp_key = PSX, "psx"
                for nb in range(12):
                    wt, wk = wbuf.get()
                    bt, bk = bbuf.get()
                    P.dma("sp", wt[:], ada_w[l][:, nb * 512:(nb + 1) * 512].rearrange("(k p) c -> p k c", p=128), writes=[wk])
                    P.dma("sp", bt[:], ada_b[l:l + 1, nb * 512:(nb + 1) * 512], writes=[bk])
                    vec = nb // 2
                    half = nb % 2
                    for s in range(2):
                        ps, pk = PSB.get()
                        pairs = [(crs[:, s, k * 128:(k + 1) * 128], wt[:, k, :]) for k in range(KD)]
                        pairs.append((ones32[0:1, :], bt[0:1, :]))
                        bld.mm(ps[:], pairs, [wk, bk, ("crs", s), "c32"], pk)
                        if vec in (2, 5):
                            dst = grow[:, s, 0 if vec == 2 else 1, half * 512:(half + 1) * 512]
                            P.op("act", lambda e, dst=dst, ps=ps: e.copy(dst, ps[:]), reads=[pk], writes=[("grow", s, vec, half)])
                        else:
                            rt, rk = rowt.get()
                            P.op("dve", lambda e, rt=rt, ps=ps: e.tensor_copy(rt[:], ps[:]), reads=[pk], writes=[rk])
                            vi = {0: 0, 1: 1, 3: 2, 4: 3}[vec]
                            for c4 in range(4):
                                col = s * 32 + vi * 8 + half * 4 + c4
                                bld.mm(pp_ps[:, col:col + 1], [(rt[:, c4 * 128:(c4 + 1) * 128], e0)], [rk, "c32"], pp_key)
                P.op("dve", lambda e: e.tensor_copy(modpp[:], pp_ps[:, 0:64]), reads=[pp_key], writes=["modpp"])
                for s in range(2):
                    for nrm in range(2):
                        sh = modpp[:, s * 32 + (2 * nrm) * 8: s * 32 + (2 * nrm) * 8 + 8]
                        sc = modpp[:, s * 32 + (2 * nrm + 1) * 8: s * 32 + (2 * nrm + 1) * 8 + 8]
                        nw = n1w(l) if nrm == 0 else n2w(l)
                        P.op("dve", lambda e, sc=sc, nw=nw, s=s, nrm=nrm: e.scalar_tensor_tensor(out=AB[:, s, nrm, 0, :], in0=sc, scalar=1.0, in1=nw, op0=ALU.add, op1=ALU.mult),
                             reads=["modpp", "spp"], writes=[("AB", s, nrm, 0)])
                        P.op("dve", lambda e, s=s, nrm=nrm: e.tensor_scalar(out=AB[:, s, nrm, 0, :], in0=AB[:, s, nrm, 0, :], scalar1=float(math.sqrt(D)), scalar2=None, op0=ALU.mult),
                             reads=[("AB", s, nrm, 0)], writes=[("AB", s, nrm, 0)])
                        P.op("dve", lambda e, sh=sh, s=s, nrm=nrm: e.tensor_copy(AB[:, s, nrm, 1, :], sh), reads=["modpp"], writes=[("AB", s, nrm, 1)])
                P.barrier()

        def phase_norm(l, nrm, xsrc, tiles):
            with contextlib.ExitStack() as st:
                xb = bld.rot(st, "nx", [128, D], F32, 3)
                junk = bld.sb(st, "njunk", [128, D], BF16)
                xn = bld.rot(st, "nxn", [128, D], BF16, 8)
                ssb = bld.rot(st, "nss", [128, 1], F32, 8)
                groups = []
                cur = []
                for t in tiles:
                    cur.append(t)
                    if len(cur) == 4:
                        groups.append(cur); cur = []
                if cur:
                    groups.append(cur)
                for grp in groups:
                    xns = []
                    for t in grp:
                        xt, xk = xb.get()
                        P.dma("sp", xt[:], xsrc[t * 128:(t + 1) * 128, :], reads=[("xs", t)], writes=[xk])
                        ss, sk = ssb.get()
                        P.op("pool", lambda e, ss=ss: e.memset(ss[:], 0.0), writes=[sk])
                        P.op("dve", lambda e, xt=xt, ss=ss: e.scalar_tensor_tensor(out=junk[:], in0=xt[:], scalar=1.0, in1=xt[:], op0=ALU.mult, op1=ALU.mult, accum_out=ss[:]), reads=[xk, sk], writes=["njunk", sk])
                        P.op("act", lambda e, ss=ss: e.activation(ss[:], ss[:], AF.Sqrt, bias=epsc(0)), reads=[sk, "c32"], writes=[sk])
                        P.op("dve", lambda e, ss=ss: e.reciprocal(ss[:], ss[:]), reads=[sk], writes=[sk])
                        xnt, xnk = xn.get()
                        P.op("act", lambda e, xnt=xnt, xt=xt, ss=ss: e.activation(xnt[:], xt[:], AF.Copy, scale=ss[:]), reads=[xk, sk], writes=[xnk])
                        xns.append((t, xnt, xnk))
                    for k in range(KD):
                        ps, pk = PSB.get()
                        psv = ps[:].bitcast(BF16)
                        for i, (t, xnt, xnk) in enumerate(xns):
                            bld.tr(psv[:, i * 128:(i + 1) * 128], xnt[:, k * 128:(k + 1) * 128], identb[:], [xnk, "identb"], pk)
                        i = 0
                        while i < len(xns):
                            s = tile_stream(xns[i][0])
                            j = i
                            while j < len(xns) and tile_stream(xns[j][0]) == s:
                                j += 1
                            t0 = xns[i][0]
                            dst = HT[:, k, t0 * 128:(t0 + (j - i)) * 128]
                            src = psv[:, i * 128:j * 128]
                            wr = [("HT", tt) for tt in range(t0, t0 + (j - i))]
                            a_ap = AB[:, s, nrm, 0, k:k + 1]
                            b_ap = AB[:, s, nrm, 1, k:k + 1]
                            if k % 2 == 0:
                                P.op("act", lambda e, dst=dst, src=src, a_ap=a_ap, b_ap=b_ap: e.activation(dst, src, AF.Identity, bias=b_ap, scale=a_ap),
                                     reads=[pk, ("AB", s, nrm, 0), ("AB", s, nrm, 1)], writes=wr)
                            else:
                                P.op("dve", lambda e, dst=dst, src=src, a_ap=a_ap, b_ap=b_ap: e.tensor_scalar(out=dst, in0=src, scalar1=a_ap, scalar2=b_ap, op0=ALU.mult, op1=ALU.add),
                                     reads=[pk, ("AB", s, nrm, 0), ("AB", s, nrm, 1)], writes=wr)
                            i = j
                P.barrier()

        HTk = lambda ts: [("HT", t) for t in ts]
        ALLT = list(range(NT))

        def load_w(rotw, src2d, reads=()):
            wt, wk = rotw.get()
            P.dma("pool", wt[:], src2d.rearrange("(k p) c -> p k c", p=128), reads=reads, writes=[wk])
            return wt, wk

        BLKS = [(0, 512), (512, 512), (1024, 512), (1536, 512), (2048, 256)]

        def proj_fm(wt, wk, rhs_of, nk, evac, m=128, blks=BLKS, extra_reads=(), coff=0):
            for bi, (t0, n) in enumerate(blks):
                ps, pk = PSB.get()
                pairs = [(wt[:, k, coff:coff + m], rhs_of(k, t0, n)) for k in range(nk)]
                tl = list(range(t0 // 128, (t0 + n) // 128))
                bld.mm(ps[0:m, 0:n], pairs, [wk] + HTk(tl) + list(extra_reads), pk)
                evac(bi, t0, n, ps, pk)

        hT_rhs = lambda k, t0, n: HT[:, k, t0:t0 + n]

        def head_out_norm(st, l, h, Oacc, zs, nrot, b_idx, tiles, tag):
            oTh = bld.sb(st, tag + "oTh", [128, T], BF16)
            ssq = bld.sb(st, tag + "ssq", [128, NT], F32)
            junk = bld.sb(st, tag + "junk", [128, 128], BF16)
            yb = bld.rot(st, tag + "yb", [128, 128], BF16, 4)
            for t in tiles:
                P.op("act", lambda e, t=t: e.activation(junk[:], Oacc[:, t, :], AF.Square, accum_out=ssq[:, t:t + 1]), reads=[(tag + "O", t)], writes=[tag + "junk", (tag + "ssq", t)])
            t0, t1 = tiles[0], tiles[-1] + 1
            P.op("act", lambda e: e.activation(ssq[:, t0:t1], ssq[:, t0:t1], AF.Sqrt, bias=epsc(2), scale=1.0 / 128.0), reads=[(tag + "ssq", t) for t in tiles] + ["c32"], writes=[tag + "ssqall"])
            P.op("dve", lambda e: e.reciprocal(ssq[:, t0:t1], ssq[:, t0:t1]), reads=[tag + "ssqall"], writes=[tag + "ssqall"])
            grp = [tiles[i:i + 4] for i in range(0, len(tiles), 4)]
            for g in grp:
                ps, pk = PSB.get()
                psv = ps[:].bitcast(BF16)
                for i, t in enumerate(g):
                    y, yk = yb.get()
                    P.op("dve", lambda e, y=y, t=t: e.scalar_tensor_tensor(out=y[:], in0=Oacc[:, t, :], scalar=ssq[:, t:t + 1], in1=zs[:, t, :], op0=ALU.mult, op1=ALU.mult),
                         reads=[(tag + "O", t), tag + "ssqall", (tag + "zs", t)], writes=[yk])
                    bld.tr(psv[:, i * 128:(i + 1) * 128], y[:], identb[:], [yk, "identb"], pk)
                n = len(g) * 128
                P.op("act", lambda e, g=g, n=n, psv=psv: e.copy(oTh[:, g[0] * 128:g[0] * 128 + n], psv[:, 0:n]), reads=[pk], writes=[(tag + "oTh", t) for t in g])
            c0 = tiles[0] * 128
            P.dma("sp", oT_s[b_idx][h][:, c0:T], oTh[:, c0:T], reads=[(tag + "oTh", t) for t in tiles], writes=[("oTs", b_idx, h)])

        def phase_gdn(l, tiles):
            with contextlib.ExitStack() as st:
                wrot = bld.rot(st, "gw", [128, KD, 128], BF16, 3)
                wab = bld.sb(st, "gwab", [128, KD, 16], BF16)
                gbeta = bld.sb(st, "gbeta", [128, NT, 16], F32)
                tmp8 = bld.sb(st, "gtmp8", [128, NT, 8], F32)
                P.dma("pool", wab[:], w_in[l][:, GAB:GAB + 16].rearrange("(k p) c -> p k c", p=128), writes=["gwab"])
                ab_ps, ab_key = PSX, "psx"
                for t in ALLT:
                    bld.mm(ab_ps[:, t * 16:(t + 1) * 16], [(HT[:, k, t * 128:(t + 1) * 128], wab[:, k, :]) for k in range(KD)], ["gwab", ("HT", t)], ab_key)
                abv = ab_ps[:, 0:NT * 16].rearrange("p (t c) -> p t c", c=16)
                for t in ALLT:
                    P.op("dve", lambda e, t=t: e.tensor_tensor(out=tmp8[:, t, :], in0=abv[:, t, 0:8], in1=dtb(l), op=ALU.add), reads=[ab_key, "rws"], writes=[("gtmp8", t)])
                al = bld.sb(st, "galog", [128, 8], F32)
                P.op("act", lambda e: e.activation(al[:], alog(l), AF.Exp), reads=["rws"], writes=["galog"])
                t8all = [("gtmp8", t) for t in ALLT]
                P.op("act", lambda e: e.activation(tmp8[:], tmp8[:], AF.Exp), reads=t8all, writes=["gtmp8all"])
                P.op("act", lambda e: e.activation(tmp8[:], tmp8[:], AF.Ln, bias=1.0), reads=["gtmp8all"], writes=["gtmp8all"])
                for t in ALLT:
                    P.op("dve", lambda e, t=t: e.scalar_tensor_tensor(out=gbeta[:, t, 0:8], in0=tmp8[:, t, :], scalar=-1.0, in1=al[:], op0=ALU.mult, op1=ALU.mult),
                         reads=["gtmp8all", "galog"], writes=[("gb_g", t)])
                P.op("act", lambda e: e.activation(gbeta[:, :, 8:16], abv[:, :, 8:16], AF.Sigmoid), reads=[ab_key], writes=["gb_beta"])
                gbk = [("gb_g", t) for t in ALLT] + ["gb_beta"]
                P.barrier()
                for h in range(4):
                    with contextlib.ExitStack() as sh:
                        gdn_head(sh, l, h, tiles, wrot, gbeta)
                    P.barrier()

        def gdn_head(st, l, h, tiles, wrot, gbeta):
            sbh = lambda name, shape, dt: bld.sb(st, name, shape, dt)
            W = T + 3
            off = lambda t0: t0 + 1 if t0 < 256 else t0 + 2
            raw = bld.rot(st, "graw", [128, W], F32, 2)
            cv = bld.rot(st, "gcv", [128, W], F32, 2)
            qT = sbh("gqT", [128, T], BF16)
            kT = sbh("gkT", [128, T], BF16)
            vT = sbh("gvT", [128, T], BF16)
            ktok = sbh("gktok", [128, NT, 128], BF16)
            vtok = sbh("gvtok", [128, NT, 128], BF16)
            zs = sbh("gzs", [128, NT, 128], F32)
            Oacc = sbh("gO", [128, NT, 128], F32)
            sqr = bld.rot(st, "gsqr", [128, 512], F32R, 2)
            rnb = bld.rot(st, "grnb", [128, 512], F32, 2)
            for fi, (c0, dst) in enumerate(((GQ, qT), (GK, kT), (GV, vT))):
                ch = fi * 4 + h
                wt, wk = load_w(wrot, w_in[l][:, c0 + h * 128: c0 + (h + 1) * 128])
                rw, rk = raw.get()
                P.op("pool", lambda e, rw=rw: e.memset(rw[:], 0.0), writes=[rk])
                def ev(bi, t0, n, ps, pk, rw=rw, rk=rk):
                    o_ = off(t0)
                    if t0 == 0:
                        P.op("act", lambda e: e.copy(rw[:, 1:257], ps[:, 0:256]), reads=[pk], writes=[rk])
                        P.op("dve", lambda e: e.tensor_copy(rw[:, 258:514], ps[:, 256:512]), reads=[pk], writes=[rk])
                    else:
                        eng = "act" if bi % 2 else "dve"
                        if eng == "act":
                            P.op("act", lambda e: e.copy(rw[:, o_:o_ + n], ps[:, 0:n]), reads=[pk], writes=[rk])
                        else:
                            P.op("dve", lambda e: e.tensor_copy(rw[:, o_:o_ + n], ps[:, 0:n]), reads=[pk], writes=[rk])
                proj_fm(wt, wk, hT_rhs, KD, ev)
                c, ck = cv.get()
                P.op("act", lambda e, c=c, rw=rw, ch=ch: e.activation(c[:, 1:W - 1], rw[:, 1:W - 1], AF.Copy, scale=gconv(l, ch, 1)), reads=[rk, "spp"], writes=[ck])
                P.op("dve", lambda e, c=c, rw=rw, ch=ch: e.scalar_tensor_tensor(out=c[:, 1:W - 1], in0=rw[:, 0:W - 2], scalar=gconv(l, ch, 0), in1=c[:, 1:W - 1], op0=ALU.mult, op1=ALU.add), reads=[rk, ck, "spp"], writes=[ck])
                P.op("dve", lambda e, c=c, rw=rw, ch=ch: e.scalar_tensor_tensor(out=c[:, 1:W - 1], in0=rw[:, 2:W], scalar=gconv(l, ch, 2), in1=c[:, 1:W - 1], op0=ALU.mult, op1=ALU.add), reads=[rk, ck, "spp"], writes=[ck])
                P.op("act", lambda e, c=c: e.activation(c[:, 1:W - 1], c[:, 1:W - 1], AF.Silu), reads=[ck], writes=[ck])
                if fi == 2:
                    P.op("dve", lambda e, c=c: e.tensor_copy(vT[:, 0:256], c[:, 1:257]), reads=[ck], writes=[("gT", 2, 0)])
                    P.op("dve", lambda e, c=c: e.tensor_copy(vT[:, 256:T], c[:, 258:W - 1]), reads=[ck], writes=[("gT", 2, 1)])
                else:
                    for bi, (t0, n) in enumerate(BLKS):
                        segs = [(0, 256), (256, 256)] if t0 == 0 else [(t0, n)]
                        sq, sqk = sqr.get()
                        for (s0, sn) in segs:
                            P.op("act", lambda e, c=c, s0=s0, sn=sn, sq=sq, t0=t0: e.activation(sq[:, s0 - t0:s0 - t0 + sn], c[:, off(s0):off(s0) + sn], AF.Square), reads=[ck], writes=[sqk])
                        ps, pk = PSB.get()
                        bld.mm(ps[:, 0:n], [(onesr[:], sq[:, 0:n])], [sqk, "onesr"], pk)
                        rn, rnk = rnb.get()
                        P.op("act", lambda e, rn=rn, ps=ps, n=n: e.activation(rn[:, 0:n], ps[:, 0:n], AF.Sqrt, bias=epsc(2)), reads=[pk, "c32"], writes=[rnk])
                        P.op("dve", lambda e, rn=rn, n=n: e.reciprocal(rn[:, 0:n], rn[:, 0:n]), reads=[rnk], writes=[rnk])
                        scl = float(128 ** -0.5) if fi == 0 else 1.0
                        for (s0, sn) in segs:
                            P.op("dve", lambda e, c=c, s0=s0, sn=sn, rn=rn, t0=t0, dst=dst, scl=scl: e.scalar_tensor_tensor(out=dst[:, s0:s0 + sn], in0=c[:, off(s0):off(s0) + sn], scalar=scl, in1=rn[:, s0 - t0:s0 - t0 + sn], op0=ALU.mult, op1=ALU.mult),
                                 reads=[ck, rnk], writes=[("gT", fi, s0)])
            gTk = lambda fi: [("gT", fi, s0) for s0 in (0, 256, 512, 1024, 1536, 2048)] + [("gT", 2, 0), ("gT", 2, 1)]
            for (src, dstt, fi, nm) in ((kT, ktok, 1, "gktok"), (vT, vtok, 2, "gvtok")):
                for g0 in range(0, NT, 4):
                    g = list(range(g0, min(g0 + 4, NT)))
                    ps, pk = PSB.get()
                    psv = ps[:].bitcast(BF16)
                    for i, t in enumerate(g):
                        bld.tr(psv[:, i * 128:(i + 1) * 128], src[:, t * 128:(t + 1) * 128], identb[:], gTk(fi) + ["identb"], pk)
                    n = len(g) * 128
                    P.op("act" if (g0 // 4) % 2 else "dve",
                         (lambda e, g=g, n=n, psv=psv, dstt=dstt: e.copy(dstt[:, g[0]:g[0] + len(g), :], psv[:, 0:n].rearrange("p (t c) -> p t c", c=128))) if (g0 // 4) % 2 else
                         (lambda e, g=g, n=n, psv=psv, dstt=dstt: e.tensor_copy(dstt[:, g[0]:g[0] + len(g), :], psv[:, 0:n].rearrange("p (t c) -> p t c", c=128))),
                         reads=[pk], writes=[(nm, t) for t in g])
            wt, wk = load_w(wrot, w_in[l][:, GZ + h * 128: GZ + (h + 1) * 128])
            for t in tiles:
                ps, pk = PSS.get()
                bld.mm(ps, [(HT[:, k, t * 128:(t + 1) * 128], wt[:, k, :]) for k in range(KD)], [wk, ("HT", t)], pk)
                P.op("act", lambda e, t=t, ps=ps: e.activation(zs[:, t, :], ps, AF.Silu), reads=[pk], writes=[("gzs", t)])
                P.op("pool", lambda e, t=t: e.tensor_tensor(out=zs[:, t, :], in0=zs[:, t, :], in1=gnw(l), op=ALU.mult), reads=[("gzs", t), "rws"], writes=[("gzs", t)])
            f32t = lambda name, n: bld.rot(st, name, [128, 128], F32, n)
            b16t = lambda name, n: bld.rot(st, name, [128, 128], BF16, n)
            gbr = f32t("g_gb", 3); egr = f32t("g_eg", 5); dsr = f32t("g_ds", 3); dir_ = f32t("g_di", 3)
            colr = bld.rot(st, "g_col", [128, 4], F32, 5)
            Pm = bld.rot(st, "g_P", [128, 256], F32, 4); PTm = bld.rot(st, "g_PT", [128, 256], F32, 4); Rm = bld.rot(st, "g_R", [128, 256], F32, 3)
            ident2 = sbh("g_id2", [128, 256], F32)
            P.op("dve", lambda e: e.tensor_copy(ident2[:, 0:128], ident32), reads=["c32"], writes=["g_id2"])
            P.op("dve", lambda e: e.tensor_copy(ident2[:, 128:256], ident32), reads=["c32"], writes=["g_id2"])
            TTb = bld.rot(st, "g_TT", [128, 256], BF16, 3); atb = b16t("g_at", 5); qdb = b16t("g_qd", 5); kdb = b16t("g_kd", 5)
            rb = b16t("g_r", 3); vnb = b16t("g_vn", 3)
            S32 = [sbh("gS32_%d" % d, [128, 128], F32) for d in range(2)]
            Sb = [bld.rot(st, "gSb%d" % d, [128, 128], BF16, 2) for d in range(2)]
            order = [list(range(NT)), [1, 0] + list(range(NT - 1, 1, -1))]
            cur_Sb = [None, None]
            for d in range(2):
                P.op("pool", lambda e, d=d: e.memset(S32[d][:], 0.0), writes=[("gS32", d)])
                sbt, sbk = Sb[d].get()
                P.op("pool", lambda e, sbt=sbt: e.memset(sbt[:], 0.0), writes=[sbk])
                cur_Sb[d] = (sbt, sbk)
            visited = set()
            kT_k, qT_k = gTk(1), gTk(0)

            def precompute(d, c, P2, P2k):
                q = d * 4 + h
                gcol = gbeta[:, c, q:q + 1]
                bcol = gbeta[:, c, 8 + q:9 + q]
                tcol = slice(c * 128, (c + 1) * 128)
                gb, gbk_ = gbr.get()
                P.op("dve", lambda e: e.tensor_scalar(out=gb[:], in0=Umask[d], scalar1=gcol, scalar2=None, op0=ALU.mult), reads=[("gb_g", c), "c32"], writes=[gbk_])
                psA, kA = PSS.get()
                bld.mm(psA, [(ones32, gb[:])], [gbk_, "c32"], kA)
                psB, kB = PSS.get()
                bld.mm(psB, [(ones32, gb[:]), (ident32, NEGS[d])], [gbk_, "c32"], kB)
                psC, kC = PSS.get()
                bld.mm(psC[:, 0:1], [(Umask[d], gcol)], [("gb_g", c), "c32"], kC)
                yield
                col, colk = colr.get()
                P.op("dve", lambda e: e.tensor_scalar(out=col[:, 1:2], in0=psC[:, 0:1], scalar1=-1.0, scalar2=None, op0=ALU.mult), reads=[kC], writes=[colk])
                P.op("act", lambda e: e.activation(col[:, 0:1], psC[:, 0:1], AF.Exp), reads=[kC], writes=[colk])
                P.op("dve", lambda e: e.tensor_scalar(out=col[:, 0:1], in0=col[:, 0:1], scalar1=-1.0, scalar2=None, op0=ALU.mult), reads=[colk], writes=[colk])
                P.op("dve", lambda e: e.tensor_scalar(out=col[:, 2:3], in0=bcol, scalar1=-1.0, scalar2=None, op0=ALU.mult), reads=["gb_beta"], writes=[colk])
                eg, egk = egr.get()
                P.op("act", lambda e: e.activation(eg[:], psA, AF.Exp), reads=[kA], writes=[egk])
                ds, dsk = dsr.get()
                P.op("act", lambda e: e.activation(ds[:], psB, AF.Exp, bias=col[:, 1:2]), reads=[kB, colk], writes=[dsk])
                yield
                di, dik = dir_.get()
                P.op("dve", lambda e: e.tensor_tensor(out=di[:], in0=ds[:], in1=ident32, op=ALU.add), reads=[dsk, "c32"], writes=[dik])
                psK, kK = PSS.get()
                bld.mm(psK, [(kT[:, tcol], kT[:, tcol])], kT_k, kK)
                psQ, kQ = PSS.get()
                bld.mm(psQ, [(kT[:, tcol], qT[:, tcol])], kT_k + qT_k, kQ)
                yield
                p0 = P2[:, d * 128:(d + 1) * 128]
                p0k = (P2k, d)
                P.op("dve", lambda e: e.scalar_tensor_tensor(out=p0, in0=psK, scalar=col[:, 2:3], in1=ds[:], op0=ALU.mult, op1=ALU.mult), reads=[kK, colk, dsk], writes=[p0k])
                at, atk = atb.get()
                P.op("dve", lambda e: e.tensor_tensor(out=at[:], in0=psQ, in1=di[:], op=ALU.mult), reads=[kQ, dik], writes=[atk])
                last = 127 if d == 0 else 0
                kd, kdk = kdb.get()
                P.op("pool", lambda e: e.tensor_scalar(out=kd[:], in0=ktok[:, c, :], scalar1=di[:, last:last + 1], scalar2=None, op0=ALU.mult), reads=[("gktok", c), dik], writes=[kdk])
                qd, qdk = qdb.get()
                P.op("pool", lambda e: e.tensor_tensor(out=qd[:], in0=qT[:, tcol], in1=eg[:], op=ALU.mult), reads=qT_k + [egk], writes=[qdk])
                pr_out[d] = dict(col=col, colk=colk, eg=eg, egk=egk, at=at, atk=atk, kd=kd, kdk=kdk, qd=qd, qdk=qdk, bcol=bcol, last=last)

            def step(d, c, pre):
                tcol = slice(c * 128, (c + 1) * 128)
                sbt, sbk = cur_Sb[d]
                psk, kk = PSS.get()
                bld.mm(psk, [(kT[:, tcol], sbt[:])], kT_k + [sbk], kk)
                yield
                r, rk_ = rb.get()
                P.op("dve", lambda e: e.scalar_tensor_tensor(out=r[:], in0=psk, scalar=pre["col"][:, 0:1], in1=vtok[:, c, :], op0=ALU.mult, op1=ALU.add), reads=[kk, pre["colk"], ("gvtok", c)], writes=[rk_])
                psv, kv = PSS.get()
                bld.mm(psv, [(pre["tt"], r[:])], [pre["ttk"], rk_], kv)
                yield
                vn, vnk = vnb.get()
                P.op("act", lambda e: e.activation(vn[:], psv, AF.Copy, scale=pre["bcol"]), reads=[kv, "gb_beta"], writes=[vnk])
                pso, ko = PSS.get()
                bld.mm(pso, [(pre["qd"][:], sbt[:]), (pre["at"][:], vn[:])], [pre["qdk"], sbk, pre["atk"], vnk], ko)
                if c in visited:
                    P.op("dve", lambda e: e.tensor_tensor(out=Oacc[:, c, :], in0=pso, in1=Oacc[:, c, :], op=ALU.add), reads=[ko, ("gO", c)], writes=[("gO", c)])
                else:
                    visited.add(c)
                    P.op("act", lambda e: e.copy(Oacc[:, c, :], pso), reads=[ko], writes=[("gO", c)])
                pss_, ks = PSS.get()
                bld.mm(pss_, [(pre["kd"][:], vn[:])], [pre["kdk"], vnk], ks)
                yield
                last = pre["last"]
                P.op("dve", lambda e: e.scalar_tensor_tensor(out=S32[d][:], in0=S32[d][:], scalar=pre["eg"][:, last:last + 1], in1=pss_, op0=ALU.mult, op1=ALU.add), reads=[("gS32", d), pre["egk"], ks], writes=[("gS32", d)])
                nsb, nsbk = Sb[d].get()
                P.op("act", lambda e: e.copy(nsb[:], S32[d][:]), reads=[("gS32", d)], writes=[nsbk])
                cur_Sb[d] = (nsb, nsbk)

            def neumann2(P2, P2k):
                pk_all = [(P2k, 0), (P2k, 1)]
                psT, kT_ = PSH.get()
                for x in range(2):
                    bld.tr(psT[:, x * 128:(x + 1) * 128], P2[:, x * 128:(x + 1) * 128], ident32, pk_all + ["c32"], kT_)
                PT2, PT2k = PTm.get()
                P.op("act", lambda e, PT2=PT2: e.copy(PT2[:], psT), reads=[kT_], writes=[PT2k])
                R2, R2k = Rm.get()
                P.op("dve", lambda e, R2=R2: e.tensor_tensor(out=R2[:], in0=P2[:], in1=ident2[:], op=ALU.add), reads=pk_all + ["g_id2"], writes=[R2k])
                pc, pck, ptc, ptck = P2, pk_all, PT2, [PT2k]
                sl = lambda t_, x: t_[:, x * 128:(x + 1) * 128]
                for lev in range(6):
                    if lev < 5:
                        S1, k1 = PSH.get()
                        for x in range(2):
                            bld.mm(sl(S1, x), [(sl(ptc, x), sl(pc, x))], pck + ptck, k1)
                    S2, k2 = PSH.get()
                    for x in range(2):
                        bld.mm(sl(S2, x), [(sl(pc, x), sl(ptc, x))], pck + ptck, k2)
                    if lev >= 1:
                        S3, k3 = PSH.get()
                        for x in range(2):
                            bld.mm(sl(S3, x), [(sl(ptc, x), sl(R2, x))], ptck + [R2k], k3)
                    Pn, Pnk = Pm.get()
                    if lev < 5:
                        P.op("act", lambda e, Pn=Pn, S1=S1: e.copy(Pn[:], S1), reads=[k1], writes=[Pnk])
                    PTn, PTnk = PTm.get()
                    P.op("dve" if lev < 1 else "act", (lambda e, PTn=PTn, S2=S2: e.tensor_copy(PTn[:], S2)) if lev < 1 else (lambda e, PTn=PTn, S2=S2: e.copy(PTn[:], S2)), reads=[k2], writes=[PTnk])
                    if lev >= 1:
                        Rn, Rnk = Rm.get()
                        P.op("dve", lambda e, Rn=Rn, R2=R2, S3=S3: e.tensor_tensor(out=Rn[:], in0=S3, in1=R2[:], op=ALU.add), reads=[k3, R2k], writes=[Rnk])
                        R2, R2k = Rn, Rnk
                    pc, pck, ptc, ptck = Pn, [Pnk], PTn, [PTnk]
                S3, k3 = PSH.get()
                for x in range(2):
                    bld.mm(sl(S3, x), [(sl(ptc, x), sl(R2, x))], ptck + [R2k], k3)
                tt2, tt2k = TTb.get()
                P.op("dve", lambda e, tt2=tt2, R2=R2, S3=S3: e.tensor_tensor(out=tt2[:], in0=S3, in1=R2[:], op=ALU.add), reads=[k3, R2k], writes=[tt2k])
                return tt2, tt2k

            pres = {}
            pr_out = {}
            P.warm = GDN_WARM
            for s_ in range(NT + 1):
                if s_ < NT:
                    P2, P2k = Pm.get()
                    pr_out.clear()
                    gens = [precompute(d, order[d][s_], P2, P2k) for d in range(2)]
                    while gens:
                        for g_ in list(gens):
                            try:
                                next(g_)
                            except StopIteration:
                                gens.remove(g_)
                    pr = [pr_out[0], pr_out[1]]
                    tt2, tt2k = neumann2(P2, P2k)
                    for d in range(2):
                        pr[d]["tt"] = tt2[:, d * 128:(d + 1) * 128]
                        pr[d]["ttk"] = tt2k
                        pres[(d, order[d][s_])] = pr[d]
                if s_ >= 1:
                    gens = [step(d, order[d][s_ - 1], pres.pop((d, order[d][s_ - 1]))) for d in range(2)]
                    while gens:
                        for g_ in list(gens):
                            try:
                                next(g_)
                            except StopIteration:
                                gens.remove(g_)
            P.warm = 0
            head_out_norm(st, l, h, Oacc, zs, None, 0, tiles, "g")
        def rope_fm(src_bf, src_key, dst, dst_key_of, nrows, rmat, cos_t, sin_t, tabk, tmpA, tmpB):
            for bi, (t0, n) in enumerate(BLKS):
                ps, pk = PSB.get()
                bld.mm(ps[0:nrows, 0:n], [(rmat, src_bf[0:nrows, t0:t0 + n])], [src_key, "rmb"], pk)
                a, ak = tmpA.get()
                b_, bk = tmpB.get()
                P.op("dve", lambda e, a=a, ps=ps, n=n, t0=t0: e.tensor_tensor(out=a[0:nrows, 0:n], in0=ps[0:nrows, 0:n], in1=sin_t[0:nrows, t0:t0 + n], op=ALU.mult), reads=[pk, tabk], writes=[ak])
                P.op("pool", lambda e, b_=b_, n=n, t0=t0: e.tensor_tensor(out=b_[0:nrows, 0:n], in0=src_bf[0:nrows, t0:t0 + n], in1=cos_t[0:nrows, t0:t0 + n], op=ALU.mult), reads=[src_key, tabk], writes=[bk])
                P.op("dve", lambda e, a=a, b_=b_, n=n, t0=t0: e.tensor_tensor(out=dst[0:nrows, t0:t0 + n], in0=a[0:nrows, 0:n], in1=b_[0:nrows, 0:n], op=ALU.add), reads=[ak, bk], writes=[dst_key_of(bi)])

        def phase_mla(l, ctx_out):
            with contextlib.ExitStack() as st:
                sbm = lambda name, shape, dt: bld.sb(st, name, shape, dt)
                wrot = bld.rot(st, "mw", [128, KD, 128], BF16, 3)
                cqn = sbm("cqn", [128, 3, T], BF16)
                ckvn = sbm("ckvn", [128, 2, T], BF16)
                krr = sbm("krr", [64, T], BF16)
                cosm = sbm("cosm", [64, T], F32)
                sinm = sbm("sinm", [64, T], F32)
                wuq = sbm("wuq", [128, 3, 768], BF16)
                wukv = sbm("wukv", [128, 2, 1024], BF16)
                tA = bld.rot(st, "mtA", [128, 512], F32, 2)
                tB = bld.rot(st, "mtB", [128, 512], F32, 2)
                qnsc = sbm("qnsc", [128, 5], F32)
                st1 = contextlib.ExitStack()
                cqraw = bld.sb(st1, "cqraw", [128, 3, T], F32)
                ckvraw = bld.sb(st1, "ckvraw", [128, 2, T], F32)
                krb = bld.sb(st1, "krb", [64, T], BF16)
                sqr = bld.rot(st1, "msqr", [128, 512], F32R, 2)
                rnb = bld.rot(st1, "mrnb", [128, 512], F32, 2)
                P.dma("sp", cosm[:], ropem[0], writes=["ropem"])
                P.dma("sp", sinm[:], ropem[1], writes=["ropem2"])
                P.dma("pool", wuq[:], w_uq[l].rearrange("(k p) c -> p k c", p=128), writes=["wuq"])
                P.dma("pool", wukv[:], w_ukv[l].rearrange("(k p) c -> p k c", p=128), writes=["wukv"])
                for k in range(3):
                    P.op("dve", lambda e, k=k: e.tensor_scalar(out=qnsc[:, k:k + 1], in0=qn(l, k), scalar1=float(math.sqrt(384.0)), scalar2=None, op0=ALU.mult), reads=["spp"], writes=["qnsc"])
                for k in range(2):
                    P.op("dve", lambda e, k=k: e.tensor_scalar(out=qnsc[:, 3 + k:4 + k], in0=kvn(l, k), scalar1=float(math.sqrt(256.0)), scalar2=None, op0=ALU.mult), reads=["spp"], writes=["qnsc"])
                for (c0, nch, rawt, nm) in ((CQ, 3, cqraw, "cqraw"), (CKV, 2, ckvraw, "ckvraw")):
                    for ch in range(nch):
                        wt, wk = load_w(wrot, w_in[l][:, c0 + ch * 128: c0 + (ch + 1) * 128])
                        def ev(bi, t0, n, ps, pk, rawt=rawt, ch=ch, nm=nm):
                            if bi % 2:
                                P.op("act", lambda e: e.copy(rawt[:, ch, t0:t0 + n], ps[:, 0:n]), reads=[pk], writes=[(nm, ch, bi)])
                            else:
                                P.op("dve", lambda e: e.tensor_copy(rawt[:, ch, t0:t0 + n], ps[:, 0:n]), reads=[pk], writes=[(nm, ch, bi)])
                        proj_fm(wt, wk, hT_rhs, KD, ev)
                wt, wk = load_w(wrot, w_in[l][:, KR:KR + 128])
                def evk(bi, t0, n, ps, pk):
                    P.op("act", lambda e: e.copy(krb[:, t0:t0 + n], ps[0:64, 0:n]), reads=[pk], writes=[("krb", bi)])
                proj_fm(wt, wk, hT_rhs, KD, evk, m=64)
                for bi, (t0, n) in enumerate(BLKS):
                    ps, pk = PSB.get()
                    bld.mm(ps[0:64, 0:n], [(rmb[0:64, 128:192], krb[:, t0:t0 + n])], [("krb", bi), "rmb"], pk)
                    a, ak = tA.get()
                    b_, bk = tB.get()
                    P.op("dve", lambda e, a=a, ps=ps, n=n, t0=t0: e.tensor_tensor(out=a[0:64, 0:n], in0=ps[0:64, 0:n], in1=sinm[:, t0:t0 + n], op=ALU.mult), reads=[pk, "ropem2"], writes=[ak])
                    P.op("pool", lambda e, b_=b_, n=n, t0=t0: e.tensor_tensor(out=b_[0:64, 0:n], in0=krb[:, t0:t0 + n], in1=cosm[:, t0:t0 + n], op=ALU.mult), reads=[("krb", bi), "ropem"], writes=[bk])
                    P.op("dve", lambda e, a=a, b_=b_, n=n, t0=t0: e.tensor_tensor(out=krr[:, t0:t0 + n], in0=a[0:64, 0:n], in1=b_[0:64, 0:n], op=ALU.add), reads=[ak, bk], writes=[("krr", bi)])
                for (nch, rawt, nm, dstn, dnm, eps_i, q0) in ((3, cqraw, "cqraw", cqn, "cqn", 3, 0), (2, ckvraw, "ckvraw", ckvn, "ckvn", 4, 3)):
                    for bi, (t0, n) in enumerate(BLKS):
                        ps, pk = PSB.get()
                        for ch in range(nch):
                            sq, sqk = sqr.get()
                            P.op("act", lambda e, sq=sq, ch=ch, t0=t0, n=n, rawt=rawt: e.activation(sq[:, 0:n], rawt[:, ch, t0:t0 + n], AF.Square), reads=[(nm, ch, bi)], writes=[sqk])
                            bld.mm_acc(ps[:, 0:n], onesr[:], sq[:, 0:n], ch == 0, ch == nch - 1, [sqk, "onesr"], pk)
                        rn, rnk = rnb.get()
                        P.op("act", lambda e, rn=rn, ps=ps, n=n, eps_i=eps_i: e.activation(rn[:, 0:n], ps[:, 0:n], AF.Sqrt, bias=epsc(eps_i)), reads=[pk, "c32"], writes=[rnk])
                        P.op("dve", lambda e, rn=rn, n=n: e.reciprocal(rn[:, 0:n], rn[:, 0:n]), reads=[rnk], writes=[rnk])
                        for ch in range(nch):
                            P.op("dve", lambda e, ch=ch, rn=rn, t0=t0, n=n, rawt=rawt, dstn=dstn, q0=q0: e.scalar_tensor_tensor(out=dstn[:, ch, t0:t0 + n], in0=rawt[:, ch, t0:t0 + n], scalar=qnsc[:, q0 + ch:q0 + ch + 1], in1=rn[:, 0:n], op0=ALU.mult, op1=ALU.mult),
                                 reads=[(nm, ch, bi), rnk, "qnsc"], writes=[(dnm, bi)])
                P.barrier()
                st1.close()
                qnope = sbm("qnope", [128, T], BF16)
                qrb = sbm("qrb", [64, T], BF16)
                qrr = sbm("qrr", [64, T], BF16)
                knope = sbm("knope", [128, T], BF16)
                vtok = sbm("mvtok", [128, NT, 128], BF16)
                oTh = sbm("moTh", [128, T], BF16)
                pT = bld.rot(st, "mpT", [128, 512], BF16, 3)
                rden = bld.rot(st, "mrden", [128, 512], F32, 2)
                cqk = lambda: [("cqn", bi) for bi in range(5)]
                ckk = lambda: [("ckvn", bi) for bi in range(5)]
                for h in range(4):
                    for bi, (t0, n) in enumerate(BLKS):
                        ps, pk = PSB.get()
                        bld.mm(ps[:, 0:n], [(wuq[:, k, h * 192:h * 192 + 128], cqn[:, k, t0:t0 + n]) for k in range(3)], ["wuq", ("cqn", bi)], pk)
                        P.op("act", lambda e, ps=ps, t0=t0, n=n: e.activation(qnope[:, t0:t0 + n], ps[:, 0:n], AF.Copy, scale=float(MLA_SCALE)), reads=[pk], writes=[("qnope", bi)])
                        ps, pk = PSB.get()
                        bld.mm(ps[0:64, 0:n], [(wuq[:, k, h * 192 + 128:h * 192 + 192], cqn[:, k, t0:t0 + n]) for k in range(3)], ["wuq", ("cqn", bi)], pk)
                        P.op("act", lambda e, ps=ps, t0=t0, n=n: e.activation(qrb[:, t0:t0 + n], ps[0:64, 0:n], AF.Copy, scale=float(MLA_SCALE)), reads=[pk], writes=[("qrb", bi)])
                        ps, pk = PSB.get()
                        bld.mm(ps[:, 0:n], [(wukv[:, k, h * 256:h * 256 + 128], ckvn[:, k, t0:t0 + n]) for k in range(2)], ["wukv", ("ckvn", bi)], pk)
                        P.op("dve", lambda e, ps=ps, t0=t0, n=n: e.tensor_copy(knope[:, t0:t0 + n], ps[:, 0:n]), reads=[pk], writes=[("knope", bi)])
                        ps, pk = PSB.get()
                        bld.mm(ps[0:64, 0:n], [(rmb[0:64, 128:192], qrb[:, t0:t0 + n])], [("qrb", bi), "rmb"], pk)
                        a, ak = tA.get()
                        b_, bk = tB.get()
                        P.op("dve", lambda e, a=a, ps=ps, n=n, t0=t0: e.tensor_tensor(out=a[0:64, 0:n], in0=ps[0:64, 0:n], in1=sinm[:, t0:t0 + n], op=ALU.mult), reads=[pk, "ropem2"], writes=[ak])
                        P.op("pool", lambda e, b_=b_, n=n, t0=t0: e.tensor_tensor(out=b_[0:64, 0:n], in0=qrb[:, t0:t0 + n], in1=cosm[:, t0:t0 + n], op=ALU.mult), reads=[("qrb", bi), "ropem"], writes=[bk])
                        P.op("dve", lambda e, a=a, b_=b_, n=n, t0=t0: e.tensor_tensor(out=qrr[:, t0:t0 + n], in0=a[0:64, 0:n], in1=b_[0:64, 0:n], op=ALU.add), reads=[ak, bk], writes=[("qrr", bi)])
                    for t in ALLT:
                        ps, pk = PSB.get()
                        bld.mm(ps[:, 0:128], [(ckvn[:, k, t * 128:(t + 1) * 128], wukv[:, k, h * 256 + 128:h * 256 + 256]) for k in range(2)], ["wukv", ("ckvn", min(t // 4, 4))], pk)
                        P.op("act", lambda e, t=t, ps=ps: e.copy(vtok[:, t, :], ps[:, 0:128]), reads=[pk], writes=[("mvtok", t)])
                    qgroups = [(256 + g * 512, 512, ALLT) for g in range(4)]
                    if ctx_out:
                        qgroups = [(0, 256, [0, 1])] + qgroups
                    for (q0, nq, ktiles) in qgroups:
                        qb = min(q0 // 512, 4)
                        qbs = sorted(set([min(q0 // 512, 4), min((q0 + nq - 1) // 512, 4)]))
                        o_ps, o_k = banks[6], "bank6"
                        d_ps, d_k = banks[7], "psx"
                        def s_mm(kt):
                            kb = min(kt // 4, 4)
                            ps, pk = PSB.get()
                            bld.mm(ps[:, 0:nq], [(knope[:, kt * 128:(kt + 1) * 128], qnope[:, q0:q0 + nq]), (krr[:, kt * 128:(kt + 1) * 128], qrr[:, q0:q0 + nq])],
                                   [("knope", kb), ("krr", kb)] + [("qnope", b) for b in qbs] + [("qrr", b) for b in qbs], pk)
                            return ps, pk
                        pend = [s_mm(kt) for kt in ktiles[:2]]
                        for i, kt in enumerate(ktiles):
                            ps, pk = pend.pop(0)
                            p_, pkk = pT.get()
                            P.op("act", lambda e, p_=p_, ps=ps, nq=nq: e.activation(p_[:, 0:nq], ps[:, 0:nq], AF.Exp), reads=[pk], writes=[pkk])
                            if i + 2 < len(ktiles):
                                pend.append(s_mm(ktiles[i + 2]))
                            bld.mm_acc(o_ps[:, 0:nq], vtok[:, kt, :], p_[:, 0:nq], i == 0, i == len(ktiles) - 1, [("mvtok", kt), pkk], o_k)
                            bld.mm_acc(d_ps[:, 0:nq], onesb[:], p_[:, 0:nq], i == 0, i == len(ktiles) - 1, ["onesb", pkk], d_k)
                        rd, rdk = rden.get()
                        P.op("dve", lambda e, rd=rd, nq=nq: e.reciprocal(rd[:, 0:nq], d_ps[:, 0:nq]), reads=[d_k], writes=[rdk])
                        P.op("dve", lambda e, rd=rd, nq=nq, q0=q0: e.tensor_tensor(out=oTh[:, q0:q0 + nq], in0=o_ps[:, 0:nq], in1=rd[:, 0:nq], op=ALU.mult), reads=[o_k, rdk], writes=[("moTh", q0)])
                    c0 = 0 if ctx_out else 256
                    P.dma("sp", oT_s[1][h][:, c0:T], oTh[:, c0:T], reads=[("moTh", q) for q in ([0] if ctx_out else []) + [256 + g * 512 for g in range(4)]], writes=[("oTs", 1, h)])
                P.barrier()
        def phase_ret(l, tiles):
            with contextlib.ExitStack() as st:
                sbm = lambda name, shape, dt: bld.sb(st, name, shape, dt)
                wrot = bld.rot(st, "rw", [128, KD, 128], BF16, 3)
                cosr = sbm("cosr", [128, T], F32)
                sinr = sbm("sinr", [128, T], F32)
                rcs = sbm("rcs", [128, 8 * 128 * 2 + 8], F32)
                P.dma("sp", cosr[:], roper[0], writes=["roper"])
                P.dma("sp", sinr[:], roper[1], writes=["roper2"])
                P.dma("sp", rcs[:], retc, writes=["rcs"])
                DTm = lambda q: rcs[:, q * 128:(q + 1) * 128]
                GWm = lambda q: rcs[:, 1024 + q * 128:1024 + (q + 1) * 128]
                kwc = lambda q: rcs[:, 2048 + q:2049 + q]
                rawb = bld.rot(st, "rrawb", [128, T], BF16, 2)
                qT = sbm("rqT", [128, T], BF16)
                kT = sbm("rkT", [128, T], BF16)
                ktok = sbm("rktok", [128, NT, 128], BF16)
                vtok = sbm("rvtok", [128, NT, 128], BF16)
                zs = sbm("rzs", [128, NT, 128], F32)
                tA = bld.rot(st, "rtA", [128, 512], F32, 2)
                tB = bld.rot(st, "rtB", [128, 512], F32, 2)
                atb = bld.rot(st, "r_at", [128, 128], BF16, 5)
                qwb = bld.rot(st, "r_qw", [128, 128], BF16, 5)
                kwb = bld.rot(st, "r_kw", [128, 128], BF16, 5)
                for h in range(4):
                    with contextlib.ExitStack() as sh:
                        Oacc = bld.sb(sh, "rO", [128, NT, 128], F32)
                        for fi, (c0, dst, scl) in enumerate(((RQ, qT, float(128 ** -0.5)), (RK, kT, 1.0))):
                            wt, wk = load_w(wrot, w_in[l][:, c0 + h * 128:c0 + (h + 1) * 128])
                            rb_, rbk = rawb.get()
                            def ev(bi, t0, n, ps, pk, rb_=rb_, rbk=rbk, scl=scl):
                                P.op("act", lambda e: e.activation(rb_[:, t0:t0 + n], ps[:, 0:n], AF.Copy, scale=scl), reads=[pk], writes=[(rbk, bi)])
                            proj_fm(wt, wk, hT_rhs, KD, ev)
                            for bi, (t0, n) in enumerate(BLKS):
                                ps, pk = PSB.get()
                                bld.mm(ps[:, 0:n], [(rmb[:, 0:128], rb_[:, t0:t0 + n])], [(rbk, bi), "rmb"], pk)
                                a, ak = tA.get()
                                b_, bk = tB.get()
                                P.op("dve", lambda e, a=a, ps=ps, n=n, t0=t0: e.tensor_tensor(out=a[:, 0:n], in0=ps[:, 0:n], in1=sinr[:, t0:t0 + n], op=ALU.mult), reads=[pk, "roper2"], writes=[ak])
                                P.op("pool", lambda e, b_=b_, n=n, t0=t0, rb_=rb_: e.tensor_tensor(out=b_[:, 0:n], in0=rb_[:, t0:t0 + n], in1=cosr[:, t0:t0 + n], op=ALU.mult), reads=[(rbk, bi), "roper"], writes=[bk])
                                P.op("dve", lambda e, a=a, b_=b_, n=n, t0=t0, dst=dst: e.tensor_tensor(out=dst[:, t0:t0 + n], in0=a[:, 0:n], in1=b_[:, 0:n], op=ALU.add), reads=[ak, bk], writes=[("rT", fi, bi)])
                        rTk = lambda fi: [("rT", fi, bi) for bi in range(5)]
                        for g0 in range(0, NT, 4):
                            g = list(range(g0, min(g0 + 4, NT)))
                            ps, pk = PSB.get()
                            psv = ps[:].bitcast(BF16)
                            for i, t in enumerate(g):
                                bld.tr(psv[:, i * 128:(i + 1) * 128], kT[:, t * 128:(t + 1) * 128], identb[:], [("rT", 1, min(t // 4, 4)), "identb"], pk)
                            n = len(g) * 128
                            P.op("dve", lambda e, g=g, n=n, psv=psv: e.tensor_copy(ktok[:, g[0]:g[0] + len(g), :], psv[:, 0:n].rearrange("p (t c) -> p t c", c=128)), reads=[pk], writes=[("rktok", t) for t in g])
                        wt, wk = load_w(wrot, w_in[l][:, RV + h * 128:RV + (h + 1) * 128])
                        for t in ALLT:
                            ps, pk = PSS.get()
                            bld.mm(ps, [(HT[:, k, t * 128:(t + 1) * 128], wt[:, k, :]) for k in range(KD)], [wk, ("HT", t)], pk)
                            P.op("act", lambda e, t=t, ps=ps: e.copy(vtok[:, t, :], ps), reads=[pk], writes=[("rvtok", t)])
                        wt, wk = load_w(wrot, w_in[l][:, RG + h * 128:RG + (h + 1) * 128])
                        for t in tiles:
                            ps, pk = PSS.get()
                            bld.mm(ps, [(HT[:, k, t * 128:(t + 1) * 128], wt[:, k, :]) for k in range(KD)], [wk, ("HT", t)], pk)
                            P.op("act", lambda e, t=t, ps=ps: e.activation(zs[:, t, :], ps, AF.Silu), reads=[pk], writes=[("rzs", t)])
                            P.op("pool", lambda e, t=t, h=h: e.tensor_tensor(out=zs[:, t, :], in0=zs[:, t, :], in1=rnw(l, h), op=ALU.mult), reads=[("rzs", t), "rws"], writes=[("rzs", t)])
                        S32 = [bld.sb(sh, "rS32_%d" % d, [128, 128], F32) for d in range(2)]
                        Sb = [bld.rot(sh, "rSb%d" % d, [128, 128], BF16, 2) for d in range(2)]
                        order = [list(range(NT)), [1, 0] + list(range(NT - 1, 1, -1))]
                        cur = [None, None]
                        for d in range(2):
                            P.op("pool", lambda e, d=d, S32=S32: e.memset(S32[d][:], 0.0), writes=[("rS32", d)])
                            sbt, sbk = Sb[d].get()
                            P.op("pool", lambda e, sbt=sbt: e.memset(sbt[:], 0.0), writes=[sbk])
                            cur[d] = (sbt, sbk)
                        visited = set()
                        atA = bld.sb(sh, "r_atA", [128, 2 * NT, 128], BF16)
                        qwA = bld.sb(sh, "r_qwA", [128, 2 * NT, 128], BF16)
                        kwA = bld.sb(sh, "r_kwA", [128, 2 * NT, 128], BF16)
                        for c in ALLT:
                            tcol = slice(c * 128, (c + 1) * 128)
                            cb = min(c // 4, 4)
                            psQ, kQ = PSS.get()
                            bld.mm(psQ, [(kT[:, tcol], qT[:, tcol])], [("rT", 0, cb), ("rT", 1, cb)], kQ)
                            for d in range(2):
                                q = d * 4 + h
                                ix = d * NT + c
                                P.op("dve", lambda e, psQ=psQ, q=q, ix=ix, atA=atA: e.tensor_tensor(out=atA[:, ix, :], in0=psQ, in1=DTm(q), op=ALU.mult), reads=[kQ, "rcs"], writes=[("r_at", ix)])
                                P.op("dve" if d == 0 else "pool", lambda e, tcol=tcol, q=q, ix=ix, qwA=qwA: e.tensor_tensor(out=qwA[:, ix, :], in0=qT[:, tcol], in1=GWm(q), op=ALU.mult), reads=[("rT", 0, cb), "rcs"], writes=[("r_qw", ix)])
                                P.op("act", lambda e, c=c, q=q, ix=ix, kwA=kwA: e.activation(kwA[:, ix, :], ktok[:, c, :], AF.Copy, scale=kwc(q)), reads=[("rktok", c), "rcs"], writes=[("r_kw", ix)])
                        for s_ in range(NT):
                            for d in range(2):
                                c = order[d][s_]
                                q = d * 4 + h
                                ix = d * NT + c
                                sbt, sbk = cur[d]
                                pso, ko = PSS.get()
                                bld.mm(pso, [(qwA[:, ix, :], sbt[:]), (atA[:, ix, :], vtok[:, c, :])], [("r_qw", ix), sbk, ("r_at", ix), ("rvtok", c)], ko)
                                pss_, ks = PSS.get()
                                bld.mm(pss_, [(kwA[:, ix, :], vtok[:, c, :])], [("r_kw", ix), ("rvtok", c)], ks)
                                P.op("dve", lambda e, d=d, q=q, pss_=pss_, S32=S32: e.scalar_tensor_tensor(out=S32[d][:], in0=S32[d][:], scalar=float(RET_CDEC[q]), in1=pss_, op0=ALU.mult, op1=ALU.add), reads=[("rS32", d), ks], writes=[("rS32", d)])
                                nsb, nsbk = Sb[d].get()
                                P.op("act", lambda e, nsb=nsb, d=d, S32=S32: e.copy(nsb[:], S32[d][:]), reads=[("rS32", d)], writes=[nsbk])
                                cur[d] = (nsb, nsbk)
                                if c in visited:
                                    P.op("dve", lambda e, c=c, pso=pso, Oacc=Oacc: e.tensor_tensor(out=Oacc[:, c, :], in0=pso, in1=Oacc[:, c, :], op=ALU.add), reads=[ko, ("rO", c)], writes=[("rO", c)])
                                else:
                                    visited.add(c)
                                    P.op("act", lambda e, c=c, pso=pso, Oacc=Oacc: e.copy(Oacc[:, c, :], pso), reads=[ko], writes=[("rO", c)])
                        head_out_norm(sh, l, h, Oacc, zs, None, 2, tiles, "r")
                    P.barrier()
        def phase_gates(l, tiles):
            with contextlib.ExitStack() as st:
                wrot = bld.rot(st, "gtw", [128, KD, 512], BF16, 2)
                gb = bld.rot(st, "gtb", [128, 512], BF16, 4)
                for cb in range(6):
                    wt, wk = load_w(wrot, w_in[l][:, GATE + cb * 512:GATE + (cb + 1) * 512])
                    for t in tiles:
                        ps, pk = PSB.get()
                        bld.mm(ps[:], [(HT[:, k, t * 128:(t + 1) * 128], wt[:, k, :]) for k in range(KD)], [wk, ("HT", t)], pk)
                        g, gk = gb.get()
                        P.op("act", lambda e, g=g, ps=ps: e.activation(g[:], ps[:], AF.Sigmoid), reads=[pk], writes=[gk])
                        P.dma("sp", gates_s[t * 128:(t + 1) * 128, cb * 512:(cb + 1) * 512], g[:], reads=[gk], writes=[("gates", t, cb)])
                P.barrier()

        def phase_merge(l, tiles, xsrc):
            stw = contextlib.ExitStack()
            wo = bld.sb(stw, "wo", [128, KD, D], BF16)
            P.dma("pool", wo[:], w_out[l].rearrange("(k p) c -> p k c", p=128), writes=["wo"])
            with contextlib.ExitStack() as st:
                wbr = [bld.sb(st, "wbr%d" % b, [128, 4, D], BF16) for b in range(3)]
                for b in range(3):
                    P.dma("pool", wbr[b][:], w_br[b][l].rearrange("(k p) c -> p k c", p=128), writes=[("wbr", b)])
                gt = bld.rot(st, "mgt", [128, 3 * D], BF16, 2)
                ot = bld.rot(st, "mot", [128, 3, 4, 128], BF16, 2)
                t32 = bld.rot(st, "mt32", [128, 512], F32, 4)
                mb = bld.rot(st, "mmb", [128, D], BF16, 2)
                for t in tiles:
                    g, gk = gt.get()
                    P.dma("sp", g[:], gates_s[t * 128:(t + 1) * 128, :], reads=[("gates", t, cb) for cb in range(6)], writes=[gk])
                    o_, ok_ = ot.get()
                    for b in range(3):
                        P.dma("sp", o_[:, b, :, :], oT_s[b][:, :, t * 128:(t + 1) * 128].rearrange("h p c -> p h c"), reads=[("oTs", b, h) for h in range(4)], writes=[(ok_, b)])
                    m, mk = mb.get()
                    for half in range(2):
                        acc, acck = t32.get()
                        for b in range(3):
                            ps, pk = PSB.get()
                            bld.mm(ps[:], [(o_[:, b, k, :], wbr[b][:, k, half * 512:(half + 1) * 512]) for k in range(4)], [(ok_, b), ("wbr", b)], pk)
                            gsl = g[:, b * D + half * 512: b * D + (half + 1) * 512]
                            if b == 0:
                                P.op("dve", lambda e, acc=acc, ps=ps, gsl=gsl: e.tensor_tensor(out=acc[:], in0=ps[:], in1=gsl, op=ALU.mult), reads=[pk, gk], writes=[acck])
                            else:
                                tmp, tk = t32.get()
                                P.op("dve", lambda e, tmp=tmp, ps=ps, gsl=gsl: e.tensor_tensor(out=tmp[:], in0=ps[:], in1=gsl, op=ALU.mult), reads=[pk, gk], writes=[tk])
                                if b == 1:
                                    P.op("pool", lambda e, acc=acc, tmp=tmp: e.tensor_tensor(out=acc[:], in0=acc[:], in1=tmp[:], op=ALU.add), reads=[acck, tk], writes=[acck])
                                else:
                                    P.op("pool", lambda e, acc=acc, tmp=tmp, m=m, half=half: e.tensor_tensor(out=m[:, half * 512:(half + 1) * 512], in0=acc[:], in1=tmp[:], op=ALU.add), reads=[acck, tk], writes=[(mk, half)])
                    ps, pk = PSB.get()
                    psv = ps[:].bitcast(BF16)
                    for k in range(KD):
                        bld.tr(psv[:, k * 128:(k + 1) * 128], m[:, k * 128:(k + 1) * 128], identb[:], [(mk, 0), (mk, 1), "identb"], pk)
                    P.op("act", lambda e, t=t, psv=psv: e.copy(HT[:, :, t * 128:(t + 1) * 128], psv[:, 0:1024].rearrange("p (k c) -> p k c", c=128)), reads=[pk], writes=[("HT", t)])
                P.barrier()
            with contextlib.ExitStack() as st:
                residual_phase(st, l, tiles, xsrc, 0, lambda t, half: [(HT[:, k, t * 128:(t + 1) * 128], wo[:, k, half * 512:(half + 1) * 512]) for k in range(KD)],
                               lambda t: [("HT", t), "wo"])
                P.barrier()
            stw.close()

        def residual_phase(st, l, tiles, xsrc, which, pairs_of, reads_of):
            xb = bld.rot(st, "rx", [128, D], F32, 3)
            yb = bld.rot(st, "ry", [128, D], F32, 2)
            for t in tiles:
                s = tile_stream(t)
                xt, xk = xb.get()
                P.dma("sp", xt[:], xsrc[t * 128:(t + 1) * 128, :], reads=[("xs", t)], writes=[xk])
                y, yk = yb.get()
                for half in range(2):
                    ps, pk = PSB.get()
                    bld.mm(ps[:], pairs_of(t, half), reads_of(t), pk)
                    sl = slice(half * 512, (half + 1) * 512)
                    P.op("dve", lambda e, y=y, ps=ps, sl=sl, s=s: e.tensor_tensor(out=y[:, sl], in0=ps[:], in1=grow[:, s, which, sl], op=ALU.mult),
                         reads=[pk] + [("grow", s, 2 if which == 0 else 5, h_) for h_ in range(2)], writes=[(yk, half)])
                    P.op("pool", lambda e, y=y, xt=xt, sl=sl: e.tensor_tensor(out=y[:, sl], in0=y[:, sl], in1=xt[:, sl], op=ALU.add), reads=[(yk, half), xk], writes=[(yk, half)])
                P.dma("pool", xs[t * 128:(t + 1) * 128, :], y[:], reads=[(yk, 0), (yk, 1)], writes=[("xs", t)])

        def phase_ffn(l, tiles):
            t_lo = tiles[0] * 128
            stw = contextlib.ExitStack()
            wd = bld.sb(stw, "wd", [128, NJ, D], BF16)
            with contextlib.ExitStack() as st:
                wrot = bld.rot(st, "fw", [128, KD, 512], BF16, 3)
                wcur = {}
                W = T + 3
                off = lambda t0: t0 + 1 if t0 < 256 else t0 + 2
                raw = bld.rot(st, "fraw", [128, W], F32, 3)
                cv = bld.rot(st, "fcv", [128, W], F32, 2)
                aT = bld.rot(st, "faT", [128, T], BF16, 2)
                blks = BLKS if t_lo == 0 else [(256 + i * 512, 512) for i in range(4)]
                for rw_ in raw.tiles:
                    P.op("pool", lambda e, rw_=rw_: e.memset(rw_[:], 0.0), writes=[("fraw", raw.tiles.index(rw_))])
                for j in range(NJ):
                    cvs = []
                    for gi, c0 in enumerate((j * 128, DFF + j * 128)):
                        ch = c0 // 128
                        if j % 4 == 0:
                            ng = min(4, NJ - j)
                            wtf, wk = wrot.get()
                            P.dma("pool", wtf[:, :, 0:ng * 128], w_up[l][:, c0:c0 + ng * 128].rearrange("(k p) c -> p k c", p=128), writes=[wk])
                            wcur[gi] = (wtf, wk)
                        wt, wk = wcur[gi]
                        rw, rk = raw.get()
                        def ev(bi, t0, n, ps, pk, rw=rw, rk=rk):
                            if t0 == 0:
                                P.op("act", lambda e: e.copy(rw[:, 1:257], ps[:, 0:256]), reads=[pk], writes=[rk])
                                P.op("dve", lambda e: e.tensor_copy(rw[:, 258:514], ps[:, 256:512]), reads=[pk], writes=[rk])
                            elif bi % 2:
                                P.op("act", lambda e: e.copy(rw[:, off(t0):off(t0) + n], ps[:, 0:n]), reads=[pk], writes=[rk])
                            else:
                                P.op("dve", lambda e: e.tensor_copy(rw[:, off(t0):off(t0) + n], ps[:, 0:n]), reads=[pk], writes=[rk])
                        proj_fm(wt, wk, hT_rhs, KD, ev, blks=blks, coff=(j % 4) * 128)
                        c, ck = cv.get()
                        lo = off(t_lo)
                        P.op("act", lambda e, c=c, rw=rw, ch=ch: e.activation(c[:, lo:W - 1], rw[:, lo:W - 1], AF.Identity, bias=fconvb(l, ch), scale=fconv(l, ch, 1)), reads=[rk, "spp"], writes=[ck])
                        P.op("dve", lambda e, c=c, rw=rw, ch=ch: e.scalar_tensor_tensor(out=c[:, lo:W - 1], in0=rw[:, lo - 1:W - 2], scalar=fconv(l, ch, 0), in1=c[:, lo:W - 1], op0=ALU.mult, op1=ALU.add), reads=[rk, ck, "spp"], writes=[ck])
                        P.op("dve", lambda e, c=c, rw=rw, ch=ch: e.scalar_tensor_tensor(out=c[:, lo:W - 1], in0=rw[:, lo + 1:W], scalar=fconv(l, ch, 2), in1=c[:, lo:W - 1], op0=ALU.mult, op1=ALU.add), reads=[rk, ck, "spp"], writes=[ck])
                        cvs.append((c, ck))
                    (cg, cgk), (cval, cvk) = cvs
                    P.op("act", lambda e, cg=cg: e.activation(cg[:, lo:W - 1], cg[:, lo:W - 1], AF.Silu), reads=[cgk], writes=[cgk])
                    a, ak = aT.get()
                    if t_lo == 0:
                        P.op("dve", lambda e, a=a, cg=cg, cval=cval: e.tensor_tensor(out=a[:, 0:256], in0=cg[:, 1:257], in1=cval[:, 1:257], op=ALU.mult), reads=[cgk, cvk], writes=[(ak, 0)])
                    P.op("dve", lambda e, a=a, cg=cg, cval=cval: e.tensor_tensor(out=a[:, 256:T], in0=cg[:, 258:W - 1], in1=cval[:, 258:W - 1], op=ALU.mult), reads=[cgk, cvk], writes=[(ak, 1)])
                    P.dma("sp", aT_s[tiles[0]:NT, :, j, :].rearrange("t p c -> p t c"), a[:, t_lo:T].rearrange("p (t c) -> p t c", c=128), reads=[(ak, 0), (ak, 1)], writes=[("aTs", j)])
                    P.dma("pool", wd[:, j, :], w_down[l][j * 128:(j + 1) * 128, :], writes=[("wd", j)])
                P.barrier()
            with contextlib.ExitStack() as st:
                ab_ = bld.rot(st, "fab", [128, NJ, 128], BF16, 2)
                cur = {}
                def pairs_of(t, half):
                    if half == 0:
                        a, ak = ab_.get()
                        P.dma("sp", a[:], aT_s[t], reads=[("aTs", j) for j in range(NJ)], writes=[ak])
                        cur[t] = (a, ak)
                    a, ak = cur[t]
                    return [(a[:, j, :], wd[:, j, half * 512:(half + 1) * 512]) for j in range(NJ)]
                residual_phase(st, l, tiles, xs, 1, pairs_of, lambda t: [cur[t][1]] + [("wd", j) for j in range(NJ)])
                P.barrier()
            stw.close()

        def phase_final():
            with contextlib.ExitStack() as st:
                xb = bld.rot(st, "fx", [128, D], F32, 3)
                junk = bld.sb(st, "fjunk", [128, D], BF16)
                ssb = bld.rot(st, "fss", [128, 1], F32, 4)
                ob = bld.rot(st, "fo", [128, D], F32, 3)
                for t in range(2, NT):
                    xt, xk = xb.get()
                    P.dma("sp", xt[:], xs[t * 128:(t + 1) * 128, :], reads=[("xs", t)], writes=[xk])
                    ss, sk = ssb.get()
                    P.op("act", lambda e, xt=xt, ss=ss: e.activation(junk[:], xt[:], AF.Square, accum_out=ss[:]), reads=[xk], writes=["fjunk", sk])
                    P.op("act", lambda e, ss=ss: e.activation(ss[:], ss[:], AF.Sqrt, bias=epsc(0)), reads=[sk, "c32"], writes=[sk])
                    P.op("dve", lambda e, ss=ss: e.reciprocal(ss[:], ss[:]), reads=[sk], writes=[sk])
                    P.op("dve", lambda e, ss=ss: e.tensor_scalar(out=ss[:], in0=ss[:], scalar1=float(math.sqrt(D)), scalar2=None, op0=ALU.mult), reads=[sk], writes=[sk])
                    o_, ok_ = ob.get()
                    P.op("dve", lambda e, o_=o_, xt=xt, ss=ss: e.scalar_tensor_tensor(out=o_[:], in0=xt[:], scalar=ss[:], in1=fnw, op0=ALU.mult, op1=ALU.mult), reads=[xk, sk, "rws"], writes=[ok_])
                    P.dma("pool", out[(t - 2) * 128:(t - 1) * 128, :], o_[:], reads=[ok_], writes=[("out", t)])
        P.barrier()
        for l in range(2):
            ctx_out = (l == 0)
            tiles = ALLT if ctx_out else list(range(2, NT))
            xsrc = xin if l == 0 else xs
            phase_mod(l)
            if stop_after == "mod":
                dump("modpp", modpp[:], ["modpp"])
                break
            phase_norm(l, 0, xsrc, ALLT)
            if l == 0:
                dump("HT", HT[:], HTk(ALLT))
                dump("modpp", modpp[:], ["modpp"])
                dump("grow", grow[:], [("grow", s_, v_, h_) for s_ in range(2) for v_ in (2, 5) for h_ in range(2)])
            if stop_after == "norm":
                break
            if stop_after in (None, "all", "gdn", "merge", "ffn"):
                phase_gdn(l, tiles)
                if l == 0:
                    dump("oTa", oT_s[0], [("oTs", 0, h_) for h_ in range(4)])
                if stop_after == "gdn":
                    break
            if stop_after in (None, "all", "mla", "merge", "ffn"):
                phase_mla(l, ctx_out)
                if l == 0:
                    dump("oTb", oT_s[1], [("oTs", 1, h_) for h_ in range(4)])
                if stop_after == "mla":
                    break
            if stop_after in (None, "all", "ret", "merge", "ffn"):
                phase_ret(l, tiles)
                if l == 0:
                    dump("oTc", oT_s[2], [("oTs", 2, h_) for h_ in range(4)])
                if stop_after == "ret":
                    break
            phase_gates(l, tiles)
            phase_merge(l, tiles, xsrc)
            if l == 0:
                dump("xmid", xs, [("xs", t_) for t_ in ALLT])
            if stop_after == "merge":
                break
            phase_norm(l, 1, xs, tiles)
            phase_ffn(l, tiles)
            if l == 0:
                dump("xl0", xs, [("xs", t_) for t_ in ALLT])
            if stop_after == "ffn":
                break
        if stop_after in (None, "all"):
            phase_final()
        P.emit(final_reads=[("out", t) for t in range(2, NT)] + bld.dbg_keys)
    return nc, P.stats


def _rope_tables(d):
    n = 2048
    rows_ = n // 64
    r = np.repeat(np.arange(rows_, dtype=np.float32), 64)
    col = np.tile(np.arange(64, dtype=np.float32), rows_)
    quarter = d // 4
    inv = (np.float32(10000.0) ** (-np.arange(quarter, dtype=np.float32) / np.float32(quarter))).astype(np.float32)
    ang = np.concatenate([r[:, None] * inv, col[:, None] * inv], axis=-1).astype(np.float32)
    cos = np.cos(ang).astype(np.float32)
    sin = np.sin(ang).astype(np.float32)
    C = np.ones((d, T), np.float32)
    S = np.zeros((d, T), np.float32)
    C[:, 256:] = np.concatenate([cos, cos], axis=1).T
    S[:, 256:] = np.concatenate([sin, sin], axis=1).T
    return np.stack([C, S])


def _rot_mat(d):
    R = np.zeros((d, d), np.float32)
    h = d // 2
    for m in range(h):
        R[m + h, m] = -1.0
    for m in range(h, d):
        R[m - h, m] = 1.0
    return R


def _host_consts():
    c = np.zeros((128, 128 * 6 + 8), np.float32)
    idx = np.arange(128)
    k = idx[:, None]
    i = idx[None, :]
    c[:, 0:128] = np.eye(128)
    c[:, 128:256] = 1.0
    c[:, 256:384] = (k <= i)
    c[:, 384:512] = (k >= i)
    c[:, 512:640] = np.where(i > k, 0.0, NEGBIG)
    c[:, 640:768] = np.where(i < k, 0.0, NEGBIG)
    c[0, 768] = 1.0
    c[:, 769] = 1024 * EPS; c[:, 770] = 128 * EPS; c[:, 771] = EPS; c[:, 772] = 384 * EPS; c[:, 773] = 256 * EPS
    rm = np.zeros((128, 192), np.float32)
    rm[:, 0:128] = _rot_mat(128)
    rm[0:64, 128:192] = _rot_mat(64)
    hh = np.arange(4, dtype=np.float64)
    lg = np.stack([np.log1p(-(2.0 ** (-(5.0 + hh + 0.5 * d)))) for d in range(2)])
    rc = np.zeros((128, 8 * 128 * 2 + 8), np.float64)
    jj = idx[:, None].astype(np.float64)
    ii = idx[None, :].astype(np.float64)
    cdec = []
    for d in range(2):
        for h in range(4):
            g = lg[d, h]
            q = d * 4 + h
            if d == 0:
                DT = np.where(ii >= jj, np.exp(g * np.maximum(ii - jj, 0)), 0.0)
                GW = np.exp(g * (ii + 1)) * np.ones((128, 1))
                kw = np.exp(g * (127 - idx))
            else:
                DT = np.where(ii <= jj, np.exp(g * np.maximum(jj - ii, 0)), 0.0)
                GW = np.exp(g * (128 - ii)) * np.ones((128, 1))
                kw = np.exp(g * idx)
            rc[:, q * 128:(q + 1) * 128] = DT
            rc[:, 1024 + q * 128: 1024 + (q + 1) * 128] = GW
            rc[:, 2048 + q] = kw
            cdec.append(float(np.exp(g * 128)))
    return c, rm, rc.astype(np.float32), cdec


RET_CDEC = _host_consts()[3]
_NC_CACHE = {}


def _prep_common(inp):
    f = lambda a: np.ascontiguousarray(np.asarray(a, dtype=np.float32))
    pp = lambda v, nch: v.reshape(nch, 128).T
    sm = []
    n1, n2 = f(inp["norm1_w"]), f(inp["norm2_w"])
    sm.append(np.concatenate([pp(n1[l], 8) for l in range(2)], axis=1))
    sm.append(np.concatenate([pp(n2[l], 8) for l in range(2)], axis=1))
    gc = f(inp["gdn_conv_w"])
    sm.append(np.concatenate([gc[l].reshape(3, 12, 128).transpose(2, 1, 0).reshape(128, 36) for l in range(2)], axis=1))
    fc = f(inp["ffn_conv_w"])
    sm.append(np.concatenate([fc[l].reshape(3, 44, 128).transpose(2, 1, 0).reshape(128, 132) for l in range(2)], axis=1))
    fb = f(inp["ffn_conv_b"])
    sm.append(np.concatenate([pp(fb[l], 44) for l in range(2)], axis=1))
    qn_, kvn_ = f(inp["mla_q_norm"]), f(inp["mla_kv_norm"])
    sm.append(np.concatenate([pp(qn_[l], 3) for l in range(2)], axis=1))
    sm.append(np.concatenate([pp(kvn_[l], 2) for l in range(2)], axis=1))
    smallpp = np.ascontiguousarray(np.concatenate(sm, axis=1))
    rep = lambda v: np.broadcast_to(v[None, :], (128, v.shape[0]))
    rw = [rep(f(inp["gdn_norm_w"]).reshape(-1)), rep(f(inp["ret_norm_w"]).reshape(-1)),
          rep(f(inp["gdn_A_log"]).reshape(-1)), rep(f(inp["gdn_dt_bias"]).reshape(-1)), rep(f(inp["final_norm_w"]))]
    rows = np.ascontiguousarray(np.concatenate(rw, axis=1))
    c, rm, rc, _ = _host_consts()
    common = dict(smallpp=smallpp, rows=rows, consts=c, rmats=rm, retc=rc,
                  ropem=_rope_tables(64), roper=_rope_tables(128))
    for k in ("ada_w", "ada_b", "w_in", "mla_w_uq", "mla_w_ukv", "w_br_gdn", "w_br_mla", "w_br_ret", "w_out",
              "ffn_w_up", "ffn_w_down"):
        common[k] = f(inp[k])
    return common


def _prep_core(inp, b):
    f = lambda a: np.ascontiguousarray(np.asarray(a, dtype=np.float32))
    xin = np.concatenate([f(inp["ctx"][b]), f(inp["x"][b])], axis=0)
    def crep_of(v):
        return np.broadcast_to(v.reshape(8, 128).T[:, :, None], (128, 8, 128)).reshape(128, 1024)
    crep = np.stack([crep_of(f(inp["c_ctx"])), crep_of(f(inp["c"][b]))])
    return dict(xin=np.ascontiguousarray(xin), crep=np.ascontiguousarray(crep))


def kernel(**inputs):
    if "nc" not in _NC_CACHE:
        _NC_CACHE["nc"] = build()[0]
    nc = _NC_CACHE["nc"]
    common = _prep_common(inputs)
    in_maps = []
    for b in range(NCORES):
        m = dict(common)
        m.update(_prep_core(inputs, b))
        in_maps.append(m)
    res = run_bass_kernel_spmd(nc, in_maps, core_ids=list(range(NCORES)))
    return np.stack([np.asarray(r["out"], dtype=np.float32) for r in res.results], axis=0)
```
